# Optimizing a Trainium2 kernel written in Bass

```python
import math
import jax, jax.numpy as jnp
from jax import lax
import numpy as np

D_MODEL = 2048
BATCH = 2
SEQ = 4096
DEPTH = 1
DEC_BATCH = 128
DEC_SEQ = 4
PAST_LEN = 16384
PAGE_SIZE = 128

SSM_WIDTH = D_MODEL // 2
GROUP_CH = 16
N_GROUPS = SSM_WIDTH // GROUP_CH
STATE_DIM = 64
HEAD_DIM = 64
N_HEADS = (D_MODEL // 2) // HEAD_DIM
N_KV_HEADS = 4
KV_GROUP = N_HEADS // N_KV_HEADS
ATTN_WIDTH = N_HEADS * HEAD_DIM
KV_WIDTH = N_KV_HEADS * HEAD_DIM
WINDOW = 128
NUM_BUCKETS = 32
MAX_DISTANCE = 128
FFN_HIDDEN = -(-8 * D_MODEL // (3 * 256)) * 256
IN_SPLITS = [SSM_WIDTH, ATTN_WIDTH, KV_WIDTH, KV_WIDTH, 2 * D_MODEL]
IN_COLS = sum(IN_SPLITS)
EPS = 1e-6
NEG_INF = -1e30

kernel_name = 'hybrid_s5_swa_sink_gated_decoder_step'


def rmsnorm(x, g):
    xf = x.astype(jnp.float32)
    y = xf * lax.rsqrt(jnp.mean(xf * xf, axis=-1, keepdims=True) + EPS)
    return (y * g.astype(jnp.float32)).astype(x.dtype)


def t5_bucket(dist):
    n = np.maximum(dist, 0)
    max_exact = NUM_BUCKETS // 2
    large = max_exact + (np.log(np.maximum(n, 1) / max_exact) / np.log(MAX_DISTANCE / max_exact)
                         * (NUM_BUCKETS - max_exact)).astype(np.int32)
    large = np.minimum(large, NUM_BUCKETS - 1)
    return np.where(n < max_exact, n, large).astype(np.int32)


def _complex_affine_combine(e1, e2):
    a1r, a1i, b1r, b1i = e1
    a2r, a2i, b2r, b2i = e2
    ar = a1r * a2r - a1i * a2i
    ai = a1r * a2i + a1i * a2r
    br = a2r * b1r - a2i * b1i + b2r
    bi = a2r * b1i + a2i * b1r + b2i
    return (ar, ai, br, bi)


def ssm_scan(u, h0_re, h0_im, lam_re, lam_im, log_dt, b_re, b_im, c_re, c_im, d_skip):
    f32 = jnp.float32
    n, L = u.shape[:2]
    lr = jnp.minimum(lam_re.astype(f32), -1e-4)
    li = lam_im.astype(f32)
    dt = jnp.exp(log_dt.astype(f32))[:, None]
    mag = jnp.exp(lr * dt)
    a_re = mag * jnp.cos(li * dt)
    a_im = mag * jnp.sin(li * dt)
    den = lr * lr + li * li
    coef_re = ((a_re - 1.0) * lr + a_im * li) / den
    coef_im = (a_im * lr - (a_re - 1.0) * li) / den
    uf = u.astype(f32)
    bu_re = jnp.einsum('nlgc,gpc->nlgp', uf, b_re.astype(f32))
    bu_im = jnp.einsum('nlgc,gpc->nlgp', uf, b_im.astype(f32))
    x_re = coef_re * bu_re - coef_im * bu_im
    x_im = coef_re * bu_im + coef_im * bu_re
    shp = x_re.shape
    elems = (jnp.broadcast_to(a_re, shp), jnp.broadcast_to(a_im, shp), x_re, x_im)
    acum_re, acum_im, h_re, h_im = lax.associative_scan(_complex_affine_combine, elems, axis=1)
    if h0_re is not None:
        s_re = h0_re.astype(f32)[:, None]
        s_im = h0_im.astype(f32)[:, None]
        h_re, h_im = (h_re + acum_re * s_re - acum_im * s_im,
                      h_im + acum_re * s_im + acum_im * s_re)
    y = (jnp.einsum('nlgp,gcp->nlgc', h_re, c_re.astype(f32))
         - jnp.einsum('nlgp,gcp->nlgc', h_im, c_im.astype(f32))
         + d_skip.astype(f32).reshape(N_GROUPS, GROUP_CH) * uf)
    return y.reshape(n, L, SSM_WIDTH), h_re[:, -1], h_im[:, -1]


def window_attention(q, k, v, dist, valid, rel_bias, sinks):
    lead = q.shape[:-3]
    nq = q.shape[-3]
    qg = q.reshape(*lead, nq, N_KV_HEADS, KV_GROUP, HEAD_DIM)
    s = jnp.einsum('...qhgd,...khd->...hgqk', qg, k,
                   preferred_element_type=jnp.float32) * (HEAD_DIM ** -0.5)
    bias = rel_bias.astype(jnp.float32)[t5_bucket(dist)]
    bias = jnp.transpose(bias, (2, 0, 1)).reshape(N_KV_HEADS, KV_GROUP, *dist.shape)
    s = jnp.where(valid, s + bias, NEG_INF)
    sink = jnp.broadcast_to(sinks.astype(jnp.float32).reshape(N_KV_HEADS, KV_GROUP, 1, 1),
                            s.shape[:-1] + (1,))
    p = jax.nn.softmax(jnp.concatenate([s, sink], axis=-1), axis=-1)[..., :-1]
    o = jnp.einsum('...hgqk,...khd->...qhgd', p.astype(v.dtype), v)
    return o.reshape(*lead, nq, ATTN_WIDTH)


def prompt_window_attention(q, k, v, rel_bias, sinks):
    b, L = q.shape[:2]
    nb = L // WINDOW
    qb = q.reshape(b, nb, WINDOW, N_HEADS, HEAD_DIM)
    kb = k.reshape(b, nb, WINDOW, N_KV_HEADS, HEAD_DIM)
    vb = v.reshape(b, nb, WINDOW, N_KV_HEADS, HEAD_DIM)
    pad_k = jnp.zeros_like(kb[:, :1])
    pad_v = jnp.zeros_like(vb[:, :1])
    kk = jnp.concatenate([jnp.concatenate([pad_k, kb[:, :-1]], axis=1), kb], axis=2)
    vv = jnp.concatenate([jnp.concatenate([pad_v, vb[:, :-1]], axis=1), vb], axis=2)
    i = np.arange(WINDOW)[:, None]
    j = np.arange(2 * WINDOW)[None, :]
    dist = i + WINDOW - j
    kpos = (np.arange(nb)[:, None, None] - 1) * WINDOW + j
    valid = (dist >= 0) & (dist < WINDOW) & (kpos >= 0)
    o = window_attention(qb, kk, vv, dist, valid[:, None, None], rel_bias, sinks)
    return o.reshape(b, L, ATTN_WIDTH)


def sample_window_attention(q, k, v, k_buf, v_buf, rel_bias, sinks):
    w_buf = k_buf.shape[1]
    nq = q.shape[1]
    kk = jnp.concatenate([k_buf.astype(k.dtype), k], axis=1)
    vv = jnp.concatenate([v_buf.astype(v.dtype), v], axis=1)
    dist = np.arange(nq)[:, None] + w_buf - np.arange(w_buf + nq)[None, :]
    valid = (dist >= 0) & (dist < WINDOW)
    o = window_attention(q, kk, vv, dist, valid, rel_bias, sinks)
    return o, kk[:, -w_buf:], vv[:, -w_buf:]


def decoder_layer(x, h0_re, h0_im, k_buf, v_buf, rel_bias,
                  norm_attn, w_in, lam_re, lam_im, log_dt, b_re, b_im, c_re, c_im, d_skip,
                  w_glu_val, w_glu_gate, w_attn_br, sinks, w_out, norm_ffn, w_ffn_in, w_ffn_out):
    n, L, _ = x.shape
    h = rmsnorm(x, norm_attn)
    proj = h @ w_in
    u, q, k, v, gates = jnp.split(proj, list(np.cumsum(IN_SPLITS)[:-1]), axis=-1)
    y, st_re, st_im = ssm_scan(u.reshape(n, L, N_GROUPS, GROUP_CH), h0_re, h0_im,
                               lam_re, lam_im, log_dt, b_re, b_im, c_re, c_im, d_skip)
    g = jax.nn.gelu(y).astype(x.dtype)
    a_out = (g @ w_glu_val) * jax.nn.sigmoid(g @ w_glu_gate)
    q = q.reshape(n, L, N_HEADS, HEAD_DIM)
    k = k.reshape(n, L, N_KV_HEADS, HEAD_DIM)
    v = v.reshape(n, L, N_KV_HEADS, HEAD_DIM)
    if k_buf is None:
        w_buf = min(WINDOW, PAST_LEN)
        o = prompt_window_attention(q, k, v, rel_bias, sinks)
        new_k, new_v = k[:, L - w_buf:], v[:, L - w_buf:]
    else:
        o, new_k, new_v = sample_window_attention(q, k, v, k_buf, v_buf, rel_bias, sinks)
    b_out = o @ w_attn_br
    g_a, g_b = jnp.split(jax.nn.sigmoid(gates), 2, axis=-1)
    x = x + (g_a * a_out + g_b * b_out) @ w_out
    h2 = rmsnorm(x, norm_ffn)
    f_gate, f_up = jnp.split(h2 @ w_ffn_in, 2, axis=-1)
    x = x + (jax.nn.silu(f_gate) * f_up) @ w_ffn_out
    return x, st_re.astype(x.dtype), st_im.astype(x.dtype), new_k, new_v


def setup_inputs(seed: int = 0) -> dict:
    key = jax.random.key(seed)
    ks = iter(jax.random.split(key, 32))
    f32 = jnp.float32
    nrm = lambda shape, scale: jax.random.normal(next(ks), shape, f32) * scale
    w_buf = min(WINDOW, PAST_LEN)
    lam_im_base = jnp.pi * jnp.arange(STATE_DIM, dtype=f32)
    return {
        'x_prompt': nrm((BATCH, SEQ, D_MODEL), 1.0),
        'x_sample': nrm((DEC_BATCH, DEC_SEQ, D_MODEL), 1.0),
        'state_ssm_re': nrm((DEPTH, DEC_BATCH, N_GROUPS, STATE_DIM), 0.5),
        'state_ssm_im': nrm((DEPTH, DEC_BATCH, N_GROUPS, STATE_DIM), 0.5),
        'cache_win_k': nrm((DEPTH, DEC_BATCH, w_buf, N_KV_HEADS, HEAD_DIM), 1.0),
        'cache_win_v': nrm((DEPTH, DEC_BATCH, w_buf, N_KV_HEADS, HEAD_DIM), 1.0),
        'rel_bias': nrm((NUM_BUCKETS, N_HEADS), 0.2),
        'norm_attn': 1.0 + nrm((DEPTH, D_MODEL), 0.02),
        'w_in': nrm((DEPTH, D_MODEL, IN_COLS), D_MODEL ** -0.5),
        'lam_re': -0.5 * (1.0 + 0.1 * jax.random.uniform(next(ks), (DEPTH, N_GROUPS, STATE_DIM), f32)),
        'lam_im': lam_im_base + nrm((DEPTH, N_GROUPS, STATE_DIM), 0.01),
        'log_dt': jax.random.uniform(next(ks), (DEPTH, N_GROUPS), f32, math.log(1e-3), math.log(1e-1)),
        'b_re': nrm((DEPTH, N_GROUPS, STATE_DIM, GROUP_CH), (2 * GROUP_CH) ** -0.5),
        'b_im': nrm((DEPTH, N_GROUPS, STATE_DIM, GROUP_CH), (2 * GROUP_CH) ** -0.5),
        'c_re': nrm((DEPTH, N_GROUPS, GROUP_CH, STATE_DIM), STATE_DIM ** -0.5),
        'c_im': nrm((DEPTH, N_GROUPS, GROUP_CH, STATE_DIM), STATE_DIM ** -0.5),
        'd_skip': nrm((DEPTH, SSM_WIDTH), 1.0),
        'w_glu_val': nrm((DEPTH, SSM_WIDTH, D_MODEL), SSM_WIDTH ** -0.5),
        'w_glu_gate': nrm((DEPTH, SSM_WIDTH, D_MODEL), SSM_WIDTH ** -0.5),
        'w_attn_br': nrm((DEPTH, ATTN_WIDTH, D_MODEL), ATTN_WIDTH ** -0.5),
        'sinks': nrm((DEPTH, N_HEADS), 0.5),
        'w_out': nrm((DEPTH, D_MODEL, D_MODEL), D_MODEL ** -0.5),
        'norm_ffn': 1.0 + nrm((DEPTH, D_MODEL), 0.02),
        'w_ffn_in': nrm((DEPTH, D_MODEL, 2 * FFN_HIDDEN), D_MODEL ** -0.5),
        'w_ffn_out': nrm((DEPTH, FFN_HIDDEN, D_MODEL), FFN_HIDDEN ** -0.5),
        'norm_final': 1.0 + nrm((D_MODEL,), 0.02),
    }


def reference(x_prompt, x_sample, state_ssm_re, state_ssm_im, cache_win_k, cache_win_v, rel_bias,
              norm_attn, w_in, lam_re, lam_im, log_dt, b_re, b_im, c_re, c_im, d_skip,
              w_glu_val, w_glu_gate, w_attn_br, sinks, w_out, norm_ffn, w_ffn_in, w_ffn_out,
              norm_final):
    xp, xs = x_prompt, x_sample
    p_re, p_im, p_k, p_v = [], [], [], []
    s_re, s_im, s_k, s_v = [], [], [], []
    for l in range(DEPTH):
        lw = (norm_attn[l], w_in[l], lam_re[l], lam_im[l], log_dt[l], b_re[l], b_im[l],
              c_re[l], c_im[l], d_skip[l], w_glu_val[l], w_glu_gate[l], w_attn_br[l],
              sinks[l], w_out[l], norm_ffn[l], w_ffn_in[l], w_ffn_out[l])
        xp, hr, hi, nk, nv = decoder_layer(xp, None, None, None, None, rel_bias, *lw)
        p_re.append(hr); p_im.append(hi); p_k.append(nk); p_v.append(nv)
        xs, hr, hi, nk, nv = decoder_layer(xs, state_ssm_re[l], state_ssm_im[l],
                                           cache_win_k[l], cache_win_v[l], rel_bias, *lw)
        s_re.append(hr); s_im.append(hi); s_k.append(nk); s_v.append(nv)
    y_prompt = rmsnorm(xp, norm_final)
    y_sample = rmsnorm(xs, norm_final)
    return (y_prompt, y_sample,
            jnp.stack(p_re), jnp.stack(p_im), jnp.stack(p_k), jnp.stack(p_v),
            jnp.stack(s_re), jnp.stack(s_im), jnp.stack(s_k), jnp.stack(s_v))
```

```python
import numpy as np
import concourse.bass as bass
import concourse.mybir as mybir
from concourse.bass_utils import run_bass_kernel_spmd

F32 = mybir.dt.float32
BF16 = mybir.dt.bfloat16
I32 = mybir.dt.int32
AF = mybir.ActivationFunctionType
ALU = mybir.AluOpType
AX = mybir.AxisListType

D = 2048
NT = 1088
NPRE = 3072
NCH = 16
HID = 5632
INC = 6656
EPS = 1e-6
NEG = -1e30
BLKS = [(0, 512), (512, 512), (1024, 64)]
NDS = 24
ARENA_BYTES = 206 * 1024
DEBUG = False
STOP = None
ASTOP = None
DT_SIZE = {F32: 4, BF16: 2, I32: 4}


class Buf:
    __slots__ = ("name", "w", "r")

    def __init__(self, name="", guards=None):
        self.name = name
        self.w = None
        self.r = dict(guards) if guards else {}


def _merge(dst, src):
    for k, ev in src.items():
        if k not in dst or dst[k][1] < ev[1]:
            dst[k] = ev


class Tile:
    __slots__ = ("ap", "b", "off", "nbytes", "name")


class Arena:
    def __init__(self, nc):
        self.t = nc.alloc_sbuf_tensor("arena", [128, ARENA_BYTES // 4], F32)
        self.free_list = [[0, ARENA_BYTES, {}]]

    def alloc(self, name, shape, dt, nbufs=1, top=False):
        n = int(np.prod(shape)) * DT_SIZE[dt]
        n = (n + 63) // 64 * 64
        order = range(len(self.free_list) - 1, -1, -1) if top else range(len(self.free_list))
        for i in order:
            off, size, g = self.free_list[i]
            if size >= n:
                if size == n:
                    self.free_list.pop(i)
                elif top:
                    self.free_list[i] = [off, size - n, dict(g)]
                    off = off + size - n
                else:
                    self.free_list[i] = [off + n, size - n, dict(g)]
                t = Tile()
                t.name, t.off, t.nbytes = name, off, n
                v = self.t[:, off // 4:(off + n) // 4]
                if dt != F32:
                    v = v.bitcast(dt)
                ne = int(np.prod(shape))
                v = v[:, 0:ne]
                if len(shape) == 2:
                    v = v.rearrange("p (a b) -> p a b", a=shape[0])
                elif len(shape) == 3:
                    v = v.rearrange("p (a b c) -> p a b c", a=shape[0], b=shape[1])
                elif len(shape) == 4:
                    v = v.rearrange("p (a b c d) -> p a b c d", a=shape[0], b=shape[1], c=shape[2])
                t.ap = v
                if nbufs == 1:
                    t.b = Buf(name, g)
                else:
                    t.b = [Buf(f"{name}{j}", g) for j in range(nbufs)]
                return t
        raise RuntimeError(f"arena OOM for {name} ({n} B); free={[(o, s) for o, s, _ in self.free_list]}")

    def free(self, t):
        g = {}
        bufs = t.b if isinstance(t.b, list) else [t.b]
        for b in bufs:
            if b.w is not None:
                _merge(g, {b.w[0].num: b.w})
            _merge(g, b.r)
        self.free_list.append([t.off, t.nbytes, g])
        self.free_list.sort(key=lambda x: x[0])
        out = []
        for blk in self.free_list:
            if out and out[-1][0] + out[-1][1] == blk[0]:
                out[-1][1] += blk[1]
                _merge(out[-1][2], blk[2])
            else:
                out.append(blk)
        self.free_list = out


class Prog:
    def __init__(self, nc):
        self.nc = nc
        self.eng = {"pe": nc.tensor, "act": nc.scalar, "dve": nc.vector, "pool": nc.gpsimd, "sp": nc.sync}
        self.csem = {e: nc.alloc_semaphore(name=f"c_{e}") for e in self.eng}
        self.ccnt = {e: 0 for e in self.eng}
        self.dsems = [nc.alloc_semaphore(name=f"d_{i}") for i in range(NDS)]
        self.dcnt = [0] * NDS
        self.dnext = 0
        self.waited = {e: {} for e in self.eng}
        self.ninstr = 0

    def _wait(self, e, ev):
        sem, val = ev
        k = sem.num
        if self.waited[e].get(k, 0) >= val:
            return
        self.eng[e].wait_ge(sem, val)
        self.waited[e][k] = val

    def _deps(self, e, reads, writes, skip=None, relax=False):
        own = self.csem[e].num
        lim = self.ccnt[e] - 1

        def need(ev):
            if ev[0].num == skip:
                return False
            if relax and ev[0].num == own and ev[1] <= lim:
                return False
            return True
        for b in reads:
            if b.w is not None and need(b.w):
                self._wait(e, b.w)
        for b in writes:
            if b.w is not None and need(b.w):
                self._wait(e, b.w)
            for k, ev in b.r.items():
                if need(ev):
                    self._wait(e, ev)

    @staticmethod
    def _mark(ev, reads, writes):
        for b in reads:
            b.r[ev[0].num] = ev
        for b in writes:
            b.w = ev
            b.r = {}

    def op(self, e, fn, reads=(), writes=(), relax=False):
        self._deps(e, reads, writes, relax=relax)
        ins = fn(self.eng[e])
        self.ccnt[e] += 1
        self.ninstr += 1
        ins.then_inc(self.csem[e], 1)
        self._mark((self.csem[e], self.ccnt[e]), reads, writes)

    def group(self, e, fns, reads=(), writes=()):
        self._deps(e, reads, writes)
        ins = None
        for fn in fns:
            ins = fn(self.eng[e])
            self.ninstr += 1
        self.ccnt[e] += 1
        ins.then_inc(self.csem[e], 1)
        self._mark((self.csem[e], self.ccnt[e]), reads, writes)

    def pe(self, fns, reads=(), writes=()):
        self._deps("pe", reads, writes, skip=self.csem["pe"].num)
        ins = None
        for fn in fns:
            ins = fn(self.nc.tensor)
            self.ninstr += 1
        self.ccnt["pe"] += 1
        ins.then_inc(self.csem["pe"], 1)
        self._mark((self.csem["pe"], self.ccnt["pe"]), reads, writes)

    def dma(self, e, out, in_, reads=(), writes=(), **kw):
        i = self.dnext
        self.dnext = (i + 1) % NDS
        if self.dcnt[i] > 0:
            self._wait(e, (self.dsems[i], self.dcnt[i]))
        self._deps(e, reads, writes)
        ins = self.eng[e].dma_start(out=out, in_=in_, **kw)
        self.ninstr += 1
        self.dcnt[i] += 16
        ins.then_inc(self.dsems[i], 16)
        self._mark((self.dsems[i], self.dcnt[i]), reads, writes)

    def finish(self):
        for e in self.eng:
            if e != "sp" and self.ccnt[e] > 0:
                self._wait("sp", (self.csem[e], self.ccnt[e]))
        for i in range(NDS):
            if self.dcnt[i] > 0:
                self._wait("sp", (self.dsems[i], self.dcnt[i]))


def t5_bucket(dist):
    n = np.maximum(dist, 0)
    max_exact = 16
    large = max_exact + (np.log(np.maximum(n, 1) / max_exact) / np.log(128 / max_exact) * 16).astype(np.int32)
    large = np.minimum(large, 31)
    return np.where(n < max_exact, n, large).astype(np.int32)


class Ctx:
    pass


def share(t, buf):
    if isinstance(t.b, Buf) and t.b is not buf:
        _merge(buf.r, t.b.r)
    t.b = buf


def bl(b):
    return b if isinstance(b, list) else [b]


def range_reduce_sin(K, ang, bang, out, bout, shape, part=128):
    P, A = K.P, K.A
    ni = A.alloc("rr_i", shape, I32)
    nf = A.alloc("rr_f", shape, F32)
    C1 = 6.28125
    C2 = float(2 * np.pi - 6.28125)
    P.op("dve", lambda e: e.tensor_scalar(out=nf.ap, in0=ang, scalar1=float(1.0 / (2 * np.pi)), scalar2=None, op0=ALU.mult),
         reads=[bang], writes=[nf.b])
    P.op("dve", lambda e: e.tensor_copy(out=ni.ap, in_=nf.ap), reads=[nf.b], writes=[ni.b])
    P.op("dve", lambda e: e.tensor_copy(out=nf.ap, in_=ni.ap), reads=[ni.b], writes=[nf.b])
    P.op("dve", lambda e: e.scalar_tensor_tensor(out=ang, in0=nf.ap, scalar=-C1, in1=ang, op0=ALU.mult, op1=ALU.add),
         reads=[nf.b, bang], writes=[bang])
    P.op("dve", lambda e: e.scalar_tensor_tensor(out=ang, in0=nf.ap, scalar=-C2, in1=ang, op0=ALU.mult, op1=ALU.add),
         reads=[nf.b, bang], writes=[bang])
    P.op("dve", lambda e: e.tensor_scalar(out=ang, in0=ang, scalar1=3.1415925, scalar2=-3.1415925, op0=ALU.min, op1=ALU.max),
         reads=[bang], writes=[bang])
    P.op("act", lambda e: e.activation(out=out, in_=ang, func=AF.Sin), reads=[bang], writes=[bout])
    A.free(ni)
    A.free(nf)


def accurate_exp(K, x, bx, shape):
    P, A = K.P, K.A
    y = A.alloc("aexp_y", shape, F32)
    acc = A.alloc("aexp_a", shape, F32)
    P.op("dve", lambda e: e.tensor_scalar(out=y.ap, in0=x, scalar1=0.125, scalar2=None, op0=ALU.mult), reads=[bx], writes=[y.b])
    fact = [1.0]
    for i in range(1, 12):
        fact.append(fact[-1] * i)
    P.op("dve", lambda e: e.tensor_scalar(out=acc.ap, in0=y.ap, scalar1=1.0 / fact[11], scalar2=1.0 / fact[10], op0=ALU.mult, op1=ALU.add),
         reads=[y.b], writes=[acc.b])
    for i in range(9, -1, -1):
        P.op("dve", lambda e: e.tensor_tensor(out=acc.ap, in0=acc.ap, in1=y.ap, op=ALU.mult), reads=[acc.b, y.b], writes=[acc.b])
        P.op("dve", lambda e, i=i: e.tensor_scalar(out=acc.ap, in0=acc.ap, scalar1=float(1.0 / fact[i]), scalar2=None, op0=ALU.add),
             reads=[acc.b], writes=[acc.b])
    for _ in range(3):
        P.op("dve", lambda e: e.tensor_tensor(out=acc.ap, in0=acc.ap, in1=acc.ap, op=ALU.mult), reads=[acc.b], writes=[acc.b])
    P.op("dve", lambda e: e.tensor_copy(out=x, in_=acc.ap), reads=[acc.b, bx], writes=[bx])
    A.free(y)
    A.free(acc)


def ssm_tables(K):
    nc, P, I, A = K.nc, K.P, K.I, K.A
    PSB = K.psb
    lrG = A.alloc("lrG", [32], F32)
    liG = A.alloc("liG", [32], F32)
    dtG = A.alloc("dtG", [32], F32)
    bG = Buf("G")
    with nc.allow_non_contiguous_dma(reason="tiny transposed param loads"):
        for gh in range(2):
            rows = slice(64 * gh, 64 * gh + 64)
            P.dma("sp", lrG.ap[rows, :], I["lam_re"][32 * gh:32 * gh + 32, :].rearrange("g p -> p g"), writes=[bG])
            P.dma("sp", liG.ap[rows, :], I["lam_im"][32 * gh:32 * gh + 32, :].rearrange("g p -> p g"), writes=[bG])
            P.dma("sp", dtG.ap[rows, :], I["log_dt"][32 * gh:32 * gh + 32].unsqueeze(0).broadcast_to([64, 32]), writes=[bG])
    accurate_exp(K, dtG.ap, bG, [32])
    P.op("dve", lambda e: e.tensor_scalar(out=lrG.ap, in0=lrG.ap, scalar1=-1e-4, scalar2=None, op0=ALU.min), reads=[bG], writes=[bG])
    rho = A.alloc("rhoG", [32], F32)
    th = A.alloc("thG", [32], F32)
    P.op("dve", lambda e: e.tensor_tensor(out=rho.ap, in0=lrG.ap, in1=dtG.ap, op=ALU.mult), reads=[bG], writes=[bG])
    P.op("dve", lambda e: e.tensor_tensor(out=th.ap, in0=liG.ap, in1=dtG.ap, op=ALU.mult), reads=[bG], writes=[bG])
    tmps = [lrG, liG, dtG, rho, th]

    def TTv(o, a, b_, op, rd, wb):
        P.op("dve", lambda e: e.tensor_tensor(out=o, in0=a, in1=b_, op=op), reads=rd, writes=[wb])

    def TSv(o, a, s1, s2, op0, op1, rd, wb):
        if op1 is None:
            P.op("dve", lambda e: e.tensor_scalar(out=o, in0=a, scalar1=s1, scalar2=None, op0=op0), reads=rd, writes=[wb])
        else:
            P.op("dve", lambda e: e.tensor_scalar(out=o, in0=a, scalar1=s1, scalar2=s2, op0=op0, op1=op1), reads=rd, writes=[wb])

    b1 = Buf("a1")
    nm = lambda n: A.alloc(n, [32], F32)
    r_, x2, sn, cs, t_, ex, a1r, a1i = [nm(n) for n in ("r_", "x2", "sn", "cs", "t_", "ex", "a1r", "a1i")]
    ni = A.alloc("ni", [32], I32)
    for t in (r_, x2, sn, cs, t_, ex, a1r, a1i, ni):
        share(t, b1)
    tmps.extend([r_, x2, sn, cs, t_, ex, a1r, a1i, ni])
    C1 = 6.28125
    C2 = float(2 * np.pi - 6.28125)
    TSv(t_.ap, th.ap, float(1.0 / (2 * np.pi)), None, ALU.mult, None, [bG], b1)
    P.op("dve", lambda e: e.tensor_copy(out=ni.ap, in_=t_.ap), reads=[b1], writes=[b1])
    P.op("dve", lambda e: e.tensor_copy(out=t_.ap, in_=ni.ap), reads=[b1], writes=[b1])
    P.op("dve", lambda e: e.scalar_tensor_tensor(out=r_.ap, in0=t_.ap, scalar=-C1, in1=th.ap, op0=ALU.mult, op1=ALU.add), reads=[b1, bG], writes=[b1])
    P.op("dve", lambda e: e.scalar_tensor_tensor(out=r_.ap, in0=t_.ap, scalar=-C2, in1=r_.ap, op0=ALU.mult, op1=ALU.add), reads=[b1], writes=[b1])
    TSv(r_.ap, r_.ap, 0.125, None, ALU.mult, None, [b1], b1)
    TTv(x2.ap, r_.ap, r_.ap, ALU.mult, [b1], b1)
    TSv(sn.ap, x2.ap, 1.0 / 362880, -1.0 / 5040, ALU.mult, ALU.add, [b1], b1)
    for cf in (1.0 / 120, -1.0 / 6, 1.0):
        TTv(sn.ap, sn.ap, x2.ap, ALU.mult, [b1], b1)
        TSv(sn.ap, sn.ap, float(cf), None, ALU.add, None, [b1], b1)
    TTv(sn.ap, sn.ap, r_.ap, ALU.mult, [b1], b1)
    TSv(cs.ap, x2.ap, -1.0 / 3628800, 1.0 / 40320, ALU.mult, ALU.add, [b1], b1)
    for cf in (-1.0 / 720, 1.0 / 24, -0.5, 1.0):
        TTv(cs.ap, cs.ap, x2.ap, ALU.mult, [b1], b1)
        TSv(cs.ap, cs.ap, float(cf), None, ALU.add, None, [b1], b1)
    for _ in range(3):
        TTv(t_.ap, sn.ap, cs.ap, ALU.mult, [b1], b1)
        TTv(x2.ap, sn.ap, sn.ap, ALU.mult, [b1], b1)
        TSv(sn.ap, t_.ap, 2.0, None, ALU.mult, None, [b1], b1)
        TSv(cs.ap, x2.ap, -2.0, 1.0, ALU.mult, ALU.add, [b1], b1)
    TSv(ex.ap, rho.ap, 1.0 / 120, 1.0 / 24, ALU.mult, ALU.add, [bG], b1)
    for cf in (1.0 / 6, 0.5, 1.0, 1.0):
        TTv(ex.ap, ex.ap, rho.ap, ALU.mult, [b1, bG], b1)
        TSv(ex.ap, ex.ap, float(cf), None, ALU.add, None, [b1], b1)
    TTv(a1r.ap, ex.ap, cs.ap, ALU.mult, [b1], b1)
    TTv(a1i.ap, ex.ap, sn.ap, ALU.mult, [b1], b1)
    b128 = Buf("a128")
    a128r, a128i, q1, q2 = [nm(n) for n in ("a128r", "a128i", "q1", "q2")]
    for t in (a128r, a128i, q1, q2):
        share(t, b128)
    tmps.extend([a128r, a128i, q1, q2])
    P.op("dve", lambda e: e.tensor_copy(out=a128r.ap, in_=a1r.ap), reads=[b1], writes=[b128])
    P.op("dve", lambda e: e.tensor_copy(out=a128i.ap, in_=a1i.ap), reads=[b1], writes=[b128])
    K.Rm = A.alloc("Rm", [32], F32)
    K.thG = A.alloc("thGk", [32], F32)
    K.c64 = A.alloc("c64", [32], F32)
    K.s64 = A.alloc("s64", [32], F32)
    r64 = nm("r64")
    share(r64, b128)
    tmps.append(r64)
    P.op("dve", lambda e: e.tensor_copy(out=K.Rm.ap, in_=ex.ap), reads=[b1], writes=[K.Rm.b])
    P.op("dve", lambda e: e.tensor_copy(out=K.thG.ap, in_=th.ap), reads=[bG], writes=[K.thG.b])
    P.op("dve", lambda e: e.tensor_copy(out=r64.ap, in_=ex.ap), reads=[b1], writes=[b128])
    for _ in range(6):
        TTv(r64.ap, r64.ap, r64.ap, ALU.mult, [b128], b128)
    P.op("dve", lambda e: e.reciprocal(out=r64.ap, in_=r64.ap), reads=[b128], writes=[b128])
    for it in range(7):
        if it == 6:
            TTv(K.c64.ap, a128r.ap, r64.ap, ALU.mult, [b128], K.c64.b)
            TTv(K.s64.ap, a128i.ap, r64.ap, ALU.mult, [b128], K.s64.b)
        TTv(q1.ap, a128r.ap, a128r.ap, ALU.mult, [b128], b128)
        TTv(q2.ap, a128i.ap, a128i.ap, ALU.mult, [b128], b128)
        TTv(a128i.ap, a128r.ap, a128i.ap, ALU.mult, [b128], b128)
        TSv(a128i.ap, a128i.ap, 2.0, None, ALU.mult, None, [b128], b128)
        TTv(a128r.ap, q1.ap, q2.ap, ALU.subtract, [b128], b128)

    def make_a4(ar, ai, b, name):
        a4 = A.alloc(name, [2, 2, 32], F32)
        P.op("dve", lambda e: e.tensor_copy(out=a4.ap[:, 0, 0, :], in_=ar.ap), reads=[b], writes=[a4.b])
        P.op("dve", lambda e: e.tensor_scalar(out=a4.ap[:, 0, 1, :], in0=ai.ap, scalar1=-1.0, scalar2=None, op0=ALU.mult), reads=[b], writes=[a4.b])
        P.op("dve", lambda e: e.tensor_copy(out=a4.ap[:, 1, 0, :], in_=ai.ap), reads=[b], writes=[a4.b])
        P.op("dve", lambda e: e.tensor_copy(out=a4.ap[:, 1, 1, :], in_=ar.ap), reads=[b], writes=[a4.b])
        return a4

    K.A4 = make_a4(a1r, a1i, b1, "A4_1")
    K.A4c = make_a4(a128r, a128i, b128, "A4_128")

    den = A.alloc("den", [32], F32)
    t1 = A.alloc("ct1", [32], F32)
    t2 = A.alloc("ct2", [32], F32)
    cr = A.alloc("coefr", [32], F32)
    ci = A.alloc("coefi", [32], F32)
    am1 = A.alloc("am1", [32], F32)
    bc = Buf("coef")
    for t in (den, t1, t2, cr, ci, am1):
        share(t, bc)
    tmps.extend([den, t1, t2, cr, ci, am1])
    TT = lambda o, a, b_, op, rd: P.op("dve", lambda e: e.tensor_tensor(out=o, in0=a, in1=b_, op=op), reads=rd, writes=[bc])
    TT(den.ap, lrG.ap, lrG.ap, ALU.mult, [bG])
    TT(t1.ap, liG.ap, liG.ap, ALU.mult, [bG])
    TT(den.ap, den.ap, t1.ap, ALU.add, [bc])
    P.op("dve", lambda e: e.reciprocal(out=den.ap, in_=den.ap), reads=[bc], writes=[bc])
    P.op("dve", lambda e: e.tensor_scalar(out=am1.ap, in0=a1r.ap, scalar1=-1.0, scalar2=None, op0=ALU.add), reads=[b1], writes=[bc])
    TT(t1.ap, am1.ap, lrG.ap, ALU.mult, [bc, bG])
    TT(t2.ap, a1i.ap, liG.ap, ALU.mult, [b1, bG])
    TT(t1.ap, t1.ap, t2.ap, ALU.add, [bc])
    TT(cr.ap, t1.ap, den.ap, ALU.mult, [bc])
    TT(t1.ap, a1i.ap, lrG.ap, ALU.mult, [bc, b1, bG])
    TT(t2.ap, am1.ap, liG.ap, ALU.mult, [bc, bG])
    TT(t1.ap, t1.ap, t2.ap, ALU.subtract, [bc])
    TT(ci.ap, t1.ap, den.ap, ALU.mult, [bc])

    K.BbR = A.alloc("BbR", [32, 16], F32)
    K.BbI = A.alloc("BbI", [32, 16], F32)
    bBb = Buf("Bbar")
    share(K.BbR, bBb)
    share(K.BbI, bBb)
    BreG = A.alloc("BreG", [32, 16], F32)
    BimG = A.alloc("BimG", [32, 16], F32)
    tb1 = A.alloc("tb1", [32, 16], F32)
    bB = Buf("B")
    share(BreG, bB)
    share(BimG, bB)
    with nc.allow_non_contiguous_dma(reason="64B runs param load"):
        for gh in range(2):
            rows = slice(64 * gh, 64 * gh + 64)
            P.dma("sp", BreG.ap[rows], I["b_re"][32 * gh:32 * gh + 32].rearrange("g p c -> p g c"), writes=[bB])
            P.dma("sp", BimG.ap[rows], I["b_im"][32 * gh:32 * gh + 32].rearrange("g p c -> p g c"), writes=[bB])
    crb = cr.ap.unsqueeze(2).broadcast_to([128, 32, 16])
    cib = ci.ap.unsqueeze(2).broadcast_to([128, 32, 16])
    P.op("dve", lambda e: e.tensor_tensor(out=K.BbR.ap, in0=BreG.ap, in1=crb, op=ALU.mult), reads=[bB, bc], writes=[bBb])
    P.op("dve", lambda e: e.tensor_tensor(out=tb1.ap, in0=BimG.ap, in1=cib, op=ALU.mult), reads=[bB, bc], writes=[tb1.b])
    P.op("dve", lambda e: e.tensor_tensor(out=K.BbR.ap, in0=K.BbR.ap, in1=tb1.ap, op=ALU.subtract), reads=[tb1.b, bBb], writes=[bBb])
    P.op("dve", lambda e: e.tensor_tensor(out=K.BbI.ap, in0=BimG.ap, in1=crb, op=ALU.mult), reads=[bB, bc], writes=[bBb])
    P.op("dve", lambda e: e.tensor_tensor(out=tb1.ap, in0=BreG.ap, in1=cib, op=ALU.mult), reads=[bB, bc, bBb], writes=[tb1.b])
    P.op("dve", lambda e: e.tensor_tensor(out=K.BbI.ap, in0=K.BbI.ap, in1=tb1.ap, op=ALU.add), reads=[tb1.b, bBb], writes=[bBb])
    share(BreG, bB)
    share(BimG, bB)

    K.Dcol = A.alloc("Dcol", [8], F32)
    with nc.allow_non_contiguous_dma(reason="tiny"):
        P.dma("sp", K.Dcol.ap, I["d_skip"].rearrange("(j p) -> p j", p=128), writes=[K.Dcol.b])

    K.ApR = A.alloc("ApR", [64, 64], BF16)
    K.ApI = A.alloc("ApI", [64, 64], BF16)
    bAp = Buf("ApowT")
    share(K.ApR, bAp)
    share(K.ApI, bAp)
    kcol = A.alloc("kcol", [1], F32)
    dtF = A.alloc("dtF", [64], F32)
    bF = Buf("F")
    share(kcol, bF)
    share(dtF, bF)
    P.dma("sp", kcol.ap, I["kcol"], writes=[bF])
    P.dma("sp", dtF.ap, I["log_dt"].unsqueeze(0).broadcast_to([128, 64]), writes=[bF])
    accurate_exp(K, dtF.ap, bF, [64])
    for hf in range(2):
        gs = slice(32 * hf, 32 * hf + 32)
        lrF = A.alloc("lrF", [32, 64], F32)
        liF = A.alloc("liF", [32, 64], F32)
        magF = A.alloc("magF", [32, 64], F32)
        snF = A.alloc("snF", [32, 64], F32)
        bH = Buf("Fh")
        for t in (lrF, liF, magF, snF):
            share(t, bH)
        P.dma("sp", lrF.ap, I["lam_re"][gs].unsqueeze(0).broadcast_to([128, 32, 64]), writes=[bH])
        P.dma("sp", liF.ap, I["lam_im"][gs].unsqueeze(0).broadcast_to([128, 32, 64]), writes=[bH])
        dtb = dtF.ap[:, gs].unsqueeze(2).broadcast_to([128, 32, 64])
        P.op("dve", lambda e: e.tensor_scalar(out=lrF.ap, in0=lrF.ap, scalar1=-1e-4, scalar2=None, op0=ALU.min), reads=[bH], writes=[bH])
        P.op("dve", lambda e: e.tensor_tensor(out=lrF.ap, in0=lrF.ap, in1=dtb, op=ALU.mult), reads=[bH, bF], writes=[bH])
        P.op("dve", lambda e: e.tensor_tensor(out=liF.ap, in0=liF.ap, in1=dtb, op=ALU.mult), reads=[bH, bF], writes=[bH])
        P.op("act", lambda e: e.activation(out=magF.ap, in_=lrF.ap, func=AF.Exp, scale=kcol.ap), reads=[bH, bF], writes=[bH])
        P.op("dve", lambda e: e.tensor_scalar(out=lrF.ap, in0=liF.ap, scalar1=kcol.ap, scalar2=None, op0=ALU.mult), reads=[bH, bF], writes=[bH])
        range_reduce_sin(K, lrF.ap, bH, snF.ap, bH, [32, 64])
        P.op("dve", lambda e: e.tensor_tensor(out=K.ApI.ap[:, gs, :], in0=magF.ap, in1=snF.ap, op=ALU.mult), reads=[bH], writes=[bAp])
        P.op("dve", lambda e: e.tensor_scalar(out=lrF.ap, in0=liF.ap, scalar1=kcol.ap, scalar2=float(np.pi / 2), op0=ALU.mult, op1=ALU.add),
             reads=[bH, bF], writes=[bH])
        range_reduce_sin(K, lrF.ap, bH, snF.ap, bH, [32, 64])
        P.op("dve", lambda e: e.tensor_tensor(out=K.ApR.ap[:, gs, :], in0=magF.ap, in1=snF.ap, op=ALU.mult), reads=[bH], writes=[bAp])
        for t in (lrF, liF, magF, snF):
            A.free(t)
    for t in (kcol, dtF, BreG, BimG, tb1):
        A.free(t)
    for t in tmps:
        A.free(t)


def ssm_tables_own(K):
    nc, P, I, A = K.nc, K.P, K.I, K.A
    PSB = K.psb
    bBb = K.BbR.b
    m8 = A.alloc("m8", [8], F32)
    m88 = A.alloc("m88", [8, 8], F32)
    bm = Buf("masks")
    share(m8, bm)
    share(m88, bm)
    P.dma("sp", m8.ap, I["m8"], writes=[bm])
    P.dma("sp", m88.ap, I["m88"], writes=[bm])
    K.BtabR = A.alloc("BtabR", [64, 64], BF16)
    K.BtabI = A.alloc("BtabI", [64, 64], BF16)
    bBtab = Buf("Btab")
    share(K.BtabR, bBtab)
    share(K.BtabI, bBtab)
    K.CtabR = A.alloc("CtabR", [32, 128], BF16)
    K.CtabI = A.alloc("CtabI", [32, 128], BF16)
    bCtab = Buf("Ctab")
    share(K.CtabR, bCtab)
    share(K.CtabI, bCtab)
    cnat_r = A.alloc("cnat_r", [8, 64], F32)
    cnat_i = A.alloc("cnat_i", [8, 64], F32)
    bcn = Buf("cnat")
    share(cnat_r, bcn)
    share(cnat_i, bcn)
    P.dma("sp", cnat_r.ap, I["c_re"].rearrange("(j g) c p -> (g c) j p", j=8), writes=[bcn])
    P.dma("sp", cnat_i.ap, I["c_im"].rearrange("(j g) c p -> (g c) j p", j=8), writes=[bcn])
    cnt = 0
    for src, dst in ((K.BbR, K.BtabR), (K.BbI, K.BtabI)):
        for j in range(8):
            gh, gq = j // 4, (j % 4) * 8
            pb = cnt % 2
            cnt += 1
            rows = slice(64 * gh, 64 * gh + 64)
            inp = src.ap[rows, gq:gq + 8, :]
            pt = K.ps[:, pb, 0:64]
            P.pe([lambda e, inp=inp, pt=pt, rows=rows: e.transpose(out=pt, in_=inp, identity=K.identf.ap[rows, rows])],
                 reads=[bBb, K.bC], writes=[PSB[pb]])
            P.op("dve", lambda e, pt=pt, dst=dst, j=j: e.tensor_tensor(
                out=dst.ap[:, 8 * j:8 * j + 8, :], in0=pt.unsqueeze(1).broadcast_to([128, 8, 64]),
                in1=m8.ap.unsqueeze(2).broadcast_to([128, 8, 64]), op=ALU.mult), reads=[PSB[pb], bm], writes=[bBtab])
    P.op("dve", lambda e: e.tensor_scalar(out=cnat_i.ap, in0=cnat_i.ap, scalar1=-1.0, scalar2=None, op0=ALU.mult), reads=[bcn], writes=[bcn])
    for src, dst in ((cnat_r, K.CtabR), (cnat_i, K.CtabI)):
        for j in range(8):
            gh, gq = j // 4, (j % 4) * 8
            pb = cnt % 2
            cnt += 1
            rows = slice(64 * gh, 64 * gh + 64)
            pt = K.ps[rows, pb, 0:128]
            P.pe([lambda e, src=src, j=j, pt=pt, gh=gh: e.matmul(pt, lhsT=src.ap[:, j, :], rhs=K.identf.ap, start=True, stop=True,
                                                                 tile_position=(0, 64 * gh))],
                 reads=[bcn, K.bC], writes=[PSB[pb]])
            P.op("dve", lambda e, pt=pt, dst=dst, rows=rows, gq=gq: e.tensor_tensor(
                out=dst.ap[rows, gq:gq + 8, :].rearrange("p g (h c) -> p g h c", h=8),
                in0=pt.rearrange("p (g c) -> p g c", g=8).unsqueeze(2).broadcast_to([64, 8, 8, 16]),
                in1=m88.ap[rows].unsqueeze(3).broadcast_to([64, 8, 8, 16]), op=ALU.mult),
                reads=[PSB[pb], bm], writes=[bCtab])
    for t in (cnat_r, cnat_i, m8, m88):
        A.free(t)
    K.CT = A.alloc("CT", [32, 64], BF16)
    K.ST = A.alloc("ST", [32, 64], BF16)
    trow = A.alloc("trow", [64], F32)
    ang = A.alloc("angT", [32, 64], F32)
    sct = A.alloc("sct", [32, 64], F32)
    P.dma("sp", trow.ap, I["trow"], writes=[trow.b])
    thb = K.thG.ap.unsqueeze(2).broadcast_to([128, 32, 64])
    trb = trow.ap.unsqueeze(1).broadcast_to([128, 32, 64])
    P.op("dve", lambda e: e.tensor_tensor(out=ang.ap, in0=thb, in1=trb, op=ALU.mult), reads=[K.thG.b, trow.b], writes=[ang.b])
    range_reduce_sin(K, ang.ap, ang.b, sct.ap, sct.b, [32, 64])
    P.op("dve", lambda e: e.tensor_copy(out=K.ST.ap, in_=sct.ap), reads=[sct.b], writes=[K.ST.b])
    P.op("dve", lambda e: e.tensor_tensor(out=ang.ap, in0=thb, in1=trb, op=ALU.mult), reads=[K.thG.b, trow.b, ang.b], writes=[ang.b])
    P.op("dve", lambda e: e.tensor_scalar(out=ang.ap, in0=ang.ap, scalar1=float(np.pi / 2), scalar2=None, op0=ALU.add), reads=[ang.b], writes=[ang.b])
    range_reduce_sin(K, ang.ap, ang.b, sct.ap, sct.b, [32, 64])
    P.op("dve", lambda e: e.tensor_copy(out=K.CT.ap, in_=sct.ap), reads=[sct.b], writes=[K.CT.b])
    for t in (trow, ang, sct):
        A.free(t)


def load_weights_resident(K, name, src, kc, ncols):
    t = K.A.alloc(name, [kc, ncols], BF16)
    for c0 in range(0, ncols, 512):
        w = min(512, ncols - c0)
        K.P.dma("pool", t.ap[:, :, c0:c0 + w], src[:, c0:c0 + w].rearrange("(c p) n -> p c n", p=128), writes=[t.b])
    return t


def norm_rows(K, xt, n, gb, xs, junk, ss):
    P = K.P
    P.op("act", lambda e: e.activation(out=junk.ap[:n], in_=xt.ap[:n], func=AF.Square, accum_out=ss.ap[:n]),
         reads=[xt.b], writes=[junk.b, ss.b])
    P.op("act", lambda e: e.activation(out=ss.ap[:n], in_=ss.ap[:n], func=AF.Sqrt, bias=K.epsc.ap[:n], scale=1.0 / D),
         reads=[ss.b, K.bC], writes=[ss.b])
    P.op("dve", lambda e: e.reciprocal(out=ss.ap[:n], in_=ss.ap[:n]), reads=[ss.b], writes=[ss.b])
    P.op("dve", lambda e: e.scalar_tensor_tensor(out=xs.ap[:n], in0=xt.ap[:n], scalar=ss.ap[:n], in1=gb.ap[:n],
                                                 op0=ALU.mult, op1=ALU.mult), reads=[xt.b, ss.b, gb.b], writes=[xs.b])


def transpose_rows(K, xs, n, dst_ap, dst_bufs, pbank):
    P = K.P
    ptb = K.ps[:, pbank:pbank + 2, :].rearrange("p a b -> p (a b)").bitcast(BF16)
    ptv = ptb.rearrange("p (c n) -> p c n", c=16)
    pbufs = [K.psb[pbank], K.psb[pbank + 1]]
    P.pe([lambda e, c=c: e.transpose(out=ptv[:, c, 0:n], in_=xs.ap[:n, c * 128:(c + 1) * 128], identity=K.identb.ap[:n, :n])
          for c in range(16)], reads=[xs.b, K.bC], writes=pbufs)
    P.op("act", lambda e: e.activation(out=dst_ap, in_=ptv[:, :, 0:n], func=AF.Copy), reads=pbufs, writes=dst_bufs)


def prefix_phase(K):
    nc, P, I, A = K.nc, K.P, K.I, K.A
    PSB = K.psb
    Wu = load_weights_resident(K, "Wu", I["w_in"][:, 0:1024], 16, 1024)
    K.gb = A.alloc("gb", [D], F32)
    P.dma("sp", K.gb.ap, I["norm_attn"].unsqueeze(0).broadcast_to([128, D]), writes=[K.gb.b])
    xts = [A.alloc(f"xt{i}", [D], F32) for i in range(2)]
    K.junk = A.alloc("junk", [D], BF16)
    K.ss = A.alloc("ss", [1], F32)
    xss = [A.alloc(f"xs{i}", [D], BF16) for i in range(2)]
    hTt = [A.alloc(f"hTt{i}", [16, 128], BF16) for i in range(2)]
    U = [A.alloc(f"U{i}", [1024], BF16) for i in range(2)]
    pr1 = A.alloc("pr1", [2, 32, 16], F32)
    pr2 = A.alloc("pr2", [2, 32, 16], F32)
    Tt = A.alloc("Tt", [2, 32, 16], F32)
    S = A.alloc("S", [2, 32], F32)
    Mm = A.alloc("Mm", [2, 2, 32], F32)
    K.Hc = A.alloc("Hc", [2, 32], F32)
    H = K.Hc
    P.op("dve", lambda e: e.memset(H.ap, 0.0), writes=[H.b])
    BbRb = K.BbR.ap.unsqueeze(1).broadcast_to([128, 2, 32, 16])
    BbIb = K.BbI.ap.unsqueeze(1).broadcast_to([128, 2, 32, 16])
    ntile = NPRE // 128

    def stA(i):
        xt, xs, ht = xts[i % 2], xss[i % 2], hTt[i % 2]
        P.dma("sp", xt.ap, I["xp"][i * 128:(i + 1) * 128, :], writes=[xt.b])
        norm_rows(K, xt, 128, K.gb, xs, K.junk, K.ss)
        transpose_rows(K, xs, 128, ht.ap, [ht.b], 0)

    def stB(i):
        ht, u = hTt[i % 2], U[i % 2]
        for half in range(2):
            pu = K.ps[:, 2 + half, :]
            P.pe([lambda e, c=c, pu=pu, half=half: e.matmul(pu, lhsT=ht.ap[:, c, :], rhs=Wu.ap[:, c, half * 512:(half + 1) * 512],
                                                           start=(c == 0), stop=(c == 15)) for c in range(16)],
                 reads=[ht.b, Wu.b], writes=[PSB[2 + half]])
        P.op("act", lambda e, u=u: e.activation(out=u.ap, in_=K.ps[:, 2:4, :].rearrange("p a b -> p (a b)"), func=AF.Copy),
             reads=[PSB[2], PSB[3]], writes=[u.b])

    zp = K.ps[:, 4:6, :].rearrange("p a (g c) -> p a g c", c=16)

    def stC(i):
        u = U[i % 2]
        fns = []
        for g in range(64):
            gh, gq = g // 32, g % 32
            rows = slice(64 * gh, 64 * gh + 64)
            for ri, tab in ((0, K.ApR), (1, K.ApI)):
                fns.append(lambda e, g=g, gh=gh, gq=gq, rows=rows, ri=ri, tab=tab, u=u: e.matmul(
                    zp[rows, ri, gq, :], lhsT=tab.ap[:, g, :], rhs=u.ap[:, g * 16:(g + 1) * 16], start=True, stop=True,
                    tile_position=(0, 64 * gh)))
        P.pe(fns, reads=[u.b, K.ApR.b], writes=[PSB[4], PSB[5]])

    def stD(i):
        P.op("dve", lambda e: e.tensor_tensor(out=pr1.ap, in0=zp, in1=BbRb, op=ALU.mult), reads=[PSB[4], PSB[5], K.BbR.b], writes=[pr1.b])
        P.op("dve", lambda e: e.tensor_tensor(out=pr2.ap, in0=zp, in1=BbIb, op=ALU.mult), reads=[PSB[4], PSB[5], K.BbR.b], writes=[pr2.b])
        P.op("dve", lambda e: e.tensor_tensor(out=Tt.ap[:, 0], in0=pr1.ap[:, 0], in1=pr2.ap[:, 1], op=ALU.subtract),
             reads=[pr1.b, pr2.b], writes=[Tt.b])
        P.op("dve", lambda e: e.tensor_tensor(out=Tt.ap[:, 1], in0=pr1.ap[:, 1], in1=pr2.ap[:, 0], op=ALU.add),
             reads=[pr1.b, pr2.b], writes=[Tt.b])
        P.op("dve", lambda e: e.tensor_reduce(out=S.ap, in_=Tt.ap, axis=AX.X, op=ALU.add), reads=[Tt.b], writes=[S.b])
        P.op("dve", lambda e: e.tensor_tensor(out=Mm.ap, in0=K.A4c.ap, in1=H.ap.unsqueeze(1).broadcast_to([128, 2, 2, 32]), op=ALU.mult),
             reads=[K.A4c.b, H.b], writes=[Mm.b])
        P.op("dve", lambda e: e.tensor_tensor(out=H.ap, in0=Mm.ap[:, :, 0, :], in1=Mm.ap[:, :, 1, :], op=ALU.add), reads=[Mm.b], writes=[H.b])
        P.op("dve", lambda e: e.tensor_tensor(out=H.ap, in0=H.ap, in1=S.ap, op=ALU.add), reads=[H.b, S.b], writes=[H.b])

    stA(0)
    for i in range(ntile):
        stB(i)
        stC(i)
        if i + 1 < ntile:
            stA(i + 1)
        stD(i)
    if DEBUG:
        P.dma("sp", K.O["dbg_h"], H.ap, reads=[H.b], writes=[K.bout])
    for t in [Wu] + U + [pr1, pr2, Tt, S, Mm, K.ApR, K.ApI]:
        A.free(t)
    K.Wkv = load_weights_resident(K, "Wkv", I["w_in"][:, 2048:2560], 16, 512)
    K.Wkd = A.alloc("Wkd", [16, 4, 128], BF16)
    for kv in range(4):
        P.op("pool", lambda e, kv=kv: e.tensor_copy(out=K.Wkd.ap[:, :, kv, :].rearrange("p c (d h) -> p c d h", d=2),
                                                    in_=K.Wkv.ap[:, :, kv * 64:(kv + 1) * 64].unsqueeze(2).broadcast_to([128, 16, 2, 64])),
             reads=[K.Wkv.b], writes=[K.Wkd.b])
    ht = hTt[(ntile - 1) % 2]
    kv_tokmajor(K, ht.ap, [ht.b], 128, slice(0, 128), 0)
    kT_dup(K, ht.ap, [ht.b], 128, 0, [0], slice(0, 128))
    for t in hTt:
        A.free(t)
    K.xts, K.xss = xts, xss


def kT_dup(K, hT_ap, hbufs, n, tok0, kbuf_idx, cols):
    P = K.P
    PSB = K.psb
    for kv in range(4):
        pk = K.ps[:, 7, 0:n]
        P.pe([lambda e, c=c, pk=pk, kv=kv: e.matmul(pk, lhsT=K.Wkd.ap[:, c, kv, :],
                                                   rhs=hT_ap[:, c, cols], start=(c == 0), stop=(c == 15)) for c in range(16)],
             reads=hbufs + [K.Wkd.b], writes=[PSB[7]])
        P.op("act", lambda e, kv=kv, pk=pk: e.activation(out=K.kTd.ap[:, kv, tok0:tok0 + n], in_=pk, func=AF.Copy),
             reads=[PSB[7]], writes=[K.kTd.b[j] for j in kbuf_idx])


def kv_tokmajor(K, hT_ap, hbufs, n, cols, vt, kdst=None):
    P, PSB = K.P, K.psb
    pk = K.ps[:n, 6, :]
    P.pe([lambda e, c=c: e.matmul(pk, lhsT=hT_ap[:, c, cols], rhs=K.Wkv.ap[:, c, :], start=(c == 0), stop=(c == 15)) for c in range(16)],
         reads=hbufs + [K.Wkv.b], writes=[PSB[6]])
    P.op("act", lambda e: e.activation(out=K.vtok.ap[:n, vt, :], in_=K.ps[:n, 6, 256:512], func=AF.Copy), reads=[PSB[6]], writes=[K.vtok.b[vt]])
    if kdst is not None:
        P.op("act", lambda e: e.activation(out=kdst.ap[:n], in_=pk, func=AF.Copy), reads=[PSB[6]], writes=[kdst.b])


class Ring:
    def __init__(self, K, nbig, nsmall=0):
        self.K = K
        self.big = [K.A.alloc(f"ringb{i}", [16, 512], BF16) for i in range(nbig)]
        self.small = [K.A.alloc(f"rings{i}", [8, 512], BF16) for i in range(nsmall)]
        self.ib = 0
        self.is_ = 0

    def load(self, src, kc, ncols):
        if kc <= 8 and self.small:
            s = self.small[self.is_ % len(self.small)]
            self.is_ += 1
        else:
            s = self.big[self.ib % len(self.big)]
            self.ib += 1
        v = s.ap[:, 0:kc, 0:ncols]
        self.K.P.dma("pool", v, src.rearrange("(c p) n -> p c n", p=128), writes=[s.b])
        return v, s.b

    def free(self):
        for s in self.big + self.small:
            self.K.A.free(s)


def acc_views(K, acc):
    return [(K.ps[:, 3 * acc:3 * acc + 2, :].rearrange("p a b -> p (a b)"), slice(0, 1024)),
            (K.ps[:, 3 * acc + 2, 0:64], slice(1024, 1088))]


def acc_bufs(K, acc):
    return [K.psb[3 * acc], K.psb[3 * acc + 1], K.psb[3 * acc + 2]]


def fm_matmul(K, wv, wb, kc, nt, act_ap, act_bufs, acc):
    fns = []
    for bi, (t0, n) in enumerate(BLKS):
        for c in range(kc):
            fns.append(lambda e, bi=bi, t0=t0, n=n, c=c: e.matmul(
                K.ps[:, 3 * acc + bi, 0:n], lhsT=wv[:, c, nt * 128:(nt + 1) * 128], rhs=act_ap[:, c, t0:t0 + n],
                start=(c == 0), stop=(c == kc - 1)))
    K.P.pe(fns, reads=[wb] + act_bufs, writes=acc_bufs(K, acc))


def own_norm(K):
    P, I, A = K.P, K.I, K.A
    for t in range(9):
        n = 128 if t < 8 else 64
        xt, xs = K.xts[t % 2], K.xss[t % 2]
        P.dma("sp", xt.ap[:n], I["xo"][t * 128:t * 128 + n, :], writes=[xt.b])
        norm_rows(K, xt, n, K.gb, xs, K.junk, K.ss)
        transpose_rows(K, xs, n, K.hT.ap[:, :, t * 128:t * 128 + n], [K.hT.b[t]], 0)
    A.free(K.gb)


def proj_phase(K):
    P, I, A = K.P, K.I, K.A
    hb = K.hT.b
    for t in range(9):
        n = 128 if t < 8 else 64
        kdst = K.kvlast if t == 7 else (K.kvsamp if t == 8 else None)
        kv_tokmajor(K, K.hT.ap, [hb[t]], n, slice(t * 128, t * 128 + n), t + 1, kdst)
    for bi, (t0, n) in enumerate(BLKS):
        tiles = list(range(t0 // 128, (t0 + n + 127) // 128))
        kT_dup(K, K.hT.ap, [hb[t] for t in tiles], n, 128 + t0, [t + 1 for t in tiles], slice(t0, t0 + n))
    A.free(K.Wkv)
    A.free(K.Wkd)
    ring = Ring(K, 3)
    acc = 0
    for g in range(2):
        wv, wb = ring.load(I["w_in"][:, g * 512:(g + 1) * 512], 16, 512)
        for nt in range(4):
            fm_matmul(K, wv, wb, 16, nt, K.hT.ap, hb, acc)
            for pv, sl in acc_views(K, acc):
                P.op("act", lambda e, pv=pv, sl=sl, j=4 * g + nt: e.activation(out=K.uT.ap[:, j, sl], in_=pv, func=AF.Copy),
                     reads=acc_bufs(K, acc), writes=K.uT.b)
            acc ^= 1
    ring.free()


def ssm_own(K):
    P, I, A, O = K.P, K.I, K.A, K.O
    PSB = K.psb
    Hist = A.alloc("Hist", [2, 32, 64], F32)
    T1 = A.alloc("T1", [2, 32, 64], F32)
    H2 = A.alloc("H2", [2, 32, 64], BF16)
    HistB = A.alloc("HistB", [2, 32, 64], BF16)
    cA = A.alloc("cA", [2, 32], F32)
    cB = A.alloc("cB", [2, 32], F32)
    ysb = A.alloc("ysb", [8, 64], F32)
    g1 = A.alloc("g1", [8, 64], F32)
    g2 = A.alloc("g2", [8, 64], F32)
    zb = K.ps[:, 0:4, :].rearrange("p (r a) (g t) -> p r (a g) t", r=2, t=128)
    yps = K.ps[:, 4:6, :].rearrange("p a (j t) -> p (a j) t", t=128)
    ypb = [PSB[4], PSB[5]]
    zbb = [PSB[0], PSB[1], PSB[2], PSB[3]]
    Hc = K.Hc

    def bu_batch(t, b4, n, c0):
        fns = []
        for gh in range(2):
            rows = slice(64 * gh, 64 * gh + 64)
            for gl in range(8):
                g = 32 * gh + 8 * b4 + gl
                for ri, tab in ((0, K.BtabR), (1, K.BtabI)):
                    fns.append(lambda e, rows=rows, gl=gl, g=g, ri=ri, tab=tab, gh=gh: e.matmul(
                        zb[rows, ri, gl, 0:n], lhsT=tab.ap[:, g, :], rhs=K.uT.ap[:, g // 8, c0:c0 + n], start=True, stop=True,
                        tile_position=(0, 64 * gh)))
        P.pe(fns, reads=[K.BtabR.b] + K.uT.b, writes=zbb)

    def cside(n, hb_ap, hb_buf):
        for j in range(8):
            gh = j // 4
            rows = slice(64 * gh, 64 * gh + 64)
            fns = []
            for g8 in range(8):
                gq = (8 * j + g8) % 32
                for ri, tab in ((0, K.CtabR), (1, K.CtabI)):
                    fns.append(lambda e, rows=rows, gq=gq, ri=ri, tab=tab, j=j, first=(g8 == 0 and ri == 0), last=(g8 == 7 and ri == 1):
                               e.matmul(yps[:, j, 0:n], lhsT=tab.ap[rows, gq, :], rhs=hb_ap[rows, ri, gq, 0:n], start=first, stop=last))
            P.pe(fns, reads=[K.CtabR.b, hb_buf], writes=ypb)

    def gelu_out(n, c0, perm):
        yv, a, b = ysb.ap[:, :, 0:n], g1.ap[:, :, 0:n], g2.ap[:, :, 0:n]
        P.op("act", lambda e: e.activation(out=a, in_=yv, func=AF.Square), reads=[ysb.b], writes=[g1.b])
        P.op("dve", lambda e: e.tensor_scalar(out=a, in0=a, scalar1=0.044715, scalar2=1.0, op0=ALU.mult, op1=ALU.add), reads=[g1.b], writes=[g1.b])
        P.op("dve", lambda e: e.tensor_tensor(out=a, in0=a, in1=yv, op=ALU.mult), reads=[g1.b, ysb.b], writes=[g1.b])
        P.op("act", lambda e: e.activation(out=b, in_=a, func=AF.Sigmoid, scale=1.5957691216057308), reads=[g1.b], writes=[g2.b])
        P.op("dve", lambda e: e.tensor_tensor(out=K.gT.ap[:, :, c0:c0 + n], in0=b, in1=yv, op=ALU.mult), reads=[g2.b, ysb.b], writes=K.gT.b)

    CTb = K.CT.ap.rearrange("p g t -> p (g t)").unsqueeze(1).broadcast_to([128, 2, 2048])
    STf = K.ST.ap.rearrange("p g t -> p (g t)")
    Hf = Hist.ap.rearrange("p r g t -> p r (g t)")
    T1f = T1.ap.rearrange("p r g t -> p r (g t)")
    H2f = H2.ap.rearrange("p r g t -> p r (g t)")
    HBf = HistB.ap.rearrange("p r g t -> p r (g t)")
    for t in range(16):
        c0 = t * 64
        for b4 in range(4):
            bu_batch(t, b4, 64, c0)
            P.op("act", lambda e, b4=b4: e.activation(out=Hist.ap[:, :, 8 * b4:8 * b4 + 8, :], in_=zb[:, :, :, 0:64], func=AF.Copy), reads=zbb, writes=[Hist.b])
        P.op("dve", lambda e: e.tensor_tensor(out=T1f, in0=Hf, in1=CTb, op=ALU.mult), reads=[Hist.b, K.CT.b], writes=[T1.b])
        P.op("pool", lambda e: e.tensor_tensor(out=H2f[:, 0], in0=Hf[:, 1], in1=STf, op=ALU.mult), reads=[Hist.b, K.ST.b], writes=[H2.b])
        P.op("pool", lambda e: e.tensor_tensor(out=H2f[:, 1], in0=Hf[:, 0], in1=STf, op=ALU.mult), reads=[Hist.b, K.ST.b], writes=[H2.b])
        P.op("dve", lambda e: e.tensor_tensor(out=Hf[:, 0], in0=T1f[:, 0], in1=H2f[:, 0], op=ALU.add), reads=[T1.b, H2.b], writes=[Hist.b])
        P.op("dve", lambda e: e.tensor_tensor(out=Hf[:, 1], in0=T1f[:, 1], in1=H2f[:, 1], op=ALU.subtract), reads=[T1.b, H2.b], writes=[Hist.b])
        fns = []
        for ri in range(2):
            for gq in range(32):
                fns.append(lambda e, ri=ri, gq=gq: e.tensor_tensor_scan(
                    out=T1.ap[:, ri, gq, :], data0=K.Rm.ap[:, gq:gq + 1].broadcast_to([128, 64]), data1=Hist.ap[:, ri, gq, :],
                    initial=Hc.ap[:, ri, gq:gq + 1], op0=ALU.mult, op1=ALU.add))
        P.group("dve", fns, reads=[Hist.b, K.Rm.b, Hc.b], writes=[T1.b])
        P.op("dve", lambda e: e.tensor_tensor(out=Hf, in0=T1f, in1=CTb, op=ALU.mult), reads=[T1.b, K.CT.b], writes=[Hist.b])
        P.op("pool", lambda e: e.tensor_tensor(out=H2f[:, 0], in0=T1f[:, 1], in1=STf, op=ALU.mult), reads=[T1.b, K.ST.b], writes=[H2.b])
        P.op("pool", lambda e: e.tensor_tensor(out=H2f[:, 1], in0=T1f[:, 0], in1=STf, op=ALU.mult), reads=[T1.b, K.ST.b], writes=[H2.b])
        P.op("dve", lambda e: e.tensor_tensor(out=HBf[:, 0], in0=Hf[:, 0], in1=H2f[:, 0], op=ALU.subtract), reads=[Hist.b, H2.b], writes=[HistB.b])
        P.op("dve", lambda e: e.tensor_tensor(out=HBf[:, 1], in0=Hf[:, 1], in1=H2f[:, 1], op=ALU.add), reads=[Hist.b, H2.b], writes=[HistB.b])
        P.op("dve", lambda e: e.tensor_tensor(out=cA.ap, in0=T1.ap[:, :, :, 63], in1=K.c64.ap.unsqueeze(1).broadcast_to([128, 2, 32]), op=ALU.mult),
             reads=[T1.b, K.c64.b], writes=[cA.b])
        P.op("dve", lambda e: e.tensor_tensor(out=cB.ap, in0=T1.ap[:, :, :, 63], in1=K.s64.ap.unsqueeze(1).broadcast_to([128, 2, 32]), op=ALU.mult),
             reads=[T1.b, K.s64.b], writes=[cB.b])
        P.op("dve", lambda e: e.tensor_tensor(out=Hc.ap[:, 0], in0=cA.ap[:, 0], in1=cB.ap[:, 1], op=ALU.subtract), reads=[cA.b, cB.b], writes=[Hc.b])
        P.op("dve", lambda e: e.tensor_tensor(out=Hc.ap[:, 1], in0=cA.ap[:, 1], in1=cB.ap[:, 0], op=ALU.add), reads=[cA.b, cB.b], writes=[Hc.b])
        cside(64, HistB.ap, HistB.b)
        for j in range(8):
            P.op("dve", lambda e, j=j: e.scalar_tensor_tensor(out=ysb.ap[:, j, 0:64], in0=K.uT.ap[:, j, c0:c0 + 64], scalar=K.Dcol.ap[:, j:j + 1],
                                                              in1=yps[:, j, 0:64], op0=ALU.mult, op1=ALU.add),
                 reads=K.uT.b + [K.Dcol.b] + ypb, writes=[ysb.b])
        gelu_out(64, c0, False)
    stg = A.alloc("stg", [128], F32)
    for ri in range(2):
        pt = K.ps[0:32, 6, 0:128]
        P.pe([lambda e, ri=ri: e.transpose(out=pt, in_=Hc.ap[:, ri, :], identity=K.identf.ap)], reads=[Hc.b, K.bC], writes=[PSB[6]])
        P.op("act", lambda e: e.activation(out=stg.ap[0:32, :], in_=pt, func=AF.Copy), reads=[PSB[6]], writes=[stg.b])
        P.dma("sp", O["pst"][ri].rearrange("(h g) p -> g h p", h=2), stg.ap[0:32, :].rearrange("g (h p) -> g h p", h=2), reads=[stg.b], writes=[K.bout])

    for t in [Hist, HistB, T1, H2, cA, cB]:
        A.free(t)
    Hs0 = A.alloc("Hs0", [2, 32, 16], F32)
    HistS = A.alloc("HistS", [2, 32, 4, 16], F32)
    HistSB = A.alloc("HistSB", [2, 32, 64], BF16)
    MmS = A.alloc("MmS", [2, 2, 32, 16], F32)
    RrS = A.alloc("RrS", [2, 32, 16], F32)
    xin = [A.alloc(f"xin{i}", [2, 64], F32) for i in range(2)]
    for ri, nm in ((0, "sre"), (1, "sim")):
        for sq in range(4):
            xi = xin[(2 * ri + sq) % 2]
            for s in range(4):
                P.dma("sp", xi.ap[32 * s:32 * s + 32], I[nm][4 * sq + s].rearrange("(h g) p -> g h p", h=2), writes=[xi.b])
            pt = K.ps[:, 6, 0:128]
            P.pe([lambda e, xi=xi: e.transpose(out=pt, in_=xi.ap.rearrange("p h q -> p (h q)"), identity=K.identf.ap)],
                 reads=[xi.b, K.bC], writes=[PSB[6]])
            P.op("act", lambda e, ri=ri, sq=sq: e.activation(out=Hs0.ap[:, ri, :, 4 * sq:4 * sq + 4].rearrange("p g s -> p s g"),
                                                            in_=pt.rearrange("p (s g) -> p s g", s=4), func=AF.Copy),
                 reads=[PSB[6]], writes=[Hs0.b])
    c0 = 1024
    for b4 in range(4):
        bu_batch(8, b4, 64, c0)
        for ri in range(2):
            P.op("act", lambda e, b4=b4, ri=ri: e.activation(out=HistS.ap[:, ri, 8 * b4:8 * b4 + 8, :, :],
                                                            in_=zb[:, ri, :, 0:64].rearrange("p g (s t) -> p g t s", t=4), func=AF.Copy),
                 reads=zbb, writes=[HistS.b])
    A4b = K.A4.ap.unsqueeze(4).broadcast_to([128, 2, 2, 32, 16])
    for tt in range(4):
        prev = Hs0.ap if tt == 0 else HistS.ap[:, :, :, tt - 1, :]
        pb = [Hs0.b] if tt == 0 else [HistS.b]
        P.op("dve", lambda e, prev=prev: e.tensor_tensor(out=MmS.ap, in0=A4b, in1=prev.unsqueeze(1).broadcast_to([128, 2, 2, 32, 16]), op=ALU.mult),
             reads=[K.A4.b] + pb, writes=[MmS.b])
        P.op("dve", lambda e: e.tensor_tensor(out=RrS.ap, in0=MmS.ap[:, :, 0], in1=MmS.ap[:, :, 1], op=ALU.add), reads=[MmS.b], writes=[RrS.b])
        P.op("dve", lambda e, tt=tt: e.tensor_tensor(out=HistS.ap[:, :, :, tt, :], in0=HistS.ap[:, :, :, tt, :], in1=RrS.ap, op=ALU.add),
             reads=[RrS.b, HistS.b], writes=[HistS.b])
    P.op("act", lambda e: e.activation(out=HistSB.ap, in_=HistS.ap.rearrange("p r g t s -> p r g (t s)"), func=AF.Copy), reads=[HistS.b], writes=[HistSB.b])
    cside(64, HistSB.ap, HistSB.b)
    for j in range(8):
        P.op("dve", lambda e, j=j: e.scalar_tensor_tensor(
            out=ysb.ap[:, j, 0:64].rearrange("p (s t) -> p s t", t=4), in0=K.uT.ap[:, j, c0:c0 + 64].rearrange("p (s t) -> p s t", t=4),
            scalar=K.Dcol.ap[:, j:j + 1], in1=yps[:, j, 0:64].rearrange("p (t s) -> p s t", t=4), op0=ALU.mult, op1=ALU.add),
            reads=K.uT.b + [K.Dcol.b] + ypb, writes=[ysb.b])
    gelu_out(64, c0, True)
    stg2 = A.alloc("stg2", [4, 32], F32)
    for ri in range(2):
        for sq in range(4):
            pt = K.ps[:, 6, 0:128]
            P.op("dve", lambda e, ri=ri, sq=sq: e.tensor_copy(out=stg2.ap, in_=HistS.ap[:, ri, :, 3, 4 * sq:4 * sq + 4].rearrange("p g s -> p s g")),
                 reads=[HistS.b], writes=[stg2.b])
            P.pe([lambda e: e.transpose(out=pt, in_=stg2.ap.rearrange("p s g -> p (s g)"), identity=K.identf.ap)],
                 reads=[stg2.b, K.bC], writes=[PSB[6]])
            P.op("act", lambda e: e.activation(out=stg.ap, in_=pt, func=AF.Copy), reads=[PSB[6]], writes=[stg.b])
            for s in range(4):
                P.dma("sp", O["sst"][ri, 4 * sq + s].rearrange("(h g) p -> g h p", h=2),
                      stg.ap[32 * s:32 * s + 32, :].rearrange("g (h p) -> g h p", h=2), reads=[stg.b], writes=[K.bout])
    for t in [ysb, g1, g2, stg, stg2, Hs0, HistS, HistSB, MmS, RrS] + xin:
        A.free(t)
    for t in [K.A4, K.A4c, K.BtabR, K.BtabI, K.CtabR, K.CtabI, K.Dcol, K.Hc, K.uT, K.CT, K.ST, K.Rm, K.thG, K.c64, K.s64]:
        A.free(t)


def attention(K):
    P, I, A, O = K.P, K.I, K.A, K.O
    PSB = K.psb
    hb = K.hT.b
    K.qT = A.alloc("qT", [8, NT], BF16, nbufs=1, top=True)
    ring = Ring(K, 2)
    acc = 0
    for g in range(2):
        wv, wb = ring.load(I["w_in"][:, 1024 + g * 512:1024 + (g + 1) * 512], 16, 512)
        for nt in range(4):
            fm_matmul(K, wv, wb, 16, nt, K.hT.ap, hb, acc)
            for pv, sl in acc_views(K, acc):
                P.op("act", lambda e, pv=pv, sl=sl, j=4 * g + nt: e.activation(out=K.qT.ap[:, j, sl], in_=pv, func=AF.Copy),
                     reads=acc_bufs(K, acc), writes=[K.qT.b])
            acc ^= 1
    ring.free()
    if ASTOP == 'q':
        return
    bmt = A.alloc("bmt", [16, 2, 128], F32)
    bmt0 = A.alloc("bmt0", [16, 128], F32)
    bmsc = A.alloc("bmsc", [2, 8, 4], F32)
    bmsn = A.alloc("bmsn", [2, 8, 4], F32)
    esk = A.alloc("esk", [8], F32)
    P.dma("sp", bmt.ap, I["bmt"], writes=[bmt.b])
    P.dma("sp", bmt0.ap, I["bmt0"], writes=[bmt0.b])
    P.dma("sp", bmsc.ap, I["bmsc"].rearrange("k (j a) t -> k j a t", j=2), writes=[bmsc.b])
    P.dma("sp", bmsn.ap[0:4], I["bmsn"].rearrange("k (j a) t -> k j a t", j=2), writes=[bmsn.b])
    with K.nc.allow_non_contiguous_dma(reason="tiny"):
        for j in range(2):
            P.dma("sp", esk.ap[64 * j:64 * j + 64, :], I["sinks"].rearrange("(p j) -> j p", j=2)[j:j + 1, :].broadcast_to([64, 8]), writes=[esk.b])
    P.op("act", lambda e: e.activation(out=esk.ap, in_=esk.ap, func=AF.Exp), reads=[esk.b], writes=[esk.b])
    if ASTOP == 'tabs':
        return
    ee = [A.alloc(f"ee{i}", [2, 2, 2, 128], F32) for i in range(2)]
    pT = [A.alloc(f"pT{i}", [2, 2, 2, 128], BF16) for i in range(2)]
    dn = A.alloc("dn", [2, 128], F32)
    spsv = K.ps[:, 0:2, :].rearrange("p j (a k q) -> p j a k q", a=2, k=2)
    spb = [PSB[0], PSB[1]]
    ops = K.ps[:, 2, 0:256].rearrange("p (a q) -> p a q", q=128)
    dps = K.ps[:, 3, 0:256].rearrange("p (a q) -> p a q", q=128)
    it = 0
    lim = str(ASTOP).startswith('p_')
    for i in range(1 if lim else 8):
        qc = slice(i * 128, (i + 1) * 128)
        for hbk in range(1 if lim else 4):
            kv = hbk
            e_, p_ = ee[it % 2], pT[it % 2]
            it += 1
            fns = []
            for hh in range(4):
                h = 4 * hbk + hh
                pair, j = h // 2, h % 2
                rows = slice(64 * j, 64 * j + 64)
                for kb in range(2):
                    kc0 = 128 * (i + kb)
                    fns.append(lambda e, hh=hh, kb=kb, rows=rows, pair=pair, kc0=kc0, j=j: e.matmul(
                        spsv[:, j, hh // 2, kb, :], lhsT=K.kTd.ap[rows, kv, kc0:kc0 + 128], rhs=K.qT.ap[rows, pair, qc], start=True, stop=True))
            P.pe(fns, reads=[K.kTd.b[i], K.kTd.b[i + 1], K.qT.b], writes=spb)
            if i == 0:
                for j in range(2):
                    P.op("dve", lambda e, e_=e_, j=j: e.scalar_tensor_tensor(out=e_.ap[:, j, :, 0, :], in0=spsv[:, j, :, 0, :], scalar=0.125,
                                                                             in1=bmt0.ap[:, 4 * hbk + 2 * j:4 * hbk + 2 * j + 2, :], op0=ALU.mult, op1=ALU.add),
                         reads=spb + [bmt0.b], writes=[e_.b])
                    P.op("dve", lambda e, e_=e_, j=j: e.scalar_tensor_tensor(out=e_.ap[:, j, :, 1, :], in0=spsv[:, j, :, 1, :], scalar=0.125,
                                                                             in1=bmt.ap[:, 4 * hbk + 2 * j:4 * hbk + 2 * j + 2, 1, :], op0=ALU.mult, op1=ALU.add),
                         reads=spb + [bmt.b], writes=[e_.b])
            else:
                P.op("dve", lambda e, e_=e_: e.scalar_tensor_tensor(
                    out=e_.ap, in0=spsv, scalar=0.125, in1=bmt.ap[:, 4 * hbk:4 * hbk + 4, :, :],
                    op0=ALU.mult, op1=ALU.add), reads=spb + [bmt.b], writes=[e_.b])
            if ASTOP == 'p_s':
                continue
            P.op("act", lambda e, e_=e_, p_=p_: e.activation(out=p_.ap, in_=e_.ap, func=AF.Exp), reads=[e_.b], writes=[p_.b])
            if ASTOP == 'p_e':
                continue
            fns = []
            for hh in range(4):
                pp, j = hh // 2, hh % 2
                rows = slice(64 * j, 64 * j + 64)
                for kb in range(2):
                    fns.append(lambda e, hh=hh, kb=kb, pp=pp, j=j, rows=rows, p_=p_: e.matmul(
                        ops[rows, pp, :], lhsT=K.vtok.ap[:, i + kb, kv * 64:(kv + 1) * 64], rhs=p_.ap[:, j, hh // 2, kb, :],
                        start=(kb == 0), stop=(kb == 1), tile_position=(0, 64 * j)))
                for kb in range(2):
                    fns.append(lambda e, hh=hh, kb=kb, pp=pp, j=j, rows=rows, p_=p_: e.matmul(
                        dps[rows, pp, :], lhsT=K.ones64.ap, rhs=p_.ap[:, j, hh // 2, kb, :],
                        start=(kb == 0), stop=(kb == 1), tile_position=(0, 64 * j)))
            P.pe(fns, reads=[K.vtok.b[i], K.vtok.b[i + 1], p_.b, K.bC], writes=[PSB[2], PSB[3]])
            P.op("dve", lambda e: e.tensor_tensor(out=dn.ap, in0=dps, in1=esk.ap[:, 2 * hbk:2 * hbk + 2].unsqueeze(2).broadcast_to([128, 2, 128]),
                                                  op=ALU.add), reads=[PSB[3], esk.b], writes=[dn.b])
            P.op("dve", lambda e: e.reciprocal(out=dn.ap, in_=dn.ap), reads=[dn.b], writes=[dn.b])
            P.op("dve", lambda e: e.tensor_tensor(out=K.oT.ap[:, 2 * hbk:2 * hbk + 2, qc], in0=ops, in1=dn.ap, op=ALU.mult),
                 reads=[PSB[2], dn.b], writes=K.oT.b)
    if ASTOP == 'prompt' or lim:
        return
    kc_ = [A.alloc(f"kc{i}", [4, 2, 64], BF16) for i in range(2)]
    vc_ = [A.alloc(f"vc{i}", [256], BF16) for i in range(2)]
    kcT = [A.alloc(f"kcT{i}", [4, 128], BF16) for i in range(2)]
    vnew = A.alloc("vnew", [16, 256], BF16)
    eS = A.alloc("eS", [2, 8, 4], F32)
    eN = A.alloc("eN", [2, 8, 4], F32)
    pS = [A.alloc(f"pS{i}", [2, 8, 4], BF16) for i in range(2)]
    pN = [A.alloc(f"pN{i}", [2, 8, 4], BF16) for i in range(2)]
    dS = A.alloc("dS", [8, 4], F32)
    with K.nc.allow_non_contiguous_dma(reason="tiny relayout"):
        for s in range(16):
            P.dma("sp", vnew.ap[0:4, s, :], K.vtok.ap[4 * s:4 * s + 4, 9, :], reads=[K.vtok.b[9]], writes=[vnew.b])
    ptk = K.ps[:, 4, :].bitcast(BF16)[:, 0:512].rearrange("p (k n) -> p k n", k=4)
    sscb = [K.ps[:, 5 + 2 * j, 0:32].rearrange("p (h t) -> p h t", t=4) for j in range(2)]
    ssnb = [K.ps[0:4, 5 + 2 * j, 32:64].rearrange("p (h t) -> p h t", t=4) for j in range(2)]
    osp = K.ps[:, 6, 0:32].rearrange("p (a t) -> p a t", t=4)
    dsp = K.ps[:, 6, 32:64].rearrange("p (a t) -> p a t", t=4)
    for s in range(16):
        kc, vc, kt, ps_, pn_ = kc_[s % 2], vc_[s % 2], kcT[s % 2], pS[s % 2], pN[s % 2]
        P.dma("pool", kc.ap, I["ck"][s].rearrange("k (v d) -> k v d", v=4).unsqueeze(2).broadcast_to([128, 4, 2, 64]), writes=[kc.b])
        P.dma("pool", vc.ap, I["cv"][s], writes=[vc.b])
        P.dma("sp", O["skk"][s, 0:124, :], I["ck"][s, 4:128, :], writes=[K.bout])
        P.dma("sp", O["skv"][s, 0:124, :], I["cv"][s, 4:128, :], writes=[K.bout])
        P.dma("sp", O["skk"][s, 124:128, :], K.kvsamp.ap[4 * s:4 * s + 4, 0:256], reads=[K.kvsamp.b], writes=[K.bout])
        P.dma("sp", O["skv"][s, 124:128, :], K.kvsamp.ap[4 * s:4 * s + 4, 256:512], reads=[K.kvsamp.b], writes=[K.bout])
        P.pe([lambda e, kv=kv: e.transpose(out=ptk[:, kv, :], in_=kc.ap[:, kv].rearrange("k a d -> k (a d)"),
                                           identity=K.identb.ap) for kv in range(4)], reads=[kc.b, K.bC], writes=[PSB[4]])
        P.op("act", lambda e: e.activation(out=kt.ap, in_=ptk, func=AF.Copy), reads=[PSB[4]], writes=[kt.b])
        qc0 = 1024 + 4 * s
        fns = []
        for h in range(16):
            kv, pair, j = h // 4, h // 2, h % 2
            rows = slice(64 * j, 64 * j + 64)
            fns.append(lambda e, h=h, kv=kv, pair=pair, rows=rows, j=j: e.matmul(sscb[j][:, pair, :], lhsT=kt.ap[rows, kv, :], rhs=K.qT.ap[rows, pair, qc0:qc0 + 4],
                                                                                start=True, stop=True))
            fns.append(lambda e, h=h, kv=kv, pair=pair, rows=rows, j=j: e.matmul(ssnb[j][:, pair, :], lhsT=K.kTd.ap[rows, kv, 128 + qc0:128 + qc0 + 4],
                                                                                rhs=K.qT.ap[rows, pair, qc0:qc0 + 4], start=True, stop=True))
        P.pe(fns, reads=[kt.b, K.kTd.b[9], K.qT.b], writes=[PSB[5], PSB[7]])
        for j in range(2):
            P.op("dve", lambda e, j=j: e.scalar_tensor_tensor(out=eS.ap[:, j], in0=sscb[j], scalar=0.125, in1=bmsc.ap[:, j], op0=ALU.mult, op1=ALU.add),
                 reads=[PSB[5], PSB[7], bmsc.b], writes=[eS.b])
            P.op("dve", lambda e, j=j: e.scalar_tensor_tensor(out=eN.ap[0:4, j], in0=ssnb[j], scalar=0.125, in1=bmsn.ap[0:4, j], op0=ALU.mult, op1=ALU.add),
                 reads=[PSB[5], PSB[7], bmsn.b], writes=[eN.b])
        P.op("act", lambda e, ps_=ps_: e.activation(out=ps_.ap, in_=eS.ap, func=AF.Exp), reads=[eS.b], writes=[ps_.b])
        P.op("act", lambda e, pn_=pn_: e.activation(out=pn_.ap[0:4], in_=eN.ap[0:4], func=AF.Exp), reads=[eN.b], writes=[pn_.b])
        fns = []
        for h in range(16):
            kv, pair, j = h // 4, h // 2, h % 2
            rows = slice(64 * j, 64 * j + 64)
            fns.append(lambda e, h=h, kv=kv, pair=pair, rows=rows, j=j: e.matmul(osp[rows, pair, :], lhsT=vc.ap[:, kv * 64:(kv + 1) * 64], rhs=ps_.ap[:, j, pair, :],
                                                                                start=True, stop=False, tile_position=(0, 64 * j)))
            fns.append(lambda e, h=h, kv=kv, pair=pair, rows=rows, j=j: e.matmul(osp[rows, pair, :], lhsT=vnew.ap[0:4, s, kv * 64:(kv + 1) * 64], rhs=pn_.ap[0:4, j, pair, :],
                                                                                start=False, stop=True, tile_position=(0, 64 * j)))
            fns.append(lambda e, h=h, pair=pair, rows=rows, j=j: e.matmul(dsp[rows, pair, :], lhsT=K.ones64.ap, rhs=ps_.ap[:, j, pair, :],
                                                                         start=True, stop=False, tile_position=(0, 64 * j)))
            fns.append(lambda e, h=h, pair=pair, rows=rows, j=j: e.matmul(dsp[rows, pair, :], lhsT=K.ones64.ap[0:4], rhs=pn_.ap[0:4, j, pair, :],
                                                                         start=False, stop=True, tile_position=(0, 64 * j)))
        P.pe(fns, reads=[vc.b, vnew.b, ps_.b, pn_.b, K.bC], writes=[PSB[6]])
        P.op("dve", lambda e: e.tensor_tensor(out=dS.ap, in0=dsp, in1=esk.ap.unsqueeze(2).broadcast_to([128, 8, 4]), op=ALU.add),
             reads=[PSB[6], esk.b], writes=[dS.b])
        P.op("dve", lambda e: e.reciprocal(out=dS.ap, in_=dS.ap), reads=[dS.b], writes=[dS.b])
        P.op("dve", lambda e: e.tensor_tensor(out=K.oT.ap[:, :, qc0:qc0 + 4], in0=osp, in1=dS.ap, op=ALU.mult),
             reads=[PSB[6], dS.b], writes=K.oT.b)
    P.dma("sp", O["pkv"], K.kvlast.ap, reads=[K.kvlast.b], writes=[K.bout])
    for t in [bmt, bmt0, bmsc, bmsn, esk, dn, vnew, eS, eN, dS] + ee + pT + kc_ + vc_ + kcT + pS + pN:
        A.free(t)
    for t in [K.qT, K.kTd, K.vtok, K.kvlast, K.kvsamp]:
        A.free(t)


def merge_phase(K):
    P, I, A = K.P, K.I, K.A
    hb = K.hT.b
    K.mT = A.alloc("mT", [16, NT], BF16, top=True)
    ring = Ring(K, 3, 4)
    s1 = A.alloc("s1", [NT], BF16)
    s2 = A.alloc("s2", [NT], BF16)
    s3 = A.alloc("s3", [NT], BF16)
    tv = A.alloc("tv", [NT], F32)
    tu = A.alloc("tu", [NT], F32)
    acc = 0
    for sg in range(4):
        c0 = sg * 512
        wga = ring.load(I["w_in"][:, 2560 + c0:2560 + c0 + 512], 16, 512)
        wgb = ring.load(I["w_in"][:, 4608 + c0:4608 + c0 + 512], 16, 512)
        wval = ring.load(I["w_glu_val"][:, c0:c0 + 512], 8, 512)
        wgate = ring.load(I["w_glu_gate"][:, c0:c0 + 512], 8, 512)
        for nt in range(4):
            n = 4 * sg + nt
            if nt == 0:
                pass
            fm_matmul(K, wga[0], wga[1], 16, nt, K.hT.ap, hb, acc)
            for pv, sl in acc_views(K, acc):
                P.op("act", lambda e, pv=pv, sl=sl: e.activation(out=s1.ap[:, sl], in_=pv, func=AF.Sigmoid), reads=acc_bufs(K, acc), writes=[s1.b])
            acc ^= 1
            fm_matmul(K, wgate[0], wgate[1], 8, nt, K.gT.ap, K.gT.b, acc)
            for pv, sl in acc_views(K, acc):
                P.op("act", lambda e, pv=pv, sl=sl: e.activation(out=s2.ap[:, sl], in_=pv, func=AF.Sigmoid), reads=acc_bufs(K, acc), writes=[s2.b])
            acc ^= 1
            fm_matmul(K, wval[0], wval[1], 8, nt, K.gT.ap, K.gT.b, acc)
            for pv, sl in acc_views(K, acc):
                P.op("dve", lambda e, pv=pv, sl=sl: e.tensor_tensor(out=tv.ap[:, sl], in0=pv, in1=s1.ap[:, sl], op=ALU.mult),
                     reads=acc_bufs(K, acc) + [s1.b], writes=[tv.b])
            P.op("dve", lambda e: e.tensor_tensor(out=tv.ap, in0=tv.ap, in1=s2.ap, op=ALU.mult), reads=[tv.b, s2.b], writes=[tv.b])
            acc ^= 1
            fm_matmul(K, wgb[0], wgb[1], 16, nt, K.hT.ap, hb, acc)
            for pv, sl in acc_views(K, acc):
                P.op("act", lambda e, pv=pv, sl=sl: e.activation(out=s3.ap[:, sl], in_=pv, func=AF.Sigmoid), reads=acc_bufs(K, acc), writes=[s3.b])
            acc ^= 1
            if nt == 0:
                wab = ring.load(I["w_attn_br"][:, c0:c0 + 512], 8, 512)
            fm_matmul(K, wab[0], wab[1], 8, nt, K.oT.ap, K.oT.b, acc)
            for pv, sl in acc_views(K, acc):
                P.op("dve", lambda e, pv=pv, sl=sl: e.tensor_tensor(out=tu.ap[:, sl], in0=pv, in1=s3.ap[:, sl], op=ALU.mult),
                     reads=acc_bufs(K, acc) + [s3.b], writes=[tu.b])
            P.op("dve", lambda e, n=n: e.tensor_tensor(out=K.mT.ap[:, n, :], in0=tu.ap, in1=tv.ap, op=ALU.add), reads=[tu.b, tv.b], writes=[K.mT.b])
            acc ^= 1
    ring.free()
    for t in (s1, s2, s3, tv, tu):
        A.free(t)


def stats_rstd(K, xT, sq, rstd):
    P = K.P
    P.op("act", lambda e: e.activation(out=sq.ap, in_=xT.ap, func=AF.Square), reads=[xT.b], writes=[sq.b])
    fns = []
    for bi, (t0, n) in enumerate(BLKS):
        for c in range(16):
            fns.append(lambda e, bi=bi, t0=t0, n=n, c=c: e.matmul(K.ps[:, bi, 0:n], lhsT=K.onesm.ap, rhs=sq.ap[:, c, t0:t0 + n],
                                                                 start=(c == 0), stop=(c == 15)))
    P.pe(fns, reads=[sq.b, K.bC], writes=acc_bufs(K, 0))
    for pv, sl in acc_views(K, 0):
        P.op("act", lambda e, pv=pv, sl=sl: e.activation(out=rstd.ap[:, sl], in_=pv, func=AF.Sqrt, bias=K.epsc.ap, scale=1.0),
             reads=acc_bufs(K, 0) + [K.bC], writes=[rstd.b])
    P.op("dve", lambda e: e.reciprocal(out=rstd.ap, in_=rstd.ap), reads=[rstd.b], writes=[rstd.b])


def post_phase(K):
    P, I, A, O = K.P, K.I, K.A, K.O
    PSB = K.psb
    xT = A.alloc("xT", [16, NT], F32, top=True)
    rstd = A.alloc("rstd", [NT], F32, top=True)
    gcol = A.alloc("gcol", [2, 16], F32, top=True)
    act = [A.alloc(f"actT{i}", [8, NT], BF16, top=True) for i in range(1)]
    sgt = [A.alloc(f"sg{i}", [NT], BF16, top=True) for i in range(2)]
    xl = [A.alloc(f"xl{i}", [D], F32) for i in range(2)]
    for t in range(9):
        n = 128 if t < 8 else 64
        x_ = xl[t % 2]
        P.dma("sp", x_.ap[:n], I["xo"][t * 128:t * 128 + n, :], writes=[x_.b])
        for q4 in range(4):
            bank = q4 % 2
            pt = K.ps[:, bank, :].rearrange("p (c n) -> p c n", c=4)
            P.pe([lambda e, c=c, pt=pt, q4=q4: e.transpose(out=pt[:, c, 0:n], in_=x_.ap[:n, (4 * q4 + c) * 128:(4 * q4 + c + 1) * 128],
                                                           identity=K.identf.ap[:n, :n]) for c in range(4)],
                 reads=[x_.b, K.bC], writes=[PSB[bank]])
            P.op("act", lambda e, pt=pt, q4=q4: e.activation(out=xT.ap[:, 4 * q4:4 * q4 + 4, t * 128:t * 128 + n], in_=pt[:, :, 0:n], func=AF.Copy),
                 reads=[PSB[bank]], writes=[xT.b])
    for x_ in xl:
        A.free(x_)
    ring = Ring(K, 2)
    acc = 0
    for g in range(4):
        wv, wb = ring.load(I["w_out"][:, g * 512:(g + 1) * 512], 16, 512)
        for nt in range(4):
            n = 4 * g + nt
            fm_matmul(K, wv, wb, 16, nt, K.mT.ap, [K.mT.b], acc)
            for pv, sl in acc_views(K, acc):
                P.op("dve", lambda e, pv=pv, sl=sl, n=n: e.tensor_tensor(out=xT.ap[:, n, sl], in0=pv, in1=xT.ap[:, n, sl], op=ALU.add),
                     reads=acc_bufs(K, acc) + [xT.b], writes=[xT.b])
            acc ^= 1
    ring.free()
    A.free(K.mT)
    sq = A.alloc("sq", [16, NT], BF16)
    with K.nc.allow_non_contiguous_dma(reason="tiny"):
        P.dma("sp", gcol.ap[:, 0, :], I["norm_ffn"].rearrange("(c p) -> p c", p=128), writes=[gcol.b])
        P.dma("sp", gcol.ap[:, 1, :], I["norm_final"].rearrange("(c p) -> p c", p=128), writes=[gcol.b])
    stats_rstd(K, xT, sq, rstd)
    h2 = K.hT
    h2b = Buf("h2T")
    _merge(h2b.r, {})
    for b in K.hT.b:
        if b.w is not None:
            _merge(h2b.r, {b.w[0].num: b.w})
        _merge(h2b.r, b.r)
    for c in range(16):
        P.op("dve", lambda e, c=c: e.scalar_tensor_tensor(out=h2.ap[:, c, :], in0=xT.ap[:, c, :], scalar=gcol.ap[:, 0, c:c + 1], in1=rstd.ap,
                                                          op0=ALU.mult, op1=ALU.mult), reads=[xT.b, gcol.b, rstd.b], writes=[h2b])
    A.free(sq)
    ring = Ring(K, 3, 2)
    nsg = (HID + 1023) // 1024
    k = 0
    for sg in range(nsg):
        h0 = sg * 1024
        hw = min(1024, HID - h0)
        nft = hw // 128
        a_ = act[0]
        for half in range(hw // 512):
            wg = ring.load(I["w_ffn_in"][:, h0 + half * 512:h0 + half * 512 + 512], 16, 512)
            wu = ring.load(I["w_ffn_in"][:, HID + h0 + half * 512:HID + h0 + half * 512 + 512], 16, 512)
            for nt in range(4):
                f = 4 * half + nt
                s_ = sgt[k % 2]
                k += 1
                fm_matmul(K, wg[0], wg[1], 16, nt, h2.ap, [h2b], acc)
                for pv, sl in acc_views(K, acc):
                    P.op("act", lambda e, pv=pv, sl=sl, s_=s_: e.activation(out=s_.ap[:, sl], in_=pv, func=AF.Silu), reads=acc_bufs(K, acc), writes=[s_.b])
                acc ^= 1
                fm_matmul(K, wu[0], wu[1], 16, nt, h2.ap, [h2b], acc)
                for pv, sl in acc_views(K, acc):
                    P.op("dve", lambda e, pv=pv, sl=sl, s_=s_, f=f, a_=a_: e.tensor_tensor(out=a_.ap[:, f, sl], in0=pv, in1=s_.ap[:, sl], op=ALU.mult),
                         reads=acc_bufs(K, acc) + [s_.b], writes=[a_.b])
                acc ^= 1
        for g in range(4):
            wv, wb = ring.load(I["w_ffn_out"][h0:h0 + hw, g * 512:(g + 1) * 512], nft, 512)
            for nt in range(4):
                n = 4 * g + nt
                fm_matmul(K, wv, wb, nft, nt, a_.ap, [a_.b], acc)
                for pv, sl in acc_views(K, acc):
                    P.op("dve", lambda e, pv=pv, sl=sl, n=n: e.tensor_tensor(out=xT.ap[:, n, sl], in0=pv, in1=xT.ap[:, n, sl], op=ALU.add),
                         reads=acc_bufs(K, acc) + [xT.b], writes=[xT.b])
                acc ^= 1
    ring.free()
    for t in act + sgt:
        A.free(t)
    sq = A.alloc("sq2", [16, NT], BF16)
    stats_rstd(K, xT, sq, rstd)
    A.free(sq)
    yf = [A.alloc(f"yf{i}", [16, 128], F32) for i in range(2)]
    yo = [A.alloc(f"yo{i}", [D], F32) for i in range(2)]
    for t in range(9):
        n = 128 if t < 8 else 64
        cs = slice(t * 128, t * 128 + n)
        y_, o_ = yf[t % 2], yo[t % 2]
        for c in range(16):
            P.op("dve", lambda e, c=c: e.scalar_tensor_tensor(out=y_.ap[:, c, 0:n], in0=xT.ap[:, c, cs], scalar=gcol.ap[:, 1, c:c + 1], in1=rstd.ap[:, cs],
                                                              op0=ALU.mult, op1=ALU.mult), reads=[xT.b, gcol.b, rstd.b], writes=[y_.b])
        for q4 in range(4):
            bank = 6 + q4 % 2
            pt = K.ps[:n, bank, :].rearrange("p (c n) -> p c n", c=4)
            P.pe([lambda e, c=c, pt=pt, q4=q4: e.transpose(out=pt[:, c, :], in_=y_.ap[:, 4 * q4 + c, 0:n], identity=K.identf.ap) for c in range(4)],
                 reads=[y_.b, K.bC], writes=[PSB[bank]])
            P.op("act", lambda e, pt=pt, q4=q4: e.activation(out=o_.ap[:n, q4 * 512:(q4 + 1) * 512], in_=pt.rearrange("p c n -> p (c n)"), func=AF.Copy),
                 reads=[PSB[bank]], writes=[o_.b])
        P.dma("sp", O["yo"][t * 128:t * 128 + n, :], o_.ap[:n], reads=[o_.b], writes=[K.bout])


def build_program():
    nc = bass.Bass("TRN2", target_bir_lowering=False)
    K = Ctx()
    K.nc = nc
    K.P = Prog(nc)
    P = K.P

    def din(name, shape, dt=F32):
        return nc.dram_tensor(name, list(shape), dt, kind="ExternalInput").ap()

    def dout(name, shape, dt=F32):
        return nc.dram_tensor(name, list(shape), dt, kind="ExternalOutput").ap()

    I = {}
    I["xo"] = din("xo", [NT, D])
    I["xp"] = din("xp", [NPRE, D])
    I["w_in"] = din("w_in", [D, INC])
    I["w_glu_val"] = din("w_glu_val", [1024, D])
    I["w_glu_gate"] = din("w_glu_gate", [1024, D])
    I["w_attn_br"] = din("w_attn_br", [1024, D])
    I["w_out"] = din("w_out", [D, D])
    I["w_ffn_in"] = din("w_ffn_in", [D, 2 * HID])
    I["w_ffn_out"] = din("w_ffn_out", [HID, D])
    for nm in ["norm_attn", "norm_ffn", "norm_final"]:
        I[nm] = din(nm, [D])
    I["lam_re"] = din("lam_re", [64, 64])
    I["lam_im"] = din("lam_im", [64, 64])
    I["log_dt"] = din("log_dt", [64])
    I["b_re"] = din("b_re", [64, 64, 16])
    I["b_im"] = din("b_im", [64, 64, 16])
    I["c_re"] = din("c_re", [64, 16, 64])
    I["c_im"] = din("c_im", [64, 16, 64])
    I["d_skip"] = din("d_skip", [1024])
    I["sinks"] = din("sinks", [16])
    I["sre"] = din("sre", [16, 64, 64])
    I["sim"] = din("sim", [16, 64, 64])
    I["ck"] = din("ck", [16, 128, 256])
    I["cv"] = din("cv", [16, 128, 256])
    I["bmt"] = din("bmt", [128, 16, 2, 128])
    I["bmt0"] = din("bmt0", [128, 16, 128])
    I["bmsc"] = din("bmsc", [128, 16, 4])
    I["bmsn"] = din("bmsn", [4, 16, 4])
    I["kcol"] = din("kcol", [128, 1])
    I["m8"] = din("m8", [128, 8])
    I["m88"] = din("m88", [128, 8, 8])
    I["trow"] = din("trow", [128, 64])
    O = {}
    O["yo"] = dout("yo", [NT, D])
    O["pst"] = dout("pst", [2, 64, 64])
    O["pkv"] = dout("pkv", [128, 512])
    O["sst"] = dout("sst", [2, 16, 64, 64])
    O["skk"] = dout("skk", [16, 128, 256])
    O["skv"] = dout("skv", [16, 128, 256])
    if DEBUG:
        O["dbg_h"] = dout("dbg_h", [128, 2, 32])
        O["dbg_hT"] = dout("dbg_hT", [128, 16, NT], BF16)
        O["dbg_uT"] = dout("dbg_uT", [128, 8, NT], BF16)
        O["dbg_gT"] = dout("dbg_gT", [128, 8, NT], BF16)
        O["dbg_oT"] = dout("dbg_oT", [128, 8, NT], BF16)
        O["dbg_mT"] = dout("dbg_mT", [128, 16, NT], BF16)
    K.I, K.O = I, O
    K.bout = Buf("out")
    K.A = Arena(nc)
    A = K.A
    K.ps = nc.alloc_psum_tensor("psum", [128, 8, 512], F32)
    K.psb = [Buf(f"psb{i}") for i in range(8)]

    K.identf = A.alloc("identf", [128], F32, top=True)
    K.identb = A.alloc("identb", [128], BF16, top=True)
    K.ones64 = A.alloc("ones64", [64], BF16, top=True)
    K.onesm = A.alloc("onesm", [128], BF16, top=True)
    K.epsc = A.alloc("epsc", [1], F32, top=True)
    bC = Buf("const")
    K.bC = bC
    P.op("pool", lambda e: e.memset(K.identf.ap, 0.0), writes=[bC])
    P.op("pool", lambda e: e.affine_select(out=K.identf.ap, in_=K.identf.ap, pattern=[[-1, 128]], compare_op=ALU.not_equal,
                                          fill=1.0, base=0, channel_multiplier=1), reads=[bC], writes=[bC])
    P.op("pool", lambda e: e.tensor_copy(out=K.identb.ap, in_=K.identf.ap), reads=[bC], writes=[bC])
    P.op("pool", lambda e: e.memset(K.ones64.ap, 1.0), writes=[bC])
    P.op("pool", lambda e: e.memset(K.onesm.ap, 1.0 / D), writes=[bC])
    P.op("pool", lambda e: e.memset(K.epsc.ap, EPS), writes=[bC])

    K.hT = A.alloc("hT", [16, NT], BF16, nbufs=9, top=True)
    K.gT = A.alloc("gT", [8, NT], BF16, top=True)
    K.gT.b = [K.gT.b]
    K.oT = A.alloc("oT", [8, NT], BF16, top=True)
    K.oT.b = [K.oT.b]
    def stop(name):
        return STOP == name

    ssm_tables(K)
    if DEBUG:
        O["dbg_A4"] = dout("dbg_A4", [128, 2, 2, 32])
        O["dbg_A4c"] = dout("dbg_A4c", [128, 2, 2, 32])
        O["dbg_BbR"] = dout("dbg_BbR", [128, 32, 16])
        O["dbg_BbI"] = dout("dbg_BbI", [128, 32, 16])
        O["dbg_ApR"] = dout("dbg_ApR", [128, 64, 64], BF16)
        O["dbg_ApI"] = dout("dbg_ApI", [128, 64, 64], BF16)
        for nm, t in (("dbg_A4", K.A4), ("dbg_A4c", K.A4c), ("dbg_BbR", K.BbR), ("dbg_BbI", K.BbI), ("dbg_ApR", K.ApR), ("dbg_ApI", K.ApI)):
            P.dma("sp", O[nm], t.ap, reads=bl(t.b), writes=[K.bout])
    if not stop("tables"):
        K.kTd = A.alloc("kTd", [4, 128 + NT], BF16, nbufs=10, top=True)
        K.vtok = A.alloc("vtok", [10, 256], BF16, nbufs=10, top=True)
        K.kvlast = A.alloc("kvlast", [512], F32)
        K.kvsamp = A.alloc("kvsamp", [512], F32)
        prefix_phase(K)
    if not (stop("tables") or stop("prefix")):
        K.uT = A.alloc("uT", [8, NT], BF16, top=True)
        K.uT.b = [K.uT.b]
        own_norm(K)
        for t in K.xts + K.xss + [K.junk, K.ss]:
            A.free(t)
        if DEBUG:
            P.dma("sp", O["dbg_hT"], K.hT.ap, reads=bl(K.hT.b), writes=[K.bout])
    if STOP not in ("tables", "prefix", "own_norm"):
        proj_phase(K)
        if DEBUG:
            P.dma("sp", O["dbg_uT"], K.uT.ap, reads=bl(K.uT.b), writes=[K.bout])
    if STOP not in ("tables", "prefix", "own_norm", "proj"):
        ssm_tables_own(K)
        A.free(K.BbR)
        A.free(K.BbI)
        ssm_own(K)
        if DEBUG:
            P.dma("sp", O["dbg_gT"], K.gT.ap, reads=bl(K.gT.b), writes=[K.bout])
    if STOP not in ("tables", "prefix", "own_norm", "proj", "ssm"):
        attention(K)
        if DEBUG:
            P.dma("sp", O["dbg_oT"], K.oT.ap, reads=bl(K.oT.b), writes=[K.bout])
    if STOP not in ("tables", "prefix", "own_norm", "proj", "ssm", "attn"):
        merge_phase(K)
        if DEBUG:
            P.dma("sp", O["dbg_mT"], K.mT.ap, reads=bl(K.mT.b), writes=[K.bout])
        A.free(K.gT)
        A.free(K.oT)
    if STOP not in ("tables", "prefix", "own_norm", "proj", "ssm", "attn", "merge"):
        post_phase(K)
    P.finish()
    K.ninstr = P.ninstr
    return nc, K


_CACHE = {}


def _host_consts(rel_bias, qidx):
    rb = np.asarray(rel_bias, np.float32)
    k = np.arange(128)[:, None, None]
    kb = np.arange(2)[None, :, None]
    q = np.arange(128)[None, None, :]
    dist = q + 128 - (kb * 128 + k)
    valid = (dist >= 0) & (dist < 128)
    bk = t5_bucket(dist)
    bias = rb[bk]
    bias = np.where(valid[..., None], bias, np.float32(NEG)).astype(np.float32)
    hperm = np.array([4 * (n // 4) + 2 * ((n % 4) % 2) + (n % 4) // 2 for n in range(16)])
    bmt = np.ascontiguousarray(np.transpose(bias, (0, 3, 1, 2))[:, hperm])
    bmt0 = bmt[:, :, 0, :].copy() if qidx > 0 else np.full((128, 16, 128), NEG, np.float32)
    j = np.arange(132)[:, None]
    t = np.arange(4)[None, :]
    dist = t + 128 - j
    valid = (dist >= 0) & (dist < 128)
    bs = np.where(valid[..., None], rb[t5_bucket(dist)], np.float32(NEG)).astype(np.float32)
    sperm = np.array([2 * (n % 8) + n // 8 for n in range(16)])
    bs = np.ascontiguousarray(np.transpose(bs, (0, 2, 1))[:, sperm])
    return bmt, np.ascontiguousarray(bmt0), np.ascontiguousarray(bs[:128]), np.ascontiguousarray(bs[128:])


def kernel(x_prompt, x_sample, state_ssm_re, state_ssm_im, cache_win_k, cache_win_v, rel_bias,
           norm_attn, w_in, lam_re, lam_im, log_dt, b_re, b_im, c_re, c_im, d_skip,
           w_glu_val, w_glu_gate, w_attn_br, sinks, w_out, norm_ffn, w_ffn_in, w_ffn_out, norm_final):
    f = lambda a: np.ascontiguousarray(np.asarray(a, dtype=np.float32))
    x_prompt, x_sample = f(x_prompt), f(x_sample)
    if "nc" not in _CACHE:
        _CACHE["nc"], _CACHE["K"] = build_program()
    nc = _CACHE["nc"]
    shared = {
        "w_in": f(w_in)[0], "w_glu_val": f(w_glu_val)[0], "w_glu_gate": f(w_glu_gate)[0], "w_attn_br": f(w_attn_br)[0],
        "w_out": f(w_out)[0], "w_ffn_in": f(w_ffn_in)[0], "w_ffn_out": f(w_ffn_out)[0],
        "norm_attn": f(norm_attn)[0], "norm_ffn": f(norm_ffn)[0], "norm_final": f(norm_final),
        "lam_re": f(lam_re)[0], "lam_im": f(lam_im)[0], "log_dt": f(log_dt)[0], "b_re": f(b_re)[0], "b_im": f(b_im)[0],
        "c_re": f(c_re)[0], "c_im": f(c_im)[0], "d_skip": f(d_skip)[0], "sinks": f(sinks)[0],
        "kcol": (127 - np.arange(128, dtype=np.float32)).reshape(128, 1),
        "m8": (np.arange(128)[:, None] // 16 == np.arange(8)[None, :]).astype(np.float32),
        "m88": np.ascontiguousarray(np.broadcast_to(np.eye(8, dtype=np.float32), (128, 8, 8))),
        "trow": np.ascontiguousarray(np.broadcast_to(np.arange(1, 65, dtype=np.float32), (128, 64))),
    }
    sre, sim = f(state_ssm_re)[0], f(state_ssm_im)[0]
    ck = f(cache_win_k)[0].reshape(128, 128, 256)
    cv = f(cache_win_v)[0].reshape(128, 128, 256)
    in_maps = []
    for c in range(8):
        b, q = c // 4, c % 4
        xo = np.concatenate([x_prompt[b, 1024 * q:1024 * (q + 1)], x_sample[16 * c:16 * c + 16].reshape(64, D)], axis=0)
        xp = np.zeros((NPRE, D), np.float32)
        if q > 0:
            xp[NPRE - 1024 * q:] = x_prompt[b, 0:1024 * q]
        bmt, bmt0, bmsc, bmsn = _host_consts(rel_bias, q)
        m = dict(shared)
        m.update({"xo": np.ascontiguousarray(xo), "xp": xp, "sre": sre[16 * c:16 * c + 16], "sim": sim[16 * c:16 * c + 16],
                  "ck": ck[16 * c:16 * c + 16], "cv": cv[16 * c:16 * c + 16], "bmt": bmt, "bmt0": bmt0, "bmsc": bmsc, "bmsn": bmsn})
        in_maps.append({k: np.ascontiguousarray(v) for k, v in m.items()})
    res = run_bass_kernel_spmd(nc, in_maps, core_ids=list(range(8)))
    R = res.results
    _CACHE["last"] = R
    y_prompt = np.zeros((2, 4096, D), np.float32)
    y_sample = np.zeros((128, 4, D), np.float32)
    p_re = np.zeros((1, 2, 64, 64), np.float32)
    p_im = np.zeros((1, 2, 64, 64), np.float32)
    p_k = np.zeros((1, 2, 128, 4, 64), np.float32)
    p_v = np.zeros((1, 2, 128, 4, 64), np.float32)
    s_re = np.zeros((1, 128, 64, 64), np.float32)
    s_im = np.zeros((1, 128, 64, 64), np.float32)
    s_k = np.zeros((1, 128, 128, 4, 64), np.float32)
    s_v = np.zeros((1, 128, 128, 4, 64), np.float32)
    for c in range(8):
        b, q = c // 4, c % 4
        r = R[c]
        y_prompt[b, 1024 * q:1024 * (q + 1)] = r["yo"][:1024]
        y_sample[16 * c:16 * c + 16] = r["yo"][1024:].reshape(16, 4, D)
        if q == 3:
            p_re[0, b] = r["pst"][0]
            p_im[0, b] = r["pst"][1]
            p_k[0, b] = r["pkv"][:, :256].reshape(128, 4, 64)
            p_v[0, b] = r["pkv"][:, 256:].reshape(128, 4, 64)
        s_re[0, 16 * c:16 * c + 16] = r["sst"][0]
        s_im[0, 16 * c:16 * c + 16] = r["sst"][1]
        s_k[0, 16 * c:16 * c + 16] = r["skk"].reshape(16, 128, 4, 64)
        s_v[0, 16 * c:16 * c + 16] = r["skv"].reshape(16, 128, 4, 64)
    return (y_prompt, y_sample, p_re, p_im, p_k, p_v, s_re, s_im, s_k, s_v)
```

```python
import numpy as np
import concourse.bass as bass
import concourse.mybir as mybir
from concourse.bass_utils import run_bass_kernel_spmd

F32 = mybir.dt.float32
BF16 = mybir.dt.bfloat16
I32 = mybir.dt.int32
AF = mybir.ActivationFunctionType
ALU = mybir.AluOpType
AX = mybir.AxisListType

D = 2048
NT = 1088
NPRE = 3072
NCH = 16
HID = 5632
INC = 6656
EPS = 1e-6
NEG = -1e30
BLKS = [(0, 512), (512, 512), (1024, 64)]
NDS = 24
ARENA_BYTES = 206 * 1024
DEBUG = False
STOP = None
ASTOP = None
DT_SIZE = {F32: 4, BF16: 2, I32: 4}


class Buf:
    __slots__ = ("name", "w", "r")

    def __init__(self, name="", guards=None):
        self.name = name
        self.w = None
        self.r = dict(guards) if guards else {}


def _merge(dst, src):
    for k, ev in src.items():
        if k not in dst or dst[k][1] < ev[1]:
            dst[k] = ev


class Tile:
    __slots__ = ("ap", "b", "off", "nbytes", "name")


class Arena:
    def __init__(self, nc):
        self.t = nc.alloc_sbuf_tensor("arena", [128, ARENA_BYTES // 4], F32)
        self.free_list = [[0, ARENA_BYTES, {}]]

    def alloc(self, name, shape, dt, nbufs=1, top=False):
        n = int(np.prod(shape)) * DT_SIZE[dt]
        n = (n + 63) // 64 * 64
        order = range(len(self.free_list) - 1, -1, -1) if top else range(len(self.free_list))
        for i in order:
            off, size, g = self.free_list[i]
            if size >= n:
                if size == n:
                    self.free_list.pop(i)
                elif top:
                    self.free_list[i] = [off, size - n, dict(g)]
                    off = off + size - n
                else:
                    self.free_list[i] = [off + n, size - n, dict(g)]
                t = Tile()
                t.name, t.off, t.nbytes = name, off, n
                v = self.t[:, off // 4:(off + n) // 4]
                if dt != F32:
                    v = v.bitcast(dt)
                ne = int(np.prod(shape))
                v = v[:, 0:ne]
                if len(shape) == 2:
                    v = v.rearrange("p (a b) -> p a b", a=shape[0])
                elif len(shape) == 3:
                    v = v.rearrange("p (a b c) -> p a b c", a=shape[0], b=shape[1])
                elif len(shape) == 4:
                    v = v.rearrange("p (a b c d) -> p a b c d", a=shape[0], b=shape[1], c=shape[2])
                t.ap = v
                if nbufs == 1:
                    t.b = Buf(name, g)
                else:
                    t.b = [Buf(f"{name}{j}", g) for j in range(nbufs)]
                return t
        raise RuntimeError(f"arena OOM for {name} ({n} B); free={[(o, s) for o, s, _ in self.free_list]}")

    def free(self, t):
        g = {}
        bufs = t.b if isinstance(t.b, list) else [t.b]
        for b in bufs:
            if b.w is not None:
                _merge(g, {b.w[0].num: b.w})
            _merge(g, b.r)
        self.free_list.append([t.off, t.nbytes, g])
        self.free_list.sort(key=lambda x: x[0])
        out = []
        for blk in self.free_list:
            if out and out[-1][0] + out[-1][1] == blk[0]:
                out[-1][1] += blk[1]
                _merge(out[-1][2], blk[2])
            else:
                out.append(blk)
        self.free_list = out


class Prog:
    def __init__(self, nc):
        self.nc = nc
        self.eng = {"pe": nc.tensor, "act": nc.scalar, "dve": nc.vector, "pool": nc.gpsimd, "sp": nc.sync}
        self.csem = {e: nc.alloc_semaphore(name=f"c_{e}") for e in self.eng}
        self.ccnt = {e: 0 for e in self.eng}
        self.dsems = [nc.alloc_semaphore(name=f"d_{i}") for i in range(NDS)]
        self.dcnt = [0] * NDS
        self.dnext = 0
        self.waited = {e: {} for e in self.eng}
        self.ninstr = 0

    def _wait(self, e, ev):
        sem, val = ev
        k = sem.num
        if self.waited[e].get(k, 0) >= val:
            return
        self.eng[e].wait_ge(sem, val)
        self.waited[e][k] = val

    def _deps(self, e, reads, writes, skip=None, relax=False):
        own = self.csem[e].num
        lim = self.ccnt[e] - 1

        def need(ev):
            if ev[0].num == skip:
                return False
            if relax and ev[0].num == own and ev[1] <= lim:
                return False
            return True
        for b in reads:
            if b.w is not None and need(b.w):
                self._wait(e, b.w)
        for b in writes:
            if b.w is not None and need(b.w):
                self._wait(e, b.w)
            for k, ev in b.r.items():
                if need(ev):
                    self._wait(e, ev)

    @staticmethod
    def _mark(ev, reads, writes):
        for b in reads:
            b.r[ev[0].num] = ev
        for b in writes:
            b.w = ev
            b.r = {}

    def op(self, e, fn, reads=(), writes=(), relax=False):
        self._deps(e, reads, writes, relax=relax)
        ins = fn(self.eng[e])
        self.ccnt[e] += 1
        self.ninstr += 1
        ins.then_inc(self.csem[e], 1)
        self._mark((self.csem[e], self.ccnt[e]), reads, writes)

    def group(self, e, fns, reads=(), writes=()):
        self._deps(e, reads, writes)
        ins = None
        for fn in fns:
            ins = fn(self.eng[e])
            self.ninstr += 1
        self.ccnt[e] += 1
        ins.then_inc(self.csem[e], 1)
        self._mark((self.csem[e], self.ccnt[e]), reads, writes)

    def pe(self, fns, reads=(), writes=()):
        self._deps("pe", reads, writes, skip=self.csem["pe"].num)
        ins = None
        for fn in fns:
            ins = fn(self.nc.tensor)
            self.ninstr += 1
        self.ccnt["pe"] += 1
        ins.then_inc(self.csem["pe"], 1)
        self._mark((self.csem["pe"], self.ccnt["pe"]), reads, writes)

    def dma(self, e, out, in_, reads=(), writes=(), **kw):
        i = self.dnext
        self.dnext = (i + 1) % NDS
        if self.dcnt[i] > 0:
            self._wait(e, (self.dsems[i], self.dcnt[i]))
        self._deps(e, reads, writes)
        ins = self.eng[e].dma_start(out=out, in_=in_, **kw)
        self.ninstr += 1
        self.dcnt[i] += 16
        ins.then_inc(self.dsems[i], 16)
        self._mark((self.dsems[i], self.dcnt[i]), reads, writes)

    def finish(self):
        for e in self.eng:
            if e != "sp" and self.ccnt[e] > 0:
                self._wait("sp", (self.csem[e], self.ccnt[e]))
        for i in range(NDS):
            if self.dcnt[i] > 0:
                self._wait("sp", (self.dsems[i], self.dcnt[i]))


def t5_bucket(dist):
    n = np.maximum(dist, 0)
    max_exact = 16
    large = max_exact + (np.log(np.maximum(n, 1) / max_exact) / np.log(128 / max_exact) * 16).astype(np.int32)
    large = np.minimum(large, 31)
    return np.where(n < max_exact, n, large).astype(np.int32)


class Ctx:
    pass


def share(t, buf):
    if isinstance(t.b, Buf) and t.b is not buf:
        _merge(buf.r, t.b.r)
    t.b = buf


def bl(b):
    return b if isinstance(b, list) else [b]


def range_reduce_sin(K, ang, bang, out, bout, shape, part=128):
    P, A = K.P, K.A
    ni = A.alloc("rr_i", shape, I32)
    nf = A.alloc("rr_f", shape, F32)
    C1 = 6.28125
    C2 = float(2 * np.pi - 6.28125)
    P.op("dve", lambda e: e.tensor_scalar(out=nf.ap, in0=ang, scalar1=float(1.0 / (2 * np.pi)), scalar2=None, op0=ALU.mult),
         reads=[bang], writes=[nf.b])
    P.op("dve", lambda e: e.tensor_copy(out=ni.ap, in_=nf.ap), reads=[nf.b], writes=[ni.b])
    P.op("dve", lambda e: e.tensor_copy(out=nf.ap, in_=ni.ap), reads=[ni.b], writes=[nf.b])
    P.op("dve", lambda e: e.scalar_tensor_tensor(out=ang, in0=nf.ap, scalar=-C1, in1=ang, op0=ALU.mult, op1=ALU.add),
         reads=[nf.b, bang], writes=[bang])
    P.op("dve", lambda e: e.scalar_tensor_tensor(out=ang, in0=nf.ap, scalar=-C2, in1=ang, op0=ALU.mult, op1=ALU.add),
         reads=[nf.b, bang], writes=[bang])
    P.op("dve", lambda e: e.tensor_scalar(out=ang, in0=ang, scalar1=3.1415925, scalar2=-3.1415925, op0=ALU.min, op1=ALU.max),
         reads=[bang], writes=[bang])
    P.op("act", lambda e: e.activation(out=out, in_=ang, func=AF.Sin), reads=[bang], writes=[bout])
    A.free(ni)
    A.free(nf)


def accurate_exp(K, x, bx, shape):
    P, A = K.P, K.A
    y = A.alloc("aexp_y", shape, F32)
    acc = A.alloc("aexp_a", shape, F32)
    P.op("dve", lambda e: e.tensor_scalar(out=y.ap, in0=x, scalar1=0.125, scalar2=None, op0=ALU.mult), reads=[bx], writes=[y.b])
    fact = [1.0]
    for i in range(1, 12):
        fact.append(fact[-1] * i)
    P.op("dve", lambda e: e.tensor_scalar(out=acc.ap, in0=y.ap, scalar1=1.0 / fact[11], scalar2=1.0 / fact[10], op0=ALU.mult, op1=ALU.add),
         reads=[y.b], writes=[acc.b])
    for i in range(9, -1, -1):
        P.op("dve", lambda e: e.tensor_tensor(out=acc.ap, in0=acc.ap, in1=y.ap, op=ALU.mult), reads=[acc.b, y.b], writes=[acc.b])
        P.op("dve", lambda e, i=i: e.tensor_scalar(out=acc.ap, in0=acc.ap, scalar1=float(1.0 / fact[i]), scalar2=None, op0=ALU.add),
             reads=[acc.b], writes=[acc.b])
    for _ in range(3):
        P.op("dve", lambda e: e.tensor_tensor(out=acc.ap, in0=acc.ap, in1=acc.ap, op=ALU.mult), reads=[acc.b], writes=[acc.b])
    P.op("dve", lambda e: e.tensor_copy(out=x, in_=acc.ap), reads=[acc.b, bx], writes=[bx])
    A.free(y)
    A.free(acc)


def ssm_tables(K):
    nc, P, I, A = K.nc, K.P, K.I, K.A
    PSB = K.psb
    lrG = A.alloc("lrG", [32], F32)
    liG = A.alloc("liG", [32], F32)
    dtG = A.alloc("dtG", [32], F32)
    bG = Buf("G")
    with nc.allow_non_contiguous_dma(reason="tiny transposed param loads"):
        for gh in range(2):
            rows = slice(64 * gh, 64 * gh + 64)
            P.dma("sp", lrG.ap[rows, :], I["lam_re"][32 * gh:32 * gh + 32, :].rearrange("g p -> p g"), writes=[bG])
            P.dma("sp", liG.ap[rows, :], I["lam_im"][32 * gh:32 * gh + 32, :].rearrange("g p -> p g"), writes=[bG])
            P.dma("sp", dtG.ap[rows, :], I["log_dt"][32 * gh:32 * gh + 32].unsqueeze(0).broadcast_to([64, 32]), writes=[bG])
    accurate_exp(K, dtG.ap, bG, [32])
    P.op("dve", lambda e: e.tensor_scalar(out=lrG.ap, in0=lrG.ap, scalar1=-1e-4, scalar2=None, op0=ALU.min), reads=[bG], writes=[bG])
    rho = A.alloc("rhoG", [32], F32)
    th = A.alloc("thG", [32], F32)
    P.op("dve", lambda e: e.tensor_tensor(out=rho.ap, in0=lrG.ap, in1=dtG.ap, op=ALU.mult), reads=[bG], writes=[bG])
    P.op("dve", lambda e: e.tensor_tensor(out=th.ap, in0=liG.ap, in1=dtG.ap, op=ALU.mult), reads=[bG], writes=[bG])
    tmps = [lrG, liG, dtG, rho, th]

    def TTv(o, a, b_, op, rd, wb):
        P.op("dve", lambda e: e.tensor_tensor(out=o, in0=a, in1=b_, op=op), reads=rd, writes=[wb])

    def TSv(o, a, s1, s2, op0, op1, rd, wb):
        if op1 is None:
            P.op("dve", lambda e: e.tensor_scalar(out=o, in0=a, scalar1=s1, scalar2=None, op0=op0), reads=rd, writes=[wb])
        else:
            P.op("dve", lambda e: e.tensor_scalar(out=o, in0=a, scalar1=s1, scalar2=s2, op0=op0, op1=op1), reads=rd, writes=[wb])

    b1 = Buf("a1")
    nm = lambda n: A.alloc(n, [32], F32)
    r_, x2, sn, cs, t_, ex, a1r, a1i = [nm(n) for n in ("r_", "x2", "sn", "cs", "t_", "ex", "a1r", "a1i")]
    ni = A.alloc("ni", [32], I32)
    for t in (r_, x2, sn, cs, t_, ex, a1r, a1i, ni):
        share(t, b1)
    tmps.extend([r_, x2, sn, cs, t_, ex, a1r, a1i, ni])
    C1 = 6.28125
    C2 = float(2 * np.pi - 6.28125)
    TSv(t_.ap, th.ap, float(1.0 / (2 * np.pi)), None, ALU.mult, None, [bG], b1)
    P.op("dve", lambda e: e.tensor_copy(out=ni.ap, in_=t_.ap), reads=[b1], writes=[b1])
    P.op("dve", lambda e: e.tensor_copy(out=t_.ap, in_=ni.ap), reads=[b1], writes=[b1])
    P.op("dve", lambda e: e.scalar_tensor_tensor(out=r_.ap, in0=t_.ap, scalar=-C1, in1=th.ap, op0=ALU.mult, op1=ALU.add), reads=[b1, bG], writes=[b1])
    P.op("dve", lambda e: e.scalar_tensor_tensor(out=r_.ap, in0=t_.ap, scalar=-C2, in1=r_.ap, op0=ALU.mult, op1=ALU.add), reads=[b1], writes=[b1])
    TSv(r_.ap, r_.ap, 0.125, None, ALU.mult, None, [b1], b1)
    TTv(x2.ap, r_.ap, r_.ap, ALU.mult, [b1], b1)
    TSv(sn.ap, x2.ap, 1.0 / 362880, -1.0 / 5040, ALU.mult, ALU.add, [b1], b1)
    for cf in (1.0 / 120, -1.0 / 6, 1.0):
        TTv(sn.ap, sn.ap, x2.ap, ALU.mult, [b1], b1)
        TSv(sn.ap, sn.ap, float(cf), None, ALU.add, None, [b1], b1)
    TTv(sn.ap, sn.ap, r_.ap, ALU.mult, [b1], b1)
    TSv(cs.ap, x2.ap, -1.0 / 3628800, 1.0 / 40320, ALU.mult, ALU.add, [b1], b1)
    for cf in (-1.0 / 720, 1.0 / 24, -0.5, 1.0):
        TTv(cs.ap, cs.ap, x2.ap, ALU.mult, [b1], b1)
        TSv(cs.ap, cs.ap, float(cf), None, ALU.add, None, [b1], b1)
    for _ in range(3):
        TTv(t_.ap, sn.ap, cs.ap, ALU.mult, [b1], b1)
        TTv(x2.ap, sn.ap, sn.ap, ALU.mult, [b1], b1)
        TSv(sn.ap, t_.ap, 2.0, None, ALU.mult, None, [b1], b1)
        TSv(cs.ap, x2.ap, -2.0, 1.0, ALU.mult, ALU.add, [b1], b1)
    TSv(ex.ap, rho.ap, 1.0 / 120, 1.0 / 24, ALU.mult, ALU.add, [bG], b1)
    for cf in (1.0 / 6, 0.5, 1.0, 1.0):
        TTv(ex.ap, ex.ap, rho.ap, ALU.mult, [b1, bG], b1)
        TSv(ex.ap, ex.ap, float(cf), None, ALU.add, None, [b1], b1)
    TTv(a1r.ap, ex.ap, cs.ap, ALU.mult, [b1], b1)
    TTv(a1i.ap, ex.ap, sn.ap, ALU.mult, [b1], b1)
    b128 = Buf("a128")
    a128r, a128i, q1, q2 = [nm(n) for n in ("a128r", "a128i", "q1", "q2")]
    for t in (a128r, a128i, q1, q2):
        share(t, b128)
    tmps.extend([a128r, a128i, q1, q2])
    P.op("dve", lambda e: e.tensor_copy(out=a128r.ap, in_=a1r.ap), reads=[b1], writes=[b128])
    P.op("dve", lambda e: e.tensor_copy(out=a128i.ap, in_=a1i.ap), reads=[b1], writes=[b128])
    K.Rm = A.alloc("Rm", [32], F32)
    K.thG = A.alloc("thGk", [32], F32)
    K.c64 = A.alloc("c64", [32], F32)
    K.s64 = A.alloc("s64", [32], F32)
    r64 = nm("r64")
    share(r64, b128)
    tmps.append(r64)
    P.op("dve", lambda e: e.tensor_copy(out=K.Rm.ap, in_=ex.ap), reads=[b1], writes=[K.Rm.b])
    P.op("dve", lambda e: e.tensor_copy(out=K.thG.ap, in_=th.ap), reads=[bG], writes=[K.thG.b])
    P.op("dve", lambda e: e.tensor_copy(out=r64.ap, in_=ex.ap), reads=[b1], writes=[b128])
    for _ in range(6):
        TTv(r64.ap, r64.ap, r64.ap, ALU.mult, [b128], b128)
    P.op("dve", lambda e: e.reciprocal(out=r64.ap, in_=r64.ap), reads=[b128], writes=[b128])
    for it in range(7):
        if it == 6:
            TTv(K.c64.ap, a128r.ap, r64.ap, ALU.mult, [b128], K.c64.b)
            TTv(K.s64.ap, a128i.ap, r64.ap, ALU.mult, [b128], K.s64.b)
        TTv(q1.ap, a128r.ap, a128r.ap, ALU.mult, [b128], b128)
        TTv(q2.ap, a128i.ap, a128i.ap, ALU.mult, [b128], b128)
        TTv(a128i.ap, a128r.ap, a128i.ap, ALU.mult, [b128], b128)
        TSv(a128i.ap, a128i.ap, 2.0, None, ALU.mult, None, [b128], b128)
        TTv(a128r.ap, q1.ap, q2.ap, ALU.subtract, [b128], b128)

    def make_a4(ar, ai, b, name):
        a4 = A.alloc(name, [2, 2, 32], F32)
        P.op("dve", lambda e: e.tensor_copy(out=a4.ap[:, 0, 0, :], in_=ar.ap), reads=[b], writes=[a4.b])
        P.op("dve", lambda e: e.tensor_scalar(out=a4.ap[:, 0, 1, :], in0=ai.ap, scalar1=-1.0, scalar2=None, op0=ALU.mult), reads=[b], writes=[a4.b])
        P.op("dve", lambda e: e.tensor_copy(out=a4.ap[:, 1, 0, :], in_=ai.ap), reads=[b], writes=[a4.b])
        P.op("dve", lambda e: e.tensor_copy(out=a4.ap[:, 1, 1, :], in_=ar.ap), reads=[b], writes=[a4.b])
        return a4

    K.A4 = make_a4(a1r, a1i, b1, "A4_1")
    K.A4c = make_a4(a128r, a128i, b128, "A4_128")

    den = A.alloc("den", [32], F32)
    t1 = A.alloc("ct1", [32], F32)
    t2 = A.alloc("ct2", [32], F32)
    cr = A.alloc("coefr", [32], F32)
    ci = A.alloc("coefi", [32], F32)
    am1 = A.alloc("am1", [32], F32)
    bc = Buf("coef")
    for t in (den, t1, t2, cr, ci, am1):
        share(t, bc)
    tmps.extend([den, t1, t2, cr, ci, am1])
    TT = lambda o, a, b_, op, rd: P.op("dve", lambda e: e.tensor_tensor(out=o, in0=a, in1=b_, op=op), reads=rd, writes=[bc])
    TT(den.ap, lrG.ap, lrG.ap, ALU.mult, [bG])
    TT(t1.ap, liG.ap, liG.ap, ALU.mult, [bG])
    TT(den.ap, den.ap, t1.ap, ALU.add, [bc])
    P.op("dve", lambda e: e.reciprocal(out=den.ap, in_=den.ap), reads=[bc], writes=[bc])
    P.op("dve", lambda e: e.tensor_scalar(out=am1.ap, in0=a1r.ap, scalar1=-1.0, scalar2=None, op0=ALU.add), reads=[b1], writes=[bc])
    TT(t1.ap, am1.ap, lrG.ap, ALU.mult, [bc, bG])
    TT(t2.ap, a1i.ap, liG.ap, ALU.mult, [b1, bG])
    TT(t1.ap, t1.ap, t2.ap, ALU.add, [bc])
    TT(cr.ap, t1.ap, den.ap, ALU.mult, [bc])
    TT(t1.ap, a1i.ap, lrG.ap, ALU.mult, [bc, b1, bG])
    TT(t2.ap, am1.ap, liG.ap, ALU.mult, [bc, bG])
    TT(t1.ap, t1.ap, t2.ap, ALU.subtract, [bc])
    TT(ci.ap, t1.ap, den.ap, ALU.mult, [bc])

    K.BbR = A.alloc("BbR", [32, 16], F32)
    K.BbI = A.alloc("BbI", [32, 16], F32)
    bBb = Buf("Bbar")
    share(K.BbR, bBb)
    share(K.BbI, bBb)
    BreG = A.alloc("BreG", [32, 16], F32)
    BimG = A.alloc("BimG", [32, 16], F32)
    tb1 = A.alloc("tb1", [32, 16], F32)
    bB = Buf("B")
    share(BreG, bB)
    share(BimG, bB)
    with nc.allow_non_contiguous_dma(reason="64B runs param load"):
        for gh in range(2):
            rows = slice(64 * gh, 64 * gh + 64)
            P.dma("sp", BreG.ap[rows], I["b_re"][32 * gh:32 * gh + 32].rearrange("g p c -> p g c"), writes=[bB])
            P.dma("sp", BimG.ap[rows], I["b_im"][32 * gh:32 * gh + 32].rearrange("g p c -> p g c"), writes=[bB])
    crb = cr.ap.unsqueeze(2).broadcast_to([128, 32, 16])
    cib = ci.ap.unsqueeze(2).broadcast_to([128, 32, 16])
    P.op("dve", lambda e: e.tensor_tensor(out=K.BbR.ap, in0=BreG.ap, in1=crb, op=ALU.mult), reads=[bB, bc], writes=[bBb])
    P.op("dve", lambda e: e.tensor_tensor(out=tb1.ap, in0=BimG.ap, in1=cib, op=ALU.mult), reads=[bB, bc], writes=[tb1.b])
    P.op("dve", lambda e: e.tensor_tensor(out=K.BbR.ap, in0=K.BbR.ap, in1=tb1.ap, op=ALU.subtract), reads=[tb1.b, bBb], writes=[bBb])
    P.op("dve", lambda e: e.tensor_tensor(out=K.BbI.ap, in0=BimG.ap, in1=crb, op=ALU.mult), reads=[bB, bc], writes=[bBb])
    P.op("dve", lambda e: e.tensor_tensor(out=tb1.ap, in0=BreG.ap, in1=cib, op=ALU.mult), reads=[bB, bc, bBb], writes=[tb1.b])
    P.op("dve", lambda e: e.tensor_tensor(out=K.BbI.ap, in0=K.BbI.ap, in1=tb1.ap, op=ALU.add), reads=[tb1.b, bBb], writes=[bBb])
    share(BreG, bB)
    share(BimG, bB)

    K.Dcol = A.alloc("Dcol", [8], F32)
    with nc.allow_non_contiguous_dma(reason="tiny"):
        P.dma("sp", K.Dcol.ap, I["d_skip"].rearrange("(j p) -> p j", p=128), writes=[K.Dcol.b])

    K.ApR = A.alloc("ApR", [64, 64], BF16)
    K.ApI = A.alloc("ApI", [64, 64], BF16)
    bAp = Buf("ApowT")
    share(K.ApR, bAp)
    share(K.ApI, bAp)
    kcol = A.alloc("kcol", [1], F32)
    dtF = A.alloc("dtF", [64], F32)
    bF = Buf("F")
    share(kcol, bF)
    share(dtF, bF)
    P.dma("sp", kcol.ap, I["kcol"], writes=[bF])
    P.dma("sp", dtF.ap, I["log_dt"].unsqueeze(0).broadcast_to([128, 64]), writes=[bF])
    accurate_exp(K, dtF.ap, bF, [64])
    for hf in range(2):
        gs = slice(32 * hf, 32 * hf + 32)
        lrF = A.alloc("lrF", [32, 64], F32)
        liF = A.alloc("liF", [32, 64], F32)
        magF = A.alloc("magF", [32, 64], F32)
        snF = A.alloc("snF", [32, 64], F32)
        bH = Buf("Fh")
        for t in (lrF, liF, magF, snF):
            share(t, bH)
        P.dma("sp", lrF.ap, I["lam_re"][gs].unsqueeze(0).broadcast_to([128, 32, 64]), writes=[bH])
        P.dma("sp", liF.ap, I["lam_im"][gs].unsqueeze(0).broadcast_to([128, 32, 64]), writes=[bH])
        dtb = dtF.ap[:, gs].unsqueeze(2).broadcast_to([128, 32, 64])
        P.op("dve", lambda e: e.tensor_scalar(out=lrF.ap, in0=lrF.ap, scalar1=-1e-4, scalar2=None, op0=ALU.min), reads=[bH], writes=[bH])
        P.op("dve", lambda e: e.tensor_tensor(out=lrF.ap, in0=lrF.ap, in1=dtb, op=ALU.mult), reads=[bH, bF], writes=[bH])
        P.op("dve", lambda e: e.tensor_tensor(out=liF.ap, in0=liF.ap, in1=dtb, op=ALU.mult), reads=[bH, bF], writes=[bH])
        P.op("act", lambda e: e.activation(out=magF.ap, in_=lrF.ap, func=AF.Exp, scale=kcol.ap), reads=[bH, bF], writes=[bH])
        P.op("dve", lambda e: e.tensor_scalar(out=lrF.ap, in0=liF.ap, scalar1=kcol.ap, scalar2=None, op0=ALU.mult), reads=[bH, bF], writes=[bH])
        range_reduce_sin(K, lrF.ap, bH, snF.ap, bH, [32, 64])
        P.op("dve", lambda e: e.tensor_tensor(out=K.ApI.ap[:, gs, :], in0=magF.ap, in1=snF.ap, op=ALU.mult), reads=[bH], writes=[bAp])
        P.op("dve", lambda e: e.tensor_scalar(out=lrF.ap, in0=liF.ap, scalar1=kcol.ap, scalar2=float(np.pi / 2), op0=ALU.mult, op1=ALU.add),
             reads=[bH, bF], writes=[bH])
        range_reduce_sin(K, lrF.ap, bH, snF.ap, bH, [32, 64])
        P.op("dve", lambda e: e.tensor_tensor(out=K.ApR.ap[:, gs, :], in0=magF.ap, in1=snF.ap, op=ALU.mult), reads=[bH], writes=[bAp])
        for t in (lrF, liF, magF, snF):
            A.free(t)
    for t in (kcol, dtF, BreG, BimG, tb1):
        A.free(t)
    for t in tmps:
        A.free(t)


def ssm_tables_own(K):
    nc, P, I, A = K.nc, K.P, K.I, K.A
    PSB = K.psb
    bBb = K.BbR.b
    m8 = A.alloc("m8", [8], F32)
    m88 = A.alloc("m88", [8, 8], F32)
    bm = Buf("masks")
    share(m8, bm)
    share(m88, bm)
    P.dma("sp", m8.ap, I["m8"], writes=[bm])
    P.dma("sp", m88.ap, I["m88"], writes=[bm])
    K.BtabR = A.alloc("BtabR", [64, 64], BF16)
    K.BtabI = A.alloc("BtabI", [64, 64], BF16)
    bBtab = Buf("Btab")
    share(K.BtabR, bBtab)
    share(K.BtabI, bBtab)
    K.CtabR = A.alloc("CtabR", [32, 128], BF16)
    K.CtabI = A.alloc("CtabI", [32, 128], BF16)
    bCtab = Buf("Ctab")
    share(K.CtabR, bCtab)
    share(K.CtabI, bCtab)
    cnat_r = A.alloc("cnat_r", [8, 64], F32)
    cnat_i = A.alloc("cnat_i", [8, 64], F32)
    bcn = Buf("cnat")
    share(cnat_r, bcn)
    share(cnat_i, bcn)
    P.dma("sp", cnat_r.ap, I["c_re"].rearrange("(j g) c p -> (g c) j p", j=8), writes=[bcn])
    P.dma("sp", cnat_i.ap, I["c_im"].rearrange("(j g) c p -> (g c) j p", j=8), writes=[bcn])
    cnt = 0
    for src, dst in ((K.BbR, K.BtabR), (K.BbI, K.BtabI)):
        for j in range(8):
            gh, gq = j // 4, (j % 4) * 8
            pb = cnt % 2
            cnt += 1
            rows = slice(64 * gh, 64 * gh + 64)
            inp = src.ap[rows, gq:gq + 8, :]
            pt = K.ps[:, pb, 0:64]
            P.pe([lambda e, inp=inp, pt=pt, rows=rows: e.transpose(out=pt, in_=inp, identity=K.identf.ap[rows, rows])],
                 reads=[bBb, K.bC], writes=[PSB[pb]])
            P.op("dve", lambda e, pt=pt, dst=dst, j=j: e.tensor_tensor(
                out=dst.ap[:, 8 * j:8 * j + 8, :], in0=pt.unsqueeze(1).broadcast_to([128, 8, 64]),
                in1=m8.ap.unsqueeze(2).broadcast_to([128, 8, 64]), op=ALU.mult), reads=[PSB[pb], bm], writes=[bBtab])
    P.op("dve", lambda e: e.tensor_scalar(out=cnat_i.ap, in0=cnat_i.ap, scalar1=-1.0, scalar2=None, op0=ALU.mult), reads=[bcn], writes=[bcn])
    for src, dst in ((cnat_r, K.CtabR), (cnat_i, K.CtabI)):
        for j in range(8):
            gh, gq = j // 4, (j % 4) * 8
            pb = cnt % 2
            cnt += 1
            rows = slice(64 * gh, 64 * gh + 64)
            pt = K.ps[rows, pb, 0:128]
            P.pe([lambda e, src=src, j=j, pt=pt, gh=gh: e.matmul(pt, lhsT=src.ap[:, j, :], rhs=K.identf.ap, start=True, stop=True,
                                                                 tile_position=(0, 64 * gh))],
                 reads=[bcn, K.bC], writes=[PSB[pb]])
            P.op("dve", lambda e, pt=pt, dst=dst, rows=rows, gq=gq: e.tensor_tensor(
                out=dst.ap[rows, gq:gq + 8, :].rearrange("p g (h c) -> p g h c", h=8),
                in0=pt.rearrange("p (g c) -> p g c", g=8).unsqueeze(2).broadcast_to([64, 8, 8, 16]),
                in1=m88.ap[rows].unsqueeze(3).broadcast_to([64, 8, 8, 16]), op=ALU.mult),
                reads=[PSB[pb], bm], writes=[bCtab])
    for t in (cnat_r, cnat_i, m8, m88):
        A.free(t)
    K.CT = A.alloc("CT", [32, 64], BF16)
    K.ST = A.alloc("ST", [32, 64], BF16)
    trow = A.alloc("trow", [64], F32)
    ang = A.alloc("angT", [32, 64], F32)
    sct = A.alloc("sct", [32, 64], F32)
    P.dma("sp", trow.ap, I["trow"], writes=[trow.b])
    thb = K.thG.ap.unsqueeze(2).broadcast_to([128, 32, 64])
    trb = trow.ap.unsqueeze(1).broadcast_to([128, 32, 64])
    P.op("dve", lambda e: e.tensor_tensor(out=ang.ap, in0=thb, in1=trb, op=ALU.mult), reads=[K.thG.b, trow.b], writes=[ang.b])
    range_reduce_sin(K, ang.ap, ang.b, sct.ap, sct.b, [32, 64])
    P.op("dve", lambda e: e.tensor_copy(out=K.ST.ap, in_=sct.ap), reads=[sct.b], writes=[K.ST.b])
    P.op("dve", lambda e: e.tensor_tensor(out=ang.ap, in0=thb, in1=trb, op=ALU.mult), reads=[K.thG.b, trow.b, ang.b], writes=[ang.b])
    P.op("dve", lambda e: e.tensor_scalar(out=ang.ap, in0=ang.ap, scalar1=float(np.pi / 2), scalar2=None, op0=ALU.add), reads=[ang.b], writes=[ang.b])
    range_reduce_sin(K, ang.ap, ang.b, sct.ap, sct.b, [32, 64])
    P.op("dve", lambda e: e.tensor_copy(out=K.CT.ap, in_=sct.ap), reads=[sct.b], writes=[K.CT.b])
    for t in (trow, ang, sct):
        A.free(t)


def load_weights_resident(K, name, src, kc, ncols):
    t = K.A.alloc(name, [kc, ncols], BF16)
    for c0 in range(0, ncols, 512):
        w = min(512, ncols - c0)
        K.P.dma("pool", t.ap[:, :, c0:c0 + w], src[:, c0:c0 + w].rearrange("(c p) n -> p c n", p=128), writes=[t.b])
    return t


def norm_rows(K, xt, n, gb, xs, junk, ss):
    P = K.P
    P.op("act", lambda e: e.activation(out=junk.ap[:n], in_=xt.ap[:n], func=AF.Square, accum_out=ss.ap[:n]),
         reads=[xt.b], writes=[junk.b, ss.b])
    P.op("act", lambda e: e.activation(out=ss.ap[:n], in_=ss.ap[:n], func=AF.Sqrt, bias=K.epsc.ap[:n], scale=1.0 / D),
         reads=[ss.b, K.bC], writes=[ss.b])
    P.op("dve", lambda e: e.reciprocal(out=ss.ap[:n], in_=ss.ap[:n]), reads=[ss.b], writes=[ss.b])
    P.op("dve", lambda e: e.scalar_tensor_tensor(out=xs.ap[:n], in0=xt.ap[:n], scalar=ss.ap[:n], in1=gb.ap[:n],
                                                 op0=ALU.mult, op1=ALU.mult), reads=[xt.b, ss.b, gb.b], writes=[xs.b])


def transpose_rows(K, xs, n, dst_ap, dst_bufs, pbank):
    P = K.P
    ptb = K.ps[:, pbank:pbank + 2, :].rearrange("p a b -> p (a b)").bitcast(BF16)
    ptv = ptb.rearrange("p (c n) -> p c n", c=16)
    pbufs = [K.psb[pbank], K.psb[pbank + 1]]
    P.pe([lambda e, c=c: e.transpose(out=ptv[:, c, 0:n], in_=xs.ap[:n, c * 128:(c + 1) * 128], identity=K.identb.ap[:n, :n])
          for c in range(16)], reads=[xs.b, K.bC], writes=pbufs)
    P.op("act", lambda e: e.activation(out=dst_ap, in_=ptv[:, :, 0:n], func=AF.Copy), reads=pbufs, writes=dst_bufs)


def prefix_phase(K):
    nc, P, I, A = K.nc, K.P, K.I, K.A
    PSB = K.psb
    Wu = load_weights_resident(K, "Wu", I["w_in"][:, 0:1024], 16, 1024)
    K.gb = A.alloc("gb", [D], F32)
    P.dma("sp", K.gb.ap, I["norm_attn"].unsqueeze(0).broadcast_to([128, D]), writes=[K.gb.b])
    xts = [A.alloc(f"xt{i}", [D], F32) for i in range(2)]
    K.junk = A.alloc("junk", [D], BF16)
    K.ss = A.alloc("ss", [1], F32)
    xss = [A.alloc(f"xs{i}", [D], BF16) for i in range(2)]
    hTt = [A.alloc(f"hTt{i}", [16, 128], BF16) for i in range(2)]
    U = [A.alloc(f"U{i}", [1024], BF16) for i in range(2)]
    pr1 = A.alloc("pr1", [2, 32, 16], F32)
    pr2 = A.alloc("pr2", [2, 32, 16], F32)
    Tt = A.alloc("Tt", [2, 32, 16], F32)
    S = A.alloc("S", [2, 32], F32)
    Mm = A.alloc("Mm", [2, 2, 32], F32)
    K.Hc = A.alloc("Hc", [2, 32], F32)
    H = K.Hc
    P.op("dve", lambda e: e.memset(H.ap, 0.0), writes=[H.b])
    BbRb = K.BbR.ap.unsqueeze(1).broadcast_to([128, 2, 32, 16])
    BbIb = K.BbI.ap.unsqueeze(1).broadcast_to([128, 2, 32, 16])
    ntile = NPRE // 128

    def stA(i):
        xt, xs, ht = xts[i % 2], xss[i % 2], hTt[i % 2]
        P.dma("sp", xt.ap, I["xp"][i * 128:(i + 1) * 128, :], writes=[xt.b])
        norm_rows(K, xt, 128, K.gb, xs, K.junk, K.ss)
        transpose_rows(K, xs, 128, ht.ap, [ht.b], 0)

    def stB(i):
        ht, u = hTt[i % 2], U[i % 2]
        for half in range(2):
            pu = K.ps[:, 2 + half, :]
            P.pe([lambda e, c=c, pu=pu, half=half: e.matmul(pu, lhsT=ht.ap[:, c, :], rhs=Wu.ap[:, c, half * 512:(half + 1) * 512],
                                                           start=(c == 0), stop=(c == 15)) for c in range(16)],
                 reads=[ht.b, Wu.b], writes=[PSB[2 + half]])
        P.op("act", lambda e, u=u: e.activation(out=u.ap, in_=K.ps[:, 2:4, :].rearrange("p a b -> p (a b)"), func=AF.Copy),
             reads=[PSB[2], PSB[3]], writes=[u.b])

    zp = K.ps[:, 4:6, :].rearrange("p a (g c) -> p a g c", c=16)

    def stC(i):
        u = U[i % 2]
        fns = []
        for g in range(64):
            gh, gq = g // 32, g % 32
            rows = slice(64 * gh, 64 * gh + 64)
            for ri, tab in ((0, K.ApR), (1, K.ApI)):
                fns.append(lambda e, g=g, gh=gh, gq=gq, rows=rows, ri=ri, tab=tab, u=u: e.matmul(
                    zp[rows, ri, gq, :], lhsT=tab.ap[:, g, :], rhs=u.ap[:, g * 16:(g + 1) * 16], start=True, stop=True,
                    tile_position=(0, 64 * gh)))
        P.pe(fns, reads=[u.b, K.ApR.b], writes=[PSB[4], PSB[5]])

    def stD(i):
        P.op("dve", lambda e: e.tensor_tensor(out=pr1.ap, in0=zp, in1=BbRb, op=ALU.mult), reads=[PSB[4], PSB[5], K.BbR.b], writes=[pr1.b])
        P.op("dve", lambda e: e.tensor_tensor(out=pr2.ap, in0=zp, in1=BbIb, op=ALU.mult), reads=[PSB[4], PSB[5], K.BbR.b], writes=[pr2.b])
        P.op("dve", lambda e: e.tensor_tensor(out=Tt.ap[:, 0], in0=pr1.ap[:, 0], in1=pr2.ap[:, 1], op=ALU.subtract),
             reads=[pr1.b, pr2.b], writes=[Tt.b])
        P.op("dve", lambda e: e.tensor_tensor(out=Tt.ap[:, 1], in0=pr1.ap[:, 1], in1=pr2.ap[:, 0], op=ALU.add),
             reads=[pr1.b, pr2.b], writes=[Tt.b])
        P.op("dve", lambda e: e.tensor_reduce(out=S.ap, in_=Tt.ap, axis=AX.X, op=ALU.add), reads=[Tt.b], writes=[S.b])
        P.op("dve", lambda e: e.tensor_tensor(out=Mm.ap, in0=K.A4c.ap, in1=H.ap.unsqueeze(1).broadcast_to([128, 2, 2, 32]), op=ALU.mult),
             reads=[K.A4c.b, H.b], writes=[Mm.b])
        P.op("dve", lambda e: e.tensor_tensor(out=H.ap, in0=Mm.ap[:, :, 0, :], in1=Mm.ap[:, :, 1, :], op=ALU.add), reads=[Mm.b], writes=[H.b])
        P.op("dve", lambda e: e.tensor_tensor(out=H.ap, in0=H.ap, in1=S.ap, op=ALU.add), reads=[H.b, S.b], writes=[H.b])

    stA(0)
    for i in range(ntile):
        stB(i)
        stC(i)
        if i + 1 < ntile:
            stA(i + 1)
        stD(i)
    if DEBUG:
        P.dma("sp", K.O["dbg_h"], H.ap, reads=[H.b], writes=[K.bout])
    for t in [Wu] + U + [pr1, pr2, Tt, S, Mm, K.ApR, K.ApI]:
        A.free(t)
    load_wkv(K)
    ht = hTt[(ntile - 1) % 2]
    kv_tokmajor(K, ht.ap, [ht.b], 128, slice(0, 128), K.vth.ap, K.vth.b)
    kT_dup(K, ht.ap, [ht.b], 128, lambda kv: K.kTh.ap[:, kv, :], [K.kTh.b], slice(0, 128))
    A.free(K.Wkv)
    A.free(K.Wkd)
    for t in hTt:
        A.free(t)
    K.xts, K.xss = xts, xss


def kT_dup(K, hT_ap, hbufs, n, dst_ap_fn, dst_bufs, cols):
    P = K.P
    PSB = K.psb
    for kv in range(4):
        pk = K.ps[:, 7, 0:n]
        P.pe([lambda e, c=c, pk=pk, kv=kv: e.matmul(pk, lhsT=K.Wkd.ap[:, c, kv, :],
                                                   rhs=hT_ap[:, c, cols], start=(c == 0), stop=(c == 15)) for c in range(16)],
             reads=hbufs + [K.Wkd.b], writes=[PSB[7]])
        P.op("act", lambda e, kv=kv, pk=pk: e.activation(out=dst_ap_fn(kv), in_=pk, func=AF.Copy),
             reads=[PSB[7]], writes=dst_bufs)


def load_wkv(K):
    P, A = K.P, K.A
    K.Wkv = load_weights_resident(K, "Wkv", K.I["w_in"][:, 2048:2560], 16, 512)
    K.Wkd = A.alloc("Wkd", [16, 4, 128], BF16)
    for kv in range(4):
        P.op("pool", lambda e, kv=kv: e.tensor_copy(out=K.Wkd.ap[:, :, kv, :].rearrange("p c (d h) -> p c d h", d=2),
                                                    in_=K.Wkv.ap[:, :, kv * 64:(kv + 1) * 64].unsqueeze(2).broadcast_to([128, 16, 2, 64])),
             reads=[K.Wkv.b], writes=[K.Wkd.b])


def kv_tokmajor(K, hT_ap, hbufs, n, cols, vdst_ap, vdst_buf, kdst=None):
    P, PSB = K.P, K.psb
    pk = K.ps[:n, 6, :]
    P.pe([lambda e, c=c: e.matmul(pk, lhsT=hT_ap[:, c, cols], rhs=K.Wkv.ap[:, c, :], start=(c == 0), stop=(c == 15)) for c in range(16)],
         reads=hbufs + [K.Wkv.b], writes=[PSB[6]])
    P.op("act", lambda e: e.activation(out=vdst_ap, in_=K.ps[:n, 6, 256:512], func=AF.Copy), reads=[PSB[6]], writes=[vdst_buf])
    if kdst is not None:
        P.op("act", lambda e: e.activation(out=kdst.ap[:n], in_=pk, func=AF.Copy), reads=[PSB[6]], writes=[kdst.b])


class Ring:
    def __init__(self, K, nbig, nsmall=0):
        self.K = K
        self.big = [K.A.alloc(f"ringb{i}", [16, 512], BF16) for i in range(nbig)]
        self.small = [K.A.alloc(f"rings{i}", [8, 512], BF16) for i in range(nsmall)]
        self.ib = 0
        self.is_ = 0

    def load(self, src, kc, ncols):
        if kc <= 8 and self.small:
            s = self.small[self.is_ % len(self.small)]
            self.is_ += 1
        else:
            s = self.big[self.ib % len(self.big)]
            self.ib += 1
        v = s.ap[:, 0:kc, 0:ncols]
        self.K.P.dma("pool", v, src.rearrange("(c p) n -> p c n", p=128), writes=[s.b])
        return v, s.b

    def free(self):
        for s in self.big + self.small:
            self.K.A.free(s)


def acc_views(K, acc):
    return [(K.ps[:, 3 * acc:3 * acc + 2, :].rearrange("p a b -> p (a b)"), slice(0, 1024)),
            (K.ps[:, 3 * acc + 2, 0:64], slice(1024, 1088))]


def acc_bufs(K, acc):
    return [K.psb[3 * acc], K.psb[3 * acc + 1], K.psb[3 * acc + 2]]


def fm_matmul(K, wv, wb, kc, nt, act_ap, act_bufs, acc):
    fns = []
    for bi, (t0, n) in enumerate(BLKS):
        for c in range(kc):
            fns.append(lambda e, bi=bi, t0=t0, n=n, c=c: e.matmul(
                K.ps[:, 3 * acc + bi, 0:n], lhsT=wv[:, c, nt * 128:(nt + 1) * 128], rhs=act_ap[:, c, t0:t0 + n],
                start=(c == 0), stop=(c == kc - 1)))
    K.P.pe(fns, reads=[wb] + act_bufs, writes=acc_bufs(K, acc))


def own_norm(K):
    P, I, A = K.P, K.I, K.A
    for t in range(9):
        n = 128 if t < 8 else 64
        xt, xs = K.xts[t % 2], K.xss[t % 2]
        P.dma("sp", xt.ap[:n], I["xo"][t * 128:t * 128 + n, :], writes=[xt.b])
        norm_rows(K, xt, n, K.gb, xs, K.junk, K.ss)
        transpose_rows(K, xs, n, K.hT.ap[:, :, t * 128:t * 128 + n], [K.hT.b[t]], 0)
    A.free(K.gb)


def proj_phase(K):
    P, I, A = K.P, K.I, K.A
    hb = K.hT.b
    ring = Ring(K, 3)
    acc = 0
    for g in range(2):
        wv, wb = ring.load(I["w_in"][:, g * 512:(g + 1) * 512], 16, 512)
        for nt in range(4):
            fm_matmul(K, wv, wb, 16, nt, K.hT.ap, hb, acc)
            for pv, sl in acc_views(K, acc):
                P.op("act", lambda e, pv=pv, sl=sl, j=4 * g + nt: e.activation(out=K.uT.ap[:, j, sl], in_=pv, func=AF.Copy),
                     reads=acc_bufs(K, acc), writes=K.uT.b)
            acc ^= 1
    ring.free()


def ssm_own(K):
    P, I, A, O = K.P, K.I, K.A, K.O
    PSB = K.psb
    Hist = A.alloc("Hist", [2, 32, 64], F32)
    T1 = A.alloc("T1", [2, 32, 64], F32)
    H2 = A.alloc("H2", [2, 32, 64], BF16)
    HistB = A.alloc("HistB", [2, 32, 64], BF16)
    cA = A.alloc("cA", [2, 32], F32)
    cB = A.alloc("cB", [2, 32], F32)
    ysb = A.alloc("ysb", [8, 64], F32)
    g1 = A.alloc("g1", [8, 64], F32)
    g2 = A.alloc("g2", [8, 64], F32)
    zb = K.ps[:, 0:4, :].rearrange("p (r a) (g t) -> p r (a g) t", r=2, t=128)
    yps = K.ps[:, 4:6, :].rearrange("p a (j t) -> p (a j) t", t=128)
    ypb = [PSB[4], PSB[5]]
    zbb = [PSB[0], PSB[1], PSB[2], PSB[3]]
    Hc = K.Hc

    zbs = [K.ps[:, 0:2, :].rearrange("p r (g t) -> p r g t", t=64), K.ps[:, 2:4, :].rearrange("p r (g t) -> p r g t", t=64)]
    zbbs = [[PSB[0], PSB[1]], [PSB[2], PSB[3]]]

    def bu_batch(t, b4, n, c0, zsel=None):
        zb_, zbb_ = (zb, zbb) if zsel is None else (zbs[zsel], zbbs[zsel])
        fns = []
        for gh in range(2):
            rows = slice(64 * gh, 64 * gh + 64)
            for gl in range(8):
                g = 32 * gh + 8 * b4 + gl
                for ri, tab in ((0, K.BtabR), (1, K.BtabI)):
                    fns.append(lambda e, rows=rows, gl=gl, g=g, ri=ri, tab=tab, gh=gh: e.matmul(
                        zb_[rows, ri, gl, 0:n], lhsT=tab.ap[:, g, :], rhs=K.uT.ap[:, g // 8, c0:c0 + n], start=True, stop=True,
                        tile_position=(0, 64 * gh)))
        P.pe(fns, reads=[K.BtabR.b] + K.uT.b, writes=zbb_)

    def cside(n, hb_ap, hb_buf):
        for j in range(8):
            gh = j // 4
            rows = slice(64 * gh, 64 * gh + 64)
            fns = []
            for g8 in range(8):
                gq = (8 * j + g8) % 32
                for ri, tab in ((0, K.CtabR), (1, K.CtabI)):
                    fns.append(lambda e, rows=rows, gq=gq, ri=ri, tab=tab, j=j, first=(g8 == 0 and ri == 0), last=(g8 == 7 and ri == 1):
                               e.matmul(yps[:, j, 0:n], lhsT=tab.ap[rows, gq, :], rhs=hb_ap[rows, ri, gq, 0:n], start=first, stop=last))
            P.pe(fns, reads=[K.CtabR.b, hb_buf], writes=ypb)

    def gelu_out(n, c0, perm):
        yv, a, b = ysb.ap[:, :, 0:n], g1.ap[:, :, 0:n], g2.ap[:, :, 0:n]
        P.op("act", lambda e: e.activation(out=a, in_=yv, func=AF.Square), reads=[ysb.b], writes=[g1.b])
        P.op("dve", lambda e: e.tensor_scalar(out=a, in0=a, scalar1=0.044715, scalar2=1.0, op0=ALU.mult, op1=ALU.add), reads=[g1.b], writes=[g1.b])
        P.op("dve", lambda e: e.tensor_tensor(out=a, in0=a, in1=yv, op=ALU.mult), reads=[g1.b, ysb.b], writes=[g1.b])
        P.op("act", lambda e: e.activation(out=b, in_=a, func=AF.Sigmoid, scale=1.5957691216057308), reads=[g1.b], writes=[g2.b])
        P.op("dve", lambda e: e.tensor_tensor(out=K.gT.ap[:, :, c0:c0 + n], in0=b, in1=yv, op=ALU.mult), reads=[g2.b, ysb.b], writes=K.gT.b)

    CTb = K.CT.ap.rearrange("p g t -> p (g t)").unsqueeze(1).broadcast_to([128, 2, 2048])
    STf = K.ST.ap.rearrange("p g t -> p (g t)")
    Hf = Hist.ap.rearrange("p r g t -> p r (g t)")
    T1f = T1.ap.rearrange("p r g t -> p r (g t)")
    H2f = H2.ap.rearrange("p r g t -> p r (g t)")
    HBf = HistB.ap.rearrange("p r g t -> p r (g t)")
    X0 = A.alloc("X0", [2, 32, 64], F32)
    X0f = X0.ap.rearrange("p r g t -> p r (g t)")

    def bu_all(t):
        for b4 in range(4):
            zsel = b4 % 2
            bu_batch(t, b4, 64, t * 64, zsel)
            P.op("act", lambda e, b4=b4, zsel=zsel: e.activation(out=X0.ap[:, :, 8 * b4:8 * b4 + 8, :], in_=zbs[zsel], func=AF.Copy),
                 reads=zbbs[zsel], writes=[X0.b])

    bu_all(0)
    for t in range(16):
        c0 = t * 64
        P.op("dve", lambda e: e.tensor_tensor(out=T1f, in0=X0f, in1=CTb, op=ALU.mult), reads=[X0.b, K.CT.b], writes=[T1.b])
        P.op("dve", lambda e: e.tensor_tensor(out=H2f[:, 0], in0=X0f[:, 1], in1=STf, op=ALU.mult), reads=[X0.b, K.ST.b], writes=[H2.b])
        P.op("dve", lambda e: e.tensor_tensor(out=H2f[:, 1], in0=X0f[:, 0], in1=STf, op=ALU.mult), reads=[X0.b, K.ST.b], writes=[H2.b])
        P.op("dve", lambda e: e.tensor_tensor(out=Hf[:, 0], in0=T1f[:, 0], in1=H2f[:, 0], op=ALU.add), reads=[T1.b, H2.b], writes=[Hist.b])
        P.op("dve", lambda e: e.tensor_tensor(out=Hf[:, 1], in0=T1f[:, 1], in1=H2f[:, 1], op=ALU.subtract), reads=[T1.b, H2.b], writes=[Hist.b])
        if t + 1 < 16:
            bu_all(t + 1)
        fns = []
        for ri in range(2):
            for gq in range(32):
                fns.append(lambda e, ri=ri, gq=gq: e.tensor_tensor_scan(
                    out=T1.ap[:, ri, gq, :], data0=K.Rm.ap[:, gq:gq + 1].broadcast_to([128, 64]), data1=Hist.ap[:, ri, gq, :],
                    initial=Hc.ap[:, ri, gq:gq + 1], op0=ALU.mult, op1=ALU.add))
        P.group("dve", fns, reads=[Hist.b, K.Rm.b, Hc.b], writes=[T1.b])
        P.op("dve", lambda e: e.tensor_tensor(out=Hf, in0=T1f, in1=CTb, op=ALU.mult), reads=[T1.b, K.CT.b], writes=[Hist.b])
        P.op("dve", lambda e: e.tensor_tensor(out=H2f[:, 0], in0=T1f[:, 1], in1=STf, op=ALU.mult), reads=[T1.b, K.ST.b], writes=[H2.b])
        P.op("dve", lambda e: e.tensor_tensor(out=H2f[:, 1], in0=T1f[:, 0], in1=STf, op=ALU.mult), reads=[T1.b, K.ST.b], writes=[H2.b])
        P.op("dve", lambda e: e.tensor_tensor(out=HBf[:, 0], in0=Hf[:, 0], in1=H2f[:, 0], op=ALU.subtract), reads=[Hist.b, H2.b], writes=[HistB.b])
        P.op("dve", lambda e: e.tensor_tensor(out=HBf[:, 1], in0=Hf[:, 1], in1=H2f[:, 1], op=ALU.add), reads=[Hist.b, H2.b], writes=[HistB.b])
        P.op("dve", lambda e: e.tensor_tensor(out=cA.ap, in0=T1.ap[:, :, :, 63], in1=K.c64.ap.unsqueeze(1).broadcast_to([128, 2, 32]), op=ALU.mult),
             reads=[T1.b, K.c64.b], writes=[cA.b])
        P.op("dve", lambda e: e.tensor_tensor(out=cB.ap, in0=T1.ap[:, :, :, 63], in1=K.s64.ap.unsqueeze(1).broadcast_to([128, 2, 32]), op=ALU.mult),
             reads=[T1.b, K.s64.b], writes=[cB.b])
        P.op("dve", lambda e: e.tensor_tensor(out=Hc.ap[:, 0], in0=cA.ap[:, 0], in1=cB.ap[:, 1], op=ALU.subtract), reads=[cA.b, cB.b], writes=[Hc.b])
        P.op("dve", lambda e: e.tensor_tensor(out=Hc.ap[:, 1], in0=cA.ap[:, 1], in1=cB.ap[:, 0], op=ALU.add), reads=[cA.b, cB.b], writes=[Hc.b])
        cside(64, HistB.ap, HistB.b)
        for j in range(8):
            P.op("dve", lambda e, j=j: e.scalar_tensor_tensor(out=ysb.ap[:, j, 0:64], in0=K.uT.ap[:, j, c0:c0 + 64], scalar=K.Dcol.ap[:, j:j + 1],
                                                              in1=yps[:, j, 0:64], op0=ALU.mult, op1=ALU.add),
                 reads=K.uT.b + [K.Dcol.b] + ypb, writes=[ysb.b])
        gelu_out(64, c0, False)
    stg = A.alloc("stg", [128], F32)
    for ri in range(2):
        pt = K.ps[0:32, 6, 0:128]
        P.pe([lambda e, ri=ri: e.transpose(out=pt, in_=Hc.ap[:, ri, :], identity=K.identf.ap)], reads=[Hc.b, K.bC], writes=[PSB[6]])
        P.op("act", lambda e: e.activation(out=stg.ap[0:32, :], in_=pt, func=AF.Copy), reads=[PSB[6]], writes=[stg.b])
        P.dma("sp", O["pst"][ri].rearrange("(h g) p -> g h p", h=2), stg.ap[0:32, :].rearrange("g (h p) -> g h p", h=2), reads=[stg.b], writes=[K.bout])

    for t in [Hist, HistB, T1, H2, cA, cB, X0]:
        A.free(t)
    Hs0 = A.alloc("Hs0", [2, 32, 16], F32)
    HistS = A.alloc("HistS", [2, 32, 4, 16], F32)
    HistSB = A.alloc("HistSB", [2, 32, 64], BF16)
    MmS = A.alloc("MmS", [2, 2, 32, 16], F32)
    RrS = A.alloc("RrS", [2, 32, 16], F32)
    xin = [A.alloc(f"xin{i}", [2, 64], F32) for i in range(2)]
    for ri, nm in ((0, "sre"), (1, "sim")):
        for sq in range(4):
            xi = xin[(2 * ri + sq) % 2]
            for s in range(4):
                P.dma("sp", xi.ap[32 * s:32 * s + 32], I[nm][4 * sq + s].rearrange("(h g) p -> g h p", h=2), writes=[xi.b])
            pt = K.ps[:, 6, 0:128]
            P.pe([lambda e, xi=xi: e.transpose(out=pt, in_=xi.ap.rearrange("p h q -> p (h q)"), identity=K.identf.ap)],
                 reads=[xi.b, K.bC], writes=[PSB[6]])
            P.op("act", lambda e, ri=ri, sq=sq: e.activation(out=Hs0.ap[:, ri, :, 4 * sq:4 * sq + 4].rearrange("p g s -> p s g"),
                                                            in_=pt.rearrange("p (s g) -> p s g", s=4), func=AF.Copy),
                 reads=[PSB[6]], writes=[Hs0.b])
    c0 = 1024
    for b4 in range(4):
        bu_batch(8, b4, 64, c0)
        for ri in range(2):
            P.op("act", lambda e, b4=b4, ri=ri: e.activation(out=HistS.ap[:, ri, 8 * b4:8 * b4 + 8, :, :],
                                                            in_=zb[:, ri, :, 0:64].rearrange("p g (s t) -> p g t s", t=4), func=AF.Copy),
                 reads=zbb, writes=[HistS.b])
    A4b = K.A4.ap.unsqueeze(4).broadcast_to([128, 2, 2, 32, 16])
    for tt in range(4):
        prev = Hs0.ap if tt == 0 else HistS.ap[:, :, :, tt - 1, :]
        pb = [Hs0.b] if tt == 0 else [HistS.b]
        P.op("dve", lambda e, prev=prev: e.tensor_tensor(out=MmS.ap, in0=A4b, in1=prev.unsqueeze(1).broadcast_to([128, 2, 2, 32, 16]), op=ALU.mult),
             reads=[K.A4.b] + pb, writes=[MmS.b])
        P.op("dve", lambda e: e.tensor_tensor(out=RrS.ap, in0=MmS.ap[:, :, 0], in1=MmS.ap[:, :, 1], op=ALU.add), reads=[MmS.b], writes=[RrS.b])
        P.op("dve", lambda e, tt=tt: e.tensor_tensor(out=HistS.ap[:, :, :, tt, :], in0=HistS.ap[:, :, :, tt, :], in1=RrS.ap, op=ALU.add),
             reads=[RrS.b, HistS.b], writes=[HistS.b])
    P.op("act", lambda e: e.activation(out=HistSB.ap, in_=HistS.ap.rearrange("p r g t s -> p r g (t s)"), func=AF.Copy), reads=[HistS.b], writes=[HistSB.b])
    cside(64, HistSB.ap, HistSB.b)
    for j in range(8):
        P.op("dve", lambda e, j=j: e.scalar_tensor_tensor(
            out=ysb.ap[:, j, 0:64].rearrange("p (s t) -> p s t", t=4), in0=K.uT.ap[:, j, c0:c0 + 64].rearrange("p (s t) -> p s t", t=4),
            scalar=K.Dcol.ap[:, j:j + 1], in1=yps[:, j, 0:64].rearrange("p (t s) -> p s t", t=4), op0=ALU.mult, op1=ALU.add),
            reads=K.uT.b + [K.Dcol.b] + ypb, writes=[ysb.b])
    gelu_out(64, c0, True)
    stg2 = A.alloc("stg2", [4, 32], F32)
    for ri in range(2):
        for sq in range(4):
            pt = K.ps[:, 6, 0:128]
            P.op("dve", lambda e, ri=ri, sq=sq: e.tensor_copy(out=stg2.ap, in_=HistS.ap[:, ri, :, 3, 4 * sq:4 * sq + 4].rearrange("p g s -> p s g")),
                 reads=[HistS.b], writes=[stg2.b])
            P.pe([lambda e: e.transpose(out=pt, in_=stg2.ap.rearrange("p s g -> p (s g)"), identity=K.identf.ap)],
                 reads=[stg2.b, K.bC], writes=[PSB[6]])
            P.op("act", lambda e: e.activation(out=stg.ap, in_=pt, func=AF.Copy), reads=[PSB[6]], writes=[stg.b])
            for s in range(4):
                P.dma("sp", O["sst"][ri, 4 * sq + s].rearrange("(h g) p -> g h p", h=2),
                      stg.ap[32 * s:32 * s + 32, :].rearrange("g (h p) -> g h p", h=2), reads=[stg.b], writes=[K.bout])
    for t in [ysb, g1, g2, stg, stg2, Hs0, HistS, HistSB, MmS, RrS] + xin:
        A.free(t)
    for t in [K.A4, K.A4c, K.BtabR, K.BtabI, K.CtabR, K.CtabI, K.Dcol, K.Hc, K.uT, K.CT, K.ST, K.Rm, K.thG, K.c64, K.s64]:
        A.free(t)


def attention(K):
    P, I, A, O = K.P, K.I, K.A, K.O
    PSB = K.psb
    hb = K.hT.b
    K.kTd = A.alloc("kTd", [4, 128 + NT], BF16, nbufs=10, top=True)
    K.vtok = A.alloc("vtok", [10, 256], BF16, nbufs=10, top=True)
    K.kvlast = A.alloc("kvlast", [512], F32, top=True)
    K.kvsamp = A.alloc("kvsamp", [512], F32, top=True)
    load_wkv(K)
    P.op("act", lambda e: e.activation(out=K.kTd.ap[:, :, 0:128], in_=K.kTh.ap, func=AF.Copy), reads=[K.kTh.b], writes=[K.kTd.b[0]])
    P.op("act", lambda e: e.activation(out=K.vtok.ap[:, 0, :], in_=K.vth.ap, func=AF.Copy), reads=[K.vth.b], writes=[K.vtok.b[0]])
    for t in range(9):
        n = 128 if t < 8 else 64
        kdst = K.kvlast if t == 7 else (K.kvsamp if t == 8 else None)
        kv_tokmajor(K, K.hT.ap, [hb[t]], n, slice(t * 128, t * 128 + n), K.vtok.ap[:n, t + 1, :], K.vtok.b[t + 1], kdst)
    for bi, (t0, n) in enumerate(BLKS):
        tiles = list(range(t0 // 128, (t0 + n + 127) // 128))
        kT_dup(K, K.hT.ap, [hb[t] for t in tiles], n, lambda kv, t0=t0, n=n: K.kTd.ap[:, kv, 128 + t0:128 + t0 + n],
               [K.kTd.b[t + 1] for t in tiles], slice(t0, t0 + n))
    A.free(K.Wkv)
    A.free(K.Wkd)
    A.free(K.kTh)
    A.free(K.vth)
    K.qT = A.alloc("qT", [8, NT], BF16, nbufs=1, top=True)
    ring = Ring(K, 2)
    acc = 0
    for g in range(2):
        wv, wb = ring.load(I["w_in"][:, 1024 + g * 512:1024 + (g + 1) * 512], 16, 512)
        for nt in range(4):
            fm_matmul(K, wv, wb, 16, nt, K.hT.ap, hb, acc)
            for pv, sl in acc_views(K, acc):
                P.op("act", lambda e, pv=pv, sl=sl, j=4 * g + nt: e.activation(out=K.qT.ap[:, j, sl], in_=pv, func=AF.Copy),
                     reads=acc_bufs(K, acc), writes=[K.qT.b])
            acc ^= 1
    ring.free()
    if ASTOP == 'q':
        return
    bmt = A.alloc("bmt", [16, 2, 128], F32)
    bmt0 = A.alloc("bmt0", [16, 128], F32)
    bmsc = A.alloc("bmsc", [2, 8, 4], F32)
    bmsn = A.alloc("bmsn", [2, 8, 4], F32)
    esk = A.alloc("esk", [8], F32)
    P.dma("sp", bmt.ap, I["bmt"], writes=[bmt.b])
    P.dma("sp", bmt0.ap, I["bmt0"], writes=[bmt0.b])
    P.dma("sp", bmsc.ap, I["bmsc"].rearrange("k (j a) t -> k j a t", j=2), writes=[bmsc.b])
    P.dma("sp", bmsn.ap[0:4], I["bmsn"].rearrange("k (j a) t -> k j a t", j=2), writes=[bmsn.b])
    with K.nc.allow_non_contiguous_dma(reason="tiny"):
        for j in range(2):
            P.dma("sp", esk.ap[64 * j:64 * j + 64, :], I["sinks"].rearrange("(p j) -> j p", j=2)[j:j + 1, :].broadcast_to([64, 8]), writes=[esk.b])
    P.op("act", lambda e: e.activation(out=esk.ap, in_=esk.ap, func=AF.Exp), reads=[esk.b], writes=[esk.b])
    if ASTOP == 'tabs':
        return
    ee = [A.alloc(f"ee{i}", [2, 2, 2, 128], F32) for i in range(2)]
    pT = [A.alloc(f"pT{i}", [2, 2, 2, 128], BF16) for i in range(2)]
    dn = A.alloc("dn", [2, 128], F32)
    spsv = K.ps[:, 0:2, :].rearrange("p j (a k q) -> p j a k q", a=2, k=2)
    spb = [PSB[0], PSB[1]]
    ops = K.ps[:, 2, 0:256].rearrange("p (a q) -> p a q", q=128)
    dps = K.ps[:, 3, 0:256].rearrange("p (a q) -> p a q", q=128)
    it = 0
    lim = str(ASTOP).startswith('p_')
    for i in range(1 if lim else 8):
        qc = slice(i * 128, (i + 1) * 128)
        for hbk in range(1 if lim else 4):
            kv = hbk
            e_, p_ = ee[it % 2], pT[it % 2]
            it += 1
            fns = []
            for hh in range(4):
                h = 4 * hbk + hh
                pair, j = h // 2, h % 2
                rows = slice(64 * j, 64 * j + 64)
                for kb in range(2):
                    kc0 = 128 * (i + kb)
                    fns.append(lambda e, hh=hh, kb=kb, rows=rows, pair=pair, kc0=kc0, j=j: e.matmul(
                        spsv[:, j, hh // 2, kb, :], lhsT=K.kTd.ap[rows, kv, kc0:kc0 + 128], rhs=K.qT.ap[rows, pair, qc], start=True, stop=True))
            P.pe(fns, reads=[K.kTd.b[i], K.kTd.b[i + 1], K.qT.b], writes=spb)
            if i == 0:
                for j in range(2):
                    P.op("dve", lambda e, e_=e_, j=j: e.scalar_tensor_tensor(out=e_.ap[:, j, :, 0, :], in0=spsv[:, j, :, 0, :], scalar=0.125,
                                                                             in1=bmt0.ap[:, 4 * hbk + 2 * j:4 * hbk + 2 * j + 2, :], op0=ALU.mult, op1=ALU.add),
                         reads=spb + [bmt0.b], writes=[e_.b])
                    P.op("dve", lambda e, e_=e_, j=j: e.scalar_tensor_tensor(out=e_.ap[:, j, :, 1, :], in0=spsv[:, j, :, 1, :], scalar=0.125,
                                                                             in1=bmt.ap[:, 4 * hbk + 2 * j:4 * hbk + 2 * j + 2, 1, :], op0=ALU.mult, op1=ALU.add),
                         reads=spb + [bmt.b], writes=[e_.b])
            else:
                P.op("dve", lambda e, e_=e_: e.scalar_tensor_tensor(
                    out=e_.ap, in0=spsv, scalar=0.125, in1=bmt.ap[:, 4 * hbk:4 * hbk + 4, :, :],
                    op0=ALU.mult, op1=ALU.add), reads=spb + [bmt.b], writes=[e_.b])
            if ASTOP == 'p_s':
                continue
            P.op("act", lambda e, e_=e_, p_=p_: e.activation(out=p_.ap, in_=e_.ap, func=AF.Exp), reads=[e_.b], writes=[p_.b])
            if ASTOP == 'p_e':
                continue
            fns = []
            for hh in range(4):
                pp, j = hh // 2, hh % 2
                rows = slice(64 * j, 64 * j + 64)
                for kb in range(2):
                    fns.append(lambda e, hh=hh, kb=kb, pp=pp, j=j, rows=rows, p_=p_: e.matmul(
                        ops[rows, pp, :], lhsT=K.vtok.ap[:, i + kb, kv * 64:(kv + 1) * 64], rhs=p_.ap[:, j, hh // 2, kb, :],
                        start=(kb == 0), stop=(kb == 1), tile_position=(0, 64 * j)))
                for kb in range(2):
                    fns.append(lambda e, hh=hh, kb=kb, pp=pp, j=j, rows=rows, p_=p_: e.matmul(
                        dps[rows, pp, :], lhsT=K.ones64.ap, rhs=p_.ap[:, j, hh // 2, kb, :],
                        start=(kb == 0), stop=(kb == 1), tile_position=(0, 64 * j)))
            P.pe(fns, reads=[K.vtok.b[i], K.vtok.b[i + 1], p_.b, K.bC], writes=[PSB[2], PSB[3]])
            P.op("dve", lambda e: e.tensor_tensor(out=dn.ap, in0=dps, in1=esk.ap[:, 2 * hbk:2 * hbk + 2].unsqueeze(2).broadcast_to([128, 2, 128]),
                                                  op=ALU.add), reads=[PSB[3], esk.b], writes=[dn.b])
            P.op("dve", lambda e: e.reciprocal(out=dn.ap, in_=dn.ap), reads=[dn.b], writes=[dn.b])
            P.op("dve", lambda e: e.tensor_tensor(out=K.oT.ap[:, 2 * hbk:2 * hbk + 2, qc], in0=ops, in1=dn.ap, op=ALU.mult),
                 reads=[PSB[2], dn.b], writes=K.oT.b)
    if ASTOP == 'prompt' or lim:
        return
    kc_ = [A.alloc(f"kc{i}", [4, 2, 64], BF16) for i in range(2)]
    vc_ = [A.alloc(f"vc{i}", [256], BF16) for i in range(2)]
    kcT = [A.alloc(f"kcT{i}", [4, 128], BF16) for i in range(2)]
    vnew = A.alloc("vnew", [16, 256], BF16)
    eS = A.alloc("eS", [2, 8, 4], F32)
    eN = A.alloc("eN", [2, 8, 4], F32)
    pS = [A.alloc(f"pS{i}", [2, 8, 4], BF16) for i in range(2)]
    pN = [A.alloc(f"pN{i}", [2, 8, 4], BF16) for i in range(2)]
    dS = A.alloc("dS", [8, 4], F32)
    with K.nc.allow_non_contiguous_dma(reason="tiny relayout"):
        for s in range(16):
            P.dma("sp", vnew.ap[0:4, s, :], K.vtok.ap[4 * s:4 * s + 4, 9, :], reads=[K.vtok.b[9]], writes=[vnew.b])
    ptk = K.ps[:, 4, :].bitcast(BF16)[:, 0:512].rearrange("p (k n) -> p k n", k=4)
    sscb = [K.ps[:, 5 + 2 * j, 0:32].rearrange("p (h t) -> p h t", t=4) for j in range(2)]
    ssnb = [K.ps[0:4, 5 + 2 * j, 32:64].rearrange("p (h t) -> p h t", t=4) for j in range(2)]
    osp = K.ps[:, 6, 0:32].rearrange("p (a t) -> p a t", t=4)
    dsp = K.ps[:, 6, 32:64].rearrange("p (a t) -> p a t", t=4)
    for s in range(16):
        kc, vc, kt, ps_, pn_ = kc_[s % 2], vc_[s % 2], kcT[s % 2], pS[s % 2], pN[s % 2]
        P.dma("pool", kc.ap, I["ck"][s].rearrange("k (v d) -> k v d", v=4).unsqueeze(2).broadcast_to([128, 4, 2, 64]), writes=[kc.b])
        P.dma("pool", vc.ap, I["cv"][s], writes=[vc.b])
        P.dma("sp", O["skk"][s, 0:124, :], I["ck"][s, 4:128, :], writes=[K.bout])
        P.dma("sp", O["skv"][s, 0:124, :], I["cv"][s, 4:128, :], writes=[K.bout])
        P.dma("sp", O["skk"][s, 124:128, :], K.kvsamp.ap[4 * s:4 * s + 4, 0:256], reads=[K.kvsamp.b], writes=[K.bout])
        P.dma("sp", O["skv"][s, 124:128, :], K.kvsamp.ap[4 * s:4 * s + 4, 256:512], reads=[K.kvsamp.b], writes=[K.bout])
        P.pe([lambda e, kv=kv: e.transpose(out=ptk[:, kv, :], in_=kc.ap[:, kv].rearrange("k a d -> k (a d)"),
                                           identity=K.identb.ap) for kv in range(4)], reads=[kc.b, K.bC], writes=[PSB[4]])
        P.op("act", lambda e: e.activation(out=kt.ap, in_=ptk, func=AF.Copy), reads=[PSB[4]], writes=[kt.b])
        qc0 = 1024 + 4 * s
        fns = []
        for h in range(16):
            kv, pair, j = h // 4, h // 2, h % 2
            rows = slice(64 * j, 64 * j + 64)
            fns.append(lambda e, h=h, kv=kv, pair=pair, rows=rows, j=j: e.matmul(sscb[j][:, pair, :], lhsT=kt.ap[rows, kv, :], rhs=K.qT.ap[rows, pair, qc0:qc0 + 4],
                                                                                start=True, stop=True))
            fns.append(lambda e, h=h, kv=kv, pair=pair, rows=rows, j=j: e.matmul(ssnb[j][:, pair, :], lhsT=K.kTd.ap[rows, kv, 128 + qc0:128 + qc0 + 4],
                                                                                rhs=K.qT.ap[rows, pair, qc0:qc0 + 4], start=True, stop=True))
        P.pe(fns, reads=[kt.b, K.kTd.b[9], K.qT.b], writes=[PSB[5], PSB[7]])
        for j in range(2):
            P.op("dve", lambda e, j=j: e.scalar_tensor_tensor(out=eS.ap[:, j], in0=sscb[j], scalar=0.125, in1=bmsc.ap[:, j], op0=ALU.mult, op1=ALU.add),
                 reads=[PSB[5], PSB[7], bmsc.b], writes=[eS.b])
            P.op("dve", lambda e, j=j: e.scalar_tensor_tensor(out=eN.ap[0:4, j], in0=ssnb[j], scalar=0.125, in1=bmsn.ap[0:4, j], op0=ALU.mult, op1=ALU.add),
                 reads=[PSB[5], PSB[7], bmsn.b], writes=[eN.b])
        P.op("act", lambda e, ps_=ps_: e.activation(out=ps_.ap, in_=eS.ap, func=AF.Exp), reads=[eS.b], writes=[ps_.b])
        P.op("act", lambda e, pn_=pn_: e.activation(out=pn_.ap[0:4], in_=eN.ap[0:4], func=AF.Exp), reads=[eN.b], writes=[pn_.b])
        fns = []
        for h in range(16):
            kv, pair, j = h // 4, h // 2, h % 2
            rows = slice(64 * j, 64 * j + 64)
            fns.append(lambda e, h=h, kv=kv, pair=pair, rows=rows, j=j: e.matmul(osp[rows, pair, :], lhsT=vc.ap[:, kv * 64:(kv + 1) * 64], rhs=ps_.ap[:, j, pair, :],
                                                                                start=True, stop=False, tile_position=(0, 64 * j)))
            fns.append(lambda e, h=h, kv=kv, pair=pair, rows=rows, j=j: e.matmul(osp[rows, pair, :], lhsT=vnew.ap[0:4, s, kv * 64:(kv + 1) * 64], rhs=pn_.ap[0:4, j, pair, :],
                                                                                start=False, stop=True, tile_position=(0, 64 * j)))
            fns.append(lambda e, h=h, pair=pair, rows=rows, j=j: e.matmul(dsp[rows, pair, :], lhsT=K.ones64.ap, rhs=ps_.ap[:, j, pair, :],
                                                                         start=True, stop=False, tile_position=(0, 64 * j)))
            fns.append(lambda e, h=h, pair=pair, rows=rows, j=j: e.matmul(dsp[rows, pair, :], lhsT=K.ones64.ap[0:4], rhs=pn_.ap[0:4, j, pair, :],
                                                                         start=False, stop=True, tile_position=(0, 64 * j)))
        P.pe(fns, reads=[vc.b, vnew.b, ps_.b, pn_.b, K.bC], writes=[PSB[6]])
        P.op("dve", lambda e: e.tensor_tensor(out=dS.ap, in0=dsp, in1=esk.ap.unsqueeze(2).broadcast_to([128, 8, 4]), op=ALU.add),
             reads=[PSB[6], esk.b], writes=[dS.b])
        P.op("dve", lambda e: e.reciprocal(out=dS.ap, in_=dS.ap), reads=[dS.b], writes=[dS.b])
        P.op("dve", lambda e: e.tensor_tensor(out=K.oT.ap[:, :, qc0:qc0 + 4], in0=osp, in1=dS.ap, op=ALU.mult),
             reads=[PSB[6], dS.b], writes=K.oT.b)
    P.dma("sp", O["pkv"], K.kvlast.ap, reads=[K.kvlast.b], writes=[K.bout])
    for t in [bmt, bmt0, bmsc, bmsn, esk, dn, vnew, eS, eN, dS] + ee + pT + kc_ + vc_ + kcT + pS + pN:
        A.free(t)
    for t in [K.qT, K.kTd, K.vtok, K.kvlast, K.kvsamp]:
        A.free(t)


def merge_phase(K):
    P, I, A = K.P, K.I, K.A
    hb = K.hT.b
    K.mT = A.alloc("mT", [16, NT], BF16, top=True)
    ring = Ring(K, 3, 4)
    s1 = A.alloc("s1", [NT], BF16)
    s2 = A.alloc("s2", [NT], BF16)
    s3 = A.alloc("s3", [NT], BF16)
    tv = A.alloc("tv", [NT], F32)
    tu = A.alloc("tu", [NT], F32)
    acc = 0
    for sg in range(4):
        c0 = sg * 512
        wga = ring.load(I["w_in"][:, 2560 + c0:2560 + c0 + 512], 16, 512)
        wgb = ring.load(I["w_in"][:, 4608 + c0:4608 + c0 + 512], 16, 512)
        wval = ring.load(I["w_glu_val"][:, c0:c0 + 512], 8, 512)
        wgate = ring.load(I["w_glu_gate"][:, c0:c0 + 512], 8, 512)
        for nt in range(4):
            n = 4 * sg + nt
            if nt == 0:
                pass
            fm_matmul(K, wga[0], wga[1], 16, nt, K.hT.ap, hb, acc)
            for pv, sl in acc_views(K, acc):
                P.op("act", lambda e, pv=pv, sl=sl: e.activation(out=s1.ap[:, sl], in_=pv, func=AF.Sigmoid), reads=acc_bufs(K, acc), writes=[s1.b])
            acc ^= 1
            fm_matmul(K, wgate[0], wgate[1], 8, nt, K.gT.ap, K.gT.b, acc)
            for pv, sl in acc_views(K, acc):
                P.op("act", lambda e, pv=pv, sl=sl: e.activation(out=s2.ap[:, sl], in_=pv, func=AF.Sigmoid), reads=acc_bufs(K, acc), writes=[s2.b])
            acc ^= 1
            fm_matmul(K, wval[0], wval[1], 8, nt, K.gT.ap, K.gT.b, acc)
            for pv, sl in acc_views(K, acc):
                P.op("dve", lambda e, pv=pv, sl=sl: e.tensor_tensor(out=tv.ap[:, sl], in0=pv, in1=s1.ap[:, sl], op=ALU.mult),
                     reads=acc_bufs(K, acc) + [s1.b], writes=[tv.b])
            P.op("dve", lambda e: e.tensor_tensor(out=tv.ap, in0=tv.ap, in1=s2.ap, op=ALU.mult), reads=[tv.b, s2.b], writes=[tv.b])
            acc ^= 1
            fm_matmul(K, wgb[0], wgb[1], 16, nt, K.hT.ap, hb, acc)
            for pv, sl in acc_views(K, acc):
                P.op("act", lambda e, pv=pv, sl=sl: e.activation(out=s3.ap[:, sl], in_=pv, func=AF.Sigmoid), reads=acc_bufs(K, acc), writes=[s3.b])
            acc ^= 1
            if nt == 0:
                wab = ring.load(I["w_attn_br"][:, c0:c0 + 512], 8, 512)
            fm_matmul(K, wab[0], wab[1], 8, nt, K.oT.ap, K.oT.b, acc)
            for pv, sl in acc_views(K, acc):
                P.op("dve", lambda e, pv=pv, sl=sl: e.tensor_tensor(out=tu.ap[:, sl], in0=pv, in1=s3.ap[:, sl], op=ALU.mult),
                     reads=acc_bufs(K, acc) + [s3.b], writes=[tu.b])
            P.op("dve", lambda e, n=n: e.tensor_tensor(out=K.mT.ap[:, n, :], in0=tu.ap, in1=tv.ap, op=ALU.add), reads=[tu.b, tv.b], writes=[K.mT.b])
            acc ^= 1
    ring.free()
    for t in (s1, s2, s3, tv, tu):
        A.free(t)


def stats_rstd(K, xT, sq, rstd):
    P = K.P
    P.op("act", lambda e: e.activation(out=sq.ap, in_=xT.ap, func=AF.Square), reads=[xT.b], writes=[sq.b])
    fns = []
    for bi, (t0, n) in enumerate(BLKS):
        for c in range(16):
            fns.append(lambda e, bi=bi, t0=t0, n=n, c=c: e.matmul(K.ps[:, bi, 0:n], lhsT=K.onesm.ap, rhs=sq.ap[:, c, t0:t0 + n],
                                                                 start=(c == 0), stop=(c == 15)))
    P.pe(fns, reads=[sq.b, K.bC], writes=acc_bufs(K, 0))
    for pv, sl in acc_views(K, 0):
        P.op("act", lambda e, pv=pv, sl=sl: e.activation(out=rstd.ap[:, sl], in_=pv, func=AF.Sqrt, bias=K.epsc.ap, scale=1.0),
             reads=acc_bufs(K, 0) + [K.bC], writes=[rstd.b])
    P.op("dve", lambda e: e.reciprocal(out=rstd.ap, in_=rstd.ap), reads=[rstd.b], writes=[rstd.b])


def post_phase(K):
    P, I, A, O = K.P, K.I, K.A, K.O
    PSB = K.psb
    xT = A.alloc("xT", [16, NT], F32, top=True)
    rstd = A.alloc("rstd", [NT], F32, top=True)
    gcol = A.alloc("gcol", [2, 16], F32, top=True)
    act = [A.alloc(f"actT{i}", [8, NT], BF16, top=True) for i in range(1)]
    sgt = [A.alloc(f"sg{i}", [NT], BF16, top=True) for i in range(2)]
    xl = [A.alloc(f"xl{i}", [D], F32) for i in range(2)]
    for t in range(9):
        n = 128 if t < 8 else 64
        x_ = xl[t % 2]
        P.dma("sp", x_.ap[:n], I["xo"][t * 128:t * 128 + n, :], writes=[x_.b])
        for q4 in range(4):
            bank = q4 % 2
            pt = K.ps[:, bank, :].rearrange("p (c n) -> p c n", c=4)
            P.pe([lambda e, c=c, pt=pt, q4=q4: e.transpose(out=pt[:, c, 0:n], in_=x_.ap[:n, (4 * q4 + c) * 128:(4 * q4 + c + 1) * 128],
                                                           identity=K.identf.ap[:n, :n]) for c in range(4)],
                 reads=[x_.b, K.bC], writes=[PSB[bank]])
            P.op("act", lambda e, pt=pt, q4=q4: e.activation(out=xT.ap[:, 4 * q4:4 * q4 + 4, t * 128:t * 128 + n], in_=pt[:, :, 0:n], func=AF.Copy),
                 reads=[PSB[bank]], writes=[xT.b])
    for x_ in xl:
        A.free(x_)
    ring = Ring(K, 2)
    acc = 0
    for g in range(4):
        wv, wb = ring.load(I["w_out"][:, g * 512:(g + 1) * 512], 16, 512)
        for nt in range(4):
            n = 4 * g + nt
            fm_matmul(K, wv, wb, 16, nt, K.mT.ap, [K.mT.b], acc)
            for pv, sl in acc_views(K, acc):
                P.op("dve", lambda e, pv=pv, sl=sl, n=n: e.tensor_tensor(out=xT.ap[:, n, sl], in0=pv, in1=xT.ap[:, n, sl], op=ALU.add),
                     reads=acc_bufs(K, acc) + [xT.b], writes=[xT.b])
            acc ^= 1
    ring.free()
    A.free(K.mT)
    sq = A.alloc("sq", [16, NT], BF16)
    with K.nc.allow_non_contiguous_dma(reason="tiny"):
        P.dma("sp", gcol.ap[:, 0, :], I["norm_ffn"].rearrange("(c p) -> p c", p=128), writes=[gcol.b])
        P.dma("sp", gcol.ap[:, 1, :], I["norm_final"].rearrange("(c p) -> p c", p=128), writes=[gcol.b])
    stats_rstd(K, xT, sq, rstd)
    h2 = K.hT
    h2b = Buf("h2T")
    _merge(h2b.r, {})
    for b in K.hT.b:
        if b.w is not None:
            _merge(h2b.r, {b.w[0].num: b.w})
        _merge(h2b.r, b.r)
    for c in range(16):
        P.op("dve", lambda e, c=c: e.scalar_tensor_tensor(out=h2.ap[:, c, :], in0=xT.ap[:, c, :], scalar=gcol.ap[:, 0, c:c + 1], in1=rstd.ap,
                                                          op0=ALU.mult, op1=ALU.mult), reads=[xT.b, gcol.b, rstd.b], writes=[h2b])
    A.free(sq)
    ring = Ring(K, 3, 2)
    nsg = (HID + 1023) // 1024
    k = 0
    for sg in range(nsg):
        h0 = sg * 1024
        hw = min(1024, HID - h0)
        nft = hw // 128
        a_ = act[0]
        for half in range(hw // 512):
            wg = ring.load(I["w_ffn_in"][:, h0 + half * 512:h0 + half * 512 + 512], 16, 512)
            wu = ring.load(I["w_ffn_in"][:, HID + h0 + half * 512:HID + h0 + half * 512 + 512], 16, 512)
            for nt in range(4):
                f = 4 * half + nt
                s_ = sgt[k % 2]
                k += 1
                fm_matmul(K, wg[0], wg[1], 16, nt, h2.ap, [h2b], acc)
                for pv, sl in acc_views(K, acc):
                    P.op("act", lambda e, pv=pv, sl=sl, s_=s_: e.activation(out=s_.ap[:, sl], in_=pv, func=AF.Silu), reads=acc_bufs(K, acc), writes=[s_.b])
                acc ^= 1
                fm_matmul(K, wu[0], wu[1], 16, nt, h2.ap, [h2b], acc)
                for pv, sl in acc_views(K, acc):
                    P.op("dve", lambda e, pv=pv, sl=sl, s_=s_, f=f, a_=a_: e.tensor_tensor(out=a_.ap[:, f, sl], in0=pv, in1=s_.ap[:, sl], op=ALU.mult),
                         reads=acc_bufs(K, acc) + [s_.b], writes=[a_.b])
                acc ^= 1
        for g in range(4):
            wv, wb = ring.load(I["w_ffn_out"][h0:h0 + hw, g * 512:(g + 1) * 512], nft, 512)
            for nt in range(4):
                n = 4 * g + nt
                fm_matmul(K, wv, wb, nft, nt, a_.ap, [a_.b], acc)
                for pv, sl in acc_views(K, acc):
                    P.op("dve", lambda e, pv=pv, sl=sl, n=n: e.tensor_tensor(out=xT.ap[:, n, sl], in0=pv, in1=xT.ap[:, n, sl], op=ALU.add),
                         reads=acc_bufs(K, acc) + [xT.b], writes=[xT.b])
                acc ^= 1
    ring.free()
    for t in act + sgt:
        A.free(t)
    sq = A.alloc("sq2", [16, NT], BF16)
    stats_rstd(K, xT, sq, rstd)
    A.free(sq)
    yf = [A.alloc(f"yf{i}", [16, 128], F32) for i in range(2)]
    yo = [A.alloc(f"yo{i}", [D], F32) for i in range(2)]
    for t in range(9):
        n = 128 if t < 8 else 64
        cs = slice(t * 128, t * 128 + n)
        y_, o_ = yf[t % 2], yo[t % 2]
        for c in range(16):
            P.op("dve", lambda e, c=c: e.scalar_tensor_tensor(out=y_.ap[:, c, 0:n], in0=xT.ap[:, c, cs], scalar=gcol.ap[:, 1, c:c + 1], in1=rstd.ap[:, cs],
                                                              op0=ALU.mult, op1=ALU.mult), reads=[xT.b, gcol.b, rstd.b], writes=[y_.b])
        for q4 in range(4):
            bank = 6 + q4 % 2
            pt = K.ps[:n, bank, :].rearrange("p (c n) -> p c n", c=4)
            P.pe([lambda e, c=c, pt=pt, q4=q4: e.transpose(out=pt[:, c, :], in_=y_.ap[:, 4 * q4 + c, 0:n], identity=K.identf.ap) for c in range(4)],
                 reads=[y_.b, K.bC], writes=[PSB[bank]])
            P.op("act", lambda e, pt=pt, q4=q4: e.activation(out=o_.ap[:n, q4 * 512:(q4 + 1) * 512], in_=pt.rearrange("p c n -> p (c n)"), func=AF.Copy),
                 reads=[PSB[bank]], writes=[o_.b])
        P.dma("sp", O["yo"][t * 128:t * 128 + n, :], o_.ap[:n], reads=[o_.b], writes=[K.bout])


def build_program():
    nc = bass.Bass("TRN2", target_bir_lowering=False)
    K = Ctx()
    K.nc = nc
    K.P = Prog(nc)
    P = K.P

    def din(name, shape, dt=F32):
        return nc.dram_tensor(name, list(shape), dt, kind="ExternalInput").ap()

    def dout(name, shape, dt=F32):
        return nc.dram_tensor(name, list(shape), dt, kind="ExternalOutput").ap()

    I = {}
    I["xo"] = din("xo", [NT, D])
    I["xp"] = din("xp", [NPRE, D])
    I["w_in"] = din("w_in", [D, INC])
    I["w_glu_val"] = din("w_glu_val", [1024, D])
    I["w_glu_gate"] = din("w_glu_gate", [1024, D])
    I["w_attn_br"] = din("w_attn_br", [1024, D])
    I["w_out"] = din("w_out", [D, D])
    I["w_ffn_in"] = din("w_ffn_in", [D, 2 * HID])
    I["w_ffn_out"] = din("w_ffn_out", [HID, D])
    for nm in ["norm_attn", "norm_ffn", "norm_final"]:
        I[nm] = din(nm, [D])
    I["lam_re"] = din("lam_re", [64, 64])
    I["lam_im"] = din("lam_im", [64, 64])
    I["log_dt"] = din("log_dt", [64])
    I["b_re"] = din("b_re", [64, 64, 16])
    I["b_im"] = din("b_im", [64, 64, 16])
    I["c_re"] = din("c_re", [64, 16, 64])
    I["c_im"] = din("c_im", [64, 16, 64])
    I["d_skip"] = din("d_skip", [1024])
    I["sinks"] = din("sinks", [16])
    I["sre"] = din("sre", [16, 64, 64])
    I["sim"] = din("sim", [16, 64, 64])
    I["ck"] = din("ck", [16, 128, 256])
    I["cv"] = din("cv", [16, 128, 256])
    I["bmt"] = din("bmt", [128, 16, 2, 128])
    I["bmt0"] = din("bmt0", [128, 16, 128])
    I["bmsc"] = din("bmsc", [128, 16, 4])
    I["bmsn"] = din("bmsn", [4, 16, 4])
    I["kcol"] = din("kcol", [128, 1])
    I["m8"] = din("m8", [128, 8])
    I["m88"] = din("m88", [128, 8, 8])
    I["trow"] = din("trow", [128, 64])
    O = {}
    O["yo"] = dout("yo", [NT, D])
    O["pst"] = dout("pst", [2, 64, 64])
    O["pkv"] = dout("pkv", [128, 512])
    O["sst"] = dout("sst", [2, 16, 64, 64])
    O["skk"] = dout("skk", [16, 128, 256])
    O["skv"] = dout("skv", [16, 128, 256])
    if DEBUG:
        O["dbg_h"] = dout("dbg_h", [128, 2, 32])
        O["dbg_hT"] = dout("dbg_hT", [128, 16, NT], BF16)
        O["dbg_uT"] = dout("dbg_uT", [128, 8, NT], BF16)
        O["dbg_gT"] = dout("dbg_gT", [128, 8, NT], BF16)
        O["dbg_oT"] = dout("dbg_oT", [128, 8, NT], BF16)
        O["dbg_mT"] = dout("dbg_mT", [128, 16, NT], BF16)
    K.I, K.O = I, O
    K.bout = Buf("out")
    K.A = Arena(nc)
    A = K.A
    K.ps = nc.alloc_psum_tensor("psum", [128, 8, 512], F32)
    K.psb = [Buf(f"psb{i}") for i in range(8)]

    K.identf = A.alloc("identf", [128], F32, top=True)
    K.identb = A.alloc("identb", [128], BF16, top=True)
    K.ones64 = A.alloc("ones64", [64], BF16, top=True)
    K.onesm = A.alloc("onesm", [128], BF16, top=True)
    K.epsc = A.alloc("epsc", [1], F32, top=True)
    bC = Buf("const")
    K.bC = bC
    P.op("pool", lambda e: e.memset(K.identf.ap, 0.0), writes=[bC])
    P.op("pool", lambda e: e.affine_select(out=K.identf.ap, in_=K.identf.ap, pattern=[[-1, 128]], compare_op=ALU.not_equal,
                                          fill=1.0, base=0, channel_multiplier=1), reads=[bC], writes=[bC])
    P.op("pool", lambda e: e.tensor_copy(out=K.identb.ap, in_=K.identf.ap), reads=[bC], writes=[bC])
    P.op("pool", lambda e: e.memset(K.ones64.ap, 1.0), writes=[bC])
    P.op("pool", lambda e: e.memset(K.onesm.ap, 1.0 / D), writes=[bC])
    P.op("pool", lambda e: e.memset(K.epsc.ap, EPS), writes=[bC])

    K.hT = A.alloc("hT", [16, NT], BF16, nbufs=9, top=True)
    K.gT = A.alloc("gT", [8, NT], BF16, top=True)
    K.gT.b = [K.gT.b]
    K.oT = A.alloc("oT", [8, NT], BF16, top=True)
    K.oT.b = [K.oT.b]
    def stop(name):
        return STOP == name

    ssm_tables(K)
    if DEBUG:
        O["dbg_A4"] = dout("dbg_A4", [128, 2, 2, 32])
        O["dbg_A4c"] = dout("dbg_A4c", [128, 2, 2, 32])
        O["dbg_BbR"] = dout("dbg_BbR", [128, 32, 16])
        O["dbg_BbI"] = dout("dbg_BbI", [128, 32, 16])
        O["dbg_ApR"] = dout("dbg_ApR", [128, 64, 64], BF16)
        O["dbg_ApI"] = dout("dbg_ApI", [128, 64, 64], BF16)
        for nm, t in (("dbg_A4", K.A4), ("dbg_A4c", K.A4c), ("dbg_BbR", K.BbR), ("dbg_BbI", K.BbI), ("dbg_ApR", K.ApR), ("dbg_ApI", K.ApI)):
            P.dma("sp", O[nm], t.ap, reads=bl(t.b), writes=[K.bout])
    if not stop("tables"):
        K.kTh = A.alloc("kTh", [4, 128], BF16, top=True)
        K.vth = A.alloc("vth", [256], BF16, top=True)
        prefix_phase(K)
    if not (stop("tables") or stop("prefix")):
        K.uT = A.alloc("uT", [8, NT], BF16, top=True)
        K.uT.b = [K.uT.b]
        own_norm(K)
        for t in K.xts + K.xss + [K.junk, K.ss]:
            A.free(t)
        if DEBUG:
            P.dma("sp", O["dbg_hT"], K.hT.ap, reads=bl(K.hT.b), writes=[K.bout])
    if STOP not in ("tables", "prefix", "own_norm"):
        proj_phase(K)
        if DEBUG:
            P.dma("sp", O["dbg_uT"], K.uT.ap, reads=bl(K.uT.b), writes=[K.bout])
    if STOP not in ("tables", "prefix", "own_norm", "proj"):
        ssm_tables_own(K)
        A.free(K.BbR)
        A.free(K.BbI)
        ssm_own(K)
        if DEBUG:
            P.dma("sp", O["dbg_gT"], K.gT.ap, reads=bl(K.gT.b), writes=[K.bout])
    if STOP not in ("tables", "prefix", "own_norm", "proj", "ssm"):
        attention(K)
        if DEBUG:
            P.dma("sp", O["dbg_oT"], K.oT.ap, reads=bl(K.oT.b), writes=[K.bout])
    if STOP not in ("tables", "prefix", "own_norm", "proj", "ssm", "attn"):
        merge_phase(K)
        if DEBUG:
            P.dma("sp", O["dbg_mT"], K.mT.ap, reads=bl(K.mT.b), writes=[K.bout])
        A.free(K.gT)
        A.free(K.oT)
    if STOP not in ("tables", "prefix", "own_norm", "proj", "ssm", "attn", "merge"):
        post_phase(K)
    P.finish()
    K.ninstr = P.ninstr
    return nc, K


_CACHE = {}


def _host_consts(rel_bias, qidx):
    rb = np.asarray(rel_bias, np.float32)
    k = np.arange(128)[:, None, None]
    kb = np.arange(2)[None, :, None]
    q = np.arange(128)[None, None, :]
    dist = q + 128 - (kb * 128 + k)
    valid = (dist >= 0) & (dist < 128)
    bk = t5_bucket(dist)
    bias = rb[bk]
    bias = np.where(valid[..., None], bias, np.float32(NEG)).astype(np.float32)
    hperm = np.array([4 * (n // 4) + 2 * ((n % 4) % 2) + (n % 4) // 2 for n in range(16)])
    bmt = np.ascontiguousarray(np.transpose(bias, (0, 3, 1, 2))[:, hperm])
    bmt0 = bmt[:, :, 0, :].copy() if qidx > 0 else np.full((128, 16, 128), NEG, np.float32)
    j = np.arange(132)[:, None]
    t = np.arange(4)[None, :]
    dist = t + 128 - j
    valid = (dist >= 0) & (dist < 128)
    bs = np.where(valid[..., None], rb[t5_bucket(dist)], np.float32(NEG)).astype(np.float32)
    sperm = np.array([2 * (n % 8) + n // 8 for n in range(16)])
    bs = np.ascontiguousarray(np.transpose(bs, (0, 2, 1))[:, sperm])
    return bmt, np.ascontiguousarray(bmt0), np.ascontiguousarray(bs[:128]), np.ascontiguousarray(bs[128:])


def kernel(x_prompt, x_sample, state_ssm_re, state_ssm_im, cache_win_k, cache_win_v, rel_bias,
           norm_attn, w_in, lam_re, lam_im, log_dt, b_re, b_im, c_re, c_im, d_skip,
           w_glu_val, w_glu_gate, w_attn_br, sinks, w_out, norm_ffn, w_ffn_in, w_ffn_out, norm_final):
    f = lambda a: np.ascontiguousarray(np.asarray(a, dtype=np.float32))
    x_prompt, x_sample = f(x_prompt), f(x_sample)
    if "nc" not in _CACHE:
        _CACHE["nc"], _CACHE["K"] = build_program()
    nc = _CACHE["nc"]
    shared = {
        "w_in": f(w_in)[0], "w_glu_val": f(w_glu_val)[0], "w_glu_gate": f(w_glu_gate)[0], "w_attn_br": f(w_attn_br)[0],
        "w_out": f(w_out)[0], "w_ffn_in": f(w_ffn_in)[0], "w_ffn_out": f(w_ffn_out)[0],
        "norm_attn": f(norm_attn)[0], "norm_ffn": f(norm_ffn)[0], "norm_final": f(norm_final),
        "lam_re": f(lam_re)[0], "lam_im": f(lam_im)[0], "log_dt": f(log_dt)[0], "b_re": f(b_re)[0], "b_im": f(b_im)[0],
        "c_re": f(c_re)[0], "c_im": f(c_im)[0], "d_skip": f(d_skip)[0], "sinks": f(sinks)[0],
        "kcol": (127 - np.arange(128, dtype=np.float32)).reshape(128, 1),
        "m8": (np.arange(128)[:, None] // 16 == np.arange(8)[None, :]).astype(np.float32),
        "m88": np.ascontiguousarray(np.broadcast_to(np.eye(8, dtype=np.float32), (128, 8, 8))),
        "trow": np.ascontiguousarray(np.broadcast_to(np.arange(1, 65, dtype=np.float32), (128, 64))),
    }
    sre, sim = f(state_ssm_re)[0], f(state_ssm_im)[0]
    ck = f(cache_win_k)[0].reshape(128, 128, 256)
    cv = f(cache_win_v)[0].reshape(128, 128, 256)
    in_maps = []
    for c in range(8):
        b, q = c // 4, c % 4
        xo = np.concatenate([x_prompt[b, 1024 * q:1024 * (q + 1)], x_sample[16 * c:16 * c + 16].reshape(64, D)], axis=0)
        xp = np.zeros((NPRE, D), np.float32)
        if q > 0:
            xp[NPRE - 1024 * q:] = x_prompt[b, 0:1024 * q]
        bmt, bmt0, bmsc, bmsn = _host_consts(rel_bias, q)
        m = dict(shared)
        m.update({"xo": np.ascontiguousarray(xo), "xp": xp, "sre": sre[16 * c:16 * c + 16], "sim": sim[16 * c:16 * c + 16],
                  "ck": ck[16 * c:16 * c + 16], "cv": cv[16 * c:16 * c + 16], "bmt": bmt, "bmt0": bmt0, "bmsc": bmsc, "bmsn": bmsn})
        in_maps.append({k: np.ascontiguousarray(v) for k, v in m.items()})
    res = run_bass_kernel_spmd(nc, in_maps, core_ids=list(range(8)))
    R = res.results
    _CACHE["last"] = R
    y_prompt = np.zeros((2, 4096, D), np.float32)
    y_sample = np.zeros((128, 4, D), np.float32)
    p_re = np.zeros((1, 2, 64, 64), np.float32)
    p_im = np.zeros((1, 2, 64, 64), np.float32)
    p_k = np.zeros((1, 2, 128, 4, 64), np.float32)
    p_v = np.zeros((1, 2, 128, 4, 64), np.float32)
    s_re = np.zeros((1, 128, 64, 64), np.float32)
    s_im = np.zeros((1, 128, 64, 64), np.float32)
    s_k = np.zeros((1, 128, 128, 4, 64), np.float32)
    s_v = np.zeros((1, 128, 128, 4, 64), np.float32)
    for c in range(8):
        b, q = c // 4, c % 4
        r = R[c]
        y_prompt[b, 1024 * q:1024 * (q + 1)] = r["yo"][:1024]
        y_sample[16 * c:16 * c + 16] = r["yo"][1024:].reshape(16, 4, D)
        if q == 3:
            p_re[0, b] = r["pst"][0]
            p_im[0, b] = r["pst"][1]
            p_k[0, b] = r["pkv"][:, :256].reshape(128, 4, 64)
            p_v[0, b] = r["pkv"][:, 256:].reshape(128, 4, 64)
        s_re[0, 16 * c:16 * c + 16] = r["sst"][0]
        s_im[0, 16 * c:16 * c + 16] = r["sst"][1]
        s_k[0, 16 * c:16 * c + 16] = r["skk"].reshape(16, 128, 4, 64)
        s_v[0, 16 * c:16 * c + 16] = r["skv"].reshape(16, 128, 4, 64)
    return (y_prompt, y_sample, p_re, p_im, p_k, p_v, s_re, s_im, s_k, s_v)
```

```python
import numpy as np
import concourse.bass as bass
import concourse.mybir as mybir
from concourse.bass_utils import run_bass_kernel_spmd

F32 = mybir.dt.float32
BF16 = mybir.dt.bfloat16
I32 = mybir.dt.int32
AF = mybir.ActivationFunctionType
ALU = mybir.AluOpType
AX = mybir.AxisListType

D = 2048
NT = 1088
NPRE = 3072
NCH = 16
HID = 5632
INC = 6656
EPS = 1e-6
NEG = -1e30
BLKS = [(0, 512), (512, 512), (1024, 64)]
NDS = 24
ARENA_BYTES = 206 * 1024
DEBUG = False
STOP = None
ASTOP = None
DT_SIZE = {F32: 4, BF16: 2, I32: 4}


class Buf:
    __slots__ = ("name", "w", "r")

    def __init__(self, name="", guards=None):
        self.name = name
        self.w = None
        self.r = dict(guards) if guards else {}


def _merge(dst, src):
    for k, ev in src.items():
        if k not in dst or dst[k][1] < ev[1]:
            dst[k] = ev


class Tile:
    __slots__ = ("ap", "b", "off", "nbytes", "name")


class Arena:
    def __init__(self, nc):
        self.t = nc.alloc_sbuf_tensor("arena", [128, ARENA_BYTES // 4], F32)
        self.free_list = [[0, ARENA_BYTES, {}]]

    def alloc(self, name, shape, dt, nbufs=1, top=False):
        n = int(np.prod(shape)) * DT_SIZE[dt]
        n = (n + 63) // 64 * 64
        order = range(len(self.free_list) - 1, -1, -1) if top else range(len(self.free_list))
        for i in order:
            off, size, g = self.free_list[i]
            if size >= n:
                if size == n:
                    self.free_list.pop(i)
                elif top:
                    self.free_list[i] = [off, size - n, dict(g)]
                    off = off + size - n
                else:
                    self.free_list[i] = [off + n, size - n, dict(g)]
                t = Tile()
                t.name, t.off, t.nbytes = name, off, n
                v = self.t[:, off // 4:(off + n) // 4]
                if dt != F32:
                    v = v.bitcast(dt)
                ne = int(np.prod(shape))
                v = v[:, 0:ne]
                if len(shape) == 2:
                    v = v.rearrange("p (a b) -> p a b", a=shape[0])
                elif len(shape) == 3:
                    v = v.rearrange("p (a b c) -> p a b c", a=shape[0], b=shape[1])
                elif len(shape) == 4:
                    v = v.rearrange("p (a b c d) -> p a b c d", a=shape[0], b=shape[1], c=shape[2])
                t.ap = v
                if nbufs == 1:
                    t.b = Buf(name, g)
                else:
                    t.b = [Buf(f"{name}{j}", g) for j in range(nbufs)]
                return t
        raise RuntimeError(f"arena OOM for {name} ({n} B); free={[(o, s) for o, s, _ in self.free_list]}")

    def free(self, t):
        g = {}
        bufs = t.b if isinstance(t.b, list) else [t.b]
        for b in bufs:
            if b.w is not None:
                _merge(g, {b.w[0].num: b.w})
            _merge(g, b.r)
        self.free_list.append([t.off, t.nbytes, g])
        self.free_list.sort(key=lambda x: x[0])
        out = []
        for blk in self.free_list:
            if out and out[-1][0] + out[-1][1] == blk[0]:
                out[-1][1] += blk[1]
                _merge(out[-1][2], blk[2])
            else:
                out.append(blk)
        self.free_list = out


class Prog:
    def __init__(self, nc):
        self.nc = nc
        self.eng = {"pe": nc.tensor, "act": nc.scalar, "dve": nc.vector, "pool": nc.gpsimd, "sp": nc.sync}
        self.csem = {e: nc.alloc_semaphore(name=f"c_{e}") for e in self.eng}
        self.ccnt = {e: 0 for e in self.eng}
        self.dsems = [nc.alloc_semaphore(name=f"d_{i}") for i in range(NDS)]
        self.dcnt = [0] * NDS
        self.dnext = 0
        self.waited = {e: {} for e in self.eng}
        self.ninstr = 0

    def _wait(self, e, ev):
        sem, val = ev
        k = sem.num
        if self.waited[e].get(k, 0) >= val:
            return
        self.eng[e].wait_ge(sem, val)
        self.waited[e][k] = val

    def _deps(self, e, reads, writes, skip=None, relax=False):
        own = self.csem[e].num
        lim = self.ccnt[e] - 1

        def need(ev):
            if ev[0].num == skip:
                return False
            if relax and ev[0].num == own and ev[1] <= lim:
                return False
            return True
        for b in reads:
            if b.w is not None and need(b.w):
                self._wait(e, b.w)
        for b in writes:
            if b.w is not None and need(b.w):
                self._wait(e, b.w)
            for k, ev in b.r.items():
                if need(ev):
                    self._wait(e, ev)

    @staticmethod
    def _mark(ev, reads, writes):
        for b in reads:
            b.r[ev[0].num] = ev
        for b in writes:
            b.w = ev
            b.r = {}

    def op(self, e, fn, reads=(), writes=(), relax=False):
        self._deps(e, reads, writes, relax=relax)
        ins = fn(self.eng[e])
        self.ccnt[e] += 1
        self.ninstr += 1
        ins.then_inc(self.csem[e], 1)
        self._mark((self.csem[e], self.ccnt[e]), reads, writes)

    def group(self, e, fns, reads=(), writes=()):
        self._deps(e, reads, writes)
        ins = None
        for fn in fns:
            ins = fn(self.eng[e])
            self.ninstr += 1
        self.ccnt[e] += 1
        ins.then_inc(self.csem[e], 1)
        self._mark((self.csem[e], self.ccnt[e]), reads, writes)

    def pe(self, fns, reads=(), writes=()):
        self._deps("pe", reads, writes, skip=self.csem["pe"].num)
        ins = None
        for fn in fns:
            ins = fn(self.nc.tensor)
            self.ninstr += 1
        self.ccnt["pe"] += 1
        ins.then_inc(self.csem["pe"], 1)
        self._mark((self.csem["pe"], self.ccnt["pe"]), reads, writes)

    def dma(self, e, out, in_, reads=(), writes=(), **kw):
        i = self.dnext
        self.dnext = (i + 1) % NDS
        if self.dcnt[i] > 0:
            self._wait(e, (self.dsems[i], self.dcnt[i]))
        self._deps(e, reads, writes)
        ins = self.eng[e].dma_start(out=out, in_=in_, **kw)
        self.ninstr += 1
        self.dcnt[i] += 16
        ins.then_inc(self.dsems[i], 16)
        self._mark((self.dsems[i], self.dcnt[i]), reads, writes)

    def finish(self):
        for e in self.eng:
            if e != "sp" and self.ccnt[e] > 0:
                self._wait("sp", (self.csem[e], self.ccnt[e]))
        for i in range(NDS):
            if self.dcnt[i] > 0:
                self._wait("sp", (self.dsems[i], self.dcnt[i]))


def t5_bucket(dist):
    n = np.maximum(dist, 0)
    max_exact = 16
    large = max_exact + (np.log(np.maximum(n, 1) / max_exact) / np.log(128 / max_exact) * 16).astype(np.int32)
    large = np.minimum(large, 31)
    return np.where(n < max_exact, n, large).astype(np.int32)


class Ctx:
    pass


def share(t, buf):
    if isinstance(t.b, Buf) and t.b is not buf:
        _merge(buf.r, t.b.r)
    t.b = buf


def bl(b):
    return b if isinstance(b, list) else [b]


def range_reduce_sin(K, ang, bang, out, bout, shape, part=128):
    P, A = K.P, K.A
    ni = A.alloc("rr_i", shape, I32)
    nf = A.alloc("rr_f", shape, F32)
    C1 = 6.28125
    C2 = float(2 * np.pi - 6.28125)
    P.op("dve", lambda e: e.tensor_scalar(out=nf.ap, in0=ang, scalar1=float(1.0 / (2 * np.pi)), scalar2=None, op0=ALU.mult),
         reads=[bang], writes=[nf.b])
    P.op("dve", lambda e: e.tensor_copy(out=ni.ap, in_=nf.ap), reads=[nf.b], writes=[ni.b])
    P.op("dve", lambda e: e.tensor_copy(out=nf.ap, in_=ni.ap), reads=[ni.b], writes=[nf.b])
    P.op("dve", lambda e: e.scalar_tensor_tensor(out=ang, in0=nf.ap, scalar=-C1, in1=ang, op0=ALU.mult, op1=ALU.add),
         reads=[nf.b, bang], writes=[bang])
    P.op("dve", lambda e: e.scalar_tensor_tensor(out=ang, in0=nf.ap, scalar=-C2, in1=ang, op0=ALU.mult, op1=ALU.add),
         reads=[nf.b, bang], writes=[bang])
    P.op("dve", lambda e: e.tensor_scalar(out=ang, in0=ang, scalar1=3.1415925, scalar2=-3.1415925, op0=ALU.min, op1=ALU.max),
         reads=[bang], writes=[bang])
    P.op("act", lambda e: e.activation(out=out, in_=ang, func=AF.Sin), reads=[bang], writes=[bout])
    A.free(ni)
    A.free(nf)


def accurate_exp(K, x, bx, shape):
    P, A = K.P, K.A
    y = A.alloc("aexp_y", shape, F32)
    acc = A.alloc("aexp_a", shape, F32)
    P.op("dve", lambda e: e.tensor_scalar(out=y.ap, in0=x, scalar1=0.125, scalar2=None, op0=ALU.mult), reads=[bx], writes=[y.b])
    fact = [1.0]
    for i in range(1, 12):
        fact.append(fact[-1] * i)
    P.op("dve", lambda e: e.tensor_scalar(out=acc.ap, in0=y.ap, scalar1=1.0 / fact[11], scalar2=1.0 / fact[10], op0=ALU.mult, op1=ALU.add),
         reads=[y.b], writes=[acc.b])
    for i in range(9, -1, -1):
        P.op("dve", lambda e: e.tensor_tensor(out=acc.ap, in0=acc.ap, in1=y.ap, op=ALU.mult), reads=[acc.b, y.b], writes=[acc.b])
        P.op("dve", lambda e, i=i: e.tensor_scalar(out=acc.ap, in0=acc.ap, scalar1=float(1.0 / fact[i]), scalar2=None, op0=ALU.add),
             reads=[acc.b], writes=[acc.b])
    for _ in range(3):
        P.op("dve", lambda e: e.tensor_tensor(out=acc.ap, in0=acc.ap, in1=acc.ap, op=ALU.mult), reads=[acc.b], writes=[acc.b])
    P.op("dve", lambda e: e.tensor_copy(out=x, in_=acc.ap), reads=[acc.b, bx], writes=[bx])
    A.free(y)
    A.free(acc)


def ssm_tables(K):
    nc, P, I, A = K.nc, K.P, K.I, K.A
    PSB = K.psb
    lrG = A.alloc("lrG", [32], F32)
    liG = A.alloc("liG", [32], F32)
    dtG = A.alloc("dtG", [32], F32)
    bG = Buf("G")
    with nc.allow_non_contiguous_dma(reason="tiny transposed param loads"):
        for gh in range(2):
            rows = slice(64 * gh, 64 * gh + 64)
            P.dma("sp", lrG.ap[rows, :], I["lam_re"][32 * gh:32 * gh + 32, :].rearrange("g p -> p g"), writes=[bG])
            P.dma("sp", liG.ap[rows, :], I["lam_im"][32 * gh:32 * gh + 32, :].rearrange("g p -> p g"), writes=[bG])
            P.dma("sp", dtG.ap[rows, :], I["log_dt"][32 * gh:32 * gh + 32].unsqueeze(0).broadcast_to([64, 32]), writes=[bG])
    accurate_exp(K, dtG.ap, bG, [32])
    P.op("dve", lambda e: e.tensor_scalar(out=lrG.ap, in0=lrG.ap, scalar1=-1e-4, scalar2=None, op0=ALU.min), reads=[bG], writes=[bG])
    rho = A.alloc("rhoG", [32], F32)
    th = A.alloc("thG", [32], F32)
    P.op("dve", lambda e: e.tensor_tensor(out=rho.ap, in0=lrG.ap, in1=dtG.ap, op=ALU.mult), reads=[bG], writes=[bG])
    P.op("dve", lambda e: e.tensor_tensor(out=th.ap, in0=liG.ap, in1=dtG.ap, op=ALU.mult), reads=[bG], writes=[bG])
    tmps = [lrG, liG, dtG, rho, th]

    def TTv(o, a, b_, op, rd, wb):
        P.op("dve", lambda e: e.tensor_tensor(out=o, in0=a, in1=b_, op=op), reads=rd, writes=[wb])

    def TSv(o, a, s1, s2, op0, op1, rd, wb):
        if op1 is None:
            P.op("dve", lambda e: e.tensor_scalar(out=o, in0=a, scalar1=s1, scalar2=None, op0=op0), reads=rd, writes=[wb])
        else:
            P.op("dve", lambda e: e.tensor_scalar(out=o, in0=a, scalar1=s1, scalar2=s2, op0=op0, op1=op1), reads=rd, writes=[wb])

    b1 = Buf("a1")
    nm = lambda n: A.alloc(n, [32], F32)
    r_, x2, sn, cs, t_, ex, a1r, a1i = [nm(n) for n in ("r_", "x2", "sn", "cs", "t_", "ex", "a1r", "a1i")]
    ni = A.alloc("ni", [32], I32)
    for t in (r_, x2, sn, cs, t_, ex, a1r, a1i, ni):
        share(t, b1)
    tmps.extend([r_, x2, sn, cs, t_, ex, a1r, a1i, ni])
    C1 = 6.28125
    C2 = float(2 * np.pi - 6.28125)
    TSv(t_.ap, th.ap, float(1.0 / (2 * np.pi)), None, ALU.mult, None, [bG], b1)
    P.op("dve", lambda e: e.tensor_copy(out=ni.ap, in_=t_.ap), reads=[b1], writes=[b1])
    P.op("dve", lambda e: e.tensor_copy(out=t_.ap, in_=ni.ap), reads=[b1], writes=[b1])
    P.op("dve", lambda e: e.scalar_tensor_tensor(out=r_.ap, in0=t_.ap, scalar=-C1, in1=th.ap, op0=ALU.mult, op1=ALU.add), reads=[b1, bG], writes=[b1])
    P.op("dve", lambda e: e.scalar_tensor_tensor(out=r_.ap, in0=t_.ap, scalar=-C2, in1=r_.ap, op0=ALU.mult, op1=ALU.add), reads=[b1], writes=[b1])
    TSv(r_.ap, r_.ap, 0.125, None, ALU.mult, None, [b1], b1)
    TTv(x2.ap, r_.ap, r_.ap, ALU.mult, [b1], b1)
    TSv(sn.ap, x2.ap, 1.0 / 362880, -1.0 / 5040, ALU.mult, ALU.add, [b1], b1)
    for cf in (1.0 / 120, -1.0 / 6, 1.0):
        TTv(sn.ap, sn.ap, x2.ap, ALU.mult, [b1], b1)
        TSv(sn.ap, sn.ap, float(cf), None, ALU.add, None, [b1], b1)
    TTv(sn.ap, sn.ap, r_.ap, ALU.mult, [b1], b1)
    TSv(cs.ap, x2.ap, -1.0 / 3628800, 1.0 / 40320, ALU.mult, ALU.add, [b1], b1)
    for cf in (-1.0 / 720, 1.0 / 24, -0.5, 1.0):
        TTv(cs.ap, cs.ap, x2.ap, ALU.mult, [b1], b1)
        TSv(cs.ap, cs.ap, float(cf), None, ALU.add, None, [b1], b1)
    for _ in range(3):
        TTv(t_.ap, sn.ap, cs.ap, ALU.mult, [b1], b1)
        TTv(x2.ap, sn.ap, sn.ap, ALU.mult, [b1], b1)
        TSv(sn.ap, t_.ap, 2.0, None, ALU.mult, None, [b1], b1)
        TSv(cs.ap, x2.ap, -2.0, 1.0, ALU.mult, ALU.add, [b1], b1)
    TSv(ex.ap, rho.ap, 1.0 / 120, 1.0 / 24, ALU.mult, ALU.add, [bG], b1)
    for cf in (1.0 / 6, 0.5, 1.0, 1.0):
        TTv(ex.ap, ex.ap, rho.ap, ALU.mult, [b1, bG], b1)
        TSv(ex.ap, ex.ap, float(cf), None, ALU.add, None, [b1], b1)
    TTv(a1r.ap, ex.ap, cs.ap, ALU.mult, [b1], b1)
    TTv(a1i.ap, ex.ap, sn.ap, ALU.mult, [b1], b1)
    b128 = Buf("a128")
    a128r, a128i, q1, q2 = [nm(n) for n in ("a128r", "a128i", "q1", "q2")]
    for t in (a128r, a128i, q1, q2):
        share(t, b128)
    tmps.extend([a128r, a128i, q1, q2])
    P.op("dve", lambda e: e.tensor_copy(out=a128r.ap, in_=a1r.ap), reads=[b1], writes=[b128])
    P.op("dve", lambda e: e.tensor_copy(out=a128i.ap, in_=a1i.ap), reads=[b1], writes=[b128])
    K.Rm = A.alloc("Rm", [32], F32)
    K.thG = A.alloc("thGk", [32], F32)
    K.c64 = A.alloc("c64", [32], F32)
    K.s64 = A.alloc("s64", [32], F32)
    r64 = nm("r64")
    share(r64, b128)
    tmps.append(r64)
    P.op("dve", lambda e: e.tensor_copy(out=K.Rm.ap, in_=ex.ap), reads=[b1], writes=[K.Rm.b])
    P.op("dve", lambda e: e.tensor_copy(out=K.thG.ap, in_=th.ap), reads=[bG], writes=[K.thG.b])
    P.op("dve", lambda e: e.tensor_copy(out=r64.ap, in_=ex.ap), reads=[b1], writes=[b128])
    for _ in range(6):
        TTv(r64.ap, r64.ap, r64.ap, ALU.mult, [b128], b128)
    P.op("dve", lambda e: e.reciprocal(out=r64.ap, in_=r64.ap), reads=[b128], writes=[b128])
    for it in range(7):
        if it == 6:
            TTv(K.c64.ap, a128r.ap, r64.ap, ALU.mult, [b128], K.c64.b)
            TTv(K.s64.ap, a128i.ap, r64.ap, ALU.mult, [b128], K.s64.b)
        TTv(q1.ap, a128r.ap, a128r.ap, ALU.mult, [b128], b128)
        TTv(q2.ap, a128i.ap, a128i.ap, ALU.mult, [b128], b128)
        TTv(a128i.ap, a128r.ap, a128i.ap, ALU.mult, [b128], b128)
        TSv(a128i.ap, a128i.ap, 2.0, None, ALU.mult, None, [b128], b128)
        TTv(a128r.ap, q1.ap, q2.ap, ALU.subtract, [b128], b128)

    def make_a4(ar, ai, b, name):
        a4 = A.alloc(name, [2, 2, 32], F32)
        P.op("dve", lambda e: e.tensor_copy(out=a4.ap[:, 0, 0, :], in_=ar.ap), reads=[b], writes=[a4.b])
        P.op("dve", lambda e: e.tensor_scalar(out=a4.ap[:, 0, 1, :], in0=ai.ap, scalar1=-1.0, scalar2=None, op0=ALU.mult), reads=[b], writes=[a4.b])
        P.op("dve", lambda e: e.tensor_copy(out=a4.ap[:, 1, 0, :], in_=ai.ap), reads=[b], writes=[a4.b])
        P.op("dve", lambda e: e.tensor_copy(out=a4.ap[:, 1, 1, :], in_=ar.ap), reads=[b], writes=[a4.b])
        return a4

    K.A4 = make_a4(a1r, a1i, b1, "A4_1")
    K.A4c = make_a4(a128r, a128i, b128, "A4_128")

    den = A.alloc("den", [32], F32)
    t1 = A.alloc("ct1", [32], F32)
    t2 = A.alloc("ct2", [32], F32)
    cr = A.alloc("coefr", [32], F32)
    ci = A.alloc("coefi", [32], F32)
    am1 = A.alloc("am1", [32], F32)
    bc = Buf("coef")
    for t in (den, t1, t2, cr, ci, am1):
        share(t, bc)
    tmps.extend([den, t1, t2, cr, ci, am1])
    TT = lambda o, a, b_, op, rd: P.op("dve", lambda e: e.tensor_tensor(out=o, in0=a, in1=b_, op=op), reads=rd, writes=[bc])
    TT(den.ap, lrG.ap, lrG.ap, ALU.mult, [bG])
    TT(t1.ap, liG.ap, liG.ap, ALU.mult, [bG])
    TT(den.ap, den.ap, t1.ap, ALU.add, [bc])
    P.op("dve", lambda e: e.reciprocal(out=den.ap, in_=den.ap), reads=[bc], writes=[bc])
    P.op("dve", lambda e: e.tensor_scalar(out=am1.ap, in0=a1r.ap, scalar1=-1.0, scalar2=None, op0=ALU.add), reads=[b1], writes=[bc])
    TT(t1.ap, am1.ap, lrG.ap, ALU.mult, [bc, bG])
    TT(t2.ap, a1i.ap, liG.ap, ALU.mult, [b1, bG])
    TT(t1.ap, t1.ap, t2.ap, ALU.add, [bc])
    TT(cr.ap, t1.ap, den.ap, ALU.mult, [bc])
    TT(t1.ap, a1i.ap, lrG.ap, ALU.mult, [bc, b1, bG])
    TT(t2.ap, am1.ap, liG.ap, ALU.mult, [bc, bG])
    TT(t1.ap, t1.ap, t2.ap, ALU.subtract, [bc])
    TT(ci.ap, t1.ap, den.ap, ALU.mult, [bc])

    K.BbR = A.alloc("BbR", [32, 16], F32)
    K.BbI = A.alloc("BbI", [32, 16], F32)
    bBb = Buf("Bbar")
    share(K.BbR, bBb)
    share(K.BbI, bBb)
    BreG = A.alloc("BreG", [32, 16], F32)
    BimG = A.alloc("BimG", [32, 16], F32)
    tb1 = A.alloc("tb1", [32, 16], F32)
    bB = Buf("B")
    share(BreG, bB)
    share(BimG, bB)
    with nc.allow_non_contiguous_dma(reason="64B runs param load"):
        for gh in range(2):
            rows = slice(64 * gh, 64 * gh + 64)
            P.dma("sp", BreG.ap[rows], I["b_re"][32 * gh:32 * gh + 32].rearrange("g p c -> p g c"), writes=[bB])
            P.dma("sp", BimG.ap[rows], I["b_im"][32 * gh:32 * gh + 32].rearrange("g p c -> p g c"), writes=[bB])
    crb = cr.ap.unsqueeze(2).broadcast_to([128, 32, 16])
    cib = ci.ap.unsqueeze(2).broadcast_to([128, 32, 16])
    P.op("dve", lambda e: e.tensor_tensor(out=K.BbR.ap, in0=BreG.ap, in1=crb, op=ALU.mult), reads=[bB, bc], writes=[bBb])
    P.op("dve", lambda e: e.tensor_tensor(out=tb1.ap, in0=BimG.ap, in1=cib, op=ALU.mult), reads=[bB, bc], writes=[tb1.b])
    P.op("dve", lambda e: e.tensor_tensor(out=K.BbR.ap, in0=K.BbR.ap, in1=tb1.ap, op=ALU.subtract), reads=[tb1.b, bBb], writes=[bBb])
    P.op("dve", lambda e: e.tensor_tensor(out=K.BbI.ap, in0=BimG.ap, in1=crb, op=ALU.mult), reads=[bB, bc], writes=[bBb])
    P.op("dve", lambda e: e.tensor_tensor(out=tb1.ap, in0=BreG.ap, in1=cib, op=ALU.mult), reads=[bB, bc, bBb], writes=[tb1.b])
    P.op("dve", lambda e: e.tensor_tensor(out=K.BbI.ap, in0=K.BbI.ap, in1=tb1.ap, op=ALU.add), reads=[tb1.b, bBb], writes=[bBb])
    share(BreG, bB)
    share(BimG, bB)

    K.Dcol = A.alloc("Dcol", [8], F32)
    with nc.allow_non_contiguous_dma(reason="tiny"):
        P.dma("sp", K.Dcol.ap, I["d_skip"].rearrange("(j p) -> p j", p=128), writes=[K.Dcol.b])

    K.ApR = A.alloc("ApR", [64, 64], BF16)
    K.ApI = A.alloc("ApI", [64, 64], BF16)
    bAp = Buf("ApowT")
    share(K.ApR, bAp)
    share(K.ApI, bAp)
    kcol = A.alloc("kcol", [1], F32)
    dtF = A.alloc("dtF", [64], F32)
    bF = Buf("F")
    share(kcol, bF)
    share(dtF, bF)
    P.dma("sp", kcol.ap, I["kcol"], writes=[bF])
    P.dma("sp", dtF.ap, I["log_dt"].unsqueeze(0).broadcast_to([128, 64]), writes=[bF])
    accurate_exp(K, dtF.ap, bF, [64])
    for hf in range(2):
        gs = slice(32 * hf, 32 * hf + 32)
        lrF = A.alloc("lrF", [32, 64], F32)
        liF = A.alloc("liF", [32, 64], F32)
        magF = A.alloc("magF", [32, 64], F32)
        snF = A.alloc("snF", [32, 64], F32)
        bH = Buf("Fh")
        for t in (lrF, liF, magF, snF):
            share(t, bH)
        P.dma("sp", lrF.ap, I["lam_re"][gs].unsqueeze(0).broadcast_to([128, 32, 64]), writes=[bH])
        P.dma("sp", liF.ap, I["lam_im"][gs].unsqueeze(0).broadcast_to([128, 32, 64]), writes=[bH])
        dtb = dtF.ap[:, gs].unsqueeze(2).broadcast_to([128, 32, 64])
        P.op("dve", lambda e: e.tensor_scalar(out=lrF.ap, in0=lrF.ap, scalar1=-1e-4, scalar2=None, op0=ALU.min), reads=[bH], writes=[bH])
        P.op("dve", lambda e: e.tensor_tensor(out=lrF.ap, in0=lrF.ap, in1=dtb, op=ALU.mult), reads=[bH, bF], writes=[bH])
        P.op("dve", lambda e: e.tensor_tensor(out=liF.ap, in0=liF.ap, in1=dtb, op=ALU.mult), reads=[bH, bF], writes=[bH])
        P.op("act", lambda e: e.activation(out=magF.ap, in_=lrF.ap, func=AF.Exp, scale=kcol.ap), reads=[bH, bF], writes=[bH])
        P.op("dve", lambda e: e.tensor_scalar(out=lrF.ap, in0=liF.ap, scalar1=kcol.ap, scalar2=None, op0=ALU.mult), reads=[bH, bF], writes=[bH])
        range_reduce_sin(K, lrF.ap, bH, snF.ap, bH, [32, 64])
        P.op("dve", lambda e: e.tensor_tensor(out=K.ApI.ap[:, gs, :], in0=magF.ap, in1=snF.ap, op=ALU.mult), reads=[bH], writes=[bAp])
        P.op("dve", lambda e: e.tensor_scalar(out=lrF.ap, in0=liF.ap, scalar1=kcol.ap, scalar2=float(np.pi / 2), op0=ALU.mult, op1=ALU.add),
             reads=[bH, bF], writes=[bH])
        range_reduce_sin(K, lrF.ap, bH, snF.ap, bH, [32, 64])
        P.op("dve", lambda e: e.tensor_tensor(out=K.ApR.ap[:, gs, :], in0=magF.ap, in1=snF.ap, op=ALU.mult), reads=[bH], writes=[bAp])
        for t in (lrF, liF, magF, snF):
            A.free(t)
    for t in (kcol, dtF, BreG, BimG, tb1):
        A.free(t)
    for t in tmps:
        A.free(t)


def ssm_tables_own(K):
    nc, P, I, A = K.nc, K.P, K.I, K.A
    PSB = K.psb
    bBb = K.BbR.b
    m8 = A.alloc("m8", [8], F32)
    m88 = A.alloc("m88", [8, 8], F32)
    bm = Buf("masks")
    share(m8, bm)
    share(m88, bm)
    P.dma("sp", m8.ap, I["m8"], writes=[bm])
    P.dma("sp", m88.ap, I["m88"], writes=[bm])
    K.BtabR = A.alloc("BtabR", [64, 64], BF16)
    K.BtabI = A.alloc("BtabI", [64, 64], BF16)
    bBtab = Buf("Btab")
    share(K.BtabR, bBtab)
    share(K.BtabI, bBtab)
    K.CtabR = A.alloc("CtabR", [32, 128], BF16)
    K.CtabI = A.alloc("CtabI", [32, 128], BF16)
    bCtab = Buf("Ctab")
    share(K.CtabR, bCtab)
    share(K.CtabI, bCtab)
    cnat_r = A.alloc("cnat_r", [8, 64], F32)
    cnat_i = A.alloc("cnat_i", [8, 64], F32)
    bcn = Buf("cnat")
    share(cnat_r, bcn)
    share(cnat_i, bcn)
    P.dma("sp", cnat_r.ap, I["c_re"].rearrange("(j g) c p -> (g c) j p", j=8), writes=[bcn])
    P.dma("sp", cnat_i.ap, I["c_im"].rearrange("(j g) c p -> (g c) j p", j=8), writes=[bcn])
    cnt = 0
    for src, dst in ((K.BbR, K.BtabR), (K.BbI, K.BtabI)):
        for j in range(8):
            gh, gq = j // 4, (j % 4) * 8
            pb = cnt % 2
            cnt += 1
            rows = slice(64 * gh, 64 * gh + 64)
            inp = src.ap[rows, gq:gq + 8, :]
            pt = K.ps[:, pb, 0:64]
            P.pe([lambda e, inp=inp, pt=pt, rows=rows: e.transpose(out=pt, in_=inp, identity=K.identf.ap[rows, rows])],
                 reads=[bBb, K.bC], writes=[PSB[pb]])
            P.op("dve", lambda e, pt=pt, dst=dst, j=j: e.tensor_tensor(
                out=dst.ap[:, 8 * j:8 * j + 8, :], in0=pt.unsqueeze(1).broadcast_to([128, 8, 64]),
                in1=m8.ap.unsqueeze(2).broadcast_to([128, 8, 64]), op=ALU.mult), reads=[PSB[pb], bm], writes=[bBtab])
    P.op("dve", lambda e: e.tensor_scalar(out=cnat_i.ap, in0=cnat_i.ap, scalar1=-1.0, scalar2=None, op0=ALU.mult), reads=[bcn], writes=[bcn])
    for src, dst in ((cnat_r, K.CtabR), (cnat_i, K.CtabI)):
        for j in range(8):
            gh, gq = j // 4, (j % 4) * 8
            pb = cnt % 2
            cnt += 1
            rows = slice(64 * gh, 64 * gh + 64)
            pt = K.ps[rows, pb, 0:128]
            P.pe([lambda e, src=src, j=j, pt=pt, gh=gh: e.matmul(pt, lhsT=src.ap[:, j, :], rhs=K.identf.ap, start=True, stop=True,
                                                                 tile_position=(0, 64 * gh))],
                 reads=[bcn, K.bC], writes=[PSB[pb]])
            P.op("dve", lambda e, pt=pt, dst=dst, rows=rows, gq=gq: e.tensor_tensor(
                out=dst.ap[rows, gq:gq + 8, :].rearrange("p g (h c) -> p g h c", h=8),
                in0=pt.rearrange("p (g c) -> p g c", g=8).unsqueeze(2).broadcast_to([64, 8, 8, 16]),
                in1=m88.ap[rows].unsqueeze(3).broadcast_to([64, 8, 8, 16]), op=ALU.mult),
                reads=[PSB[pb], bm], writes=[bCtab])
    for t in (cnat_r, cnat_i, m8, m88):
        A.free(t)
    K.CT = A.alloc("CT", [32, 64], BF16)
    K.ST = A.alloc("ST", [32, 64], BF16)
    trow = A.alloc("trow", [64], F32)
    ang = A.alloc("angT", [32, 64], F32)
    sct = A.alloc("sct", [32, 64], F32)
    P.dma("sp", trow.ap, I["trow"], writes=[trow.b])
    thb = K.thG.ap.unsqueeze(2).broadcast_to([128, 32, 64])
    trb = trow.ap.unsqueeze(1).broadcast_to([128, 32, 64])
    P.op("dve", lambda e: e.tensor_tensor(out=ang.ap, in0=thb, in1=trb, op=ALU.mult), reads=[K.thG.b, trow.b], writes=[ang.b])
    range_reduce_sin(K, ang.ap, ang.b, sct.ap, sct.b, [32, 64])
    P.op("dve", lambda e: e.tensor_copy(out=K.ST.ap, in_=sct.ap), reads=[sct.b], writes=[K.ST.b])
    P.op("dve", lambda e: e.tensor_tensor(out=ang.ap, in0=thb, in1=trb, op=ALU.mult), reads=[K.thG.b, trow.b, ang.b], writes=[ang.b])
    P.op("dve", lambda e: e.tensor_scalar(out=ang.ap, in0=ang.ap, scalar1=float(np.pi / 2), scalar2=None, op0=ALU.add), reads=[ang.b], writes=[ang.b])
    range_reduce_sin(K, ang.ap, ang.b, sct.ap, sct.b, [32, 64])
    P.op("dve", lambda e: e.tensor_copy(out=K.CT.ap, in_=sct.ap), reads=[sct.b], writes=[K.CT.b])
    for t in (trow, ang, sct):
        A.free(t)


def load_weights_resident(K, name, src, kc, ncols):
    t = K.A.alloc(name, [kc, ncols], BF16)
    for c0 in range(0, ncols, 512):
        w = min(512, ncols - c0)
        K.P.dma("pool", t.ap[:, :, c0:c0 + w], src[:, c0:c0 + w].rearrange("(c p) n -> p c n", p=128), writes=[t.b])
    return t


def norm_rows(K, xt, n, gb, xs, junk, ss):
    P = K.P
    P.op("act", lambda e: e.activation(out=junk.ap[:n], in_=xt.ap[:n], func=AF.Square, accum_out=ss.ap[:n]),
         reads=[xt.b], writes=[junk.b, ss.b])
    P.op("act", lambda e: e.activation(out=ss.ap[:n], in_=ss.ap[:n], func=AF.Sqrt, bias=K.epsc.ap[:n], scale=1.0 / D),
         reads=[ss.b, K.bC], writes=[ss.b])
    P.op("dve", lambda e: e.reciprocal(out=ss.ap[:n], in_=ss.ap[:n]), reads=[ss.b], writes=[ss.b])
    P.op("dve", lambda e: e.scalar_tensor_tensor(out=xs.ap[:n], in0=xt.ap[:n], scalar=ss.ap[:n], in1=gb.ap[:n],
                                                 op0=ALU.mult, op1=ALU.mult), reads=[xt.b, ss.b, gb.b], writes=[xs.b])


def transpose_rows(K, xs, n, dst_ap, dst_bufs, pbank):
    P = K.P
    ptb = K.ps[:, pbank:pbank + 2, :].rearrange("p a b -> p (a b)").bitcast(BF16)
    ptv = ptb.rearrange("p (c n) -> p c n", c=16)
    pbufs = [K.psb[pbank], K.psb[pbank + 1]]
    P.pe([lambda e, c=c: e.transpose(out=ptv[:, c, 0:n], in_=xs.ap[:n, c * 128:(c + 1) * 128], identity=K.identb.ap[:n, :n])
          for c in range(16)], reads=[xs.b, K.bC], writes=pbufs)
    P.op("act", lambda e: e.activation(out=dst_ap, in_=ptv[:, :, 0:n], func=AF.Copy), reads=pbufs, writes=dst_bufs)


def prefix_phase(K):
    nc, P, I, A = K.nc, K.P, K.I, K.A
    PSB = K.psb
    Wu = load_weights_resident(K, "Wu", I["w_in"][:, 0:1024], 16, 1024)
    K.gb = A.alloc("gb", [D], F32)
    P.dma("sp", K.gb.ap, I["norm_attn"].unsqueeze(0).broadcast_to([128, D]), writes=[K.gb.b])
    xts = [A.alloc(f"xt{i}", [D], F32) for i in range(2)]
    K.junk = A.alloc("junk", [D], BF16)
    K.ss = A.alloc("ss", [1], F32)
    xss = [A.alloc(f"xs{i}", [D], BF16) for i in range(2)]
    hTt = [A.alloc(f"hTt{i}", [16, 128], BF16) for i in range(2)]
    U = [A.alloc(f"U{i}", [1024], BF16) for i in range(2)]
    pr1 = A.alloc("pr1", [2, 32, 16], F32)
    pr2 = A.alloc("pr2", [2, 32, 16], F32)
    Tt = A.alloc("Tt", [2, 32, 16], F32)
    S = A.alloc("S", [2, 32], F32)
    Mm = A.alloc("Mm", [2, 2, 32], F32)
    K.Hc = A.alloc("Hc", [2, 32], F32)
    H = K.Hc
    P.op("dve", lambda e: e.memset(H.ap, 0.0), writes=[H.b])
    BbRb = K.BbR.ap.unsqueeze(1).broadcast_to([128, 2, 32, 16])
    BbIb = K.BbI.ap.unsqueeze(1).broadcast_to([128, 2, 32, 16])
    ntile = NPRE // 128

    def stA(i):
        xt, xs, ht = xts[i % 2], xss[i % 2], hTt[i % 2]
        P.dma("sp", xt.ap, I["xp"][i * 128:(i + 1) * 128, :], writes=[xt.b])
        norm_rows(K, xt, 128, K.gb, xs, K.junk, K.ss)
        transpose_rows(K, xs, 128, ht.ap, [ht.b], 0)

    def stB(i):
        ht, u = hTt[i % 2], U[i % 2]
        for half in range(2):
            pu = K.ps[:, 2 + half, :]
            P.pe([lambda e, c=c, pu=pu, half=half: e.matmul(pu, lhsT=ht.ap[:, c, :], rhs=Wu.ap[:, c, half * 512:(half + 1) * 512],
                                                           start=(c == 0), stop=(c == 15)) for c in range(16)],
                 reads=[ht.b, Wu.b], writes=[PSB[2 + half]])
        P.op("act", lambda e, u=u: e.activation(out=u.ap, in_=K.ps[:, 2:4, :].rearrange("p a b -> p (a b)"), func=AF.Copy),
             reads=[PSB[2], PSB[3]], writes=[u.b])

    zp = K.ps[:, 4:6, :].rearrange("p a (g c) -> p a g c", c=16)

    def stC(i):
        u = U[i % 2]
        fns = []
        for g in range(64):
            gh, gq = g // 32, g % 32
            rows = slice(64 * gh, 64 * gh + 64)
            for ri, tab in ((0, K.ApR), (1, K.ApI)):
                fns.append(lambda e, g=g, gh=gh, gq=gq, rows=rows, ri=ri, tab=tab, u=u: e.matmul(
                    zp[rows, ri, gq, :], lhsT=tab.ap[:, g, :], rhs=u.ap[:, g * 16:(g + 1) * 16], start=True, stop=True,
                    tile_position=(0, 64 * gh)))
        P.pe(fns, reads=[u.b, K.ApR.b], writes=[PSB[4], PSB[5]])

    def stD(i):
        P.op("dve", lambda e: e.tensor_tensor(out=pr1.ap, in0=zp, in1=BbRb, op=ALU.mult), reads=[PSB[4], PSB[5], K.BbR.b], writes=[pr1.b])
        P.op("dve", lambda e: e.tensor_tensor(out=pr2.ap, in0=zp, in1=BbIb, op=ALU.mult), reads=[PSB[4], PSB[5], K.BbR.b], writes=[pr2.b])
        P.op("dve", lambda e: e.tensor_tensor(out=Tt.ap[:, 0], in0=pr1.ap[:, 0], in1=pr2.ap[:, 1], op=ALU.subtract),
             reads=[pr1.b, pr2.b], writes=[Tt.b])
        P.op("dve", lambda e: e.tensor_tensor(out=Tt.ap[:, 1], in0=pr1.ap[:, 1], in1=pr2.ap[:, 0], op=ALU.add),
             reads=[pr1.b, pr2.b], writes=[Tt.b])
        P.op("dve", lambda e: e.tensor_reduce(out=S.ap, in_=Tt.ap, axis=AX.X, op=ALU.add), reads=[Tt.b], writes=[S.b])
        P.op("dve", lambda e: e.tensor_tensor(out=Mm.ap, in0=K.A4c.ap, in1=H.ap.unsqueeze(1).broadcast_to([128, 2, 2, 32]), op=ALU.mult),
             reads=[K.A4c.b, H.b], writes=[Mm.b])
        P.op("dve", lambda e: e.tensor_tensor(out=H.ap, in0=Mm.ap[:, :, 0, :], in1=Mm.ap[:, :, 1, :], op=ALU.add), reads=[Mm.b], writes=[H.b])
        P.op("dve", lambda e: e.tensor_tensor(out=H.ap, in0=H.ap, in1=S.ap, op=ALU.add), reads=[H.b, S.b], writes=[H.b])

    stA(0)
    for i in range(ntile):
        stB(i)
        stC(i)
        if i + 1 < ntile:
            stA(i + 1)
        stD(i)
    if DEBUG:
        P.dma("sp", K.O["dbg_h"], H.ap, reads=[H.b], writes=[K.bout])
    for t in [Wu] + U + [pr1, pr2, Tt, S, Mm, K.ApR, K.ApI]:
        A.free(t)
    load_wkv(K)
    ht = hTt[(ntile - 1) % 2]
    kv_tokmajor(K, ht.ap, [ht.b], 128, slice(0, 128), K.vth.ap, K.vth.b)
    kT_dup(K, ht.ap, [ht.b], 128, lambda kv: K.kTh.ap[:, kv, :], [K.kTh.b], slice(0, 128))
    A.free(K.Wkv)
    A.free(K.Wkd)
    for t in hTt:
        A.free(t)
    K.xts, K.xss = xts, xss


def kT_dup(K, hT_ap, hbufs, n, dst_ap_fn, dst_bufs, cols):
    P = K.P
    PSB = K.psb
    for kv in range(4):
        pk = K.ps[:, 7, 0:n]
        P.pe([lambda e, c=c, pk=pk, kv=kv: e.matmul(pk, lhsT=K.Wkd.ap[:, c, kv, :],
                                                   rhs=hT_ap[:, c, cols], start=(c == 0), stop=(c == 15)) for c in range(16)],
             reads=hbufs + [K.Wkd.b], writes=[PSB[7]])
        P.op("act", lambda e, kv=kv, pk=pk: e.activation(out=dst_ap_fn(kv), in_=pk, func=AF.Copy),
             reads=[PSB[7]], writes=dst_bufs)


def load_wkv(K):
    P, A = K.P, K.A
    K.Wkv = load_weights_resident(K, "Wkv", K.I["w_in"][:, 2048:2560], 16, 512)
    K.Wkd = A.alloc("Wkd", [16, 4, 128], BF16)
    for kv in range(4):
        P.op("pool", lambda e, kv=kv: e.tensor_copy(out=K.Wkd.ap[:, :, kv, :].rearrange("p c (d h) -> p c d h", d=2),
                                                    in_=K.Wkv.ap[:, :, kv * 64:(kv + 1) * 64].unsqueeze(2).broadcast_to([128, 16, 2, 64])),
             reads=[K.Wkv.b], writes=[K.Wkd.b])


def kv_tokmajor(K, hT_ap, hbufs, n, cols, vdst_ap, vdst_buf, kdst=None):
    P, PSB = K.P, K.psb
    pk = K.ps[:n, 6, :]
    P.pe([lambda e, c=c: e.matmul(pk, lhsT=hT_ap[:, c, cols], rhs=K.Wkv.ap[:, c, :], start=(c == 0), stop=(c == 15)) for c in range(16)],
         reads=hbufs + [K.Wkv.b], writes=[PSB[6]])
    P.op("act", lambda e: e.activation(out=vdst_ap, in_=K.ps[:n, 6, 256:512], func=AF.Copy), reads=[PSB[6]], writes=[vdst_buf])
    if kdst is not None:
        P.op("act", lambda e: e.activation(out=kdst.ap[:n], in_=pk, func=AF.Copy), reads=[PSB[6]], writes=[kdst.b])


class Ring:
    def __init__(self, K, nbig, nsmall=0):
        self.K = K
        self.big = [K.A.alloc(f"ringb{i}", [16, 512], BF16) for i in range(nbig)]
        self.small = [K.A.alloc(f"rings{i}", [8, 512], BF16) for i in range(nsmall)]
        self.ib = 0
        self.is_ = 0

    def load(self, src, kc, ncols):
        if kc <= 8 and self.small:
            s = self.small[self.is_ % len(self.small)]
            self.is_ += 1
        else:
            s = self.big[self.ib % len(self.big)]
            self.ib += 1
        v = s.ap[:, 0:kc, 0:ncols]
        self.K.P.dma("pool", v, src.rearrange("(c p) n -> p c n", p=128), writes=[s.b])
        return v, s.b

    def free(self):
        for s in self.big + self.small:
            self.K.A.free(s)


def acc_views(K, acc):
    return [(K.ps[:, 3 * acc:3 * acc + 2, :].rearrange("p a b -> p (a b)"), slice(0, 1024)),
            (K.ps[:, 3 * acc + 2, 0:64], slice(1024, 1088))]


def acc_bufs(K, acc):
    return [K.psb[3 * acc], K.psb[3 * acc + 1], K.psb[3 * acc + 2]]


def fm_matmul(K, wv, wb, kc, nt, act_ap, act_bufs, acc):
    fns = []
    for bi, (t0, n) in enumerate(BLKS):
        for c in range(kc):
            fns.append(lambda e, bi=bi, t0=t0, n=n, c=c: e.matmul(
                K.ps[:, 3 * acc + bi, 0:n], lhsT=wv[:, c, nt * 128:(nt + 1) * 128], rhs=act_ap[:, c, t0:t0 + n],
                start=(c == 0), stop=(c == kc - 1)))
    K.P.pe(fns, reads=[wb] + act_bufs, writes=acc_bufs(K, acc))


def own_norm(K):
    P, I, A = K.P, K.I, K.A
    for t in range(9):
        n = 128 if t < 8 else 64
        xt, xs = K.xts[t % 2], K.xss[t % 2]
        P.dma("sp", xt.ap[:n], I["xo"][t * 128:t * 128 + n, :], writes=[xt.b])
        norm_rows(K, xt, n, K.gb, xs, K.junk, K.ss)
        transpose_rows(K, xs, n, K.hT.ap[:, :, t * 128:t * 128 + n], [K.hT.b[t]], 0)
    A.free(K.gb)


def proj_phase(K):
    P, I, A = K.P, K.I, K.A
    hb = K.hT.b
    ring = Ring(K, 3)
    acc = 0
    for g in range(2):
        wv, wb = ring.load(I["w_in"][:, g * 512:(g + 1) * 512], 16, 512)
        for nt in range(4):
            fm_matmul(K, wv, wb, 16, nt, K.hT.ap, hb, acc)
            for pv, sl in acc_views(K, acc):
                P.op("act", lambda e, pv=pv, sl=sl, j=4 * g + nt: e.activation(out=K.uT.ap[:, j, sl], in_=pv, func=AF.Copy),
                     reads=acc_bufs(K, acc), writes=K.uT.b)
            acc ^= 1
    ring.free()


def ssm_own(K):
    P, I, A, O = K.P, K.I, K.A, K.O
    PSB = K.psb
    Hist = A.alloc("Hist", [2, 32, 64], F32)
    T1 = A.alloc("T1", [2, 32, 64], F32)
    H2 = A.alloc("H2", [2, 32, 64], BF16)
    HistB = A.alloc("HistB", [2, 32, 64], BF16)
    cA = A.alloc("cA", [2, 32], F32)
    cB = A.alloc("cB", [2, 32], F32)
    ysb = A.alloc("ysb", [8, 64], F32)
    g1 = A.alloc("g1", [8, 64], F32)
    g2 = A.alloc("g2", [8, 64], F32)
    zb = K.ps[:, 0:4, :].rearrange("p (r a) (g t) -> p r (a g) t", r=2, t=128)
    yps = K.ps[:, 4:6, :].rearrange("p a (j t) -> p (a j) t", t=128)
    ypb = [PSB[4], PSB[5]]
    zbb = [PSB[0], PSB[1], PSB[2], PSB[3]]
    Hc = K.Hc

    zbs = [K.ps[:, 0:2, :].rearrange("p r (g t) -> p r g t", t=64), K.ps[:, 2:4, :].rearrange("p r (g t) -> p r g t", t=64)]
    zbbs = [[PSB[0], PSB[1]], [PSB[2], PSB[3]]]

    def bu_batch(t, b4, n, c0, zsel=None):
        zb_, zbb_ = (zb, zbb) if zsel is None else (zbs[zsel], zbbs[zsel])
        fns = []
        for gh in range(2):
            rows = slice(64 * gh, 64 * gh + 64)
            for gl in range(8):
                g = 32 * gh + 8 * b4 + gl
                for ri, tab in ((0, K.BtabR), (1, K.BtabI)):
                    fns.append(lambda e, rows=rows, gl=gl, g=g, ri=ri, tab=tab, gh=gh: e.matmul(
                        zb_[rows, ri, gl, 0:n], lhsT=tab.ap[:, g, :], rhs=K.uT.ap[:, g // 8, c0:c0 + n], start=True, stop=True,
                        tile_position=(0, 64 * gh)))
        P.pe(fns, reads=[K.BtabR.b] + K.uT.b, writes=zbb_)

    def cside(n, hb_ap, hb_buf):
        for j in range(8):
            gh = j // 4
            rows = slice(64 * gh, 64 * gh + 64)
            fns = []
            for g8 in range(8):
                gq = (8 * j + g8) % 32
                for ri, tab in ((0, K.CtabR), (1, K.CtabI)):
                    fns.append(lambda e, rows=rows, gq=gq, ri=ri, tab=tab, j=j, first=(g8 == 0 and ri == 0), last=(g8 == 7 and ri == 1):
                               e.matmul(yps[:, j, 0:n], lhsT=tab.ap[rows, gq, :], rhs=hb_ap[rows, ri, gq, 0:n], start=first, stop=last))
            P.pe(fns, reads=[K.CtabR.b, hb_buf], writes=ypb)

    def gelu_out(n, c0, perm):
        yv, a, b = ysb.ap[:, :, 0:n], g1.ap[:, :, 0:n], g2.ap[:, :, 0:n]
        P.op("act", lambda e: e.activation(out=a, in_=yv, func=AF.Square), reads=[ysb.b], writes=[g1.b])
        P.op("dve", lambda e: e.tensor_scalar(out=a, in0=a, scalar1=0.044715, scalar2=1.0, op0=ALU.mult, op1=ALU.add), reads=[g1.b], writes=[g1.b])
        P.op("dve", lambda e: e.tensor_tensor(out=a, in0=a, in1=yv, op=ALU.mult), reads=[g1.b, ysb.b], writes=[g1.b])
        P.op("act", lambda e: e.activation(out=b, in_=a, func=AF.Sigmoid, scale=1.5957691216057308), reads=[g1.b], writes=[g2.b])
        P.op("dve", lambda e: e.tensor_tensor(out=K.gT.ap[:, :, c0:c0 + n], in0=b, in1=yv, op=ALU.mult), reads=[g2.b, ysb.b], writes=K.gT.b)

    CTb = K.CT.ap.rearrange("p g t -> p (g t)").unsqueeze(1).broadcast_to([128, 2, 2048])
    STf = K.ST.ap.rearrange("p g t -> p (g t)")
    Hf = Hist.ap.rearrange("p r g t -> p r (g t)")
    T1f = T1.ap.rearrange("p r g t -> p r (g t)")
    H2f = H2.ap.rearrange("p r g t -> p r (g t)")
    HBf = HistB.ap.rearrange("p r g t -> p r (g t)")
    X0 = A.alloc("X0", [2, 32, 64], BF16)
    Rz = A.alloc("Rz", [32, 64], F32)
    P.op("dve", lambda e: e.tensor_copy(out=Rz.ap, in_=K.Rm.ap.unsqueeze(2).broadcast_to([128, 32, 64])), reads=[K.Rm.b], writes=[Rz.b])
    P.op("dve", lambda e: e.memset(Rz.ap[:, :, 0:1], 0.0), reads=[Rz.b], writes=[Rz.b])
    Rzf = Rz.ap.rearrange("p g t -> p (g t)")
    ini = A.alloc("ini", [2, 32], F32)
    X0f = X0.ap.rearrange("p r g t -> p r (g t)")

    def bu_all(t):
        for b4 in range(4):
            zsel = b4 % 2
            bu_batch(t, b4, 64, t * 64, zsel)
            P.op("act", lambda e, b4=b4, zsel=zsel: e.activation(out=X0.ap[:, :, 8 * b4:8 * b4 + 8, :], in_=zbs[zsel], func=AF.Copy),
                 reads=zbbs[zsel], writes=[X0.b])

    bu_all(0)
    for t in range(16):
        c0 = t * 64
        P.op("dve", lambda e: e.tensor_tensor(out=Hf, in0=X0f, in1=CTb, op=ALU.mult), reads=[X0.b, K.CT.b], writes=[Hist.b])
        P.op("dve", lambda e: e.tensor_tensor(out=H2f[:, 0], in0=X0f[:, 1], in1=STf, op=ALU.mult), reads=[X0.b, K.ST.b], writes=[H2.b])
        P.op("dve", lambda e: e.tensor_tensor(out=H2f[:, 1], in0=X0f[:, 0], in1=STf, op=ALU.mult), reads=[X0.b, K.ST.b], writes=[H2.b])
        P.op("dve", lambda e: e.tensor_tensor(out=Hf[:, 0], in0=Hf[:, 0], in1=H2f[:, 0], op=ALU.add), reads=[Hist.b, H2.b], writes=[Hist.b])
        P.op("dve", lambda e: e.tensor_tensor(out=Hf[:, 1], in0=Hf[:, 1], in1=H2f[:, 1], op=ALU.subtract), reads=[Hist.b, H2.b], writes=[Hist.b])
        P.op("dve", lambda e: e.tensor_tensor(out=ini.ap, in0=Hc.ap, in1=K.Rm.ap.unsqueeze(1).broadcast_to([128, 2, 32]), op=ALU.mult),
             reads=[Hc.b, K.Rm.b], writes=[ini.b])
        P.op("dve", lambda e: e.tensor_tensor(out=Hist.ap[:, :, :, 0], in0=Hist.ap[:, :, :, 0], in1=ini.ap, op=ALU.add), reads=[Hist.b, ini.b], writes=[Hist.b])
        if t + 1 < 16:
            bu_all(t + 1)
        P.group("dve", [lambda e, ri=ri: e.tensor_tensor_scan(out=T1f[:, ri], data0=Rzf, data1=Hf[:, ri], initial=0.0, op0=ALU.mult, op1=ALU.add)
                        for ri in range(2)], reads=[Hist.b, Rz.b], writes=[T1.b])
        P.op("dve", lambda e: e.tensor_tensor(out=Hf, in0=T1f, in1=CTb, op=ALU.mult), reads=[T1.b, K.CT.b], writes=[Hist.b])
        P.op("dve", lambda e: e.tensor_tensor(out=H2f[:, 0], in0=T1f[:, 1], in1=STf, op=ALU.mult), reads=[T1.b, K.ST.b], writes=[H2.b])
        P.op("dve", lambda e: e.tensor_tensor(out=H2f[:, 1], in0=T1f[:, 0], in1=STf, op=ALU.mult), reads=[T1.b, K.ST.b], writes=[H2.b])
        P.op("dve", lambda e: e.tensor_tensor(out=HBf[:, 0], in0=Hf[:, 0], in1=H2f[:, 0], op=ALU.subtract), reads=[Hist.b, H2.b], writes=[HistB.b])
        P.op("dve", lambda e: e.tensor_tensor(out=HBf[:, 1], in0=Hf[:, 1], in1=H2f[:, 1], op=ALU.add), reads=[Hist.b, H2.b], writes=[HistB.b])
        P.op("dve", lambda e: e.tensor_tensor(out=cA.ap, in0=T1.ap[:, :, :, 63], in1=K.c64.ap.unsqueeze(1).broadcast_to([128, 2, 32]), op=ALU.mult),
             reads=[T1.b, K.c64.b], writes=[cA.b])
        P.op("dve", lambda e: e.tensor_tensor(out=cB.ap, in0=T1.ap[:, :, :, 63], in1=K.s64.ap.unsqueeze(1).broadcast_to([128, 2, 32]), op=ALU.mult),
             reads=[T1.b, K.s64.b], writes=[cB.b])
        P.op("dve", lambda e: e.tensor_tensor(out=Hc.ap[:, 0], in0=cA.ap[:, 0], in1=cB.ap[:, 1], op=ALU.subtract), reads=[cA.b, cB.b], writes=[Hc.b])
        P.op("dve", lambda e: e.tensor_tensor(out=Hc.ap[:, 1], in0=cA.ap[:, 1], in1=cB.ap[:, 0], op=ALU.add), reads=[cA.b, cB.b], writes=[Hc.b])
        cside(64, HistB.ap, HistB.b)
        for j in range(8):
            P.op("dve", lambda e, j=j: e.scalar_tensor_tensor(out=ysb.ap[:, j, 0:64], in0=K.uT.ap[:, j, c0:c0 + 64], scalar=K.Dcol.ap[:, j:j + 1],
                                                              in1=yps[:, j, 0:64], op0=ALU.mult, op1=ALU.add),
                 reads=K.uT.b + [K.Dcol.b] + ypb, writes=[ysb.b])
        gelu_out(64, c0, False)
    stg = A.alloc("stg", [128], F32)
    for ri in range(2):
        pt = K.ps[0:32, 6, 0:128]
        P.pe([lambda e, ri=ri: e.transpose(out=pt, in_=Hc.ap[:, ri, :], identity=K.identf.ap)], reads=[Hc.b, K.bC], writes=[PSB[6]])
        P.op("act", lambda e: e.activation(out=stg.ap[0:32, :], in_=pt, func=AF.Copy), reads=[PSB[6]], writes=[stg.b])
        P.dma("sp", O["pst"][ri].rearrange("(h g) p -> g h p", h=2), stg.ap[0:32, :].rearrange("g (h p) -> g h p", h=2), reads=[stg.b], writes=[K.bout])

    for t in [Hist, HistB, T1, H2, cA, cB, X0, Rz, ini]:
        A.free(t)
    Hs0 = A.alloc("Hs0", [2, 32, 16], F32)
    HistS = A.alloc("HistS", [2, 32, 4, 16], F32)
    HistSB = A.alloc("HistSB", [2, 32, 64], BF16)
    MmS = A.alloc("MmS", [2, 2, 32, 16], F32)
    RrS = A.alloc("RrS", [2, 32, 16], F32)
    xin = [A.alloc(f"xin{i}", [2, 64], F32) for i in range(2)]
    for ri, nm in ((0, "sre"), (1, "sim")):
        for sq in range(4):
            xi = xin[(2 * ri + sq) % 2]
            for s in range(4):
                P.dma("sp", xi.ap[32 * s:32 * s + 32], I[nm][4 * sq + s].rearrange("(h g) p -> g h p", h=2), writes=[xi.b])
            pt = K.ps[:, 6, 0:128]
            P.pe([lambda e, xi=xi: e.transpose(out=pt, in_=xi.ap.rearrange("p h q -> p (h q)"), identity=K.identf.ap)],
                 reads=[xi.b, K.bC], writes=[PSB[6]])
            P.op("act", lambda e, ri=ri, sq=sq: e.activation(out=Hs0.ap[:, ri, :, 4 * sq:4 * sq + 4].rearrange("p g s -> p s g"),
                                                            in_=pt.rearrange("p (s g) -> p s g", s=4), func=AF.Copy),
                 reads=[PSB[6]], writes=[Hs0.b])
    c0 = 1024
    for b4 in range(4):
        bu_batch(8, b4, 64, c0)
        for ri in range(2):
            P.op("act", lambda e, b4=b4, ri=ri: e.activation(out=HistS.ap[:, ri, 8 * b4:8 * b4 + 8, :, :],
                                                            in_=zb[:, ri, :, 0:64].rearrange("p g (s t) -> p g t s", t=4), func=AF.Copy),
                 reads=zbb, writes=[HistS.b])
    A4b = K.A4.ap.unsqueeze(4).broadcast_to([128, 2, 2, 32, 16])
    for tt in range(4):
        prev = Hs0.ap if tt == 0 else HistS.ap[:, :, :, tt - 1, :]
        pb = [Hs0.b] if tt == 0 else [HistS.b]
        P.op("dve", lambda e, prev=prev: e.tensor_tensor(out=MmS.ap, in0=A4b, in1=prev.unsqueeze(1).broadcast_to([128, 2, 2, 32, 16]), op=ALU.mult),
             reads=[K.A4.b] + pb, writes=[MmS.b])
        P.op("dve", lambda e: e.tensor_tensor(out=RrS.ap, in0=MmS.ap[:, :, 0], in1=MmS.ap[:, :, 1], op=ALU.add), reads=[MmS.b], writes=[RrS.b])
        P.op("dve", lambda e, tt=tt: e.tensor_tensor(out=HistS.ap[:, :, :, tt, :], in0=HistS.ap[:, :, :, tt, :], in1=RrS.ap, op=ALU.add),
             reads=[RrS.b, HistS.b], writes=[HistS.b])
    P.op("act", lambda e: e.activation(out=HistSB.ap, in_=HistS.ap.rearrange("p r g t s -> p r g (t s)"), func=AF.Copy), reads=[HistS.b], writes=[HistSB.b])
    cside(64, HistSB.ap, HistSB.b)
    for j in range(8):
        P.op("dve", lambda e, j=j: e.scalar_tensor_tensor(
            out=ysb.ap[:, j, 0:64].rearrange("p (s t) -> p s t", t=4), in0=K.uT.ap[:, j, c0:c0 + 64].rearrange("p (s t) -> p s t", t=4),
            scalar=K.Dcol.ap[:, j:j + 1], in1=yps[:, j, 0:64].rearrange("p (t s) -> p s t", t=4), op0=ALU.mult, op1=ALU.add),
            reads=K.uT.b + [K.Dcol.b] + ypb, writes=[ysb.b])
    gelu_out(64, c0, True)
    stg2 = A.alloc("stg2", [4, 32], F32)
    for ri in range(2):
        for sq in range(4):
            pt = K.ps[:, 6, 0:128]
            P.op("dve", lambda e, ri=ri, sq=sq: e.tensor_copy(out=stg2.ap, in_=HistS.ap[:, ri, :, 3, 4 * sq:4 * sq + 4].rearrange("p g s -> p s g")),
                 reads=[HistS.b], writes=[stg2.b])
            P.pe([lambda e: e.transpose(out=pt, in_=stg2.ap.rearrange("p s g -> p (s g)"), identity=K.identf.ap)],
                 reads=[stg2.b, K.bC], writes=[PSB[6]])
            P.op("act", lambda e: e.activation(out=stg.ap, in_=pt, func=AF.Copy), reads=[PSB[6]], writes=[stg.b])
            for s in range(4):
                P.dma("sp", O["sst"][ri, 4 * sq + s].rearrange("(h g) p -> g h p", h=2),
                      stg.ap[32 * s:32 * s + 32, :].rearrange("g (h p) -> g h p", h=2), reads=[stg.b], writes=[K.bout])
    for t in [ysb, g1, g2, stg, stg2, Hs0, HistS, HistSB, MmS, RrS] + xin:
        A.free(t)
    for t in [K.A4, K.A4c, K.BtabR, K.BtabI, K.CtabR, K.CtabI, K.Dcol, K.Hc, K.uT, K.CT, K.ST, K.Rm, K.thG, K.c64, K.s64]:
        A.free(t)


def attention(K):
    P, I, A, O = K.P, K.I, K.A, K.O
    PSB = K.psb
    hb = K.hT.b
    K.kTd = A.alloc("kTd", [4, 128 + NT], BF16, nbufs=10, top=True)
    K.vtok = A.alloc("vtok", [10, 256], BF16, nbufs=10, top=True)
    K.kvlast = A.alloc("kvlast", [512], F32, top=True)
    K.kvsamp = A.alloc("kvsamp", [512], F32, top=True)
    load_wkv(K)
    P.op("act", lambda e: e.activation(out=K.kTd.ap[:, :, 0:128], in_=K.kTh.ap, func=AF.Copy), reads=[K.kTh.b], writes=[K.kTd.b[0]])
    P.op("act", lambda e: e.activation(out=K.vtok.ap[:, 0, :], in_=K.vth.ap, func=AF.Copy), reads=[K.vth.b], writes=[K.vtok.b[0]])
    for t in range(9):
        n = 128 if t < 8 else 64
        kdst = K.kvlast if t == 7 else (K.kvsamp if t == 8 else None)
        kv_tokmajor(K, K.hT.ap, [hb[t]], n, slice(t * 128, t * 128 + n), K.vtok.ap[:n, t + 1, :], K.vtok.b[t + 1], kdst)
    for bi, (t0, n) in enumerate(BLKS):
        tiles = list(range(t0 // 128, (t0 + n + 127) // 128))
        kT_dup(K, K.hT.ap, [hb[t] for t in tiles], n, lambda kv, t0=t0, n=n: K.kTd.ap[:, kv, 128 + t0:128 + t0 + n],
               [K.kTd.b[t + 1] for t in tiles], slice(t0, t0 + n))
    A.free(K.Wkv)
    A.free(K.Wkd)
    A.free(K.kTh)
    A.free(K.vth)
    K.qT = A.alloc("qT", [8, NT], BF16, nbufs=1, top=True)
    ring = Ring(K, 2)
    acc = 0
    for g in range(2):
        wv, wb = ring.load(I["w_in"][:, 1024 + g * 512:1024 + (g + 1) * 512], 16, 512)
        for nt in range(4):
            fm_matmul(K, wv, wb, 16, nt, K.hT.ap, hb, acc)
            for pv, sl in acc_views(K, acc):
                P.op("act", lambda e, pv=pv, sl=sl, j=4 * g + nt: e.activation(out=K.qT.ap[:, j, sl], in_=pv, func=AF.Copy),
                     reads=acc_bufs(K, acc), writes=[K.qT.b])
            acc ^= 1
    ring.free()
    if ASTOP == 'q':
        return
    bmt = A.alloc("bmt", [16, 2, 128], F32)
    bmt0 = A.alloc("bmt0", [16, 128], F32)
    bmsc = A.alloc("bmsc", [2, 8, 4], F32)
    bmsn = A.alloc("bmsn", [2, 8, 4], F32)
    esk = A.alloc("esk", [8], F32)
    P.dma("sp", bmt.ap, I["bmt"], writes=[bmt.b])
    P.dma("sp", bmt0.ap, I["bmt0"], writes=[bmt0.b])
    P.dma("sp", bmsc.ap, I["bmsc"].rearrange("k (j a) t -> k j a t", j=2), writes=[bmsc.b])
    P.dma("sp", bmsn.ap[0:4], I["bmsn"].rearrange("k (j a) t -> k j a t", j=2), writes=[bmsn.b])
    with K.nc.allow_non_contiguous_dma(reason="tiny"):
        for j in range(2):
            P.dma("sp", esk.ap[64 * j:64 * j + 64, :], I["sinks"].rearrange("(p j) -> j p", j=2)[j:j + 1, :].broadcast_to([64, 8]), writes=[esk.b])
    P.op("act", lambda e: e.activation(out=esk.ap, in_=esk.ap, func=AF.Exp), reads=[esk.b], writes=[esk.b])
    if ASTOP == 'tabs':
        return
    ee = [A.alloc(f"ee{i}", [2, 2, 2, 128], F32) for i in range(2)]
    pT = [A.alloc(f"pT{i}", [2, 2, 2, 128], BF16) for i in range(2)]
    dn = A.alloc("dn", [2, 128], F32)
    spsv = K.ps[:, 0:2, :].rearrange("p j (a k q) -> p j a k q", a=2, k=2)
    spb = [PSB[0], PSB[1]]
    ops = K.ps[:, 2, 0:256].rearrange("p (a q) -> p a q", q=128)
    dps = K.ps[:, 3, 0:256].rearrange("p (a q) -> p a q", q=128)
    it = 0
    lim = str(ASTOP).startswith('p_')
    for i in range(1 if lim else 8):
        qc = slice(i * 128, (i + 1) * 128)
        for hbk in range(1 if lim else 4):
            kv = hbk
            e_, p_ = ee[it % 2], pT[it % 2]
            it += 1
            fns = []
            for hh in range(4):
                h = 4 * hbk + hh
                pair, j = h // 2, h % 2
                rows = slice(64 * j, 64 * j + 64)
                for kb in range(2):
                    kc0 = 128 * (i + kb)
                    fns.append(lambda e, hh=hh, kb=kb, rows=rows, pair=pair, kc0=kc0, j=j: e.matmul(
                        spsv[:, j, hh // 2, kb, :], lhsT=K.kTd.ap[rows, kv, kc0:kc0 + 128], rhs=K.qT.ap[rows, pair, qc], start=True, stop=True))
            P.pe(fns, reads=[K.kTd.b[i], K.kTd.b[i + 1], K.qT.b], writes=spb)
            if i == 0:
                for j in range(2):
                    P.op("dve", lambda e, e_=e_, j=j: e.scalar_tensor_tensor(out=e_.ap[:, j, :, 0, :], in0=spsv[:, j, :, 0, :], scalar=0.125,
                                                                             in1=bmt0.ap[:, 4 * hbk + 2 * j:4 * hbk + 2 * j + 2, :], op0=ALU.mult, op1=ALU.add),
                         reads=spb + [bmt0.b], writes=[e_.b])
                    P.op("dve", lambda e, e_=e_, j=j: e.scalar_tensor_tensor(out=e_.ap[:, j, :, 1, :], in0=spsv[:, j, :, 1, :], scalar=0.125,
                                                                             in1=bmt.ap[:, 4 * hbk + 2 * j:4 * hbk + 2 * j + 2, 1, :], op0=ALU.mult, op1=ALU.add),
                         reads=spb + [bmt.b], writes=[e_.b])
            else:
                P.op("dve", lambda e, e_=e_: e.scalar_tensor_tensor(
                    out=e_.ap, in0=spsv, scalar=0.125, in1=bmt.ap[:, 4 * hbk:4 * hbk + 4, :, :],
                    op0=ALU.mult, op1=ALU.add), reads=spb + [bmt.b], writes=[e_.b])
            if ASTOP == 'p_s':
                continue
            P.op("act", lambda e, e_=e_, p_=p_: e.activation(out=p_.ap, in_=e_.ap, func=AF.Exp), reads=[e_.b], writes=[p_.b])
            if ASTOP == 'p_e':
                continue
            fns = []
            for hh in range(4):
                pp, j = hh // 2, hh % 2
                rows = slice(64 * j, 64 * j + 64)
                for kb in range(2):
                    fns.append(lambda e, hh=hh, kb=kb, pp=pp, j=j, rows=rows, p_=p_: e.matmul(
                        ops[rows, pp, :], lhsT=K.vtok.ap[:, i + kb, kv * 64:(kv + 1) * 64], rhs=p_.ap[:, j, hh // 2, kb, :],
                        start=(kb == 0), stop=(kb == 1), tile_position=(0, 64 * j)))
                for kb in range(2):
                    fns.append(lambda e, hh=hh, kb=kb, pp=pp, j=j, rows=rows, p_=p_: e.matmul(
                        dps[rows, pp, :], lhsT=K.ones64.ap, rhs=p_.ap[:, j, hh // 2, kb, :],
                        start=(kb == 0), stop=(kb == 1), tile_position=(0, 64 * j)))
            P.pe(fns, reads=[K.vtok.b[i], K.vtok.b[i + 1], p_.b, K.bC], writes=[PSB[2], PSB[3]])
            P.op("dve", lambda e: e.tensor_tensor(out=dn.ap, in0=dps, in1=esk.ap[:, 2 * hbk:2 * hbk + 2].unsqueeze(2).broadcast_to([128, 2, 128]),
                                                  op=ALU.add), reads=[PSB[3], esk.b], writes=[dn.b])
            P.op("dve", lambda e: e.reciprocal(out=dn.ap, in_=dn.ap), reads=[dn.b], writes=[dn.b])
            P.op("dve", lambda e: e.tensor_tensor(out=K.oT.ap[:, 2 * hbk:2 * hbk + 2, qc], in0=ops, in1=dn.ap, op=ALU.mult),
                 reads=[PSB[2], dn.b], writes=K.oT.b)
    if ASTOP == 'prompt' or lim:
        return
    kc_ = [A.alloc(f"kc{i}", [4, 2, 64], BF16) for i in range(2)]
    vc_ = [A.alloc(f"vc{i}", [256], BF16) for i in range(2)]
    kcT = [A.alloc(f"kcT{i}", [4, 128], BF16) for i in range(2)]
    vnew = A.alloc("vnew", [16, 256], BF16)
    eS = A.alloc("eS", [2, 8, 4], F32)
    eN = A.alloc("eN", [2, 8, 4], F32)
    pS = [A.alloc(f"pS{i}", [2, 8, 4], BF16) for i in range(2)]
    pN = [A.alloc(f"pN{i}", [2, 8, 4], BF16) for i in range(2)]
    dS = A.alloc("dS", [8, 4], F32)
    with K.nc.allow_non_contiguous_dma(reason="tiny relayout"):
        for s in range(16):
            P.dma("sp", vnew.ap[0:4, s, :], K.vtok.ap[4 * s:4 * s + 4, 9, :], reads=[K.vtok.b[9]], writes=[vnew.b])
    ptk = K.ps[:, 4, :].bitcast(BF16)[:, 0:512].rearrange("p (k n) -> p k n", k=4)
    sscb = [K.ps[:, 5 + 2 * j, 0:32].rearrange("p (h t) -> p h t", t=4) for j in range(2)]
    ssnb = [K.ps[0:4, 5 + 2 * j, 32:64].rearrange("p (h t) -> p h t", t=4) for j in range(2)]
    osp = K.ps[:, 6, 0:32].rearrange("p (a t) -> p a t", t=4)
    dsp = K.ps[:, 6, 32:64].rearrange("p (a t) -> p a t", t=4)
    for s in range(16):
        kc, vc, kt, ps_, pn_ = kc_[s % 2], vc_[s % 2], kcT[s % 2], pS[s % 2], pN[s % 2]
        P.dma("pool", kc.ap, I["ck"][s].rearrange("k (v d) -> k v d", v=4).unsqueeze(2).broadcast_to([128, 4, 2, 64]), writes=[kc.b])
        P.dma("pool", vc.ap, I["cv"][s], writes=[vc.b])
        P.dma("sp", O["skk"][s, 0:124, :], I["ck"][s, 4:128, :], writes=[K.bout])
        P.dma("sp", O["skv"][s, 0:124, :], I["cv"][s, 4:128, :], writes=[K.bout])
        P.dma("sp", O["skk"][s, 124:128, :], K.kvsamp.ap[4 * s:4 * s + 4, 0:256], reads=[K.kvsamp.b], writes=[K.bout])
        P.dma("sp", O["skv"][s, 124:128, :], K.kvsamp.ap[4 * s:4 * s + 4, 256:512], reads=[K.kvsamp.b], writes=[K.bout])
        P.pe([lambda e, kv=kv: e.transpose(out=ptk[:, kv, :], in_=kc.ap[:, kv].rearrange("k a d -> k (a d)"),
                                           identity=K.identb.ap) for kv in range(4)], reads=[kc.b, K.bC], writes=[PSB[4]])
        P.op("act", lambda e: e.activation(out=kt.ap, in_=ptk, func=AF.Copy), reads=[PSB[4]], writes=[kt.b])
        qc0 = 1024 + 4 * s
        fns = []
        for h in range(16):
            kv, pair, j = h // 4, h // 2, h % 2
            rows = slice(64 * j, 64 * j + 64)
            fns.append(lambda e, h=h, kv=kv, pair=pair, rows=rows, j=j: e.matmul(sscb[j][:, pair, :], lhsT=kt.ap[rows, kv, :], rhs=K.qT.ap[rows, pair, qc0:qc0 + 4],
                                                                                start=True, stop=True))
            fns.append(lambda e, h=h, kv=kv, pair=pair, rows=rows, j=j: e.matmul(ssnb[j][:, pair, :], lhsT=K.kTd.ap[rows, kv, 128 + qc0:128 + qc0 + 4],
                                                                                rhs=K.qT.ap[rows, pair, qc0:qc0 + 4], start=True, stop=True))
        P.pe(fns, reads=[kt.b, K.kTd.b[9], K.qT.b], writes=[PSB[5], PSB[7]])
        for j in range(2):
            P.op("dve", lambda e, j=j: e.scalar_tensor_tensor(out=eS.ap[:, j], in0=sscb[j], scalar=0.125, in1=bmsc.ap[:, j], op0=ALU.mult, op1=ALU.add),
                 reads=[PSB[5], PSB[7], bmsc.b], writes=[eS.b])
            P.op("dve", lambda e, j=j: e.scalar_tensor_tensor(out=eN.ap[0:4, j], in0=ssnb[j], scalar=0.125, in1=bmsn.ap[0:4, j], op0=ALU.mult, op1=ALU.add),
                 reads=[PSB[5], PSB[7], bmsn.b], writes=[eN.b])
        P.op("act", lambda e, ps_=ps_: e.activation(out=ps_.ap, in_=eS.ap, func=AF.Exp), reads=[eS.b], writes=[ps_.b])
        P.op("act", lambda e, pn_=pn_: e.activation(out=pn_.ap[0:4], in_=eN.ap[0:4], func=AF.Exp), reads=[eN.b], writes=[pn_.b])
        fns = []
        for h in range(16):
            kv, pair, j = h // 4, h // 2, h % 2
            rows = slice(64 * j, 64 * j + 64)
            fns.append(lambda e, h=h, kv=kv, pair=pair, rows=rows, j=j: e.matmul(osp[rows, pair, :], lhsT=vc.ap[:, kv * 64:(kv + 1) * 64], rhs=ps_.ap[:, j, pair, :],
                                                                                start=True, stop=False, tile_position=(0, 64 * j)))
            fns.append(lambda e, h=h, kv=kv, pair=pair, rows=rows, j=j: e.matmul(osp[rows, pair, :], lhsT=vnew.ap[0:4, s, kv * 64:(kv + 1) * 64], rhs=pn_.ap[0:4, j, pair, :],
                                                                                start=False, stop=True, tile_position=(0, 64 * j)))
            fns.append(lambda e, h=h, pair=pair, rows=rows, j=j: e.matmul(dsp[rows, pair, :], lhsT=K.ones64.ap, rhs=ps_.ap[:, j, pair, :],
                                                                         start=True, stop=False, tile_position=(0, 64 * j)))
            fns.append(lambda e, h=h, pair=pair, rows=rows, j=j: e.matmul(dsp[rows, pair, :], lhsT=K.ones64.ap[0:4], rhs=pn_.ap[0:4, j, pair, :],
                                                                         start=False, stop=True, tile_position=(0, 64 * j)))
        P.pe(fns, reads=[vc.b, vnew.b, ps_.b, pn_.b, K.bC], writes=[PSB[6]])
        P.op("dve", lambda e: e.tensor_tensor(out=dS.ap, in0=dsp, in1=esk.ap.unsqueeze(2).broadcast_to([128, 8, 4]), op=ALU.add),
             reads=[PSB[6], esk.b], writes=[dS.b])
        P.op("dve", lambda e: e.reciprocal(out=dS.ap, in_=dS.ap), reads=[dS.b], writes=[dS.b])
        P.op("dve", lambda e: e.tensor_tensor(out=K.oT.ap[:, :, qc0:qc0 + 4], in0=osp, in1=dS.ap, op=ALU.mult),
             reads=[PSB[6], dS.b], writes=K.oT.b)
    P.dma("sp", O["pkv"], K.kvlast.ap, reads=[K.kvlast.b], writes=[K.bout])
    for t in [bmt, bmt0, bmsc, bmsn, esk, dn, vnew, eS, eN, dS] + ee + pT + kc_ + vc_ + kcT + pS + pN:
        A.free(t)
    for t in [K.qT, K.kTd, K.vtok, K.kvlast, K.kvsamp]:
        A.free(t)


def merge_phase(K):
    P, I, A = K.P, K.I, K.A
    hb = K.hT.b
    K.mT = A.alloc("mT", [16, NT], BF16, top=True)
    ring = Ring(K, 3, 4)
    s1 = A.alloc("s1", [NT], BF16)
    s2 = A.alloc("s2", [NT], BF16)
    s3 = A.alloc("s3", [NT], BF16)
    tv = A.alloc("tv", [NT], F32)
    tu = A.alloc("tu", [NT], F32)
    acc = 0
    for sg in range(4):
        c0 = sg * 512
        wga = ring.load(I["w_in"][:, 2560 + c0:2560 + c0 + 512], 16, 512)
        wgb = ring.load(I["w_in"][:, 4608 + c0:4608 + c0 + 512], 16, 512)
        wval = ring.load(I["w_glu_val"][:, c0:c0 + 512], 8, 512)
        wgate = ring.load(I["w_glu_gate"][:, c0:c0 + 512], 8, 512)
        for nt in range(4):
            n = 4 * sg + nt
            if nt == 0:
                pass
            fm_matmul(K, wga[0], wga[1], 16, nt, K.hT.ap, hb, acc)
            for pv, sl in acc_views(K, acc):
                P.op("act", lambda e, pv=pv, sl=sl: e.activation(out=s1.ap[:, sl], in_=pv, func=AF.Sigmoid), reads=acc_bufs(K, acc), writes=[s1.b])
            acc ^= 1
            fm_matmul(K, wgate[0], wgate[1], 8, nt, K.gT.ap, K.gT.b, acc)
            for pv, sl in acc_views(K, acc):
                P.op("act", lambda e, pv=pv, sl=sl: e.activation(out=s2.ap[:, sl], in_=pv, func=AF.Sigmoid), reads=acc_bufs(K, acc), writes=[s2.b])
            acc ^= 1
            fm_matmul(K, wval[0], wval[1], 8, nt, K.gT.ap, K.gT.b, acc)
            for pv, sl in acc_views(K, acc):
                P.op("dve", lambda e, pv=pv, sl=sl: e.tensor_tensor(out=tv.ap[:, sl], in0=pv, in1=s1.ap[:, sl], op=ALU.mult),
                     reads=acc_bufs(K, acc) + [s1.b], writes=[tv.b])
            P.op("dve", lambda e: e.tensor_tensor(out=tv.ap, in0=tv.ap, in1=s2.ap, op=ALU.mult), reads=[tv.b, s2.b], writes=[tv.b])
            acc ^= 1
            fm_matmul(K, wgb[0], wgb[1], 16, nt, K.hT.ap, hb, acc)
            for pv, sl in acc_views(K, acc):
                P.op("act", lambda e, pv=pv, sl=sl: e.activation(out=s3.ap[:, sl], in_=pv, func=AF.Sigmoid), reads=acc_bufs(K, acc), writes=[s3.b])
            acc ^= 1
            if nt == 0:
                wab = ring.load(I["w_attn_br"][:, c0:c0 + 512], 8, 512)
            fm_matmul(K, wab[0], wab[1], 8, nt, K.oT.ap, K.oT.b, acc)
            for pv, sl in acc_views(K, acc):
                P.op("dve", lambda e, pv=pv, sl=sl: e.tensor_tensor(out=tu.ap[:, sl], in0=pv, in1=s3.ap[:, sl], op=ALU.mult),
                     reads=acc_bufs(K, acc) + [s3.b], writes=[tu.b])
            P.op("dve", lambda e, n=n: e.tensor_tensor(out=K.mT.ap[:, n, :], in0=tu.ap, in1=tv.ap, op=ALU.add), reads=[tu.b, tv.b], writes=[K.mT.b])
            acc ^= 1
    ring.free()
    for t in (s1, s2, s3, tv, tu):
        A.free(t)


def stats_rstd(K, xT, sq, rstd):
    P = K.P
    P.op("act", lambda e: e.activation(out=sq.ap, in_=xT.ap, func=AF.Square), reads=[xT.b], writes=[sq.b])
    fns = []
    for bi, (t0, n) in enumerate(BLKS):
        for c in range(16):
            fns.append(lambda e, bi=bi, t0=t0, n=n, c=c: e.matmul(K.ps[:, bi, 0:n], lhsT=K.onesm.ap, rhs=sq.ap[:, c, t0:t0 + n],
                                                                 start=(c == 0), stop=(c == 15)))
    P.pe(fns, reads=[sq.b, K.bC], writes=acc_bufs(K, 0))
    for pv, sl in acc_views(K, 0):
        P.op("act", lambda e, pv=pv, sl=sl: e.activation(out=rstd.ap[:, sl], in_=pv, func=AF.Sqrt, bias=K.epsc.ap, scale=1.0),
             reads=acc_bufs(K, 0) + [K.bC], writes=[rstd.b])
    P.op("dve", lambda e: e.reciprocal(out=rstd.ap, in_=rstd.ap), reads=[rstd.b], writes=[rstd.b])


def post_phase(K):
    P, I, A, O = K.P, K.I, K.A, K.O
    PSB = K.psb
    xT = A.alloc("xT", [16, NT], F32, top=True)
    rstd = A.alloc("rstd", [NT], F32, top=True)
    gcol = A.alloc("gcol", [2, 16], F32, top=True)
    act = [A.alloc(f"actT{i}", [8, NT], BF16, top=True) for i in range(1)]
    sgt = [A.alloc(f"sg{i}", [NT], BF16, top=True) for i in range(2)]
    xl = [A.alloc(f"xl{i}", [D], F32) for i in range(2)]
    for t in range(9):
        n = 128 if t < 8 else 64
        x_ = xl[t % 2]
        P.dma("sp", x_.ap[:n], I["xo"][t * 128:t * 128 + n, :], writes=[x_.b])
        for q4 in range(4):
            bank = q4 % 2
            pt = K.ps[:, bank, :].rearrange("p (c n) -> p c n", c=4)
            P.pe([lambda e, c=c, pt=pt, q4=q4: e.transpose(out=pt[:, c, 0:n], in_=x_.ap[:n, (4 * q4 + c) * 128:(4 * q4 + c + 1) * 128],
                                                           identity=K.identf.ap[:n, :n]) for c in range(4)],
                 reads=[x_.b, K.bC], writes=[PSB[bank]])
            P.op("act", lambda e, pt=pt, q4=q4: e.activation(out=xT.ap[:, 4 * q4:4 * q4 + 4, t * 128:t * 128 + n], in_=pt[:, :, 0:n], func=AF.Copy),
                 reads=[PSB[bank]], writes=[xT.b])
    for x_ in xl:
        A.free(x_)
    ring = Ring(K, 2)
    acc = 0
    for g in range(4):
        wv, wb = ring.load(I["w_out"][:, g * 512:(g + 1) * 512], 16, 512)
        for nt in range(4):
            n = 4 * g + nt
            fm_matmul(K, wv, wb, 16, nt, K.mT.ap, [K.mT.b], acc)
            for pv, sl in acc_views(K, acc):
                P.op("dve", lambda e, pv=pv, sl=sl, n=n: e.tensor_tensor(out=xT.ap[:, n, sl], in0=pv, in1=xT.ap[:, n, sl], op=ALU.add),
                     reads=acc_bufs(K, acc) + [xT.b], writes=[xT.b])
            acc ^= 1
    ring.free()
    A.free(K.mT)
    sq = A.alloc("sq", [16, NT], BF16)
    with K.nc.allow_non_contiguous_dma(reason="tiny"):
        P.dma("sp", gcol.ap[:, 0, :], I["norm_ffn"].rearrange("(c p) -> p c", p=128), writes=[gcol.b])
        P.dma("sp", gcol.ap[:, 1, :], I["norm_final"].rearrange("(c p) -> p c", p=128), writes=[gcol.b])
    stats_rstd(K, xT, sq, rstd)
    h2 = K.hT
    h2b = Buf("h2T")
    _merge(h2b.r, {})
    for b in K.hT.b:
        if b.w is not None:
            _merge(h2b.r, {b.w[0].num: b.w})
        _merge(h2b.r, b.r)
    for c in range(16):
        P.op("dve", lambda e, c=c: e.scalar_tensor_tensor(out=h2.ap[:, c, :], in0=xT.ap[:, c, :], scalar=gcol.ap[:, 0, c:c + 1], in1=rstd.ap,
                                                          op0=ALU.mult, op1=ALU.mult), reads=[xT.b, gcol.b, rstd.b], writes=[h2b])
    A.free(sq)
    ring = Ring(K, 3, 2)
    nsg = (HID + 1023) // 1024
    k = 0
    for sg in range(nsg):
        h0 = sg * 1024
        hw = min(1024, HID - h0)
        nft = hw // 128
        a_ = act[0]
        for half in range(hw // 512):
            wg = ring.load(I["w_ffn_in"][:, h0 + half * 512:h0 + half * 512 + 512], 16, 512)
            wu = ring.load(I["w_ffn_in"][:, HID + h0 + half * 512:HID + h0 + half * 512 + 512], 16, 512)
            for nt in range(4):
                f = 4 * half + nt
                s_ = sgt[k % 2]
                k += 1
                fm_matmul(K, wg[0], wg[1], 16, nt, h2.ap, [h2b], acc)
                for pv, sl in acc_views(K, acc):
                    P.op("act", lambda e, pv=pv, sl=sl, s_=s_: e.activation(out=s_.ap[:, sl], in_=pv, func=AF.Silu), reads=acc_bufs(K, acc), writes=[s_.b])
                acc ^= 1
                fm_matmul(K, wu[0], wu[1], 16, nt, h2.ap, [h2b], acc)
                for pv, sl in acc_views(K, acc):
                    P.op("dve", lambda e, pv=pv, sl=sl, s_=s_, f=f, a_=a_: e.tensor_tensor(out=a_.ap[:, f, sl], in0=pv, in1=s_.ap[:, sl], op=ALU.mult),
                         reads=acc_bufs(K, acc) + [s_.b], writes=[a_.b])
                acc ^= 1
        for g in range(4):
            wv, wb = ring.load(I["w_ffn_out"][h0:h0 + hw, g * 512:(g + 1) * 512], nft, 512)
            for nt in range(4):
                n = 4 * g + nt
                fm_matmul(K, wv, wb, nft, nt, a_.ap, [a_.b], acc)
                for pv, sl in acc_views(K, acc):
                    P.op("dve", lambda e, pv=pv, sl=sl, n=n: e.tensor_tensor(out=xT.ap[:, n, sl], in0=pv, in1=xT.ap[:, n, sl], op=ALU.add),
                         reads=acc_bufs(K, acc) + [xT.b], writes=[xT.b])
                acc ^= 1
    ring.free()
    for t in act + sgt:
        A.free(t)
    sq = A.alloc("sq2", [16, NT], BF16)
    stats_rstd(K, xT, sq, rstd)
    A.free(sq)
    yf = [A.alloc(f"yf{i}", [16, 128], F32) for i in range(2)]
    yo = [A.alloc(f"yo{i}", [D], F32) for i in range(2)]
    for t in range(9):
        n = 128 if t < 8 else 64
        cs = slice(t * 128, t * 128 + n)
        y_, o_ = yf[t % 2], yo[t % 2]
        for c in range(16):
            P.op("dve", lambda e, c=c: e.scalar_tensor_tensor(out=y_.ap[:, c, 0:n], in0=xT.ap[:, c, cs], scalar=gcol.ap[:, 1, c:c + 1], in1=rstd.ap[:, cs],
                                                              op0=ALU.mult, op1=ALU.mult), reads=[xT.b, gcol.b, rstd.b], writes=[y_.b])
        for q4 in range(4):
            bank = 6 + q4 % 2
            pt = K.ps[:n, bank, :].rearrange("p (c n) -> p c n", c=4)
            P.pe([lambda e, c=c, pt=pt, q4=q4: e.transpose(out=pt[:, c, :], in_=y_.ap[:, 4 * q4 + c, 0:n], identity=K.identf.ap) for c in range(4)],
                 reads=[y_.b, K.bC], writes=[PSB[bank]])
            P.op("act", lambda e, pt=pt, q4=q4: e.activation(out=o_.ap[:n, q4 * 512:(q4 + 1) * 512], in_=pt.rearrange("p c n -> p (c n)"), func=AF.Copy),
                 reads=[PSB[bank]], writes=[o_.b])
        P.dma("sp", O["yo"][t * 128:t * 128 + n, :], o_.ap[:n], reads=[o_.b], writes=[K.bout])


def build_program():
    nc = bass.Bass("TRN2", target_bir_lowering=False)
    K = Ctx()
    K.nc = nc
    K.P = Prog(nc)
    P = K.P

    def din(name, shape, dt=F32):
        return nc.dram_tensor(name, list(shape), dt, kind="ExternalInput").ap()

    def dout(name, shape, dt=F32):
        return nc.dram_tensor(name, list(shape), dt, kind="ExternalOutput").ap()

    I = {}
    I["xo"] = din("xo", [NT, D])
    I["xp"] = din("xp", [NPRE, D])
    I["w_in"] = din("w_in", [D, INC])
    I["w_glu_val"] = din("w_glu_val", [1024, D])
    I["w_glu_gate"] = din("w_glu_gate", [1024, D])
    I["w_attn_br"] = din("w_attn_br", [1024, D])
    I["w_out"] = din("w_out", [D, D])
    I["w_ffn_in"] = din("w_ffn_in", [D, 2 * HID])
    I["w_ffn_out"] = din("w_ffn_out", [HID, D])
    for nm in ["norm_attn", "norm_ffn", "norm_final"]:
        I[nm] = din(nm, [D])
    I["lam_re"] = din("lam_re", [64, 64])
    I["lam_im"] = din("lam_im", [64, 64])
    I["log_dt"] = din("log_dt", [64])
    I["b_re"] = din("b_re", [64, 64, 16])
    I["b_im"] = din("b_im", [64, 64, 16])
    I["c_re"] = din("c_re", [64, 16, 64])
    I["c_im"] = din("c_im", [64, 16, 64])
    I["d_skip"] = din("d_skip", [1024])
    I["sinks"] = din("sinks", [16])
    I["sre"] = din("sre", [16, 64, 64])
    I["sim"] = din("sim", [16, 64, 64])
    I["ck"] = din("ck", [16, 128, 256])
    I["cv"] = din("cv", [16, 128, 256])
    I["bmt"] = din("bmt", [128, 16, 2, 128])
    I["bmt0"] = din("bmt0", [128, 16, 128])
    I["bmsc"] = din("bmsc", [128, 16, 4])
    I["bmsn"] = din("bmsn", [4, 16, 4])
    I["kcol"] = din("kcol", [128, 1])
    I["m8"] = din("m8", [128, 8])
    I["m88"] = din("m88", [128, 8, 8])
    I["trow"] = din("trow", [128, 64])
    O = {}
    O["yo"] = dout("yo", [NT, D])
    O["pst"] = dout("pst", [2, 64, 64])
    O["pkv"] = dout("pkv", [128, 512])
    O["sst"] = dout("sst", [2, 16, 64, 64])
    O["skk"] = dout("skk", [16, 128, 256])
    O["skv"] = dout("skv", [16, 128, 256])
    if DEBUG:
        O["dbg_h"] = dout("dbg_h", [128, 2, 32])
        O["dbg_hT"] = dout("dbg_hT", [128, 16, NT], BF16)
        O["dbg_uT"] = dout("dbg_uT", [128, 8, NT], BF16)
        O["dbg_gT"] = dout("dbg_gT", [128, 8, NT], BF16)
        O["dbg_oT"] = dout("dbg_oT", [128, 8, NT], BF16)
        O["dbg_mT"] = dout("dbg_mT", [128, 16, NT], BF16)
    K.I, K.O = I, O
    K.bout = Buf("out")
    K.A = Arena(nc)
    A = K.A
    K.ps = nc.alloc_psum_tensor("psum", [128, 8, 512], F32)
    K.psb = [Buf(f"psb{i}") for i in range(8)]

    K.identf = A.alloc("identf", [128], F32, top=True)
    K.identb = A.alloc("identb", [128], BF16, top=True)
    K.ones64 = A.alloc("ones64", [64], BF16, top=True)
    K.onesm = A.alloc("onesm", [128], BF16, top=True)
    K.epsc = A.alloc("epsc", [1], F32, top=True)
    bC = Buf("const")
    K.bC = bC
    P.op("pool", lambda e: e.memset(K.identf.ap, 0.0), writes=[bC])
    P.op("pool", lambda e: e.affine_select(out=K.identf.ap, in_=K.identf.ap, pattern=[[-1, 128]], compare_op=ALU.not_equal,
                                          fill=1.0, base=0, channel_multiplier=1), reads=[bC], writes=[bC])
    P.op("pool", lambda e: e.tensor_copy(out=K.identb.ap, in_=K.identf.ap), reads=[bC], writes=[bC])
    P.op("pool", lambda e: e.memset(K.ones64.ap, 1.0), writes=[bC])
    P.op("pool", lambda e: e.memset(K.onesm.ap, 1.0 / D), writes=[bC])
    P.op("pool", lambda e: e.memset(K.epsc.ap, EPS), writes=[bC])

    K.hT = A.alloc("hT", [16, NT], BF16, nbufs=9, top=True)
    K.gT = A.alloc("gT", [8, NT], BF16, top=True)
    K.gT.b = [K.gT.b]
    K.oT = A.alloc("oT", [8, NT], BF16, top=True)
    K.oT.b = [K.oT.b]
    def stop(name):
        return STOP == name

    ssm_tables(K)
    if DEBUG:
        O["dbg_A4"] = dout("dbg_A4", [128, 2, 2, 32])
        O["dbg_A4c"] = dout("dbg_A4c", [128, 2, 2, 32])
        O["dbg_BbR"] = dout("dbg_BbR", [128, 32, 16])
        O["dbg_BbI"] = dout("dbg_BbI", [128, 32, 16])
        O["dbg_ApR"] = dout("dbg_ApR", [128, 64, 64], BF16)
        O["dbg_ApI"] = dout("dbg_ApI", [128, 64, 64], BF16)
        for nm, t in (("dbg_A4", K.A4), ("dbg_A4c", K.A4c), ("dbg_BbR", K.BbR), ("dbg_BbI", K.BbI), ("dbg_ApR", K.ApR), ("dbg_ApI", K.ApI)):
            P.dma("sp", O[nm], t.ap, reads=bl(t.b), writes=[K.bout])
    if not stop("tables"):
        K.kTh = A.alloc("kTh", [4, 128], BF16, top=True)
        K.vth = A.alloc("vth", [256], BF16, top=True)
        prefix_phase(K)
    if not (stop("tables") or stop("prefix")):
        K.uT = A.alloc("uT", [8, NT], BF16, top=True)
        K.uT.b = [K.uT.b]
        own_norm(K)
        for t in K.xts + K.xss + [K.junk, K.ss]:
            A.free(t)
        if DEBUG:
            P.dma("sp", O["dbg_hT"], K.hT.ap, reads=bl(K.hT.b), writes=[K.bout])
    if STOP not in ("tables", "prefix", "own_norm"):
        proj_phase(K)
        if DEBUG:
            P.dma("sp", O["dbg_uT"], K.uT.ap, reads=bl(K.uT.b), writes=[K.bout])
    if STOP not in ("tables", "prefix", "own_norm", "proj"):
        ssm_tables_own(K)
        A.free(K.BbR)
        A.free(K.BbI)
        ssm_own(K)
        if DEBUG:
            P.dma("sp", O["dbg_gT"], K.gT.ap, reads=bl(K.gT.b), writes=[K.bout])
    if STOP not in ("tables", "prefix", "own_norm", "proj", "ssm"):
        attention(K)
        if DEBUG:
            P.dma("sp", O["dbg_oT"], K.oT.ap, reads=bl(K.oT.b), writes=[K.bout])
    if STOP not in ("tables", "prefix", "own_norm", "proj", "ssm", "attn"):
        merge_phase(K)
        if DEBUG:
            P.dma("sp", O["dbg_mT"], K.mT.ap, reads=bl(K.mT.b), writes=[K.bout])
        A.free(K.gT)
        A.free(K.oT)
    if STOP not in ("tables", "prefix", "own_norm", "proj", "ssm", "attn", "merge"):
        post_phase(K)
    P.finish()
    K.ninstr = P.ninstr
    return nc, K


_CACHE = {}


def _host_consts(rel_bias, qidx):
    rb = np.asarray(rel_bias, np.float32)
    k = np.arange(128)[:, None, None]
    kb = np.arange(2)[None, :, None]
    q = np.arange(128)[None, None, :]
    dist = q + 128 - (kb * 128 + k)
    valid = (dist >= 0) & (dist < 128)
    bk = t5_bucket(dist)
    bias = rb[bk]
    bias = np.where(valid[..., None], bias, np.float32(NEG)).astype(np.float32)
    hperm = np.array([4 * (n // 4) + 2 * ((n % 4) % 2) + (n % 4) // 2 for n in range(16)])
    bmt = np.ascontiguousarray(np.transpose(bias, (0, 3, 1, 2))[:, hperm])
    bmt0 = bmt[:, :, 0, :].copy() if qidx > 0 else np.full((128, 16, 128), NEG, np.float32)
    j = np.arange(132)[:, None]
    t = np.arange(4)[None, :]
    dist = t + 128 - j
    valid = (dist >= 0) & (dist < 128)
    bs = np.where(valid[..., None], rb[t5_bucket(dist)], np.float32(NEG)).astype(np.float32)
    sperm = np.array([2 * (n % 8) + n // 8 for n in range(16)])
    bs = np.ascontiguousarray(np.transpose(bs, (0, 2, 1))[:, sperm])
    return bmt, np.ascontiguousarray(bmt0), np.ascontiguousarray(bs[:128]), np.ascontiguousarray(bs[128:])


def kernel(x_prompt, x_sample, state_ssm_re, state_ssm_im, cache_win_k, cache_win_v, rel_bias,
           norm_attn, w_in, lam_re, lam_im, log_dt, b_re, b_im, c_re, c_im, d_skip,
           w_glu_val, w_glu_gate, w_attn_br, sinks, w_out, norm_ffn, w_ffn_in, w_ffn_out, norm_final):
    f = lambda a: np.ascontiguousarray(np.asarray(a, dtype=np.float32))
    x_prompt, x_sample = f(x_prompt), f(x_sample)
    if "nc" not in _CACHE:
        _CACHE["nc"], _CACHE["K"] = build_program()
    nc = _CACHE["nc"]
    shared = {
        "w_in": f(w_in)[0], "w_glu_val": f(w_glu_val)[0], "w_glu_gate": f(w_glu_gate)[0], "w_attn_br": f(w_attn_br)[0],
        "w_out": f(w_out)[0], "w_ffn_in": f(w_ffn_in)[0], "w_ffn_out": f(w_ffn_out)[0],
        "norm_attn": f(norm_attn)[0], "norm_ffn": f(norm_ffn)[0], "norm_final": f(norm_final),
        "lam_re": f(lam_re)[0], "lam_im": f(lam_im)[0], "log_dt": f(log_dt)[0], "b_re": f(b_re)[0], "b_im": f(b_im)[0],
        "c_re": f(c_re)[0], "c_im": f(c_im)[0], "d_skip": f(d_skip)[0], "sinks": f(sinks)[0],
        "kcol": (127 - np.arange(128, dtype=np.float32)).reshape(128, 1),
        "m8": (np.arange(128)[:, None] // 16 == np.arange(8)[None, :]).astype(np.float32),
        "m88": np.ascontiguousarray(np.broadcast_to(np.eye(8, dtype=np.float32), (128, 8, 8))),
        "trow": np.ascontiguousarray(np.broadcast_to(np.arange(1, 65, dtype=np.float32), (128, 64))),
    }
    sre, sim = f(state_ssm_re)[0], f(state_ssm_im)[0]
    ck = f(cache_win_k)[0].reshape(128, 128, 256)
    cv = f(cache_win_v)[0].reshape(128, 128, 256)
    in_maps = []
    for c in range(8):
        b, q = c // 4, c % 4
        xo = np.concatenate([x_prompt[b, 1024 * q:1024 * (q + 1)], x_sample[16 * c:16 * c + 16].reshape(64, D)], axis=0)
        xp = np.zeros((NPRE, D), np.float32)
        if q > 0:
            xp[NPRE - 1024 * q:] = x_prompt[b, 0:1024 * q]
        bmt, bmt0, bmsc, bmsn = _host_consts(rel_bias, q)
        m = dict(shared)
        m.update({"xo": np.ascontiguousarray(xo), "xp": xp, "sre": sre[16 * c:16 * c + 16], "sim": sim[16 * c:16 * c + 16],
                  "ck": ck[16 * c:16 * c + 16], "cv": cv[16 * c:16 * c + 16], "bmt": bmt, "bmt0": bmt0, "bmsc": bmsc, "bmsn": bmsn})
        in_maps.append({k: np.ascontiguousarray(v) for k, v in m.items()})
    res = run_bass_kernel_spmd(nc, in_maps, core_ids=list(range(8)))
    R = res.results
    _CACHE["last"] = R
    y_prompt = np.zeros((2, 4096, D), np.float32)
    y_sample = np.zeros((128, 4, D), np.float32)
    p_re = np.zeros((1, 2, 64, 64), np.float32)
    p_im = np.zeros((1, 2, 64, 64), np.float32)
    p_k = np.zeros((1, 2, 128, 4, 64), np.float32)
    p_v = np.zeros((1, 2, 128, 4, 64), np.float32)
    s_re = np.zeros((1, 128, 64, 64), np.float32)
    s_im = np.zeros((1, 128, 64, 64), np.float32)
    s_k = np.zeros((1, 128, 128, 4, 64), np.float32)
    s_v = np.zeros((1, 128, 128, 4, 64), np.float32)
    for c in range(8):
        b, q = c // 4, c % 4
        r = R[c]
        y_prompt[b, 1024 * q:1024 * (q + 1)] = r["yo"][:1024]
        y_sample[16 * c:16 * c + 16] = r["yo"][1024:].reshape(16, 4, D)
        if q == 3:
            p_re[0, b] = r["pst"][0]
            p_im[0, b] = r["pst"][1]
            p_k[0, b] = r["pkv"][:, :256].reshape(128, 4, 64)
            p_v[0, b] = r["pkv"][:, 256:].reshape(128, 4, 64)
        s_re[0, 16 * c:16 * c + 16] = r["sst"][0]
        s_im[0, 16 * c:16 * c + 16] = r["sst"][1]
        s_k[0, 16 * c:16 * c + 16] = r["skk"].reshape(16, 128, 4, 64)
        s_v[0, 16 * c:16 * c + 16] = r["skv"].reshape(16, 128, 4, 64)
    return (y_prompt, y_sample, p_re, p_im, p_k, p_v, s_re, s_im, s_k, s_v)
```

```python
import numpy as np
import concourse.bass as bass
import concourse.mybir as mybir
from concourse.bass_utils import run_bass_kernel_spmd

F32 = mybir.dt.float32
BF16 = mybir.dt.bfloat16
I32 = mybir.dt.int32
AF = mybir.ActivationFunctionType
ALU = mybir.AluOpType
AX = mybir.AxisListType

D = 2048
NT = 1088
NPRE = 3072
NCH = 16
HID = 5632
INC = 6656
EPS = 1e-6
NEG = -1e30
BLKS = [(0, 512), (512, 512), (1024, 64)]
NDS = 24
NDS_SW = 8
ARENA_BYTES = 206 * 1024
DEBUG = False
STOP = None
ASTOP = None
DT_SIZE = {F32: 4, BF16: 2, I32: 4}


class Buf:
    __slots__ = ("name", "w", "r")

    def __init__(self, name="", guards=None):
        self.name = name
        self.w = None
        self.r = dict(guards) if guards else {}


def _merge(dst, src):
    for k, ev in src.items():
        if k not in dst or dst[k][1] < ev[1]:
            dst[k] = ev


class Tile:
    __slots__ = ("ap", "b", "off", "nbytes", "name")


class Arena:
    def __init__(self, nc):
        self.t = nc.alloc_sbuf_tensor("arena", [128, ARENA_BYTES // 4], F32)
        self.free_list = [[0, ARENA_BYTES, {}]]

    def alloc(self, name, shape, dt, nbufs=1, top=False):
        n = int(np.prod(shape)) * DT_SIZE[dt]
        n = (n + 63) // 64 * 64
        order = range(len(self.free_list) - 1, -1, -1) if top else range(len(self.free_list))
        for i in order:
            off, size, g = self.free_list[i]
            if size >= n:
                if size == n:
                    self.free_list.pop(i)
                elif top:
                    self.free_list[i] = [off, size - n, dict(g)]
                    off = off + size - n
                else:
                    self.free_list[i] = [off + n, size - n, dict(g)]
                t = Tile()
                t.name, t.off, t.nbytes = name, off, n
                v = self.t[:, off // 4:(off + n) // 4]
                if dt != F32:
                    v = v.bitcast(dt)
                ne = int(np.prod(shape))
                v = v[:, 0:ne]
                if len(shape) == 2:
                    v = v.rearrange("p (a b) -> p a b", a=shape[0])
                elif len(shape) == 3:
                    v = v.rearrange("p (a b c) -> p a b c", a=shape[0], b=shape[1])
                elif len(shape) == 4:
                    v = v.rearrange("p (a b c d) -> p a b c d", a=shape[0], b=shape[1], c=shape[2])
                t.ap = v
                if nbufs == 1:
                    t.b = Buf(name, g)
                else:
                    t.b = [Buf(f"{name}{j}", g) for j in range(nbufs)]
                return t
        raise RuntimeError(f"arena OOM for {name} ({n} B); free={[(o, s) for o, s, _ in self.free_list]}")

    def free(self, t):
        g = {}
        bufs = t.b if isinstance(t.b, list) else [t.b]
        for b in bufs:
            if b.w is not None:
                _merge(g, {b.w[0].num: b.w})
            _merge(g, b.r)
        self.free_list.append([t.off, t.nbytes, g])
        self.free_list.sort(key=lambda x: x[0])
        out = []
        for blk in self.free_list:
            if out and out[-1][0] + out[-1][1] == blk[0]:
                out[-1][1] += blk[1]
                _merge(out[-1][2], blk[2])
            else:
                out.append(blk)
        self.free_list = out


class Prog:
    def __init__(self, nc):
        self.nc = nc
        self.eng = {"pe": nc.tensor, "act": nc.scalar, "dve": nc.vector, "pool": nc.gpsimd, "sp": nc.sync}
        self.csem = {e: nc.alloc_semaphore(name=f"c_{e}") for e in self.eng}
        self.ccnt = {e: 0 for e in self.eng}
        self.dsems = [nc.alloc_semaphore(name=f"d_{i}") for i in range(NDS)]
        self.dcnt = [0] * NDS
        self.dnext = 0
        self.dnext_sw = 0
        self.waited = {e: {} for e in self.eng}
        self.ninstr = 0

    def _wait(self, e, ev):
        sem, val = ev
        k = sem.num
        if self.waited[e].get(k, 0) >= val:
            return
        self.eng[e].wait_ge(sem, val)
        self.waited[e][k] = val

    def _deps(self, e, reads, writes, skip=None, relax=False):
        own = self.csem[e].num
        lim = self.ccnt[e] - 1

        def need(ev):
            if ev[0].num == skip:
                return False
            if relax and ev[0].num == own and ev[1] <= lim:
                return False
            return True
        for b in reads:
            if b.w is not None and need(b.w):
                self._wait(e, b.w)
        for b in writes:
            if b.w is not None and need(b.w):
                self._wait(e, b.w)
            for k, ev in b.r.items():
                if need(ev):
                    self._wait(e, ev)

    @staticmethod
    def _mark(ev, reads, writes):
        for b in reads:
            b.r[ev[0].num] = ev
        for b in writes:
            b.w = ev
            b.r = {}

    def op(self, e, fn, reads=(), writes=(), relax=False):
        self._deps(e, reads, writes, relax=relax)
        ins = fn(self.eng[e])
        self.ccnt[e] += 1
        self.ninstr += 1
        ins.then_inc(self.csem[e], 1)
        self._mark((self.csem[e], self.ccnt[e]), reads, writes)

    def group(self, e, fns, reads=(), writes=()):
        self._deps(e, reads, writes)
        ins = None
        for fn in fns:
            ins = fn(self.eng[e])
            self.ninstr += 1
        self.ccnt[e] += 1
        ins.then_inc(self.csem[e], 1)
        self._mark((self.csem[e], self.ccnt[e]), reads, writes)

    def pe(self, fns, reads=(), writes=()):
        self._deps("pe", reads, writes, skip=self.csem["pe"].num)
        ins = None
        for fn in fns:
            ins = fn(self.nc.tensor)
            self.ninstr += 1
        self.ccnt["pe"] += 1
        ins.then_inc(self.csem["pe"], 1)
        self._mark((self.csem["pe"], self.ccnt["pe"]), reads, writes)

    def dma(self, e, out, in_, reads=(), writes=(), **kw):
        if e == "pool":
            i = self.dnext_sw
            self.dnext_sw = (i + 1) % NDS_SW
        else:
            i = NDS_SW + self.dnext
            self.dnext = (self.dnext + 1) % (NDS - NDS_SW)
        if self.dcnt[i] > 0:
            self._wait(e, (self.dsems[i], self.dcnt[i]))
        self._deps(e, reads, writes)
        ins = self.eng[e].dma_start(out=out, in_=in_, **kw)
        self.ninstr += 1
        self.dcnt[i] += 16
        ins.then_inc(self.dsems[i], 16)
        self._mark((self.dsems[i], self.dcnt[i]), reads, writes)

    def finish(self):
        for e in self.eng:
            if e != "sp" and self.ccnt[e] > 0:
                self._wait("sp", (self.csem[e], self.ccnt[e]))
        for i in range(NDS):
            if self.dcnt[i] > 0:
                self._wait("sp", (self.dsems[i], self.dcnt[i]))


def t5_bucket(dist):
    n = np.maximum(dist, 0)
    max_exact = 16
    large = max_exact + (np.log(np.maximum(n, 1) / max_exact) / np.log(128 / max_exact) * 16).astype(np.int32)
    large = np.minimum(large, 31)
    return np.where(n < max_exact, n, large).astype(np.int32)


class Ctx:
    pass


def share(t, buf):
    if isinstance(t.b, Buf) and t.b is not buf:
        _merge(buf.r, t.b.r)
    t.b = buf


def bl(b):
    return b if isinstance(b, list) else [b]


def range_reduce_sin(K, ang, bang, out, bout, shape, part=128):
    P, A = K.P, K.A
    ni = A.alloc("rr_i", shape, I32)
    nf = A.alloc("rr_f", shape, F32)
    C1 = 6.28125
    C2 = float(2 * np.pi - 6.28125)
    P.op("dve", lambda e: e.tensor_scalar(out=nf.ap, in0=ang, scalar1=float(1.0 / (2 * np.pi)), scalar2=None, op0=ALU.mult),
         reads=[bang], writes=[nf.b])
    P.op("dve", lambda e: e.tensor_copy(out=ni.ap, in_=nf.ap), reads=[nf.b], writes=[ni.b])
    P.op("dve", lambda e: e.tensor_copy(out=nf.ap, in_=ni.ap), reads=[ni.b], writes=[nf.b])
    P.op("dve", lambda e: e.scalar_tensor_tensor(out=ang, in0=nf.ap, scalar=-C1, in1=ang, op0=ALU.mult, op1=ALU.add),
         reads=[nf.b, bang], writes=[bang])
    P.op("dve", lambda e: e.scalar_tensor_tensor(out=ang, in0=nf.ap, scalar=-C2, in1=ang, op0=ALU.mult, op1=ALU.add),
         reads=[nf.b, bang], writes=[bang])
    P.op("dve", lambda e: e.tensor_scalar(out=ang, in0=ang, scalar1=3.1415925, scalar2=-3.1415925, op0=ALU.min, op1=ALU.max),
         reads=[bang], writes=[bang])
    P.op("act", lambda e: e.activation(out=out, in_=ang, func=AF.Sin), reads=[bang], writes=[bout])
    A.free(ni)
    A.free(nf)


def accurate_exp(K, x, bx, shape):
    P, A = K.P, K.A
    y = A.alloc("aexp_y", shape, F32)
    acc = A.alloc("aexp_a", shape, F32)
    P.op("dve", lambda e: e.tensor_scalar(out=y.ap, in0=x, scalar1=0.125, scalar2=None, op0=ALU.mult), reads=[bx], writes=[y.b])
    fact = [1.0]
    for i in range(1, 12):
        fact.append(fact[-1] * i)
    P.op("dve", lambda e: e.tensor_scalar(out=acc.ap, in0=y.ap, scalar1=1.0 / fact[11], scalar2=1.0 / fact[10], op0=ALU.mult, op1=ALU.add),
         reads=[y.b], writes=[acc.b])
    for i in range(9, -1, -1):
        P.op("dve", lambda e: e.tensor_tensor(out=acc.ap, in0=acc.ap, in1=y.ap, op=ALU.mult), reads=[acc.b, y.b], writes=[acc.b])
        P.op("dve", lambda e, i=i: e.tensor_scalar(out=acc.ap, in0=acc.ap, scalar1=float(1.0 / fact[i]), scalar2=None, op0=ALU.add),
             reads=[acc.b], writes=[acc.b])
    for _ in range(3):
        P.op("dve", lambda e: e.tensor_tensor(out=acc.ap, in0=acc.ap, in1=acc.ap, op=ALU.mult), reads=[acc.b], writes=[acc.b])
    P.op("dve", lambda e: e.tensor_copy(out=x, in_=acc.ap), reads=[acc.b, bx], writes=[bx])
    A.free(y)
    A.free(acc)


def ssm_tables(K):
    nc, P, I, A = K.nc, K.P, K.I, K.A
    PSB = K.psb
    lrG = A.alloc("lrG", [32], F32)
    liG = A.alloc("liG", [32], F32)
    dtG = A.alloc("dtG", [32], F32)
    bG = Buf("G")
    with nc.allow_non_contiguous_dma(reason="tiny transposed param loads"):
        for gh in range(2):
            rows = slice(64 * gh, 64 * gh + 64)
            P.dma("sp", lrG.ap[rows, :], I["lam_re"][32 * gh:32 * gh + 32, :].rearrange("g p -> p g"), writes=[bG])
            P.dma("sp", liG.ap[rows, :], I["lam_im"][32 * gh:32 * gh + 32, :].rearrange("g p -> p g"), writes=[bG])
            P.dma("sp", dtG.ap[rows, :], I["log_dt"][32 * gh:32 * gh + 32].unsqueeze(0).broadcast_to([64, 32]), writes=[bG])
    accurate_exp(K, dtG.ap, bG, [32])
    P.op("dve", lambda e: e.tensor_scalar(out=lrG.ap, in0=lrG.ap, scalar1=-1e-4, scalar2=None, op0=ALU.min), reads=[bG], writes=[bG])
    rho = A.alloc("rhoG", [32], F32)
    th = A.alloc("thG", [32], F32)
    P.op("dve", lambda e: e.tensor_tensor(out=rho.ap, in0=lrG.ap, in1=dtG.ap, op=ALU.mult), reads=[bG], writes=[bG])
    P.op("dve", lambda e: e.tensor_tensor(out=th.ap, in0=liG.ap, in1=dtG.ap, op=ALU.mult), reads=[bG], writes=[bG])
    tmps = [lrG, liG, dtG, rho, th]

    def TTv(o, a, b_, op, rd, wb):
        P.op("dve", lambda e: e.tensor_tensor(out=o, in0=a, in1=b_, op=op), reads=rd, writes=[wb])

    def TSv(o, a, s1, s2, op0, op1, rd, wb):
        if op1 is None:
            P.op("dve", lambda e: e.tensor_scalar(out=o, in0=a, scalar1=s1, scalar2=None, op0=op0), reads=rd, writes=[wb])
        else:
            P.op("dve", lambda e: e.tensor_scalar(out=o, in0=a, scalar1=s1, scalar2=s2, op0=op0, op1=op1), reads=rd, writes=[wb])

    b1 = Buf("a1")
    nm = lambda n: A.alloc(n, [32], F32)
    r_, x2, sn, cs, t_, ex, a1r, a1i = [nm(n) for n in ("r_", "x2", "sn", "cs", "t_", "ex", "a1r", "a1i")]
    ni = A.alloc("ni", [32], I32)
    for t in (r_, x2, sn, cs, t_, ex, a1r, a1i, ni):
        share(t, b1)
    tmps.extend([r_, x2, sn, cs, t_, ex, a1r, a1i, ni])
    C1 = 6.28125
    C2 = float(2 * np.pi - 6.28125)
    TSv(t_.ap, th.ap, float(1.0 / (2 * np.pi)), None, ALU.mult, None, [bG], b1)
    P.op("dve", lambda e: e.tensor_copy(out=ni.ap, in_=t_.ap), reads=[b1], writes=[b1])
    P.op("dve", lambda e: e.tensor_copy(out=t_.ap, in_=ni.ap), reads=[b1], writes=[b1])
    P.op("dve", lambda e: e.scalar_tensor_tensor(out=r_.ap, in0=t_.ap, scalar=-C1, in1=th.ap, op0=ALU.mult, op1=ALU.add), reads=[b1, bG], writes=[b1])
    P.op("dve", lambda e: e.scalar_tensor_tensor(out=r_.ap, in0=t_.ap, scalar=-C2, in1=r_.ap, op0=ALU.mult, op1=ALU.add), reads=[b1], writes=[b1])
    TSv(r_.ap, r_.ap, 0.125, None, ALU.mult, None, [b1], b1)
    TTv(x2.ap, r_.ap, r_.ap, ALU.mult, [b1], b1)
    TSv(sn.ap, x2.ap, 1.0 / 362880, -1.0 / 5040, ALU.mult, ALU.add, [b1], b1)
    for cf in (1.0 / 120, -1.0 / 6, 1.0):
        TTv(sn.ap, sn.ap, x2.ap, ALU.mult, [b1], b1)
        TSv(sn.ap, sn.ap, float(cf), None, ALU.add, None, [b1], b1)
    TTv(sn.ap, sn.ap, r_.ap, ALU.mult, [b1], b1)
    TSv(cs.ap, x2.ap, -1.0 / 3628800, 1.0 / 40320, ALU.mult, ALU.add, [b1], b1)
    for cf in (-1.0 / 720, 1.0 / 24, -0.5, 1.0):
        TTv(cs.ap, cs.ap, x2.ap, ALU.mult, [b1], b1)
        TSv(cs.ap, cs.ap, float(cf), None, ALU.add, None, [b1], b1)
    for _ in range(3):
        TTv(t_.ap, sn.ap, cs.ap, ALU.mult, [b1], b1)
        TTv(x2.ap, sn.ap, sn.ap, ALU.mult, [b1], b1)
        TSv(sn.ap, t_.ap, 2.0, None, ALU.mult, None, [b1], b1)
        TSv(cs.ap, x2.ap, -2.0, 1.0, ALU.mult, ALU.add, [b1], b1)
    TSv(ex.ap, rho.ap, 1.0 / 120, 1.0 / 24, ALU.mult, ALU.add, [bG], b1)
    for cf in (1.0 / 6, 0.5, 1.0, 1.0):
        TTv(ex.ap, ex.ap, rho.ap, ALU.mult, [b1, bG], b1)
        TSv(ex.ap, ex.ap, float(cf), None, ALU.add, None, [b1], b1)
    TTv(a1r.ap, ex.ap, cs.ap, ALU.mult, [b1], b1)
    TTv(a1i.ap, ex.ap, sn.ap, ALU.mult, [b1], b1)
    b128 = Buf("a128")
    a128r, a128i, q1, q2 = [nm(n) for n in ("a128r", "a128i", "q1", "q2")]
    for t in (a128r, a128i, q1, q2):
        share(t, b128)
    tmps.extend([a128r, a128i, q1, q2])
    P.op("dve", lambda e: e.tensor_copy(out=a128r.ap, in_=a1r.ap), reads=[b1], writes=[b128])
    P.op("dve", lambda e: e.tensor_copy(out=a128i.ap, in_=a1i.ap), reads=[b1], writes=[b128])
    K.Rm = A.alloc("Rm", [32], F32)
    K.thG = A.alloc("thGk", [32], F32)
    K.c64 = A.alloc("c64", [32], F32)
    K.s64 = A.alloc("s64", [32], F32)
    r64 = nm("r64")
    share(r64, b128)
    tmps.append(r64)
    P.op("dve", lambda e: e.tensor_copy(out=K.Rm.ap, in_=ex.ap), reads=[b1], writes=[K.Rm.b])
    P.op("dve", lambda e: e.tensor_copy(out=K.thG.ap, in_=th.ap), reads=[bG], writes=[K.thG.b])
    P.op("dve", lambda e: e.tensor_copy(out=r64.ap, in_=ex.ap), reads=[b1], writes=[b128])
    for _ in range(6):
        TTv(r64.ap, r64.ap, r64.ap, ALU.mult, [b128], b128)
    P.op("dve", lambda e: e.reciprocal(out=r64.ap, in_=r64.ap), reads=[b128], writes=[b128])
    for it in range(7):
        if it == 6:
            TTv(K.c64.ap, a128r.ap, r64.ap, ALU.mult, [b128], K.c64.b)
            TTv(K.s64.ap, a128i.ap, r64.ap, ALU.mult, [b128], K.s64.b)
        TTv(q1.ap, a128r.ap, a128r.ap, ALU.mult, [b128], b128)
        TTv(q2.ap, a128i.ap, a128i.ap, ALU.mult, [b128], b128)
        TTv(a128i.ap, a128r.ap, a128i.ap, ALU.mult, [b128], b128)
        TSv(a128i.ap, a128i.ap, 2.0, None, ALU.mult, None, [b128], b128)
        TTv(a128r.ap, q1.ap, q2.ap, ALU.subtract, [b128], b128)

    def make_a4(ar, ai, b, name):
        a4 = A.alloc(name, [2, 2, 32], F32)
        P.op("dve", lambda e: e.tensor_copy(out=a4.ap[:, 0, 0, :], in_=ar.ap), reads=[b], writes=[a4.b])
        P.op("dve", lambda e: e.tensor_scalar(out=a4.ap[:, 0, 1, :], in0=ai.ap, scalar1=-1.0, scalar2=None, op0=ALU.mult), reads=[b], writes=[a4.b])
        P.op("dve", lambda e: e.tensor_copy(out=a4.ap[:, 1, 0, :], in_=ai.ap), reads=[b], writes=[a4.b])
        P.op("dve", lambda e: e.tensor_copy(out=a4.ap[:, 1, 1, :], in_=ar.ap), reads=[b], writes=[a4.b])
        return a4

    K.A4 = make_a4(a1r, a1i, b1, "A4_1")
    K.A4c = make_a4(a128r, a128i, b128, "A4_128")

    den = A.alloc("den", [32], F32)
    t1 = A.alloc("ct1", [32], F32)
    t2 = A.alloc("ct2", [32], F32)
    cr = A.alloc("coefr", [32], F32)
    ci = A.alloc("coefi", [32], F32)
    am1 = A.alloc("am1", [32], F32)
    bc = Buf("coef")
    for t in (den, t1, t2, cr, ci, am1):
        share(t, bc)
    tmps.extend([den, t1, t2, cr, ci, am1])
    TT = lambda o, a, b_, op, rd: P.op("dve", lambda e: e.tensor_tensor(out=o, in0=a, in1=b_, op=op), reads=rd, writes=[bc])
    TT(den.ap, lrG.ap, lrG.ap, ALU.mult, [bG])
    TT(t1.ap, liG.ap, liG.ap, ALU.mult, [bG])
    TT(den.ap, den.ap, t1.ap, ALU.add, [bc])
    P.op("dve", lambda e: e.reciprocal(out=den.ap, in_=den.ap), reads=[bc], writes=[bc])
    P.op("dve", lambda e: e.tensor_scalar(out=am1.ap, in0=a1r.ap, scalar1=-1.0, scalar2=None, op0=ALU.add), reads=[b1], writes=[bc])
    TT(t1.ap, am1.ap, lrG.ap, ALU.mult, [bc, bG])
    TT(t2.ap, a1i.ap, liG.ap, ALU.mult, [b1, bG])
    TT(t1.ap, t1.ap, t2.ap, ALU.add, [bc])
    TT(cr.ap, t1.ap, den.ap, ALU.mult, [bc])
    TT(t1.ap, a1i.ap, lrG.ap, ALU.mult, [bc, b1, bG])
    TT(t2.ap, am1.ap, liG.ap, ALU.mult, [bc, bG])
    TT(t1.ap, t1.ap, t2.ap, ALU.subtract, [bc])
    TT(ci.ap, t1.ap, den.ap, ALU.mult, [bc])

    K.BbR = A.alloc("BbR", [32, 16], F32)
    K.BbI = A.alloc("BbI", [32, 16], F32)
    bBb = Buf("Bbar")
    share(K.BbR, bBb)
    share(K.BbI, bBb)
    BreG = A.alloc("BreG", [32, 16], F32)
    BimG = A.alloc("BimG", [32, 16], F32)
    tb1 = A.alloc("tb1", [32, 16], F32)
    bB = Buf("B")
    share(BreG, bB)
    share(BimG, bB)
    with nc.allow_non_contiguous_dma(reason="64B runs param load"):
        for gh in range(2):
            rows = slice(64 * gh, 64 * gh + 64)
            P.dma("sp", BreG.ap[rows], I["b_re"][32 * gh:32 * gh + 32].rearrange("g p c -> p g c"), writes=[bB])
            P.dma("sp", BimG.ap[rows], I["b_im"][32 * gh:32 * gh + 32].rearrange("g p c -> p g c"), writes=[bB])
    crb = cr.ap.unsqueeze(2).broadcast_to([128, 32, 16])
    cib = ci.ap.unsqueeze(2).broadcast_to([128, 32, 16])
    P.op("dve", lambda e: e.tensor_tensor(out=K.BbR.ap, in0=BreG.ap, in1=crb, op=ALU.mult), reads=[bB, bc], writes=[bBb])
    P.op("dve", lambda e: e.tensor_tensor(out=tb1.ap, in0=BimG.ap, in1=cib, op=ALU.mult), reads=[bB, bc], writes=[tb1.b])
    P.op("dve", lambda e: e.tensor_tensor(out=K.BbR.ap, in0=K.BbR.ap, in1=tb1.ap, op=ALU.subtract), reads=[tb1.b, bBb], writes=[bBb])
    P.op("dve", lambda e: e.tensor_tensor(out=K.BbI.ap, in0=BimG.ap, in1=crb, op=ALU.mult), reads=[bB, bc], writes=[bBb])
    P.op("dve", lambda e: e.tensor_tensor(out=tb1.ap, in0=BreG.ap, in1=cib, op=ALU.mult), reads=[bB, bc, bBb], writes=[tb1.b])
    P.op("dve", lambda e: e.tensor_tensor(out=K.BbI.ap, in0=K.BbI.ap, in1=tb1.ap, op=ALU.add), reads=[tb1.b, bBb], writes=[bBb])
    share(BreG, bB)
    share(BimG, bB)

    K.Dcol = A.alloc("Dcol", [8], F32)
    with nc.allow_non_contiguous_dma(reason="tiny"):
        P.dma("sp", K.Dcol.ap, I["d_skip"].rearrange("(j p) -> p j", p=128), writes=[K.Dcol.b])

    K.ApR = A.alloc("ApR", [64, 64], BF16)
    K.ApI = A.alloc("ApI", [64, 64], BF16)
    bAp = Buf("ApowT")
    share(K.ApR, bAp)
    share(K.ApI, bAp)
    kcol = A.alloc("kcol", [1], F32)
    dtF = A.alloc("dtF", [64], F32)
    bF = Buf("F")
    share(kcol, bF)
    share(dtF, bF)
    P.dma("sp", kcol.ap, I["kcol"], writes=[bF])
    P.dma("sp", dtF.ap, I["log_dt"].unsqueeze(0).broadcast_to([128, 64]), writes=[bF])
    accurate_exp(K, dtF.ap, bF, [64])
    for hf in range(2):
        gs = slice(32 * hf, 32 * hf + 32)
        lrF = A.alloc("lrF", [32, 64], F32)
        liF = A.alloc("liF", [32, 64], F32)
        magF = A.alloc("magF", [32, 64], F32)
        snF = A.alloc("snF", [32, 64], F32)
        bH = Buf("Fh")
        for t in (lrF, liF, magF, snF):
            share(t, bH)
        P.dma("sp", lrF.ap, I["lam_re"][gs].unsqueeze(0).broadcast_to([128, 32, 64]), writes=[bH])
        P.dma("sp", liF.ap, I["lam_im"][gs].unsqueeze(0).broadcast_to([128, 32, 64]), writes=[bH])
        dtb = dtF.ap[:, gs].unsqueeze(2).broadcast_to([128, 32, 64])
        P.op("dve", lambda e: e.tensor_scalar(out=lrF.ap, in0=lrF.ap, scalar1=-1e-4, scalar2=None, op0=ALU.min), reads=[bH], writes=[bH])
        P.op("dve", lambda e: e.tensor_tensor(out=lrF.ap, in0=lrF.ap, in1=dtb, op=ALU.mult), reads=[bH, bF], writes=[bH])
        P.op("dve", lambda e: e.tensor_tensor(out=liF.ap, in0=liF.ap, in1=dtb, op=ALU.mult), reads=[bH, bF], writes=[bH])
        P.op("act", lambda e: e.activation(out=magF.ap, in_=lrF.ap, func=AF.Exp, scale=kcol.ap), reads=[bH, bF], writes=[bH])
        P.op("dve", lambda e: e.tensor_scalar(out=lrF.ap, in0=liF.ap, scalar1=kcol.ap, scalar2=None, op0=ALU.mult), reads=[bH, bF], writes=[bH])
        range_reduce_sin(K, lrF.ap, bH, snF.ap, bH, [32, 64])
        P.op("dve", lambda e: e.tensor_tensor(out=K.ApI.ap[:, gs, :], in0=magF.ap, in1=snF.ap, op=ALU.mult), reads=[bH], writes=[bAp])
        P.op("dve", lambda e: e.tensor_scalar(out=lrF.ap, in0=liF.ap, scalar1=kcol.ap, scalar2=float(np.pi / 2), op0=ALU.mult, op1=ALU.add),
             reads=[bH, bF], writes=[bH])
        range_reduce_sin(K, lrF.ap, bH, snF.ap, bH, [32, 64])
        P.op("dve", lambda e: e.tensor_tensor(out=K.ApR.ap[:, gs, :], in0=magF.ap, in1=snF.ap, op=ALU.mult), reads=[bH], writes=[bAp])
        for t in (lrF, liF, magF, snF):
            A.free(t)
    for t in (kcol, dtF, BreG, BimG, tb1):
        A.free(t)
    for t in tmps:
        A.free(t)


def ssm_tables_own(K):
    nc, P, I, A = K.nc, K.P, K.I, K.A
    PSB = K.psb
    bBb = K.BbR.b
    m8 = A.alloc("m8", [8], F32)
    m88 = A.alloc("m88", [8, 8], F32)
    bm = Buf("masks")
    share(m8, bm)
    share(m88, bm)
    P.dma("sp", m8.ap, I["m8"], writes=[bm])
    P.dma("sp", m88.ap, I["m88"], writes=[bm])
    K.BtabR = A.alloc("BtabR", [64, 64], BF16)
    K.BtabI = A.alloc("BtabI", [64, 64], BF16)
    bBtab = Buf("Btab")
    share(K.BtabR, bBtab)
    share(K.BtabI, bBtab)
    K.CtabR = A.alloc("CtabR", [32, 128], BF16)
    K.CtabI = A.alloc("CtabI", [32, 128], BF16)
    bCtab = Buf("Ctab")
    share(K.CtabR, bCtab)
    share(K.CtabI, bCtab)
    cnat_r = A.alloc("cnat_r", [8, 64], F32)
    cnat_i = A.alloc("cnat_i", [8, 64], F32)
    bcn = Buf("cnat")
    share(cnat_r, bcn)
    share(cnat_i, bcn)
    P.dma("sp", cnat_r.ap, I["c_re"].rearrange("(j g) c p -> (g c) j p", j=8), writes=[bcn])
    P.dma("sp", cnat_i.ap, I["c_im"].rearrange("(j g) c p -> (g c) j p", j=8), writes=[bcn])
    cnt = 0
    for src, dst in ((K.BbR, K.BtabR), (K.BbI, K.BtabI)):
        for j in range(8):
            gh, gq = j // 4, (j % 4) * 8
            pb = cnt % 2
            cnt += 1
            rows = slice(64 * gh, 64 * gh + 64)
            inp = src.ap[rows, gq:gq + 8, :]
            pt = K.ps[:, pb, 0:64]
            P.pe([lambda e, inp=inp, pt=pt, rows=rows: e.transpose(out=pt, in_=inp, identity=K.identf.ap[rows, rows])],
                 reads=[bBb, K.bC], writes=[PSB[pb]])
            P.op("dve", lambda e, pt=pt, dst=dst, j=j: e.tensor_tensor(
                out=dst.ap[:, 8 * j:8 * j + 8, :], in0=pt.unsqueeze(1).broadcast_to([128, 8, 64]),
                in1=m8.ap.unsqueeze(2).broadcast_to([128, 8, 64]), op=ALU.mult), reads=[PSB[pb], bm], writes=[bBtab])
    P.op("dve", lambda e: e.tensor_scalar(out=cnat_i.ap, in0=cnat_i.ap, scalar1=-1.0, scalar2=None, op0=ALU.mult), reads=[bcn], writes=[bcn])
    for src, dst in ((cnat_r, K.CtabR), (cnat_i, K.CtabI)):
        for j in range(8):
            gh, gq = j // 4, (j % 4) * 8
            pb = cnt % 2
            cnt += 1
            rows = slice(64 * gh, 64 * gh + 64)
            pt = K.ps[rows, pb, 0:128]
            P.pe([lambda e, src=src, j=j, pt=pt, gh=gh: e.matmul(pt, lhsT=src.ap[:, j, :], rhs=K.identf.ap, start=True, stop=True,
                                                                 tile_position=(0, 64 * gh))],
                 reads=[bcn, K.bC], writes=[PSB[pb]])
            P.op("dve", lambda e, pt=pt, dst=dst, rows=rows, gq=gq: e.tensor_tensor(
                out=dst.ap[rows, gq:gq + 8, :].rearrange("p g (h c) -> p g h c", h=8),
                in0=pt.rearrange("p (g c) -> p g c", g=8).unsqueeze(2).broadcast_to([64, 8, 8, 16]),
                in1=m88.ap[rows].unsqueeze(3).broadcast_to([64, 8, 8, 16]), op=ALU.mult),
                reads=[PSB[pb], bm], writes=[bCtab])
    for t in (cnat_r, cnat_i, m8, m88):
        A.free(t)
    K.CT = A.alloc("CT", [32, 64], BF16)
    K.ST = A.alloc("ST", [32, 64], BF16)
    trow = A.alloc("trow", [64], F32)
    ang = A.alloc("angT", [32, 64], F32)
    sct = A.alloc("sct", [32, 64], F32)
    P.dma("sp", trow.ap, I["trow"], writes=[trow.b])
    thb = K.thG.ap.unsqueeze(2).broadcast_to([128, 32, 64])
    trb = trow.ap.unsqueeze(1).broadcast_to([128, 32, 64])
    P.op("dve", lambda e: e.tensor_tensor(out=ang.ap, in0=thb, in1=trb, op=ALU.mult), reads=[K.thG.b, trow.b], writes=[ang.b])
    range_reduce_sin(K, ang.ap, ang.b, sct.ap, sct.b, [32, 64])
    P.op("dve", lambda e: e.tensor_copy(out=K.ST.ap, in_=sct.ap), reads=[sct.b], writes=[K.ST.b])
    P.op("dve", lambda e: e.tensor_tensor(out=ang.ap, in0=thb, in1=trb, op=ALU.mult), reads=[K.thG.b, trow.b, ang.b], writes=[ang.b])
    P.op("dve", lambda e: e.tensor_scalar(out=ang.ap, in0=ang.ap, scalar1=float(np.pi / 2), scalar2=None, op0=ALU.add), reads=[ang.b], writes=[ang.b])
    range_reduce_sin(K, ang.ap, ang.b, sct.ap, sct.b, [32, 64])
    P.op("dve", lambda e: e.tensor_copy(out=K.CT.ap, in_=sct.ap), reads=[sct.b], writes=[K.CT.b])
    for t in (trow, ang, sct):
        A.free(t)


def load_weights_resident(K, name, src, kc, ncols):
    t = K.A.alloc(name, [kc, ncols], BF16)
    for c0 in range(0, ncols, 512):
        w = min(512, ncols - c0)
        K.P.dma("pool", t.ap[:, :, c0:c0 + w], src[:, c0:c0 + w].rearrange("(c p) n -> p c n", p=128), writes=[t.b])
    return t


def norm_rows(K, xt, n, gb, xs, junk, ss):
    P = K.P
    P.op("act", lambda e: e.activation(out=junk.ap[:n], in_=xt.ap[:n], func=AF.Square, accum_out=ss.ap[:n]),
         reads=[xt.b], writes=[junk.b, ss.b])
    P.op("act", lambda e: e.activation(out=ss.ap[:n], in_=ss.ap[:n], func=AF.Sqrt, bias=K.epsc.ap[:n], scale=1.0 / D),
         reads=[ss.b, K.bC], writes=[ss.b])
    P.op("dve", lambda e: e.reciprocal(out=ss.ap[:n], in_=ss.ap[:n]), reads=[ss.b], writes=[ss.b])
    P.op("dve", lambda e: e.scalar_tensor_tensor(out=xs.ap[:n], in0=xt.ap[:n], scalar=ss.ap[:n], in1=gb.ap[:n],
                                                 op0=ALU.mult, op1=ALU.mult), reads=[xt.b, ss.b, gb.b], writes=[xs.b])


def transpose_rows(K, xs, n, dst_ap, dst_bufs, pbank):
    P = K.P
    ptb = K.ps[:, pbank:pbank + 2, :].rearrange("p a b -> p (a b)").bitcast(BF16)
    ptv = ptb.rearrange("p (c n) -> p c n", c=16)
    pbufs = [K.psb[pbank], K.psb[pbank + 1]]
    P.pe([lambda e, c=c: e.transpose(out=ptv[:, c, 0:n], in_=xs.ap[:n, c * 128:(c + 1) * 128], identity=K.identb.ap[:n, :n])
          for c in range(16)], reads=[xs.b, K.bC], writes=pbufs)
    P.op("act", lambda e: e.activation(out=dst_ap, in_=ptv[:, :, 0:n], func=AF.Copy), reads=pbufs, writes=dst_bufs)


def prefix_phase(K):
    nc, P, I, A = K.nc, K.P, K.I, K.A
    PSB = K.psb
    Wu = load_weights_resident(K, "Wu", I["w_in"][:, 0:1024], 16, 1024)
    K.gb = A.alloc("gb", [D], F32)
    P.dma("sp", K.gb.ap, I["norm_attn"].unsqueeze(0).broadcast_to([128, D]), writes=[K.gb.b])
    xts = [A.alloc(f"xt{i}", [D], F32) for i in range(2)]
    K.junk = A.alloc("junk", [D], BF16)
    K.ss = A.alloc("ss", [1], F32)
    xss = [A.alloc(f"xs{i}", [D], BF16) for i in range(2)]
    hTt = [A.alloc(f"hTt{i}", [16, 128], BF16) for i in range(2)]
    U = [A.alloc(f"U{i}", [1024], BF16) for i in range(2)]
    pr1 = A.alloc("pr1", [2, 32, 16], F32)
    pr2 = A.alloc("pr2", [2, 32, 16], F32)
    Tt = A.alloc("Tt", [2, 32, 16], F32)
    S = A.alloc("S", [2, 32], F32)
    Mm = A.alloc("Mm", [2, 2, 32], F32)
    K.Hc = A.alloc("Hc", [2, 32], F32)
    H = K.Hc
    P.op("dve", lambda e: e.memset(H.ap, 0.0), writes=[H.b])
    BbRb = K.BbR.ap.unsqueeze(1).broadcast_to([128, 2, 32, 16])
    BbIb = K.BbI.ap.unsqueeze(1).broadcast_to([128, 2, 32, 16])
    ntile = NPRE // 128

    def stA(i):
        xt, xs, ht = xts[i % 2], xss[i % 2], hTt[i % 2]
        P.dma("sp", xt.ap, I["xp"][i * 128:(i + 1) * 128, :], writes=[xt.b])
        norm_rows(K, xt, 128, K.gb, xs, K.junk, K.ss)
        transpose_rows(K, xs, 128, ht.ap, [ht.b], 0)

    def stB(i):
        ht, u = hTt[i % 2], U[i % 2]
        for half in range(2):
            pu = K.ps[:, 2 + half, :]
            P.pe([lambda e, c=c, pu=pu, half=half: e.matmul(pu, lhsT=ht.ap[:, c, :], rhs=Wu.ap[:, c, half * 512:(half + 1) * 512],
                                                           start=(c == 0), stop=(c == 15)) for c in range(16)],
                 reads=[ht.b, Wu.b], writes=[PSB[2 + half]])
        P.op("act", lambda e, u=u: e.activation(out=u.ap, in_=K.ps[:, 2:4, :].rearrange("p a b -> p (a b)"), func=AF.Copy),
             reads=[PSB[2], PSB[3]], writes=[u.b])

    zp = K.ps[:, 4:6, :].rearrange("p a (g c) -> p a g c", c=16)

    def stC(i):
        u = U[i % 2]
        fns = []
        for g in range(64):
            gh, gq = g // 32, g % 32
            rows = slice(64 * gh, 64 * gh + 64)
            for ri, tab in ((0, K.ApR), (1, K.ApI)):
                fns.append(lambda e, g=g, gh=gh, gq=gq, rows=rows, ri=ri, tab=tab, u=u: e.matmul(
                    zp[rows, ri, gq, :], lhsT=tab.ap[:, g, :], rhs=u.ap[:, g * 16:(g + 1) * 16], start=True, stop=True,
                    tile_position=(0, 64 * gh)))
        P.pe(fns, reads=[u.b, K.ApR.b], writes=[PSB[4], PSB[5]])

    def stD(i):
        P.op("dve", lambda e: e.tensor_tensor(out=pr1.ap, in0=zp, in1=BbRb, op=ALU.mult), reads=[PSB[4], PSB[5], K.BbR.b], writes=[pr1.b])
        P.op("dve", lambda e: e.tensor_tensor(out=pr2.ap, in0=zp, in1=BbIb, op=ALU.mult), reads=[PSB[4], PSB[5], K.BbR.b], writes=[pr2.b])
        P.op("dve", lambda e: e.tensor_tensor(out=Tt.ap[:, 0], in0=pr1.ap[:, 0], in1=pr2.ap[:, 1], op=ALU.subtract),
             reads=[pr1.b, pr2.b], writes=[Tt.b])
        P.op("dve", lambda e: e.tensor_tensor(out=Tt.ap[:, 1], in0=pr1.ap[:, 1], in1=pr2.ap[:, 0], op=ALU.add),
             reads=[pr1.b, pr2.b], writes=[Tt.b])
        P.op("dve", lambda e: e.tensor_reduce(out=S.ap, in_=Tt.ap, axis=AX.X, op=ALU.add), reads=[Tt.b], writes=[S.b])
        P.op("dve", lambda e: e.tensor_tensor(out=Mm.ap, in0=K.A4c.ap, in1=H.ap.unsqueeze(1).broadcast_to([128, 2, 2, 32]), op=ALU.mult),
             reads=[K.A4c.b, H.b], writes=[Mm.b])
        P.op("dve", lambda e: e.tensor_tensor(out=H.ap, in0=Mm.ap[:, :, 0, :], in1=Mm.ap[:, :, 1, :], op=ALU.add), reads=[Mm.b], writes=[H.b])
        P.op("dve", lambda e: e.tensor_tensor(out=H.ap, in0=H.ap, in1=S.ap, op=ALU.add), reads=[H.b, S.b], writes=[H.b])

    stA(0)
    for i in range(ntile):
        stB(i)
        stC(i)
        if i + 1 < ntile:
            stA(i + 1)
        stD(i)
    if DEBUG:
        P.dma("sp", K.O["dbg_h"], H.ap, reads=[H.b], writes=[K.bout])
    for t in [Wu] + U + [pr1, pr2, Tt, S, Mm, K.ApR, K.ApI]:
        A.free(t)
    load_wkv(K)
    ht = hTt[(ntile - 1) % 2]
    kv_tokmajor(K, ht.ap, [ht.b], 128, slice(0, 128), K.vth.ap, K.vth.b)
    kT_dup(K, ht.ap, [ht.b], 128, lambda kv: K.kTh.ap[:, kv, :], [K.kTh.b], slice(0, 128))
    A.free(K.Wkv)
    A.free(K.Wkd)
    for t in hTt:
        A.free(t)
    K.xts, K.xss = xts, xss


def kT_dup(K, hT_ap, hbufs, n, dst_ap_fn, dst_bufs, cols):
    P = K.P
    PSB = K.psb
    for kv in range(4):
        pk = K.ps[:, 7, 0:n]
        P.pe([lambda e, c=c, pk=pk, kv=kv: e.matmul(pk, lhsT=K.Wkd.ap[:, c, kv, :],
                                                   rhs=hT_ap[:, c, cols], start=(c == 0), stop=(c == 15)) for c in range(16)],
             reads=hbufs + [K.Wkd.b], writes=[PSB[7]])
        P.op("act", lambda e, kv=kv, pk=pk: e.activation(out=dst_ap_fn(kv), in_=pk, func=AF.Copy),
             reads=[PSB[7]], writes=dst_bufs)


def load_wkv(K):
    P, A = K.P, K.A
    K.Wkv = load_weights_resident(K, "Wkv", K.I["w_in"][:, 2048:2560], 16, 512)
    K.Wkd = A.alloc("Wkd", [16, 4, 128], BF16)
    for kv in range(4):
        P.op("pool", lambda e, kv=kv: e.tensor_copy(out=K.Wkd.ap[:, :, kv, :].rearrange("p c (d h) -> p c d h", d=2),
                                                    in_=K.Wkv.ap[:, :, kv * 64:(kv + 1) * 64].unsqueeze(2).broadcast_to([128, 16, 2, 64])),
             reads=[K.Wkv.b], writes=[K.Wkd.b])


def kv_tokmajor(K, hT_ap, hbufs, n, cols, vdst_ap, vdst_buf, kdst=None):
    P, PSB = K.P, K.psb
    pk = K.ps[:n, 6, :]
    P.pe([lambda e, c=c: e.matmul(pk, lhsT=hT_ap[:, c, cols], rhs=K.Wkv.ap[:, c, :], start=(c == 0), stop=(c == 15)) for c in range(16)],
         reads=hbufs + [K.Wkv.b], writes=[PSB[6]])
    P.op("act", lambda e: e.activation(out=vdst_ap, in_=K.ps[:n, 6, 256:512], func=AF.Copy), reads=[PSB[6]], writes=[vdst_buf])
    if kdst is not None:
        P.op("act", lambda e: e.activation(out=kdst.ap[:n], in_=pk, func=AF.Copy), reads=[PSB[6]], writes=[kdst.b])


class Ring:
    def __init__(self, K, nbig, nsmall=0):
        self.K = K
        self.big = [K.A.alloc(f"ringb{i}", [16, 512], BF16) for i in range(nbig)]
        self.small = [K.A.alloc(f"rings{i}", [8, 512], BF16) for i in range(nsmall)]
        self.ib = 0
        self.is_ = 0

    def load(self, src, kc, ncols):
        if kc <= 8 and self.small:
            s = self.small[self.is_ % len(self.small)]
            self.is_ += 1
        else:
            s = self.big[self.ib % len(self.big)]
            self.ib += 1
        v = s.ap[:, 0:kc, 0:ncols]
        self.K.P.dma("pool", v, src.rearrange("(c p) n -> p c n", p=128), writes=[s.b])
        return v, s.b

    def free(self):
        for s in self.big + self.small:
            self.K.A.free(s)


def acc_views(K, acc):
    return [(K.ps[:, 3 * acc:3 * acc + 2, :].rearrange("p a b -> p (a b)"), slice(0, 1024)),
            (K.ps[:, 3 * acc + 2, 0:64], slice(1024, 1088))]


def acc_bufs(K, acc):
    return [K.psb[3 * acc], K.psb[3 * acc + 1], K.psb[3 * acc + 2]]


def fm_matmul(K, wv, wb, kc, nt, act_ap, act_bufs, acc):
    fns = []
    for bi, (t0, n) in enumerate(BLKS):
        for c in range(kc):
            fns.append(lambda e, bi=bi, t0=t0, n=n, c=c: e.matmul(
                K.ps[:, 3 * acc + bi, 0:n], lhsT=wv[:, c, nt * 128:(nt + 1) * 128], rhs=act_ap[:, c, t0:t0 + n],
                start=(c == 0), stop=(c == kc - 1)))
    K.P.pe(fns, reads=[wb] + act_bufs, writes=acc_bufs(K, acc))


def own_norm(K):
    P, I, A = K.P, K.I, K.A
    for t in range(9):
        n = 128 if t < 8 else 64
        xt, xs = K.xts[t % 2], K.xss[t % 2]
        P.dma("sp", xt.ap[:n], I["xo"][t * 128:t * 128 + n, :], writes=[xt.b])
        norm_rows(K, xt, n, K.gb, xs, K.junk, K.ss)
        transpose_rows(K, xs, n, K.hT.ap[:, :, t * 128:t * 128 + n], [K.hT.b[t]], 0)
    A.free(K.gb)


def proj_phase(K):
    P, I, A = K.P, K.I, K.A
    hb = K.hT.b
    ring = Ring(K, 3)
    acc = 0
    for g in range(2):
        wv, wb = ring.load(I["w_in"][:, g * 512:(g + 1) * 512], 16, 512)
        for nt in range(4):
            fm_matmul(K, wv, wb, 16, nt, K.hT.ap, hb, acc)
            for pv, sl in acc_views(K, acc):
                P.op("act", lambda e, pv=pv, sl=sl, j=4 * g + nt: e.activation(out=K.uT.ap[:, j, sl], in_=pv, func=AF.Copy),
                     reads=acc_bufs(K, acc), writes=K.uT.b)
            acc ^= 1
    ring.free()


def ssm_own(K):
    P, I, A, O = K.P, K.I, K.A, K.O
    PSB = K.psb
    Hist = A.alloc("Hist", [2, 32, 64], F32)
    T1 = A.alloc("T1", [2, 32, 64], F32)
    H2 = A.alloc("H2", [2, 32, 64], BF16)
    HistB = A.alloc("HistB", [2, 32, 64], BF16)
    cA = A.alloc("cA", [2, 32], F32)
    cB = A.alloc("cB", [2, 32], F32)
    ysb = A.alloc("ysb", [8, 64], F32)
    g1 = A.alloc("g1", [8, 64], F32)
    zb = K.ps[:, 0:4, :].rearrange("p (r a) (g t) -> p r (a g) t", r=2, t=128)
    yps = K.ps[:, 4:6, :].rearrange("p a (j t) -> p (a j) t", t=128)
    ypb = [PSB[4], PSB[5]]
    zbb = [PSB[0], PSB[1], PSB[2], PSB[3]]
    Hc = K.Hc

    zbs = [K.ps[:, 0:2, :].rearrange("p r (g t) -> p r g t", t=64), K.ps[:, 2:4, :].rearrange("p r (g t) -> p r g t", t=64)]
    zbbs = [[PSB[0], PSB[1]], [PSB[2], PSB[3]]]

    def bu_batch(t, b4, n, c0, zsel=None):
        zb_, zbb_ = (zb, zbb) if zsel is None else (zbs[zsel], zbbs[zsel])
        fns = []
        for gh in range(2):
            rows = slice(64 * gh, 64 * gh + 64)
            for gl in range(8):
                g = 32 * gh + 8 * b4 + gl
                for ri, tab in ((0, K.BtabR), (1, K.BtabI)):
                    fns.append(lambda e, rows=rows, gl=gl, g=g, ri=ri, tab=tab, gh=gh: e.matmul(
                        zb_[rows, ri, gl, 0:n], lhsT=tab.ap[:, g, :], rhs=K.uT.ap[:, g // 8, c0:c0 + n], start=True, stop=True,
                        tile_position=(0, 64 * gh)))
        P.pe(fns, reads=[K.BtabR.b] + K.uT.b, writes=zbb_)

    def cside(n, hb_ap, hb_buf):
        for j in range(8):
            gh = j // 4
            rows = slice(64 * gh, 64 * gh + 64)
            fns = []
            for g8 in range(8):
                gq = (8 * j + g8) % 32
                for ri, tab in ((0, K.CtabR), (1, K.CtabI)):
                    fns.append(lambda e, rows=rows, gq=gq, ri=ri, tab=tab, j=j, first=(g8 == 0 and ri == 0), last=(g8 == 7 and ri == 1):
                               e.matmul(yps[:, j, 0:n], lhsT=tab.ap[rows, gq, :], rhs=hb_ap[rows, ri, gq, 0:n], start=first, stop=last))
            P.pe(fns, reads=[K.CtabR.b, hb_buf], writes=ypb)

    def gelu_out(n, c0, perm):
        yv, a = ysb.ap[:, :, 0:n], g1.ap[:, :, 0:n]
        b = a
        P.op("act", lambda e: e.activation(out=a, in_=yv, func=AF.Square), reads=[ysb.b], writes=[g1.b])
        P.op("dve", lambda e: e.tensor_scalar(out=a, in0=a, scalar1=0.044715, scalar2=1.0, op0=ALU.mult, op1=ALU.add), reads=[g1.b], writes=[g1.b])
        P.op("dve", lambda e: e.tensor_tensor(out=a, in0=a, in1=yv, op=ALU.mult), reads=[g1.b, ysb.b], writes=[g1.b])
        P.op("act", lambda e: e.activation(out=b, in_=a, func=AF.Sigmoid, scale=1.5957691216057308), reads=[g1.b], writes=[g1.b])
        P.op("dve", lambda e: e.tensor_tensor(out=K.gT.ap[:, :, c0:c0 + n], in0=b, in1=yv, op=ALU.mult), reads=[g1.b, ysb.b], writes=K.gT.b)

    CTb = K.CT.ap.rearrange("p g t -> p (g t)").unsqueeze(1).broadcast_to([128, 2, 2048])
    STf = K.ST.ap.rearrange("p g t -> p (g t)")
    Hf = Hist.ap.rearrange("p r g t -> p r (g t)")
    T1f = T1.ap.rearrange("p r g t -> p r (g t)")
    H2f = H2.ap.rearrange("p r g t -> p r (g t)")
    HBf = HistB.ap.rearrange("p r g t -> p r (g t)")
    xin = [A.alloc(f"xin{i}", [2, 64], F32) for i in range(8)]
    for ri, nm in ((0, "sre"), (1, "sim")):
        for sq in range(4):
            xi = xin[4 * ri + sq]
            for s_ in range(4):
                P.dma("sp", xi.ap[32 * s_:32 * s_ + 32], I[nm][4 * sq + s_].rearrange("(h g) p -> g h p", h=2), writes=[xi.b])
    X0 = A.alloc("X0", [2, 32, 64], BF16)
    Rz = A.alloc("Rz", [32, 64], F32)
    P.op("dve", lambda e: e.tensor_copy(out=Rz.ap, in_=K.Rm.ap.unsqueeze(2).broadcast_to([128, 32, 64])), reads=[K.Rm.b], writes=[Rz.b])
    P.op("dve", lambda e: e.memset(Rz.ap[:, :, 0:1], 0.0), reads=[Rz.b], writes=[Rz.b])
    Rzf = Rz.ap.rearrange("p g t -> p (g t)")
    ini = A.alloc("ini", [2, 32], F32)
    X0f = X0.ap.rearrange("p r g t -> p r (g t)")

    def bu_all(t):
        for b4 in range(4):
            zsel = b4 % 2
            bu_batch(t, b4, 64, t * 64, zsel)
            P.op("act", lambda e, b4=b4, zsel=zsel: e.activation(out=X0.ap[:, :, 8 * b4:8 * b4 + 8, :], in_=zbs[zsel], func=AF.Copy),
                 reads=zbbs[zsel], writes=[X0.b])

    bu_all(0)
    for t in range(16):
        c0 = t * 64
        P.op("dve", lambda e: e.tensor_tensor(out=Hf, in0=X0f, in1=CTb, op=ALU.mult), reads=[X0.b, K.CT.b], writes=[Hist.b])
        P.op("dve", lambda e: e.tensor_tensor(out=H2f[:, 0], in0=X0f[:, 1], in1=STf, op=ALU.mult), reads=[X0.b, K.ST.b], writes=[H2.b])
        P.op("dve", lambda e: e.tensor_tensor(out=H2f[:, 1], in0=X0f[:, 0], in1=STf, op=ALU.mult), reads=[X0.b, K.ST.b], writes=[H2.b])
        P.op("dve", lambda e: e.tensor_tensor(out=Hf[:, 0], in0=Hf[:, 0], in1=H2f[:, 0], op=ALU.add), reads=[Hist.b, H2.b], writes=[Hist.b])
        P.op("dve", lambda e: e.tensor_tensor(out=Hf[:, 1], in0=Hf[:, 1], in1=H2f[:, 1], op=ALU.subtract), reads=[Hist.b, H2.b], writes=[Hist.b])
        P.op("dve", lambda e: e.tensor_tensor(out=ini.ap, in0=Hc.ap, in1=K.Rm.ap.unsqueeze(1).broadcast_to([128, 2, 32]), op=ALU.mult),
             reads=[Hc.b, K.Rm.b], writes=[ini.b])
        P.op("dve", lambda e: e.tensor_tensor(out=Hist.ap[:, :, :, 0], in0=Hist.ap[:, :, :, 0], in1=ini.ap, op=ALU.add), reads=[Hist.b, ini.b], writes=[Hist.b])
        if t + 1 < 16:
            bu_all(t + 1)
        P.group("dve", [lambda e, ri=ri: e.tensor_tensor_scan(out=T1f[:, ri], data0=Rzf, data1=Hf[:, ri], initial=0.0, op0=ALU.mult, op1=ALU.add)
                        for ri in range(2)], reads=[Hist.b, Rz.b], writes=[T1.b])
        P.op("dve", lambda e: e.tensor_tensor(out=Hf, in0=T1f, in1=CTb, op=ALU.mult), reads=[T1.b, K.CT.b], writes=[Hist.b])
        P.op("dve", lambda e: e.tensor_tensor(out=H2f[:, 0], in0=T1f[:, 1], in1=STf, op=ALU.mult), reads=[T1.b, K.ST.b], writes=[H2.b])
        P.op("dve", lambda e: e.tensor_tensor(out=H2f[:, 1], in0=T1f[:, 0], in1=STf, op=ALU.mult), reads=[T1.b, K.ST.b], writes=[H2.b])
        P.op("dve", lambda e: e.tensor_tensor(out=HBf[:, 0], in0=Hf[:, 0], in1=H2f[:, 0], op=ALU.subtract), reads=[Hist.b, H2.b], writes=[HistB.b])
        P.op("dve", lambda e: e.tensor_tensor(out=HBf[:, 1], in0=Hf[:, 1], in1=H2f[:, 1], op=ALU.add), reads=[Hist.b, H2.b], writes=[HistB.b])
        P.op("dve", lambda e: e.tensor_tensor(out=cA.ap, in0=T1.ap[:, :, :, 63], in1=K.c64.ap.unsqueeze(1).broadcast_to([128, 2, 32]), op=ALU.mult),
             reads=[T1.b, K.c64.b], writes=[cA.b])
        P.op("dve", lambda e: e.tensor_tensor(out=cB.ap, in0=T1.ap[:, :, :, 63], in1=K.s64.ap.unsqueeze(1).broadcast_to([128, 2, 32]), op=ALU.mult),
             reads=[T1.b, K.s64.b], writes=[cB.b])
        P.op("dve", lambda e: e.tensor_tensor(out=Hc.ap[:, 0], in0=cA.ap[:, 0], in1=cB.ap[:, 1], op=ALU.subtract), reads=[cA.b, cB.b], writes=[Hc.b])
        P.op("dve", lambda e: e.tensor_tensor(out=Hc.ap[:, 1], in0=cA.ap[:, 1], in1=cB.ap[:, 0], op=ALU.add), reads=[cA.b, cB.b], writes=[Hc.b])
        cside(64, HistB.ap, HistB.b)
        for j in range(8):
            P.op("dve", lambda e, j=j: e.scalar_tensor_tensor(out=ysb.ap[:, j, 0:64], in0=K.uT.ap[:, j, c0:c0 + 64], scalar=K.Dcol.ap[:, j:j + 1],
                                                              in1=yps[:, j, 0:64], op0=ALU.mult, op1=ALU.add),
                 reads=K.uT.b + [K.Dcol.b] + ypb, writes=[ysb.b])
        gelu_out(64, c0, False)
    stg = A.alloc("stg", [128], F32)
    for ri in range(2):
        pt = K.ps[0:32, 6, 0:128]
        P.pe([lambda e, ri=ri: e.transpose(out=pt, in_=Hc.ap[:, ri, :], identity=K.identf.ap)], reads=[Hc.b, K.bC], writes=[PSB[6]])
        P.op("act", lambda e: e.activation(out=stg.ap[0:32, :], in_=pt, func=AF.Copy), reads=[PSB[6]], writes=[stg.b])
        P.dma("sp", O["pst"][ri].rearrange("(h g) p -> g h p", h=2), stg.ap[0:32, :].rearrange("g (h p) -> g h p", h=2), reads=[stg.b], writes=[K.bout])

    for t in [Hist, HistB, T1, H2, cA, cB, X0, Rz, ini]:
        A.free(t)
    Hs0 = A.alloc("Hs0", [2, 32, 16], F32)
    HistS = A.alloc("HistS", [2, 32, 4, 16], F32)
    HistSB = A.alloc("HistSB", [2, 32, 64], BF16)
    MmS = A.alloc("MmS", [2, 2, 32, 16], F32)
    RrS = A.alloc("RrS", [2, 32, 16], F32)
    for ri, nm in ((0, "sre"), (1, "sim")):
        for sq in range(4):
            xi = xin[4 * ri + sq]
            pt = K.ps[:, 6, 0:128]
            P.pe([lambda e, xi=xi: e.transpose(out=pt, in_=xi.ap.rearrange("p h q -> p (h q)"), identity=K.identf.ap)],
                 reads=[xi.b, K.bC], writes=[PSB[6]])
            P.op("act", lambda e, ri=ri, sq=sq: e.activation(out=Hs0.ap[:, ri, :, 4 * sq:4 * sq + 4].rearrange("p g s -> p s g"),
                                                            in_=pt.rearrange("p (s g) -> p s g", s=4), func=AF.Copy),
                 reads=[PSB[6]], writes=[Hs0.b])
    c0 = 1024
    for b4 in range(4):
        bu_batch(8, b4, 64, c0)
        for ri in range(2):
            P.op("act", lambda e, b4=b4, ri=ri: e.activation(out=HistS.ap[:, ri, 8 * b4:8 * b4 + 8, :, :],
                                                            in_=zb[:, ri, :, 0:64].rearrange("p g (s t) -> p g t s", t=4), func=AF.Copy),
                 reads=zbb, writes=[HistS.b])
    A4b = K.A4.ap.unsqueeze(4).broadcast_to([128, 2, 2, 32, 16])
    for tt in range(4):
        prev = Hs0.ap if tt == 0 else HistS.ap[:, :, :, tt - 1, :]
        pb = [Hs0.b] if tt == 0 else [HistS.b]
        P.op("dve", lambda e, prev=prev: e.tensor_tensor(out=MmS.ap, in0=A4b, in1=prev.unsqueeze(1).broadcast_to([128, 2, 2, 32, 16]), op=ALU.mult),
             reads=[K.A4.b] + pb, writes=[MmS.b])
        P.op("dve", lambda e: e.tensor_tensor(out=RrS.ap, in0=MmS.ap[:, :, 0], in1=MmS.ap[:, :, 1], op=ALU.add), reads=[MmS.b], writes=[RrS.b])
        P.op("dve", lambda e, tt=tt: e.tensor_tensor(out=HistS.ap[:, :, :, tt, :], in0=HistS.ap[:, :, :, tt, :], in1=RrS.ap, op=ALU.add),
             reads=[RrS.b, HistS.b], writes=[HistS.b])
    P.op("act", lambda e: e.activation(out=HistSB.ap, in_=HistS.ap.rearrange("p r g t s -> p r g (t s)"), func=AF.Copy), reads=[HistS.b], writes=[HistSB.b])
    cside(64, HistSB.ap, HistSB.b)
    for j in range(8):
        P.op("dve", lambda e, j=j: e.scalar_tensor_tensor(
            out=ysb.ap[:, j, 0:64].rearrange("p (s t) -> p s t", t=4), in0=K.uT.ap[:, j, c0:c0 + 64].rearrange("p (s t) -> p s t", t=4),
            scalar=K.Dcol.ap[:, j:j + 1], in1=yps[:, j, 0:64].rearrange("p (t s) -> p s t", t=4), op0=ALU.mult, op1=ALU.add),
            reads=K.uT.b + [K.Dcol.b] + ypb, writes=[ysb.b])
    gelu_out(64, c0, True)
    stg2s = [A.alloc(f"stg2{i}", [4, 32], F32) for i in range(2)]
    stgs = [A.alloc(f"stgs{i}", [128], F32) for i in range(4)]
    for ri in range(2):
        for sq in range(4):
            pt = K.ps[:, 6, 0:128]
            sg2, sg1 = stg2s[(4 * ri + sq) % 2], stgs[(4 * ri + sq) % 4]
            pt = K.ps[:, 6 + (sq % 2), 0:128]
            P.op("dve", lambda e, ri=ri, sq=sq, sg2=sg2: e.tensor_copy(out=sg2.ap, in_=HistS.ap[:, ri, :, 3, 4 * sq:4 * sq + 4].rearrange("p g s -> p s g")),
                 reads=[HistS.b], writes=[sg2.b])
            P.pe([lambda e, sg2=sg2, pt=pt: e.transpose(out=pt, in_=sg2.ap.rearrange("p s g -> p (s g)"), identity=K.identf.ap)],
                 reads=[sg2.b, K.bC], writes=[PSB[6 + (sq % 2)]])
            P.op("act", lambda e, sg1=sg1, pt=pt: e.activation(out=sg1.ap, in_=pt, func=AF.Copy), reads=[PSB[6 + (sq % 2)]], writes=[sg1.b])
            for s in range(4):
                P.dma("sp", O["sst"][ri, 4 * sq + s].rearrange("(h g) p -> g h p", h=2),
                      sg1.ap[32 * s:32 * s + 32, :].rearrange("g (h p) -> g h p", h=2), reads=[sg1.b], writes=[K.bout])
    for t in [ysb, g1, stg, Hs0, HistS, HistSB, MmS, RrS] + xin + stg2s + stgs:
        A.free(t)
    for t in [K.A4, K.A4c, K.BtabR, K.BtabI, K.CtabR, K.CtabI, K.Dcol, K.Hc, K.uT, K.CT, K.ST, K.Rm, K.thG, K.c64, K.s64]:
        A.free(t)


def attention(K):
    P, I, A, O = K.P, K.I, K.A, K.O
    PSB = K.psb
    hb = K.hT.b
    K.kTd = A.alloc("kTd", [4, 128 + NT], BF16, nbufs=10, top=True)
    K.vtok = A.alloc("vtok", [10, 256], BF16, nbufs=10, top=True)
    K.kvlast = A.alloc("kvlast", [512], F32, top=True)
    K.kvsamp = A.alloc("kvsamp", [512], F32, top=True)
    load_wkv(K)
    P.op("act", lambda e: e.activation(out=K.kTd.ap[:, :, 0:128], in_=K.kTh.ap, func=AF.Copy), reads=[K.kTh.b], writes=[K.kTd.b[0]])
    P.op("act", lambda e: e.activation(out=K.vtok.ap[:, 0, :], in_=K.vth.ap, func=AF.Copy), reads=[K.vth.b], writes=[K.vtok.b[0]])
    for t in range(9):
        n = 128 if t < 8 else 64
        kdst = K.kvlast if t == 7 else (K.kvsamp if t == 8 else None)
        kv_tokmajor(K, K.hT.ap, [hb[t]], n, slice(t * 128, t * 128 + n), K.vtok.ap[:n, t + 1, :], K.vtok.b[t + 1], kdst)
    for bi, (t0, n) in enumerate(BLKS):
        tiles = list(range(t0 // 128, (t0 + n + 127) // 128))
        kT_dup(K, K.hT.ap, [hb[t] for t in tiles], n, lambda kv, t0=t0, n=n: K.kTd.ap[:, kv, 128 + t0:128 + t0 + n],
               [K.kTd.b[t + 1] for t in tiles], slice(t0, t0 + n))
    A.free(K.Wkv)
    A.free(K.Wkd)
    A.free(K.kTh)
    A.free(K.vth)
    K.qT = A.alloc("qT", [8, NT], BF16, nbufs=1, top=True)
    ring = Ring(K, 2)
    acc = 0
    for g in range(2):
        wv, wb = ring.load(I["w_in"][:, 1024 + g * 512:1024 + (g + 1) * 512], 16, 512)
        for nt in range(4):
            fm_matmul(K, wv, wb, 16, nt, K.hT.ap, hb, acc)
            for pv, sl in acc_views(K, acc):
                P.op("act", lambda e, pv=pv, sl=sl, j=4 * g + nt: e.activation(out=K.qT.ap[:, j, sl], in_=pv, func=AF.Copy),
                     reads=acc_bufs(K, acc), writes=[K.qT.b])
            acc ^= 1
    ring.free()
    if ASTOP == 'q':
        return
    bmt = A.alloc("bmt", [16, 2, 128], F32)
    bmt0 = A.alloc("bmt0", [16, 128], F32)
    bmsc = A.alloc("bmsc", [2, 8, 4], F32)
    bmsn = A.alloc("bmsn", [2, 8, 4], F32)
    esk = A.alloc("esk", [8], F32)
    P.dma("sp", bmt.ap, I["bmt"], writes=[bmt.b])
    P.dma("sp", bmt0.ap, I["bmt0"], writes=[bmt0.b])
    P.dma("sp", bmsc.ap, I["bmsc"].rearrange("k (j a) t -> k j a t", j=2), writes=[bmsc.b])
    P.dma("sp", bmsn.ap[0:4], I["bmsn"].rearrange("k (j a) t -> k j a t", j=2), writes=[bmsn.b])
    with K.nc.allow_non_contiguous_dma(reason="tiny"):
        for j in range(2):
            P.dma("sp", esk.ap[64 * j:64 * j + 64, :], I["sinks"].rearrange("(p j) -> j p", j=2)[j:j + 1, :].broadcast_to([64, 8]), writes=[esk.b])
    P.op("act", lambda e: e.activation(out=esk.ap, in_=esk.ap, func=AF.Exp), reads=[esk.b], writes=[esk.b])
    if ASTOP == 'tabs':
        return
    ee = [A.alloc(f"ee{i}", [2, 2, 2, 128], F32) for i in range(2)]
    pT = [A.alloc(f"pT{i}", [2, 2, 2, 128], BF16) for i in range(2)]
    dn = A.alloc("dn", [2, 128], F32)
    spss = [K.ps[:, 0:2, :].rearrange("p j (a k q) -> p j a k q", a=2, k=2), K.ps[:, 4:6, :].rearrange("p j (a k q) -> p j a k q", a=2, k=2)]
    spbs = [[PSB[0], PSB[1]], [PSB[4], PSB[5]]]
    opss = [K.ps[:, 2, 0:256].rearrange("p (a q) -> p a q", q=128), K.ps[:, 6, 0:256].rearrange("p (a q) -> p a q", q=128)]
    dpss = [K.ps[:, 3, 0:256].rearrange("p (a q) -> p a q", q=128), K.ps[:, 7, 0:256].rearrange("p (a q) -> p a q", q=128)]
    odbs = [[PSB[2], PSB[3]], [PSB[6], PSB[7]]]
    dns = [dn, A.alloc("dn2", [2, 128], F32)]
    batches = [(i, hbk) for i in range(8) for hbk in range(4)]

    def stS(b):
        i, hbk = batches[b]
        kv = hbk
        qc = slice(i * 128, (i + 1) * 128)
        spsv = spss[b % 2]
        fns = []
        for hh in range(4):
            h = 4 * hbk + hh
            pair, j = h // 2, h % 2
            rows = slice(64 * j, 64 * j + 64)
            for kb in range(2):
                kc0 = 128 * (i + kb)
                fns.append(lambda e, hh=hh, kb=kb, rows=rows, pair=pair, kc0=kc0, j=j: e.matmul(
                    spsv[:, j, hh // 2, kb, :], lhsT=K.kTd.ap[rows, kv, kc0:kc0 + 128], rhs=K.qT.ap[rows, pair, qc], start=True, stop=True))
        P.pe(fns, reads=[K.kTd.b[i], K.kTd.b[i + 1], K.qT.b], writes=spbs[b % 2])

    def stE(b):
        i, hbk = batches[b]
        spsv, spb = spss[b % 2], spbs[b % 2]
        e_, p_ = ee[b % 2], pT[b % 2]
        if i == 0:
            for j in range(2):
                P.op("dve", lambda e, j=j: e.scalar_tensor_tensor(out=e_.ap[:, j, :, 0, :], in0=spsv[:, j, :, 0, :], scalar=0.125,
                                                                  in1=bmt0.ap[:, 4 * hbk + 2 * j:4 * hbk + 2 * j + 2, :], op0=ALU.mult, op1=ALU.add),
                     reads=spb + [bmt0.b], writes=[e_.b])
                P.op("dve", lambda e, j=j: e.scalar_tensor_tensor(out=e_.ap[:, j, :, 1, :], in0=spsv[:, j, :, 1, :], scalar=0.125,
                                                                  in1=bmt.ap[:, 4 * hbk + 2 * j:4 * hbk + 2 * j + 2, 1, :], op0=ALU.mult, op1=ALU.add),
                     reads=spb + [bmt.b], writes=[e_.b])
        else:
            P.op("dve", lambda e: e.scalar_tensor_tensor(out=e_.ap, in0=spsv, scalar=0.125, in1=bmt.ap[:, 4 * hbk:4 * hbk + 4, :, :],
                                                         op0=ALU.mult, op1=ALU.add), reads=spb + [bmt.b], writes=[e_.b])
        P.op("act", lambda e: e.activation(out=p_.ap, in_=e_.ap, func=AF.Exp), reads=[e_.b], writes=[p_.b])

    def stV(b):
        i, hbk = batches[b]
        kv = hbk
        p_ = pT[b % 2]
        ops, dps = opss[b % 2], dpss[b % 2]
        fns = []
        for hh in range(4):
            pp, j = hh // 2, hh % 2
            rows = slice(64 * j, 64 * j + 64)
            for kb in range(2):
                fns.append(lambda e, hh=hh, kb=kb, pp=pp, j=j, rows=rows: e.matmul(
                    ops[rows, pp, :], lhsT=K.vtok.ap[:, i + kb, kv * 64:(kv + 1) * 64], rhs=p_.ap[:, j, hh // 2, kb, :],
                    start=(kb == 0), stop=(kb == 1), tile_position=(0, 64 * j)))
            for kb in range(2):
                fns.append(lambda e, hh=hh, kb=kb, pp=pp, j=j, rows=rows: e.matmul(
                    dps[rows, pp, :], lhsT=K.ones64.ap, rhs=p_.ap[:, j, hh // 2, kb, :],
                    start=(kb == 0), stop=(kb == 1), tile_position=(0, 64 * j)))
        P.pe(fns, reads=[K.vtok.b[i], K.vtok.b[i + 1], p_.b, K.bC], writes=odbs[b % 2])

    def stN(b):
        i, hbk = batches[b]
        qc = slice(i * 128, (i + 1) * 128)
        ops, dps, dn_ = opss[b % 2], dpss[b % 2], dns[b % 2]
        ob = odbs[b % 2]
        P.op("dve", lambda e: e.tensor_tensor(out=dn_.ap, in0=dps, in1=esk.ap[:, 2 * hbk:2 * hbk + 2].unsqueeze(2).broadcast_to([128, 2, 128]),
                                              op=ALU.add), reads=ob + [esk.b], writes=[dn_.b])
        P.op("dve", lambda e: e.reciprocal(out=dn_.ap, in_=dn_.ap), reads=[dn_.b], writes=[dn_.b])
        P.op("dve", lambda e: e.tensor_tensor(out=K.oT.ap[:, 2 * hbk:2 * hbk + 2, qc], in0=ops, in1=dn_.ap, op=ALU.mult),
             reads=ob + [dn_.b], writes=K.oT.b)

    nb = len(batches)
    stS(0)
    for b in range(nb):
        stE(b)
        if b + 1 < nb:
            stS(b + 1)
        stV(b)
        stN(b)
    lim = False
    if ASTOP == 'prompt' or lim:
        return
    kc_ = [A.alloc(f"kc{i}", [4, 2, 64], BF16) for i in range(2)]
    vc_ = [A.alloc(f"vc{i}", [256], BF16) for i in range(2)]
    kcT = [A.alloc(f"kcT{i}", [4, 128], BF16) for i in range(2)]
    vnew = A.alloc("vnew", [16, 256], BF16)
    eS = A.alloc("eS", [2, 8, 4], F32)
    eN = A.alloc("eN", [2, 8, 4], F32)
    pS = [A.alloc(f"pS{i}", [2, 8, 4], BF16) for i in range(2)]
    pN = [A.alloc(f"pN{i}", [2, 8, 4], BF16) for i in range(2)]
    dS = A.alloc("dS", [8, 4], F32)
    with K.nc.allow_non_contiguous_dma(reason="tiny relayout"):
        for s in range(16):
            P.dma("sp", vnew.ap[0:4, s, :], K.vtok.ap[4 * s:4 * s + 4, 9, :], reads=[K.vtok.b[9]], writes=[vnew.b])
    ptk = K.ps[:, 4, :].bitcast(BF16)[:, 0:512].rearrange("p (k n) -> p k n", k=4)
    sscb = [K.ps[:, 5 + 2 * j, 0:32].rearrange("p (h t) -> p h t", t=4) for j in range(2)]
    ssnb = [K.ps[0:4, 5 + 2 * j, 32:64].rearrange("p (h t) -> p h t", t=4) for j in range(2)]
    osp = K.ps[:, 6, 0:32].rearrange("p (a t) -> p a t", t=4)
    dsp = K.ps[:, 6, 32:64].rearrange("p (a t) -> p a t", t=4)
    for s in range(16):
        kc, vc, kt, ps_, pn_ = kc_[s % 2], vc_[s % 2], kcT[s % 2], pS[s % 2], pN[s % 2]
        P.dma("pool", kc.ap, I["ck"][s].rearrange("k (v d) -> k v d", v=4).unsqueeze(2).broadcast_to([128, 4, 2, 64]), writes=[kc.b])
        P.dma("pool", vc.ap, I["cv"][s], writes=[vc.b])
        P.dma("sp", O["skk"][s, 0:124, :], I["ck"][s, 4:128, :], writes=[K.bout])
        P.dma("sp", O["skv"][s, 0:124, :], I["cv"][s, 4:128, :], writes=[K.bout])
        P.dma("sp", O["skk"][s, 124:128, :], K.kvsamp.ap[4 * s:4 * s + 4, 0:256], reads=[K.kvsamp.b], writes=[K.bout])
        P.dma("sp", O["skv"][s, 124:128, :], K.kvsamp.ap[4 * s:4 * s + 4, 256:512], reads=[K.kvsamp.b], writes=[K.bout])
        P.pe([lambda e, kv=kv: e.transpose(out=ptk[:, kv, :], in_=kc.ap[:, kv].rearrange("k a d -> k (a d)"),
                                           identity=K.identb.ap) for kv in range(4)], reads=[kc.b, K.bC], writes=[PSB[4]])
        P.op("act", lambda e: e.activation(out=kt.ap, in_=ptk, func=AF.Copy), reads=[PSB[4]], writes=[kt.b])
        qc0 = 1024 + 4 * s
        fns = []
        for h in range(16):
            kv, pair, j = h // 4, h // 2, h % 2
            rows = slice(64 * j, 64 * j + 64)
            fns.append(lambda e, h=h, kv=kv, pair=pair, rows=rows, j=j: e.matmul(sscb[j][:, pair, :], lhsT=kt.ap[rows, kv, :], rhs=K.qT.ap[rows, pair, qc0:qc0 + 4],
                                                                                start=True, stop=True))
            fns.append(lambda e, h=h, kv=kv, pair=pair, rows=rows, j=j: e.matmul(ssnb[j][:, pair, :], lhsT=K.kTd.ap[rows, kv, 128 + qc0:128 + qc0 + 4],
                                                                                rhs=K.qT.ap[rows, pair, qc0:qc0 + 4], start=True, stop=True))
        P.pe(fns, reads=[kt.b, K.kTd.b[9], K.qT.b], writes=[PSB[5], PSB[7]])
        for j in range(2):
            P.op("dve", lambda e, j=j: e.scalar_tensor_tensor(out=eS.ap[:, j], in0=sscb[j], scalar=0.125, in1=bmsc.ap[:, j], op0=ALU.mult, op1=ALU.add),
                 reads=[PSB[5], PSB[7], bmsc.b], writes=[eS.b])
            P.op("dve", lambda e, j=j: e.scalar_tensor_tensor(out=eN.ap[0:4, j], in0=ssnb[j], scalar=0.125, in1=bmsn.ap[0:4, j], op0=ALU.mult, op1=ALU.add),
                 reads=[PSB[5], PSB[7], bmsn.b], writes=[eN.b])
        P.op("act", lambda e, ps_=ps_: e.activation(out=ps_.ap, in_=eS.ap, func=AF.Exp), reads=[eS.b], writes=[ps_.b])
        P.op("act", lambda e, pn_=pn_: e.activation(out=pn_.ap[0:4], in_=eN.ap[0:4], func=AF.Exp), reads=[eN.b], writes=[pn_.b])
        fns = []
        for h in range(16):
            kv, pair, j = h // 4, h // 2, h % 2
            rows = slice(64 * j, 64 * j + 64)
            fns.append(lambda e, h=h, kv=kv, pair=pair, rows=rows, j=j: e.matmul(osp[rows, pair, :], lhsT=vc.ap[:, kv * 64:(kv + 1) * 64], rhs=ps_.ap[:, j, pair, :],
                                                                                start=True, stop=False, tile_position=(0, 64 * j)))
            fns.append(lambda e, h=h, kv=kv, pair=pair, rows=rows, j=j: e.matmul(osp[rows, pair, :], lhsT=vnew.ap[0:4, s, kv * 64:(kv + 1) * 64], rhs=pn_.ap[0:4, j, pair, :],
                                                                                start=False, stop=True, tile_position=(0, 64 * j)))
            fns.append(lambda e, h=h, pair=pair, rows=rows, j=j: e.matmul(dsp[rows, pair, :], lhsT=K.ones64.ap, rhs=ps_.ap[:, j, pair, :],
                                                                         start=True, stop=False, tile_position=(0, 64 * j)))
            fns.append(lambda e, h=h, pair=pair, rows=rows, j=j: e.matmul(dsp[rows, pair, :], lhsT=K.ones64.ap[0:4], rhs=pn_.ap[0:4, j, pair, :],
                                                                         start=False, stop=True, tile_position=(0, 64 * j)))
        P.pe(fns, reads=[vc.b, vnew.b, ps_.b, pn_.b, K.bC], writes=[PSB[6]])
        P.op("dve", lambda e: e.tensor_tensor(out=dS.ap, in0=dsp, in1=esk.ap.unsqueeze(2).broadcast_to([128, 8, 4]), op=ALU.add),
             reads=[PSB[6], esk.b], writes=[dS.b])
        P.op("dve", lambda e: e.reciprocal(out=dS.ap, in_=dS.ap), reads=[dS.b], writes=[dS.b])
        P.op("dve", lambda e: e.tensor_tensor(out=K.oT.ap[:, :, qc0:qc0 + 4], in0=osp, in1=dS.ap, op=ALU.mult),
             reads=[PSB[6], dS.b], writes=K.oT.b)
    P.dma("sp", O["pkv"], K.kvlast.ap, reads=[K.kvlast.b], writes=[K.bout])
    for t in [bmt, bmt0, bmsc, bmsn, esk, vnew, eS, eN, dS] + dns + ee + pT + kc_ + vc_ + kcT + pS + pN:
        A.free(t)
    for t in [K.qT, K.kTd, K.vtok, K.kvlast, K.kvsamp]:
        A.free(t)


def merge_phase(K):
    P, I, A = K.P, K.I, K.A
    hb = K.hT.b
    K.mT = A.alloc("mT", [16, NT], BF16, top=True)
    ring = Ring(K, 3, 4)
    s1 = A.alloc("s1", [NT], BF16)
    s2 = A.alloc("s2", [NT], BF16)
    s3 = A.alloc("s3", [NT], BF16)
    tv = A.alloc("tv", [NT], F32)
    tu = A.alloc("tu", [NT], F32)
    acc = 0
    for sg in range(4):
        c0 = sg * 512
        wga = ring.load(I["w_in"][:, 2560 + c0:2560 + c0 + 512], 16, 512)
        wgb = ring.load(I["w_in"][:, 4608 + c0:4608 + c0 + 512], 16, 512)
        wval = ring.load(I["w_glu_val"][:, c0:c0 + 512], 8, 512)
        wgate = ring.load(I["w_glu_gate"][:, c0:c0 + 512], 8, 512)
        for nt in range(4):
            n = 4 * sg + nt
            if nt == 0:
                pass
            fm_matmul(K, wga[0], wga[1], 16, nt, K.hT.ap, hb, acc)
            for pv, sl in acc_views(K, acc):
                P.op("act", lambda e, pv=pv, sl=sl: e.activation(out=s1.ap[:, sl], in_=pv, func=AF.Sigmoid), reads=acc_bufs(K, acc), writes=[s1.b])
            acc ^= 1
            fm_matmul(K, wgate[0], wgate[1], 8, nt, K.gT.ap, K.gT.b, acc)
            for pv, sl in acc_views(K, acc):
                P.op("act", lambda e, pv=pv, sl=sl: e.activation(out=s2.ap[:, sl], in_=pv, func=AF.Sigmoid), reads=acc_bufs(K, acc), writes=[s2.b])
            acc ^= 1
            fm_matmul(K, wval[0], wval[1], 8, nt, K.gT.ap, K.gT.b, acc)
            for pv, sl in acc_views(K, acc):
                P.op("dve", lambda e, pv=pv, sl=sl: e.tensor_tensor(out=tv.ap[:, sl], in0=pv, in1=s1.ap[:, sl], op=ALU.mult),
                     reads=acc_bufs(K, acc) + [s1.b], writes=[tv.b])
            P.op("dve", lambda e: e.tensor_tensor(out=tv.ap, in0=tv.ap, in1=s2.ap, op=ALU.mult), reads=[tv.b, s2.b], writes=[tv.b])
            acc ^= 1
            fm_matmul(K, wgb[0], wgb[1], 16, nt, K.hT.ap, hb, acc)
            for pv, sl in acc_views(K, acc):
                P.op("act", lambda e, pv=pv, sl=sl: e.activation(out=s3.ap[:, sl], in_=pv, func=AF.Sigmoid), reads=acc_bufs(K, acc), writes=[s3.b])
            acc ^= 1
            if nt == 0:
                wab = ring.load(I["w_attn_br"][:, c0:c0 + 512], 8, 512)
            fm_matmul(K, wab[0], wab[1], 8, nt, K.oT.ap, K.oT.b, acc)
            for pv, sl in acc_views(K, acc):
                P.op("dve", lambda e, pv=pv, sl=sl: e.tensor_tensor(out=tu.ap[:, sl], in0=pv, in1=s3.ap[:, sl], op=ALU.mult),
                     reads=acc_bufs(K, acc) + [s3.b], writes=[tu.b])
            P.op("dve", lambda e, n=n: e.tensor_tensor(out=K.mT.ap[:, n, :], in0=tu.ap, in1=tv.ap, op=ALU.add), reads=[tu.b, tv.b], writes=[K.mT.b])
            acc ^= 1
    ring.free()
    for t in (s1, s2, s3, tv, tu):
        A.free(t)


def stats_rstd(K, xT, sq, rstd):
    P = K.P
    P.op("act", lambda e: e.activation(out=sq.ap, in_=xT.ap, func=AF.Square), reads=[xT.b], writes=[sq.b])
    fns = []
    for bi, (t0, n) in enumerate(BLKS):
        for c in range(16):
            fns.append(lambda e, bi=bi, t0=t0, n=n, c=c: e.matmul(K.ps[:, bi, 0:n], lhsT=K.onesm.ap, rhs=sq.ap[:, c, t0:t0 + n],
                                                                 start=(c == 0), stop=(c == 15)))
    P.pe(fns, reads=[sq.b, K.bC], writes=acc_bufs(K, 0))
    for pv, sl in acc_views(K, 0):
        P.op("act", lambda e, pv=pv, sl=sl: e.activation(out=rstd.ap[:, sl], in_=pv, func=AF.Sqrt, bias=K.epsc.ap, scale=1.0),
             reads=acc_bufs(K, 0) + [K.bC], writes=[rstd.b])
    P.op("dve", lambda e: e.reciprocal(out=rstd.ap, in_=rstd.ap), reads=[rstd.b], writes=[rstd.b])


def post_phase(K):
    P, I, A, O = K.P, K.I, K.A, K.O
    PSB = K.psb
    xT = A.alloc("xT", [16, NT], F32, top=True)
    rstd = A.alloc("rstd", [NT], F32, top=True)
    gcol = A.alloc("gcol", [2, 16], F32, top=True)
    act = [A.alloc(f"actT{i}", [8, NT], BF16, top=True) for i in range(1)]
    sgt = [A.alloc(f"sg{i}", [NT], BF16, top=True) for i in range(2)]
    xl = [A.alloc(f"xl{i}", [D], F32) for i in range(2)]
    for t in range(9):
        n = 128 if t < 8 else 64
        x_ = xl[t % 2]
        P.dma("sp", x_.ap[:n], I["xo"][t * 128:t * 128 + n, :], writes=[x_.b])
        for q4 in range(4):
            bank = q4 % 2
            pt = K.ps[:, bank, :].rearrange("p (c n) -> p c n", c=4)
            P.pe([lambda e, c=c, pt=pt, q4=q4: e.transpose(out=pt[:, c, 0:n], in_=x_.ap[:n, (4 * q4 + c) * 128:(4 * q4 + c + 1) * 128],
                                                           identity=K.identf.ap[:n, :n]) for c in range(4)],
                 reads=[x_.b, K.bC], writes=[PSB[bank]])
            P.op("act", lambda e, pt=pt, q4=q4: e.activation(out=xT.ap[:, 4 * q4:4 * q4 + 4, t * 128:t * 128 + n], in_=pt[:, :, 0:n], func=AF.Copy),
                 reads=[PSB[bank]], writes=[xT.b])
    for x_ in xl:
        A.free(x_)
    ring = Ring(K, 2)
    acc = 0
    for g in range(4):
        wv, wb = ring.load(I["w_out"][:, g * 512:(g + 1) * 512], 16, 512)
        for nt in range(4):
            n = 4 * g + nt
            fm_matmul(K, wv, wb, 16, nt, K.mT.ap, [K.mT.b], acc)
            for pv, sl in acc_views(K, acc):
                P.op("dve", lambda e, pv=pv, sl=sl, n=n: e.tensor_tensor(out=xT.ap[:, n, sl], in0=pv, in1=xT.ap[:, n, sl], op=ALU.add),
                     reads=acc_bufs(K, acc) + [xT.b], writes=[xT.b])
            acc ^= 1
    ring.free()
    A.free(K.mT)
    sq = A.alloc("sq", [16, NT], BF16)
    with K.nc.allow_non_contiguous_dma(reason="tiny"):
        P.dma("sp", gcol.ap[:, 0, :], I["norm_ffn"].rearrange("(c p) -> p c", p=128), writes=[gcol.b])
        P.dma("sp", gcol.ap[:, 1, :], I["norm_final"].rearrange("(c p) -> p c", p=128), writes=[gcol.b])
    stats_rstd(K, xT, sq, rstd)
    h2 = K.hT
    h2b = Buf("h2T")
    _merge(h2b.r, {})
    for b in K.hT.b:
        if b.w is not None:
            _merge(h2b.r, {b.w[0].num: b.w})
        _merge(h2b.r, b.r)
    for c in range(16):
        P.op("dve", lambda e, c=c: e.scalar_tensor_tensor(out=h2.ap[:, c, :], in0=xT.ap[:, c, :], scalar=gcol.ap[:, 0, c:c + 1], in1=rstd.ap,
                                                          op0=ALU.mult, op1=ALU.mult), reads=[xT.b, gcol.b, rstd.b], writes=[h2b])
    A.free(sq)
    ring = Ring(K, 3, 2)
    nsg = (HID + 1023) // 1024
    k = 0
    for sg in range(nsg):
        h0 = sg * 1024
        hw = min(1024, HID - h0)
        nft = hw // 128
        a_ = act[0]
        for half in range(hw // 512):
            wg = ring.load(I["w_ffn_in"][:, h0 + half * 512:h0 + half * 512 + 512], 16, 512)
            wu = ring.load(I["w_ffn_in"][:, HID + h0 + half * 512:HID + h0 + half * 512 + 512], 16, 512)
            for nt in range(4):
                f = 4 * half + nt
                s_ = sgt[k % 2]
                k += 1
                fm_matmul(K, wg[0], wg[1], 16, nt, h2.ap, [h2b], acc)
                for pv, sl in acc_views(K, acc):
                    P.op("act", lambda e, pv=pv, sl=sl, s_=s_: e.activation(out=s_.ap[:, sl], in_=pv, func=AF.Silu), reads=acc_bufs(K, acc), writes=[s_.b])
                acc ^= 1
                fm_matmul(K, wu[0], wu[1], 16, nt, h2.ap, [h2b], acc)
                for pv, sl in acc_views(K, acc):
                    P.op("dve", lambda e, pv=pv, sl=sl, s_=s_, f=f, a_=a_: e.tensor_tensor(out=a_.ap[:, f, sl], in0=pv, in1=s_.ap[:, sl], op=ALU.mult),
                         reads=acc_bufs(K, acc) + [s_.b], writes=[a_.b])
                acc ^= 1
        for g in range(4):
            wv, wb = ring.load(I["w_ffn_out"][h0:h0 + hw, g * 512:(g + 1) * 512], nft, 512)
            for nt in range(4):
                n = 4 * g + nt
                fm_matmul(K, wv, wb, nft, nt, a_.ap, [a_.b], acc)
                for pv, sl in acc_views(K, acc):
                    P.op("dve", lambda e, pv=pv, sl=sl, n=n: e.tensor_tensor(out=xT.ap[:, n, sl], in0=pv, in1=xT.ap[:, n, sl], op=ALU.add),
                         reads=acc_bufs(K, acc) + [xT.b], writes=[xT.b])
                acc ^= 1
    ring.free()
    for t in act + sgt:
        A.free(t)
    sq = A.alloc("sq2", [16, NT], BF16)
    stats_rstd(K, xT, sq, rstd)
    A.free(sq)
    yf = [A.alloc(f"yf{i}", [16, 128], F32) for i in range(2)]
    yo = [A.alloc(f"yo{i}", [D], F32) for i in range(2)]
    for t in range(9):
        n = 128 if t < 8 else 64
        cs = slice(t * 128, t * 128 + n)
        y_, o_ = yf[t % 2], yo[t % 2]
        for c in range(16):
            P.op("dve", lambda e, c=c: e.scalar_tensor_tensor(out=y_.ap[:, c, 0:n], in0=xT.ap[:, c, cs], scalar=gcol.ap[:, 1, c:c + 1], in1=rstd.ap[:, cs],
                                                              op0=ALU.mult, op1=ALU.mult), reads=[xT.b, gcol.b, rstd.b], writes=[y_.b])
        for q4 in range(4):
            bank = 6 + q4 % 2
            pt = K.ps[:n, bank, :].rearrange("p (c n) -> p c n", c=4)
            P.pe([lambda e, c=c, pt=pt, q4=q4: e.transpose(out=pt[:, c, :], in_=y_.ap[:, 4 * q4 + c, 0:n], identity=K.identf.ap) for c in range(4)],
                 reads=[y_.b, K.bC], writes=[PSB[bank]])
            P.op("act", lambda e, pt=pt, q4=q4: e.activation(out=o_.ap[:n, q4 * 512:(q4 + 1) * 512], in_=pt.rearrange("p c n -> p (c n)"), func=AF.Copy),
                 reads=[PSB[bank]], writes=[o_.b])
        P.dma("sp", O["yo"][t * 128:t * 128 + n, :], o_.ap[:n], reads=[o_.b], writes=[K.bout])


def build_program():
    nc = bass.Bass("TRN2", target_bir_lowering=False)
    K = Ctx()
    K.nc = nc
    K.P = Prog(nc)
    P = K.P

    def din(name, shape, dt=F32):
        return nc.dram_tensor(name, list(shape), dt, kind="ExternalInput").ap()

    def dout(name, shape, dt=F32):
        return nc.dram_tensor(name, list(shape), dt, kind="ExternalOutput").ap()

    I = {}
    I["xo"] = din("xo", [NT, D])
    I["xp"] = din("xp", [NPRE, D])
    I["w_in"] = din("w_in", [D, INC])
    I["w_glu_val"] = din("w_glu_val", [1024, D])
    I["w_glu_gate"] = din("w_glu_gate", [1024, D])
    I["w_attn_br"] = din("w_attn_br", [1024, D])
    I["w_out"] = din("w_out", [D, D])
    I["w_ffn_in"] = din("w_ffn_in", [D, 2 * HID])
    I["w_ffn_out"] = din("w_ffn_out", [HID, D])
    for nm in ["norm_attn", "norm_ffn", "norm_final"]:
        I[nm] = din(nm, [D])
    I["lam_re"] = din("lam_re", [64, 64])
    I["lam_im"] = din("lam_im", [64, 64])
    I["log_dt"] = din("log_dt", [64])
    I["b_re"] = din("b_re", [64, 64, 16])
    I["b_im"] = din("b_im", [64, 64, 16])
    I["c_re"] = din("c_re", [64, 16, 64])
    I["c_im"] = din("c_im", [64, 16, 64])
    I["d_skip"] = din("d_skip", [1024])
    I["sinks"] = din("sinks", [16])
    I["sre"] = din("sre", [16, 64, 64])
    I["sim"] = din("sim", [16, 64, 64])
    I["ck"] = din("ck", [16, 128, 256])
    I["cv"] = din("cv", [16, 128, 256])
    I["bmt"] = din("bmt", [128, 16, 2, 128])
    I["bmt0"] = din("bmt0", [128, 16, 128])
    I["bmsc"] = din("bmsc", [128, 16, 4])
    I["bmsn"] = din("bmsn", [4, 16, 4])
    I["kcol"] = din("kcol", [128, 1])
    I["m8"] = din("m8", [128, 8])
    I["m88"] = din("m88", [128, 8, 8])
    I["trow"] = din("trow", [128, 64])
    O = {}
    O["yo"] = dout("yo", [NT, D])
    O["pst"] = dout("pst", [2, 64, 64])
    O["pkv"] = dout("pkv", [128, 512])
    O["sst"] = dout("sst", [2, 16, 64, 64])
    O["skk"] = dout("skk", [16, 128, 256])
    O["skv"] = dout("skv", [16, 128, 256])
    if DEBUG:
        O["dbg_h"] = dout("dbg_h", [128, 2, 32])
        O["dbg_hT"] = dout("dbg_hT", [128, 16, NT], BF16)
        O["dbg_uT"] = dout("dbg_uT", [128, 8, NT], BF16)
        O["dbg_gT"] = dout("dbg_gT", [128, 8, NT], BF16)
        O["dbg_oT"] = dout("dbg_oT", [128, 8, NT], BF16)
        O["dbg_mT"] = dout("dbg_mT", [128, 16, NT], BF16)
    K.I, K.O = I, O
    K.bout = Buf("out")
    K.A = Arena(nc)
    A = K.A
    K.ps = nc.alloc_psum_tensor("psum", [128, 8, 512], F32)
    K.psb = [Buf(f"psb{i}") for i in range(8)]

    K.identf = A.alloc("identf", [128], F32, top=True)
    K.identb = A.alloc("identb", [128], BF16, top=True)
    K.ones64 = A.alloc("ones64", [64], BF16, top=True)
    K.onesm = A.alloc("onesm", [128], BF16, top=True)
    K.epsc = A.alloc("epsc", [1], F32, top=True)
    bC = Buf("const")
    K.bC = bC
    P.op("pool", lambda e: e.memset(K.identf.ap, 0.0), writes=[bC])
    P.op("pool", lambda e: e.affine_select(out=K.identf.ap, in_=K.identf.ap, pattern=[[-1, 128]], compare_op=ALU.not_equal,
                                          fill=1.0, base=0, channel_multiplier=1), reads=[bC], writes=[bC])
    P.op("pool", lambda e: e.tensor_copy(out=K.identb.ap, in_=K.identf.ap), reads=[bC], writes=[bC])
    P.op("pool", lambda e: e.memset(K.ones64.ap, 1.0), writes=[bC])
    P.op("pool", lambda e: e.memset(K.onesm.ap, 1.0 / D), writes=[bC])
    P.op("pool", lambda e: e.memset(K.epsc.ap, EPS), writes=[bC])

    K.hT = A.alloc("hT", [16, NT], BF16, nbufs=9, top=True)
    K.gT = A.alloc("gT", [8, NT], BF16, top=True)
    K.gT.b = [K.gT.b]
    K.oT = A.alloc("oT", [8, NT], BF16, top=True)
    K.oT.b = [K.oT.b]
    def stop(name):
        return STOP == name

    ssm_tables(K)
    if DEBUG:
        O["dbg_A4"] = dout("dbg_A4", [128, 2, 2, 32])
        O["dbg_A4c"] = dout("dbg_A4c", [128, 2, 2, 32])
        O["dbg_BbR"] = dout("dbg_BbR", [128, 32, 16])
        O["dbg_BbI"] = dout("dbg_BbI", [128, 32, 16])
        O["dbg_ApR"] = dout("dbg_ApR", [128, 64, 64], BF16)
        O["dbg_ApI"] = dout("dbg_ApI", [128, 64, 64], BF16)
        for nm, t in (("dbg_A4", K.A4), ("dbg_A4c", K.A4c), ("dbg_BbR", K.BbR), ("dbg_BbI", K.BbI), ("dbg_ApR", K.ApR), ("dbg_ApI", K.ApI)):
            P.dma("sp", O[nm], t.ap, reads=bl(t.b), writes=[K.bout])
    if not stop("tables"):
        K.kTh = A.alloc("kTh", [4, 128], BF16, top=True)
        K.vth = A.alloc("vth", [256], BF16, top=True)
        prefix_phase(K)
    if not (stop("tables") or stop("prefix")):
        K.uT = A.alloc("uT", [8, NT], BF16, top=True)
        K.uT.b = [K.uT.b]
        own_norm(K)
        for t in K.xts + K.xss + [K.junk, K.ss]:
            A.free(t)
        if DEBUG:
            P.dma("sp", O["dbg_hT"], K.hT.ap, reads=bl(K.hT.b), writes=[K.bout])
    if STOP not in ("tables", "prefix", "own_norm"):
        proj_phase(K)
        if DEBUG:
            P.dma("sp", O["dbg_uT"], K.uT.ap, reads=bl(K.uT.b), writes=[K.bout])
    if STOP not in ("tables", "prefix", "own_norm", "proj"):
        ssm_tables_own(K)
        A.free(K.BbR)
        A.free(K.BbI)
        ssm_own(K)
        if DEBUG:
            P.dma("sp", O["dbg_gT"], K.gT.ap, reads=bl(K.gT.b), writes=[K.bout])
    if STOP not in ("tables", "prefix", "own_norm", "proj", "ssm"):
        attention(K)
        if DEBUG:
            P.dma("sp", O["dbg_oT"], K.oT.ap, reads=bl(K.oT.b), writes=[K.bout])
    if STOP not in ("tables", "prefix", "own_norm", "proj", "ssm", "attn"):
        merge_phase(K)
        if DEBUG:
            P.dma("sp", O["dbg_mT"], K.mT.ap, reads=bl(K.mT.b), writes=[K.bout])
        A.free(K.gT)
        A.free(K.oT)
    if STOP not in ("tables", "prefix", "own_norm", "proj", "ssm", "attn", "merge"):
        post_phase(K)
    P.finish()
    K.ninstr = P.ninstr
    return nc, K


_CACHE = {}


def _host_consts(rel_bias, qidx):
    rb = np.asarray(rel_bias, np.float32)
    k = np.arange(128)[:, None, None]
    kb = np.arange(2)[None, :, None]
    q = np.arange(128)[None, None, :]
    dist = q + 128 - (kb * 128 + k)
    valid = (dist >= 0) & (dist < 128)
    bk = t5_bucket(dist)
    bias = rb[bk]
    bias = np.where(valid[..., None], bias, np.float32(NEG)).astype(np.float32)
    hperm = np.array([4 * (n // 4) + 2 * ((n % 4) % 2) + (n % 4) // 2 for n in range(16)])
    bmt = np.ascontiguousarray(np.transpose(bias, (0, 3, 1, 2))[:, hperm])
    bmt0 = bmt[:, :, 0, :].copy() if qidx > 0 else np.full((128, 16, 128), NEG, np.float32)
    j = np.arange(132)[:, None]
    t = np.arange(4)[None, :]
    dist = t + 128 - j
    valid = (dist >= 0) & (dist < 128)
    bs = np.where(valid[..., None], rb[t5_bucket(dist)], np.float32(NEG)).astype(np.float32)
    sperm = np.array([2 * (n % 8) + n // 8 for n in range(16)])
    bs = np.ascontiguousarray(np.transpose(bs, (0, 2, 1))[:, sperm])
    return bmt, np.ascontiguousarray(bmt0), np.ascontiguousarray(bs[:128]), np.ascontiguousarray(bs[128:])


def kernel(x_prompt, x_sample, state_ssm_re, state_ssm_im, cache_win_k, cache_win_v, rel_bias,
           norm_attn, w_in, lam_re, lam_im, log_dt, b_re, b_im, c_re, c_im, d_skip,
           w_glu_val, w_glu_gate, w_attn_br, sinks, w_out, norm_ffn, w_ffn_in, w_ffn_out, norm_final):
    f = lambda a: np.ascontiguousarray(np.asarray(a, dtype=np.float32))
    x_prompt, x_sample = f(x_prompt), f(x_sample)
    if "nc" not in _CACHE:
        _CACHE["nc"], _CACHE["K"] = build_program()
    nc = _CACHE["nc"]
    shared = {
        "w_in": f(w_in)[0], "w_glu_val": f(w_glu_val)[0], "w_glu_gate": f(w_glu_gate)[0], "w_attn_br": f(w_attn_br)[0],
        "w_out": f(w_out)[0], "w_ffn_in": f(w_ffn_in)[0], "w_ffn_out": f(w_ffn_out)[0],
        "norm_attn": f(norm_attn)[0], "norm_ffn": f(norm_ffn)[0], "norm_final": f(norm_final),
        "lam_re": f(lam_re)[0], "lam_im": f(lam_im)[0], "log_dt": f(log_dt)[0], "b_re": f(b_re)[0], "b_im": f(b_im)[0],
        "c_re": f(c_re)[0], "c_im": f(c_im)[0], "d_skip": f(d_skip)[0], "sinks": f(sinks)[0],
        "kcol": (127 - np.arange(128, dtype=np.float32)).reshape(128, 1),
        "m8": (np.arange(128)[:, None] // 16 == np.arange(8)[None, :]).astype(np.float32),
        "m88": np.ascontiguousarray(np.broadcast_to(np.eye(8, dtype=np.float32), (128, 8, 8))),
        "trow": np.ascontiguousarray(np.broadcast_to(np.arange(1, 65, dtype=np.float32), (128, 64))),
    }
    sre, sim = f(state_ssm_re)[0], f(state_ssm_im)[0]
    ck = f(cache_win_k)[0].reshape(128, 128, 256)
    cv = f(cache_win_v)[0].reshape(128, 128, 256)
    in_maps = []
    for c in range(8):
        b, q = c // 4, c % 4
        xo = np.concatenate([x_prompt[b, 1024 * q:1024 * (q + 1)], x_sample[16 * c:16 * c + 16].reshape(64, D)], axis=0)
        xp = np.zeros((NPRE, D), np.float32)
        if q > 0:
            xp[NPRE - 1024 * q:] = x_prompt[b, 0:1024 * q]
        bmt, bmt0, bmsc, bmsn = _host_consts(rel_bias, q)
        m = dict(shared)
        m.update({"xo": np.ascontiguousarray(xo), "xp": xp, "sre": sre[16 * c:16 * c + 16], "sim": sim[16 * c:16 * c + 16],
                  "ck": ck[16 * c:16 * c + 16], "cv": cv[16 * c:16 * c + 16], "bmt": bmt, "bmt0": bmt0, "bmsc": bmsc, "bmsn": bmsn})
        in_maps.append({k: np.ascontiguousarray(v) for k, v in m.items()})
    res = run_bass_kernel_spmd(nc, in_maps, core_ids=list(range(8)))
    R = res.results
    _CACHE["last"] = R
    y_prompt = np.zeros((2, 4096, D), np.float32)
    y_sample = np.zeros((128, 4, D), np.float32)
    p_re = np.zeros((1, 2, 64, 64), np.float32)
    p_im = np.zeros((1, 2, 64, 64), np.float32)
    p_k = np.zeros((1, 2, 128, 4, 64), np.float32)
    p_v = np.zeros((1, 2, 128, 4, 64), np.float32)
    s_re = np.zeros((1, 128, 64, 64), np.float32)
    s_im = np.zeros((1, 128, 64, 64), np.float32)
    s_k = np.zeros((1, 128, 128, 4, 64), np.float32)
    s_v = np.zeros((1, 128, 128, 4, 64), np.float32)
    for c in range(8):
        b, q = c // 4, c % 4
        r = R[c]
        y_prompt[b, 1024 * q:1024 * (q + 1)] = r["yo"][:1024]
        y_sample[16 * c:16 * c + 16] = r["yo"][1024:].reshape(16, 4, D)
        if q == 3:
            p_re[0, b] = r["pst"][0]
            p_im[0, b] = r["pst"][1]
            p_k[0, b] = r["pkv"][:, :256].reshape(128, 4, 64)
            p_v[0, b] = r["pkv"][:, 256:].reshape(128, 4, 64)
        s_re[0, 16 * c:16 * c + 16] = r["sst"][0]
        s_im[0, 16 * c:16 * c + 16] = r["sst"][1]
        s_k[0, 16 * c:16 * c + 16] = r["skk"].reshape(16, 128, 4, 64)
        s_v[0, 16 * c:16 * c + 16] = r["skv"].reshape(16, 128, 4, 64)
    return (y_prompt, y_sample, p_re, p_im, p_k, p_v, s_re, s_im, s_k, s_v)
```

```python
import numpy as np
import concourse.bass as bass
import concourse.mybir as mybir
from concourse.bass_utils import run_bass_kernel_spmd

F32 = mybir.dt.float32
BF16 = mybir.dt.bfloat16
I32 = mybir.dt.int32
AF = mybir.ActivationFunctionType
ALU = mybir.AluOpType
AX = mybir.AxisListType

D = 2048
NT = 1088
NPRE = 3072
NCH = 16
HID = 5632
INC = 6656
EPS = 1e-6
NEG = -1e30
BLKS = [(0, 512), (512, 512), (1024, 64)]
NDS = 24
NDS_SW = 8
ARENA_BYTES = 206 * 1024
DEBUG = False
STOP = None
ASTOP = None
DT_SIZE = {F32: 4, BF16: 2, I32: 4}


class Buf:
    __slots__ = ("name", "w", "r")

    def __init__(self, name="", guards=None):
        self.name = name
        self.w = None
        self.r = dict(guards) if guards else {}


def _merge(dst, src):
    for k, ev in src.items():
        if k not in dst or dst[k][1] < ev[1]:
            dst[k] = ev


class Tile:
    __slots__ = ("ap", "b", "off", "nbytes", "name")


class Arena:
    def __init__(self, nc):
        self.t = nc.alloc_sbuf_tensor("arena", [128, ARENA_BYTES // 4], F32)
        self.free_list = [[0, ARENA_BYTES, {}]]

    def alloc(self, name, shape, dt, nbufs=1, top=False):
        n = int(np.prod(shape)) * DT_SIZE[dt]
        n = (n + 63) // 64 * 64
        order = range(len(self.free_list) - 1, -1, -1) if top else range(len(self.free_list))
        for i in order:
            off, size, g = self.free_list[i]
            if size >= n:
                if size == n:
                    self.free_list.pop(i)
                elif top:
                    self.free_list[i] = [off, size - n, dict(g)]
                    off = off + size - n
                else:
                    self.free_list[i] = [off + n, size - n, dict(g)]
                t = Tile()
                t.name, t.off, t.nbytes = name, off, n
                v = self.t[:, off // 4:(off + n) // 4]
                if dt != F32:
                    v = v.bitcast(dt)
                ne = int(np.prod(shape))
                v = v[:, 0:ne]
                if len(shape) == 2:
                    v = v.rearrange("p (a b) -> p a b", a=shape[0])
                elif len(shape) == 3:
                    v = v.rearrange("p (a b c) -> p a b c", a=shape[0], b=shape[1])
                elif len(shape) == 4:
                    v = v.rearrange("p (a b c d) -> p a b c d", a=shape[0], b=shape[1], c=shape[2])
                t.ap = v
                if nbufs == 1:
                    t.b = Buf(name, g)
                else:
                    t.b = [Buf(f"{name}{j}", g) for j in range(nbufs)]
                return t
        raise RuntimeError(f"arena OOM for {name} ({n} B); free={[(o, s) for o, s, _ in self.free_list]}")

    def free(self, t):
        g = {}
        bufs = t.b if isinstance(t.b, list) else [t.b]
        for b in bufs:
            if b.w is not None:
                _merge(g, {b.w[0].num: b.w})
            _merge(g, b.r)
        self.free_list.append([t.off, t.nbytes, g])
        self.free_list.sort(key=lambda x: x[0])
        out = []
        for blk in self.free_list:
            if out and out[-1][0] + out[-1][1] == blk[0]:
                out[-1][1] += blk[1]
                _merge(out[-1][2], blk[2])
            else:
                out.append(blk)
        self.free_list = out


class Prog:
    def __init__(self, nc):
        self.nc = nc
        self.eng = {"pe": nc.tensor, "act": nc.scalar, "dve": nc.vector, "pool": nc.gpsimd, "sp": nc.sync}
        self.csem = {e: nc.alloc_semaphore(name=f"c_{e}") for e in self.eng}
        self.ccnt = {e: 0 for e in self.eng}
        self.dsems = [nc.alloc_semaphore(name=f"d_{i}") for i in range(NDS)]
        self.dcnt = [0] * NDS
        self.dnext = 0
        self.dnext_sw = 0
        self.waited = {e: {} for e in self.eng}
        self.ninstr = 0

    def _wait(self, e, ev):
        sem, val = ev
        k = sem.num
        if self.waited[e].get(k, 0) >= val:
            return
        self.eng[e].wait_ge(sem, val)
        self.waited[e][k] = val

    def _deps(self, e, reads, writes, skip=None, relax=False):
        own = self.csem[e].num
        lim = self.ccnt[e] - 1

        def need(ev):
            if ev[0].num == skip:
                return False
            if relax and ev[0].num == own and ev[1] <= lim:
                return False
            return True
        for b in reads:
            if b.w is not None and need(b.w):
                self._wait(e, b.w)
        for b in writes:
            if b.w is not None and need(b.w):
                self._wait(e, b.w)
            for k, ev in b.r.items():
                if need(ev):
                    self._wait(e, ev)

    @staticmethod
    def _mark(ev, reads, writes):
        for b in reads:
            b.r[ev[0].num] = ev
        for b in writes:
            b.w = ev
            b.r = {}

    def op(self, e, fn, reads=(), writes=(), relax=False):
        self._deps(e, reads, writes, relax=relax)
        ins = fn(self.eng[e])
        self.ccnt[e] += 1
        self.ninstr += 1
        ins.then_inc(self.csem[e], 1)
        self._mark((self.csem[e], self.ccnt[e]), reads, writes)

    def group(self, e, fns, reads=(), writes=()):
        self._deps(e, reads, writes)
        ins = None
        for fn in fns:
            ins = fn(self.eng[e])
            self.ninstr += 1
        self.ccnt[e] += 1
        ins.then_inc(self.csem[e], 1)
        self._mark((self.csem[e], self.ccnt[e]), reads, writes)

    def pe(self, fns, reads=(), writes=()):
        self._deps("pe", reads, writes, skip=self.csem["pe"].num)
        ins = None
        for fn in fns:
            ins = fn(self.nc.tensor)
            self.ninstr += 1
        self.ccnt["pe"] += 1
        ins.then_inc(self.csem["pe"], 1)
        self._mark((self.csem["pe"], self.ccnt["pe"]), reads, writes)

    def dma(self, e, out, in_, reads=(), writes=(), **kw):
        if e == "pool":
            i = self.dnext_sw
            self.dnext_sw = (i + 1) % NDS_SW
        else:
            i = NDS_SW + self.dnext
            self.dnext = (self.dnext + 1) % (NDS - NDS_SW)
        if self.dcnt[i] > 0:
            self._wait(e, (self.dsems[i], self.dcnt[i]))
        self._deps(e, reads, writes)
        ins = self.eng[e].dma_start(out=out, in_=in_, **kw)
        self.ninstr += 1
        self.dcnt[i] += 16
        ins.then_inc(self.dsems[i], 16)
        self._mark((self.dsems[i], self.dcnt[i]), reads, writes)

    def finish(self):
        for e in self.eng:
            if e != "sp" and self.ccnt[e] > 0:
                self._wait("sp", (self.csem[e], self.ccnt[e]))
        for i in range(NDS):
            if self.dcnt[i] > 0:
                self._wait("sp", (self.dsems[i], self.dcnt[i]))


def t5_bucket(dist):
    n = np.maximum(dist, 0)
    max_exact = 16
    large = max_exact + (np.log(np.maximum(n, 1) / max_exact) / np.log(128 / max_exact) * 16).astype(np.int32)
    large = np.minimum(large, 31)
    return np.where(n < max_exact, n, large).astype(np.int32)


class Ctx:
    pass


def share(t, buf):
    if isinstance(t.b, Buf) and t.b is not buf:
        _merge(buf.r, t.b.r)
    t.b = buf


def bl(b):
    return b if isinstance(b, list) else [b]


def range_reduce_sin(K, ang, bang, out, bout, shape, part=128):
    P, A = K.P, K.A
    ni = A.alloc("rr_i", shape, I32)
    nf = A.alloc("rr_f", shape, F32)
    C1 = 6.28125
    C2 = float(2 * np.pi - 6.28125)
    P.op("dve", lambda e: e.tensor_scalar(out=nf.ap, in0=ang, scalar1=float(1.0 / (2 * np.pi)), scalar2=None, op0=ALU.mult),
         reads=[bang], writes=[nf.b])
    P.op("dve", lambda e: e.tensor_copy(out=ni.ap, in_=nf.ap), reads=[nf.b], writes=[ni.b])
    P.op("dve", lambda e: e.tensor_copy(out=nf.ap, in_=ni.ap), reads=[ni.b], writes=[nf.b])
    P.op("dve", lambda e: e.scalar_tensor_tensor(out=ang, in0=nf.ap, scalar=-C1, in1=ang, op0=ALU.mult, op1=ALU.add),
         reads=[nf.b, bang], writes=[bang])
    P.op("dve", lambda e: e.scalar_tensor_tensor(out=ang, in0=nf.ap, scalar=-C2, in1=ang, op0=ALU.mult, op1=ALU.add),
         reads=[nf.b, bang], writes=[bang])
    P.op("dve", lambda e: e.tensor_scalar(out=ang, in0=ang, scalar1=3.1415925, scalar2=-3.1415925, op0=ALU.min, op1=ALU.max),
         reads=[bang], writes=[bang])
    P.op("act", lambda e: e.activation(out=out, in_=ang, func=AF.Sin), reads=[bang], writes=[bout])
    A.free(ni)
    A.free(nf)


def accurate_exp(K, x, bx, shape):
    P, A = K.P, K.A
    y = A.alloc("aexp_y", shape, F32)
    acc = A.alloc("aexp_a", shape, F32)
    P.op("dve", lambda e: e.tensor_scalar(out=y.ap, in0=x, scalar1=0.125, scalar2=None, op0=ALU.mult), reads=[bx], writes=[y.b])
    fact = [1.0]
    for i in range(1, 12):
        fact.append(fact[-1] * i)
    P.op("dve", lambda e: e.tensor_scalar(out=acc.ap, in0=y.ap, scalar1=1.0 / fact[11], scalar2=1.0 / fact[10], op0=ALU.mult, op1=ALU.add),
         reads=[y.b], writes=[acc.b])
    for i in range(9, -1, -1):
        P.op("dve", lambda e: e.tensor_tensor(out=acc.ap, in0=acc.ap, in1=y.ap, op=ALU.mult), reads=[acc.b, y.b], writes=[acc.b])
        P.op("dve", lambda e, i=i: e.tensor_scalar(out=acc.ap, in0=acc.ap, scalar1=float(1.0 / fact[i]), scalar2=None, op0=ALU.add),
             reads=[acc.b], writes=[acc.b])
    for _ in range(3):
        P.op("dve", lambda e: e.tensor_tensor(out=acc.ap, in0=acc.ap, in1=acc.ap, op=ALU.mult), reads=[acc.b], writes=[acc.b])
    P.op("dve", lambda e: e.tensor_copy(out=x, in_=acc.ap), reads=[acc.b, bx], writes=[bx])
    A.free(y)
    A.free(acc)


def ssm_tables(K):
    nc, P, I, A = K.nc, K.P, K.I, K.A
    PSB = K.psb
    lrG = A.alloc("lrG", [32], F32)
    liG = A.alloc("liG", [32], F32)
    dtG = A.alloc("dtG", [32], F32)
    bG = Buf("G")
    with nc.allow_non_contiguous_dma(reason="tiny transposed param loads"):
        for gh in range(2):
            rows = slice(64 * gh, 64 * gh + 64)
            P.dma("sp", lrG.ap[rows, :], I["lam_re"][32 * gh:32 * gh + 32, :].rearrange("g p -> p g"), writes=[bG])
            P.dma("sp", liG.ap[rows, :], I["lam_im"][32 * gh:32 * gh + 32, :].rearrange("g p -> p g"), writes=[bG])
            P.dma("sp", dtG.ap[rows, :], I["log_dt"][32 * gh:32 * gh + 32].unsqueeze(0).broadcast_to([64, 32]), writes=[bG])
    accurate_exp(K, dtG.ap, bG, [32])
    P.op("dve", lambda e: e.tensor_scalar(out=lrG.ap, in0=lrG.ap, scalar1=-1e-4, scalar2=None, op0=ALU.min), reads=[bG], writes=[bG])
    rho = A.alloc("rhoG", [32], F32)
    th = A.alloc("thG", [32], F32)
    P.op("dve", lambda e: e.tensor_tensor(out=rho.ap, in0=lrG.ap, in1=dtG.ap, op=ALU.mult), reads=[bG], writes=[bG])
    P.op("dve", lambda e: e.tensor_tensor(out=th.ap, in0=liG.ap, in1=dtG.ap, op=ALU.mult), reads=[bG], writes=[bG])
    tmps = [lrG, liG, dtG, rho, th]

    def TTv(o, a, b_, op, rd, wb):
        P.op("dve", lambda e: e.tensor_tensor(out=o, in0=a, in1=b_, op=op), reads=rd, writes=[wb])

    def TSv(o, a, s1, s2, op0, op1, rd, wb):
        if op1 is None:
            P.op("dve", lambda e: e.tensor_scalar(out=o, in0=a, scalar1=s1, scalar2=None, op0=op0), reads=rd, writes=[wb])
        else:
            P.op("dve", lambda e: e.tensor_scalar(out=o, in0=a, scalar1=s1, scalar2=s2, op0=op0, op1=op1), reads=rd, writes=[wb])

    b1 = Buf("a1")
    nm = lambda n: A.alloc(n, [32], F32)
    r_, x2, sn, cs, t_, ex, a1r, a1i = [nm(n) for n in ("r_", "x2", "sn", "cs", "t_", "ex", "a1r", "a1i")]
    ni = A.alloc("ni", [32], I32)
    for t in (r_, x2, sn, cs, t_, ex, a1r, a1i, ni):
        share(t, b1)
    tmps.extend([r_, x2, sn, cs, t_, ex, a1r, a1i, ni])
    C1 = 6.28125
    C2 = float(2 * np.pi - 6.28125)
    TSv(t_.ap, th.ap, float(1.0 / (2 * np.pi)), None, ALU.mult, None, [bG], b1)
    P.op("dve", lambda e: e.tensor_copy(out=ni.ap, in_=t_.ap), reads=[b1], writes=[b1])
    P.op("dve", lambda e: e.tensor_copy(out=t_.ap, in_=ni.ap), reads=[b1], writes=[b1])
    P.op("dve", lambda e: e.scalar_tensor_tensor(out=r_.ap, in0=t_.ap, scalar=-C1, in1=th.ap, op0=ALU.mult, op1=ALU.add), reads=[b1, bG], writes=[b1])
    P.op("dve", lambda e: e.scalar_tensor_tensor(out=r_.ap, in0=t_.ap, scalar=-C2, in1=r_.ap, op0=ALU.mult, op1=ALU.add), reads=[b1], writes=[b1])
    TSv(r_.ap, r_.ap, 0.125, None, ALU.mult, None, [b1], b1)
    TTv(x2.ap, r_.ap, r_.ap, ALU.mult, [b1], b1)
    TSv(sn.ap, x2.ap, 1.0 / 362880, -1.0 / 5040, ALU.mult, ALU.add, [b1], b1)
    for cf in (1.0 / 120, -1.0 / 6, 1.0):
        TTv(sn.ap, sn.ap, x2.ap, ALU.mult, [b1], b1)
        TSv(sn.ap, sn.ap, float(cf), None, ALU.add, None, [b1], b1)
    TTv(sn.ap, sn.ap, r_.ap, ALU.mult, [b1], b1)
    TSv(cs.ap, x2.ap, -1.0 / 3628800, 1.0 / 40320, ALU.mult, ALU.add, [b1], b1)
    for cf in (-1.0 / 720, 1.0 / 24, -0.5, 1.0):
        TTv(cs.ap, cs.ap, x2.ap, ALU.mult, [b1], b1)
        TSv(cs.ap, cs.ap, float(cf), None, ALU.add, None, [b1], b1)
    for _ in range(3):
        TTv(t_.ap, sn.ap, cs.ap, ALU.mult, [b1], b1)
        TTv(x2.ap, sn.ap, sn.ap, ALU.mult, [b1], b1)
        TSv(sn.ap, t_.ap, 2.0, None, ALU.mult, None, [b1], b1)
        TSv(cs.ap, x2.ap, -2.0, 1.0, ALU.mult, ALU.add, [b1], b1)
    TSv(ex.ap, rho.ap, 1.0 / 120, 1.0 / 24, ALU.mult, ALU.add, [bG], b1)
    for cf in (1.0 / 6, 0.5, 1.0, 1.0):
        TTv(ex.ap, ex.ap, rho.ap, ALU.mult, [b1, bG], b1)
        TSv(ex.ap, ex.ap, float(cf), None, ALU.add, None, [b1], b1)
    TTv(a1r.ap, ex.ap, cs.ap, ALU.mult, [b1], b1)
    TTv(a1i.ap, ex.ap, sn.ap, ALU.mult, [b1], b1)
    b128 = Buf("a128")
    a128r, a128i, q1, q2 = [nm(n) for n in ("a128r", "a128i", "q1", "q2")]
    for t in (a128r, a128i, q1, q2):
        share(t, b128)
    tmps.extend([a128r, a128i, q1, q2])
    P.op("dve", lambda e: e.tensor_copy(out=a128r.ap, in_=a1r.ap), reads=[b1], writes=[b128])
    P.op("dve", lambda e: e.tensor_copy(out=a128i.ap, in_=a1i.ap), reads=[b1], writes=[b128])
    K.Rm = A.alloc("Rm", [32], F32)
    K.thG = A.alloc("thGk", [32], F32)
    K.c64 = A.alloc("c64", [32], F32)
    K.s64 = A.alloc("s64", [32], F32)
    r64 = nm("r64")
    share(r64, b128)
    tmps.append(r64)
    P.op("dve", lambda e: e.tensor_copy(out=K.Rm.ap, in_=ex.ap), reads=[b1], writes=[K.Rm.b])
    P.op("dve", lambda e: e.tensor_copy(out=K.thG.ap, in_=th.ap), reads=[bG], writes=[K.thG.b])
    P.op("dve", lambda e: e.tensor_copy(out=r64.ap, in_=ex.ap), reads=[b1], writes=[b128])
    for _ in range(6):
        TTv(r64.ap, r64.ap, r64.ap, ALU.mult, [b128], b128)
    P.op("dve", lambda e: e.reciprocal(out=r64.ap, in_=r64.ap), reads=[b128], writes=[b128])
    for it in range(7):
        if it == 6:
            TTv(K.c64.ap, a128r.ap, r64.ap, ALU.mult, [b128], K.c64.b)
            TTv(K.s64.ap, a128i.ap, r64.ap, ALU.mult, [b128], K.s64.b)
        TTv(q1.ap, a128r.ap, a128r.ap, ALU.mult, [b128], b128)
        TTv(q2.ap, a128i.ap, a128i.ap, ALU.mult, [b128], b128)
        TTv(a128i.ap, a128r.ap, a128i.ap, ALU.mult, [b128], b128)
        TSv(a128i.ap, a128i.ap, 2.0, None, ALU.mult, None, [b128], b128)
        TTv(a128r.ap, q1.ap, q2.ap, ALU.subtract, [b128], b128)

    def make_a4(ar, ai, b, name):
        a4 = A.alloc(name, [2, 2, 32], F32)
        P.op("dve", lambda e: e.tensor_copy(out=a4.ap[:, 0, 0, :], in_=ar.ap), reads=[b], writes=[a4.b])
        P.op("dve", lambda e: e.tensor_scalar(out=a4.ap[:, 0, 1, :], in0=ai.ap, scalar1=-1.0, scalar2=None, op0=ALU.mult), reads=[b], writes=[a4.b])
        P.op("dve", lambda e: e.tensor_copy(out=a4.ap[:, 1, 0, :], in_=ai.ap), reads=[b], writes=[a4.b])
        P.op("dve", lambda e: e.tensor_copy(out=a4.ap[:, 1, 1, :], in_=ar.ap), reads=[b], writes=[a4.b])
        return a4

    K.A4 = make_a4(a1r, a1i, b1, "A4_1")
    K.A4c = make_a4(a128r, a128i, b128, "A4_128")

    den = A.alloc("den", [32], F32)
    t1 = A.alloc("ct1", [32], F32)
    t2 = A.alloc("ct2", [32], F32)
    cr = A.alloc("coefr", [32], F32)
    ci = A.alloc("coefi", [32], F32)
    am1 = A.alloc("am1", [32], F32)
    bc = Buf("coef")
    for t in (den, t1, t2, cr, ci, am1):
        share(t, bc)
    tmps.extend([den, t1, t2, cr, ci, am1])
    TT = lambda o, a, b_, op, rd: P.op("dve", lambda e: e.tensor_tensor(out=o, in0=a, in1=b_, op=op), reads=rd, writes=[bc])
    TT(den.ap, lrG.ap, lrG.ap, ALU.mult, [bG])
    TT(t1.ap, liG.ap, liG.ap, ALU.mult, [bG])
    TT(den.ap, den.ap, t1.ap, ALU.add, [bc])
    P.op("dve", lambda e: e.reciprocal(out=den.ap, in_=den.ap), reads=[bc], writes=[bc])
    P.op("dve", lambda e: e.tensor_scalar(out=am1.ap, in0=a1r.ap, scalar1=-1.0, scalar2=None, op0=ALU.add), reads=[b1], writes=[bc])
    TT(t1.ap, am1.ap, lrG.ap, ALU.mult, [bc, bG])
    TT(t2.ap, a1i.ap, liG.ap, ALU.mult, [b1, bG])
    TT(t1.ap, t1.ap, t2.ap, ALU.add, [bc])
    TT(cr.ap, t1.ap, den.ap, ALU.mult, [bc])
    TT(t1.ap, a1i.ap, lrG.ap, ALU.mult, [bc, b1, bG])
    TT(t2.ap, am1.ap, liG.ap, ALU.mult, [bc, bG])
    TT(t1.ap, t1.ap, t2.ap, ALU.subtract, [bc])
    TT(ci.ap, t1.ap, den.ap, ALU.mult, [bc])

    K.BbR = A.alloc("BbR", [32, 16], F32)
    K.BbI = A.alloc("BbI", [32, 16], F32)
    bBb = Buf("Bbar")
    share(K.BbR, bBb)
    share(K.BbI, bBb)
    BreG = A.alloc("BreG", [32, 16], F32)
    BimG = A.alloc("BimG", [32, 16], F32)
    tb1 = A.alloc("tb1", [32, 16], F32)
    bB = Buf("B")
    share(BreG, bB)
    share(BimG, bB)
    with nc.allow_non_contiguous_dma(reason="64B runs param load"):
        for gh in range(2):
            rows = slice(64 * gh, 64 * gh + 64)
            P.dma("sp", BreG.ap[rows], I["b_re"][32 * gh:32 * gh + 32].rearrange("g p c -> p g c"), writes=[bB])
            P.dma("sp", BimG.ap[rows], I["b_im"][32 * gh:32 * gh + 32].rearrange("g p c -> p g c"), writes=[bB])
    crb = cr.ap.unsqueeze(2).broadcast_to([128, 32, 16])
    cib = ci.ap.unsqueeze(2).broadcast_to([128, 32, 16])
    P.op("dve", lambda e: e.tensor_tensor(out=K.BbR.ap, in0=BreG.ap, in1=crb, op=ALU.mult), reads=[bB, bc], writes=[bBb])
    P.op("dve", lambda e: e.tensor_tensor(out=tb1.ap, in0=BimG.ap, in1=cib, op=ALU.mult), reads=[bB, bc], writes=[tb1.b])
    P.op("dve", lambda e: e.tensor_tensor(out=K.BbR.ap, in0=K.BbR.ap, in1=tb1.ap, op=ALU.subtract), reads=[tb1.b, bBb], writes=[bBb])
    P.op("dve", lambda e: e.tensor_tensor(out=K.BbI.ap, in0=BimG.ap, in1=crb, op=ALU.mult), reads=[bB, bc], writes=[bBb])
    P.op("dve", lambda e: e.tensor_tensor(out=tb1.ap, in0=BreG.ap, in1=cib, op=ALU.mult), reads=[bB, bc, bBb], writes=[tb1.b])
    P.op("dve", lambda e: e.tensor_tensor(out=K.BbI.ap, in0=K.BbI.ap, in1=tb1.ap, op=ALU.add), reads=[tb1.b, bBb], writes=[bBb])
    share(BreG, bB)
    share(BimG, bB)

    K.Dcol = A.alloc("Dcol", [8], F32)
    with nc.allow_non_contiguous_dma(reason="tiny"):
        P.dma("sp", K.Dcol.ap, I["d_skip"].rearrange("(j p) -> p j", p=128), writes=[K.Dcol.b])

    K.ApR = A.alloc("ApR", [64, 64], BF16)
    K.ApI = A.alloc("ApI", [64, 64], BF16)
    bAp = Buf("ApowT")
    share(K.ApR, bAp)
    share(K.ApI, bAp)
    kcol = A.alloc("kcol", [1], F32)
    dtF = A.alloc("dtF", [64], F32)
    bF = Buf("F")
    share(kcol, bF)
    share(dtF, bF)
    P.dma("sp", kcol.ap, I["kcol"], writes=[bF])
    P.dma("sp", dtF.ap, I["log_dt"].unsqueeze(0).broadcast_to([128, 64]), writes=[bF])
    accurate_exp(K, dtF.ap, bF, [64])
    for hf in range(2):
        gs = slice(32 * hf, 32 * hf + 32)
        lrF = A.alloc("lrF", [32, 64], F32)
        liF = A.alloc("liF", [32, 64], F32)
        magF = A.alloc("magF", [32, 64], F32)
        snF = A.alloc("snF", [32, 64], F32)
        bH = Buf("Fh")
        for t in (lrF, liF, magF, snF):
            share(t, bH)
        P.dma("sp", lrF.ap, I["lam_re"][gs].unsqueeze(0).broadcast_to([128, 32, 64]), writes=[bH])
        P.dma("sp", liF.ap, I["lam_im"][gs].unsqueeze(0).broadcast_to([128, 32, 64]), writes=[bH])
        dtb = dtF.ap[:, gs].unsqueeze(2).broadcast_to([128, 32, 64])
        P.op("dve", lambda e: e.tensor_scalar(out=lrF.ap, in0=lrF.ap, scalar1=-1e-4, scalar2=None, op0=ALU.min), reads=[bH], writes=[bH])
        P.op("dve", lambda e: e.tensor_tensor(out=lrF.ap, in0=lrF.ap, in1=dtb, op=ALU.mult), reads=[bH, bF], writes=[bH])
        P.op("dve", lambda e: e.tensor_tensor(out=liF.ap, in0=liF.ap, in1=dtb, op=ALU.mult), reads=[bH, bF], writes=[bH])
        P.op("act", lambda e: e.activation(out=magF.ap, in_=lrF.ap, func=AF.Exp, scale=kcol.ap), reads=[bH, bF], writes=[bH])
        P.op("dve", lambda e: e.tensor_scalar(out=lrF.ap, in0=liF.ap, scalar1=kcol.ap, scalar2=None, op0=ALU.mult), reads=[bH, bF], writes=[bH])
        range_reduce_sin(K, lrF.ap, bH, snF.ap, bH, [32, 64])
        P.op("dve", lambda e: e.tensor_tensor(out=K.ApI.ap[:, gs, :], in0=magF.ap, in1=snF.ap, op=ALU.mult), reads=[bH], writes=[bAp])
        P.op("dve", lambda e: e.tensor_scalar(out=lrF.ap, in0=liF.ap, scalar1=kcol.ap, scalar2=float(np.pi / 2), op0=ALU.mult, op1=ALU.add),
             reads=[bH, bF], writes=[bH])
        range_reduce_sin(K, lrF.ap, bH, snF.ap, bH, [32, 64])
        P.op("dve", lambda e: e.tensor_tensor(out=K.ApR.ap[:, gs, :], in0=magF.ap, in1=snF.ap, op=ALU.mult), reads=[bH], writes=[bAp])
        for t in (lrF, liF, magF, snF):
            A.free(t)
    for t in (kcol, dtF, BreG, BimG, tb1):
        A.free(t)
    for t in tmps:
        A.free(t)


def ssm_tables_own(K):
    nc, P, I, A = K.nc, K.P, K.I, K.A
    PSB = K.psb
    bBb = K.BbR.b
    m8 = A.alloc("m8", [8], F32)
    m88 = A.alloc("m88", [8, 8], F32)
    bm = Buf("masks")
    share(m8, bm)
    share(m88, bm)
    P.dma("sp", m8.ap, I["m8"], writes=[bm])
    P.dma("sp", m88.ap, I["m88"], writes=[bm])
    K.BtabR = A.alloc("BtabR", [64, 64], BF16)
    K.BtabI = A.alloc("BtabI", [64, 64], BF16)
    bBtab = Buf("Btab")
    share(K.BtabR, bBtab)
    share(K.BtabI, bBtab)
    K.CtabR = A.alloc("CtabR", [32, 128], BF16)
    K.CtabI = A.alloc("CtabI", [32, 128], BF16)
    bCtab = Buf("Ctab")
    share(K.CtabR, bCtab)
    share(K.CtabI, bCtab)
    cnat_r = A.alloc("cnat_r", [8, 64], F32)
    cnat_i = A.alloc("cnat_i", [8, 64], F32)
    bcn = Buf("cnat")
    share(cnat_r, bcn)
    share(cnat_i, bcn)
    P.dma("sp", cnat_r.ap, I["c_re"].rearrange("(j g) c p -> (g c) j p", j=8), writes=[bcn])
    P.dma("sp", cnat_i.ap, I["c_im"].rearrange("(j g) c p -> (g c) j p", j=8), writes=[bcn])
    cnt = 0
    for src, dst in ((K.BbR, K.BtabR), (K.BbI, K.BtabI)):
        for j in range(8):
            gh, gq = j // 4, (j % 4) * 8
            pb = cnt % 2
            cnt += 1
            rows = slice(64 * gh, 64 * gh + 64)
            inp = src.ap[rows, gq:gq + 8, :]
            pt = K.ps[:, pb, 0:64]
            P.pe([lambda e, inp=inp, pt=pt, rows=rows: e.transpose(out=pt, in_=inp, identity=K.identf.ap[rows, rows])],
                 reads=[bBb, K.bC], writes=[PSB[pb]])
            P.op("dve", lambda e, pt=pt, dst=dst, j=j: e.tensor_tensor(
                out=dst.ap[:, 8 * j:8 * j + 8, :], in0=pt.unsqueeze(1).broadcast_to([128, 8, 64]),
                in1=m8.ap.unsqueeze(2).broadcast_to([128, 8, 64]), op=ALU.mult), reads=[PSB[pb], bm], writes=[bBtab])
    P.op("dve", lambda e: e.tensor_scalar(out=cnat_i.ap, in0=cnat_i.ap, scalar1=-1.0, scalar2=None, op0=ALU.mult), reads=[bcn], writes=[bcn])
    for src, dst in ((cnat_r, K.CtabR), (cnat_i, K.CtabI)):
        for j in range(8):
            gh, gq = j // 4, (j % 4) * 8
            pb = cnt % 2
            cnt += 1
            rows = slice(64 * gh, 64 * gh + 64)
            pt = K.ps[rows, pb, 0:128]
            P.pe([lambda e, src=src, j=j, pt=pt, gh=gh: e.matmul(pt, lhsT=src.ap[:, j, :], rhs=K.identf.ap, start=True, stop=True,
                                                                 tile_position=(0, 64 * gh))],
                 reads=[bcn, K.bC], writes=[PSB[pb]])
            P.op("dve", lambda e, pt=pt, dst=dst, rows=rows, gq=gq: e.tensor_tensor(
                out=dst.ap[rows, gq:gq + 8, :].rearrange("p g (h c) -> p g h c", h=8),
                in0=pt.rearrange("p (g c) -> p g c", g=8).unsqueeze(2).broadcast_to([64, 8, 8, 16]),
                in1=m88.ap[rows].unsqueeze(3).broadcast_to([64, 8, 8, 16]), op=ALU.mult),
                reads=[PSB[pb], bm], writes=[bCtab])
    for t in (cnat_r, cnat_i, m8, m88):
        A.free(t)
    K.CT = A.alloc("CT", [32, 64], BF16)
    K.ST = A.alloc("ST", [32, 64], BF16)
    trow = A.alloc("trow", [64], F32)
    ang = A.alloc("angT", [32, 64], F32)
    sct = A.alloc("sct", [32, 64], F32)
    P.dma("sp", trow.ap, I["trow"], writes=[trow.b])
    thb = K.thG.ap.unsqueeze(2).broadcast_to([128, 32, 64])
    trb = trow.ap.unsqueeze(1).broadcast_to([128, 32, 64])
    P.op("dve", lambda e: e.tensor_tensor(out=ang.ap, in0=thb, in1=trb, op=ALU.mult), reads=[K.thG.b, trow.b], writes=[ang.b])
    range_reduce_sin(K, ang.ap, ang.b, sct.ap, sct.b, [32, 64])
    P.op("dve", lambda e: e.tensor_copy(out=K.ST.ap, in_=sct.ap), reads=[sct.b], writes=[K.ST.b])
    P.op("dve", lambda e: e.tensor_tensor(out=ang.ap, in0=thb, in1=trb, op=ALU.mult), reads=[K.thG.b, trow.b, ang.b], writes=[ang.b])
    P.op("dve", lambda e: e.tensor_scalar(out=ang.ap, in0=ang.ap, scalar1=float(np.pi / 2), scalar2=None, op0=ALU.add), reads=[ang.b], writes=[ang.b])
    range_reduce_sin(K, ang.ap, ang.b, sct.ap, sct.b, [32, 64])
    P.op("dve", lambda e: e.tensor_copy(out=K.CT.ap, in_=sct.ap), reads=[sct.b], writes=[K.CT.b])
    for t in (trow, ang, sct):
        A.free(t)


def load_weights_resident(K, name, src, kc, ncols):
    t = K.A.alloc(name, [kc, ncols], BF16)
    for c0 in range(0, ncols, 512):
        w = min(512, ncols - c0)
        K.P.dma("pool", t.ap[:, :, c0:c0 + w], src[:, c0:c0 + w].rearrange("(c p) n -> p c n", p=128), writes=[t.b])
    return t


def norm_rows(K, xt, n, gb, xs, junk, ss):
    P = K.P
    P.op("act", lambda e: e.activation(out=junk.ap[:n], in_=xt.ap[:n], func=AF.Square, accum_out=ss.ap[:n]),
         reads=[xt.b], writes=[junk.b, ss.b])
    P.op("act", lambda e: e.activation(out=ss.ap[:n], in_=ss.ap[:n], func=AF.Sqrt, bias=K.epsc.ap[:n], scale=1.0 / D),
         reads=[ss.b, K.bC], writes=[ss.b])
    P.op("dve", lambda e: e.reciprocal(out=ss.ap[:n], in_=ss.ap[:n]), reads=[ss.b], writes=[ss.b])
    P.op("dve", lambda e: e.scalar_tensor_tensor(out=xs.ap[:n], in0=xt.ap[:n], scalar=ss.ap[:n], in1=gb.ap[:n],
                                                 op0=ALU.mult, op1=ALU.mult), reads=[xt.b, ss.b, gb.b], writes=[xs.b])


def transpose_rows(K, xs, n, dst_ap, dst_bufs, pbank):
    P = K.P
    ptb = K.ps[:, pbank:pbank + 2, :].rearrange("p a b -> p (a b)").bitcast(BF16)
    ptv = ptb.rearrange("p (c n) -> p c n", c=16)
    pbufs = [K.psb[pbank], K.psb[pbank + 1]]
    P.pe([lambda e, c=c: e.transpose(out=ptv[:, c, 0:n], in_=xs.ap[:n, c * 128:(c + 1) * 128], identity=K.identb.ap[:n, :n])
          for c in range(16)], reads=[xs.b, K.bC], writes=pbufs)
    P.op("act", lambda e: e.activation(out=dst_ap, in_=ptv[:, :, 0:n], func=AF.Copy), reads=pbufs, writes=dst_bufs)


def prefix_phase(K):
    nc, P, I, A = K.nc, K.P, K.I, K.A
    PSB = K.psb
    Wu = load_weights_resident(K, "Wu", I["w_in"][:, 0:1024], 16, 1024)
    K.gb = A.alloc("gb", [D], F32)
    P.dma("sp", K.gb.ap, I["norm_attn"].unsqueeze(0).broadcast_to([128, D]), writes=[K.gb.b])
    xts = [A.alloc(f"xt{i}", [D], F32) for i in range(2)]
    K.junk = A.alloc("junk", [D], BF16)
    K.ss = A.alloc("ss", [1], F32)
    xss = [A.alloc(f"xs{i}", [D], BF16) for i in range(2)]
    hTt = [A.alloc(f"hTt{i}", [16, 128], BF16) for i in range(2)]
    U = [A.alloc(f"U{i}", [1024], BF16) for i in range(2)]
    pr1 = A.alloc("pr1", [2, 32, 16], F32)
    pr2 = A.alloc("pr2", [2, 32, 16], F32)
    Tt = A.alloc("Tt", [2, 32, 16], F32)
    S = A.alloc("S", [2, 32], F32)
    Mm = A.alloc("Mm", [2, 2, 32], F32)
    K.Hc = A.alloc("Hc", [2, 32], F32)
    H = K.Hc
    P.op("dve", lambda e: e.memset(H.ap, 0.0), writes=[H.b])
    BbRb = K.BbR.ap.unsqueeze(1).broadcast_to([128, 2, 32, 16])
    BbIb = K.BbI.ap.unsqueeze(1).broadcast_to([128, 2, 32, 16])
    ntile = NPRE // 128

    def stA(i):
        xt, xs, ht = xts[i % 2], xss[i % 2], hTt[i % 2]
        P.dma("sp", xt.ap, I["xp"][i * 128:(i + 1) * 128, :], writes=[xt.b])
        norm_rows(K, xt, 128, K.gb, xs, K.junk, K.ss)
        transpose_rows(K, xs, 128, ht.ap, [ht.b], 0)

    def stB(i):
        ht, u = hTt[i % 2], U[i % 2]
        for half in range(2):
            pu = K.ps[:, 2 + half, :]
            P.pe([lambda e, c=c, pu=pu, half=half: e.matmul(pu, lhsT=ht.ap[:, c, :], rhs=Wu.ap[:, c, half * 512:(half + 1) * 512],
                                                           start=(c == 0), stop=(c == 15)) for c in range(16)],
                 reads=[ht.b, Wu.b], writes=[PSB[2 + half]])
        P.op("act", lambda e, u=u: e.activation(out=u.ap, in_=K.ps[:, 2:4, :].rearrange("p a b -> p (a b)"), func=AF.Copy),
             reads=[PSB[2], PSB[3]], writes=[u.b])

    zp = K.ps[:, 4:6, :].rearrange("p a (g c) -> p a g c", c=16)

    def stC(i):
        u = U[i % 2]
        fns = []
        for g in range(64):
            gh, gq = g // 32, g % 32
            rows = slice(64 * gh, 64 * gh + 64)
            for ri, tab in ((0, K.ApR), (1, K.ApI)):
                fns.append(lambda e, g=g, gh=gh, gq=gq, rows=rows, ri=ri, tab=tab, u=u: e.matmul(
                    zp[rows, ri, gq, :], lhsT=tab.ap[:, g, :], rhs=u.ap[:, g * 16:(g + 1) * 16], start=True, stop=True,
                    tile_position=(0, 64 * gh)))
        P.pe(fns, reads=[u.b, K.ApR.b], writes=[PSB[4], PSB[5]])

    def stD(i):
        P.op("dve", lambda e: e.tensor_tensor(out=pr1.ap, in0=zp, in1=BbRb, op=ALU.mult), reads=[PSB[4], PSB[5], K.BbR.b], writes=[pr1.b])
        P.op("dve", lambda e: e.tensor_tensor(out=pr2.ap, in0=zp, in1=BbIb, op=ALU.mult), reads=[PSB[4], PSB[5], K.BbR.b], writes=[pr2.b])
        P.op("dve", lambda e: e.tensor_tensor(out=Tt.ap[:, 0], in0=pr1.ap[:, 0], in1=pr2.ap[:, 1], op=ALU.subtract),
             reads=[pr1.b, pr2.b], writes=[Tt.b])
        P.op("dve", lambda e: e.tensor_tensor(out=Tt.ap[:, 1], in0=pr1.ap[:, 1], in1=pr2.ap[:, 0], op=ALU.add),
             reads=[pr1.b, pr2.b], writes=[Tt.b])
        P.op("dve", lambda e: e.tensor_reduce(out=S.ap, in_=Tt.ap, axis=AX.X, op=ALU.add), reads=[Tt.b], writes=[S.b])
        P.op("dve", lambda e: e.tensor_tensor(out=Mm.ap, in0=K.A4c.ap, in1=H.ap.unsqueeze(1).broadcast_to([128, 2, 2, 32]), op=ALU.mult),
             reads=[K.A4c.b, H.b], writes=[Mm.b])
        P.op("dve", lambda e: e.tensor_tensor(out=H.ap, in0=Mm.ap[:, :, 0, :], in1=Mm.ap[:, :, 1, :], op=ALU.add), reads=[Mm.b], writes=[H.b])
        P.op("dve", lambda e: e.tensor_tensor(out=H.ap, in0=H.ap, in1=S.ap, op=ALU.add), reads=[H.b, S.b], writes=[H.b])

    stA(0)
    for i in range(ntile):
        stB(i)
        stC(i)
        if i + 1 < ntile:
            stA(i + 1)
        stD(i)
    if DEBUG:
        P.dma("sp", K.O["dbg_h"], H.ap, reads=[H.b], writes=[K.bout])
    for t in [Wu] + U + [pr1, pr2, Tt, S, Mm, K.ApR, K.ApI]:
        A.free(t)
    load_wkv(K)
    ht = hTt[(ntile - 1) % 2]
    kv_tokmajor(K, ht.ap, [ht.b], 128, slice(0, 128), K.vth.ap, K.vth.b)
    kT_dup(K, ht.ap, [ht.b], 128, lambda kv: K.kTh.ap[:, kv, :], [K.kTh.b], slice(0, 128))
    A.free(K.Wkv)
    A.free(K.Wkd)
    for t in hTt:
        A.free(t)
    K.xts, K.xss = xts, xss


def kT_dup(K, hT_ap, hbufs, n, dst_ap_fn, dst_bufs, cols):
    P = K.P
    PSB = K.psb
    for kv in range(4):
        pk = K.ps[:, 7, 0:n]
        P.pe([lambda e, c=c, pk=pk, kv=kv: e.matmul(pk, lhsT=K.Wkd.ap[:, c, kv, :],
                                                   rhs=hT_ap[:, c, cols], start=(c == 0), stop=(c == 15)) for c in range(16)],
             reads=hbufs + [K.Wkd.b], writes=[PSB[7]])
        P.op("act", lambda e, kv=kv, pk=pk: e.activation(out=dst_ap_fn(kv), in_=pk, func=AF.Copy),
             reads=[PSB[7]], writes=dst_bufs)


def load_wkv(K):
    P, A = K.P, K.A
    K.Wkv = load_weights_resident(K, "Wkv", K.I["w_in"][:, 2048:2560], 16, 512)
    K.Wkd = A.alloc("Wkd", [16, 4, 128], BF16)
    for kv in range(4):
        P.op("dve", lambda e, kv=kv: e.tensor_copy(out=K.Wkd.ap[:, :, kv, :].rearrange("p c (d h) -> p c d h", d=2),
                                                    in_=K.Wkv.ap[:, :, kv * 64:(kv + 1) * 64].unsqueeze(2).broadcast_to([128, 16, 2, 64])),
             reads=[K.Wkv.b], writes=[K.Wkd.b])


def kv_tokmajor(K, hT_ap, hbufs, n, cols, vdst_ap, vdst_buf, kdst=None):
    P, PSB = K.P, K.psb
    pk = K.ps[:n, 6, :]
    P.pe([lambda e, c=c: e.matmul(pk, lhsT=hT_ap[:, c, cols], rhs=K.Wkv.ap[:, c, :], start=(c == 0), stop=(c == 15)) for c in range(16)],
         reads=hbufs + [K.Wkv.b], writes=[PSB[6]])
    P.op("act", lambda e: e.activation(out=vdst_ap, in_=K.ps[:n, 6, 256:512], func=AF.Copy), reads=[PSB[6]], writes=[vdst_buf])
    if kdst is not None:
        P.op("act", lambda e: e.activation(out=kdst.ap[:n], in_=pk, func=AF.Copy), reads=[PSB[6]], writes=[kdst.b])


class Ring:
    def __init__(self, K, nbig, nsmall=0):
        self.K = K
        self.big = [K.A.alloc(f"ringb{i}", [16, 512], BF16) for i in range(nbig)]
        self.small = [K.A.alloc(f"rings{i}", [8, 512], BF16) for i in range(nsmall)]
        self.ib = 0
        self.is_ = 0

    def load(self, src, kc, ncols):
        if kc <= 8 and self.small:
            s = self.small[self.is_ % len(self.small)]
            self.is_ += 1
        else:
            s = self.big[self.ib % len(self.big)]
            self.ib += 1
        v = s.ap[:, 0:kc, 0:ncols]
        self.K.P.dma("pool", v, src.rearrange("(c p) n -> p c n", p=128), writes=[s.b])
        return v, s.b

    def free(self):
        for s in self.big + self.small:
            self.K.A.free(s)


def acc_views(K, acc):
    return [(K.ps[:, 3 * acc:3 * acc + 2, :].rearrange("p a b -> p (a b)"), slice(0, 1024)),
            (K.ps[:, 3 * acc + 2, 0:64], slice(1024, 1088))]


def acc_bufs(K, acc):
    return [K.psb[3 * acc], K.psb[3 * acc + 1], K.psb[3 * acc + 2]]


def fm_matmul(K, wv, wb, kc, nt, act_ap, act_bufs, acc):
    fns = []
    for bi, (t0, n) in enumerate(BLKS):
        for c in range(kc):
            fns.append(lambda e, bi=bi, t0=t0, n=n, c=c: e.matmul(
                K.ps[:, 3 * acc + bi, 0:n], lhsT=wv[:, c, nt * 128:(nt + 1) * 128], rhs=act_ap[:, c, t0:t0 + n],
                start=(c == 0), stop=(c == kc - 1)))
    K.P.pe(fns, reads=[wb] + act_bufs, writes=acc_bufs(K, acc))


def own_norm(K):
    P, I, A = K.P, K.I, K.A
    for t in range(9):
        n = 128 if t < 8 else 64
        xt, xs = K.xts[t % 2], K.xss[t % 2]
        P.dma("sp", xt.ap[:n], I["xo"][t * 128:t * 128 + n, :], writes=[xt.b])
        norm_rows(K, xt, n, K.gb, xs, K.junk, K.ss)
        transpose_rows(K, xs, n, K.hT.ap[:, :, t * 128:t * 128 + n], [K.hT.b[t]], 0)
    A.free(K.gb)


def proj_phase(K):
    P, I, A = K.P, K.I, K.A
    hb = K.hT.b
    ring = Ring(K, 3)
    acc = 0
    for g in range(2):
        wv, wb = ring.load(I["w_in"][:, g * 512:(g + 1) * 512], 16, 512)
        for nt in range(4):
            fm_matmul(K, wv, wb, 16, nt, K.hT.ap, hb, acc)
            for pv, sl in acc_views(K, acc):
                P.op("act", lambda e, pv=pv, sl=sl, j=4 * g + nt: e.activation(out=K.uT.ap[:, j, sl], in_=pv, func=AF.Copy),
                     reads=acc_bufs(K, acc), writes=K.uT.b)
            acc ^= 1
    ring.free()


def ssm_own(K):
    P, I, A, O = K.P, K.I, K.A, K.O
    PSB = K.psb
    Hist = A.alloc("Hist", [2, 32, 64], F32)
    T1 = A.alloc("T1", [2, 32, 64], F32)
    H2 = A.alloc("H2", [2, 32, 64], BF16)
    HistB = A.alloc("HistB", [2, 32, 64], BF16)
    cA = A.alloc("cA", [2, 32], F32)
    cB = A.alloc("cB", [2, 32], F32)
    ysb = A.alloc("ysb", [8, 64], F32)
    g1 = A.alloc("g1", [8, 64], F32)
    zb = K.ps[:, 0:4, :].rearrange("p (r a) (g t) -> p r (a g) t", r=2, t=128)
    yps = K.ps[:, 4:6, :].rearrange("p a (j t) -> p (a j) t", t=128)
    ypb = [PSB[4], PSB[5]]
    zbb = [PSB[0], PSB[1], PSB[2], PSB[3]]
    Hc = K.Hc

    zbs = [K.ps[:, 0:2, :].rearrange("p r (g t) -> p r g t", t=64), K.ps[:, 2:4, :].rearrange("p r (g t) -> p r g t", t=64)]
    zbbs = [[PSB[0], PSB[1]], [PSB[2], PSB[3]]]

    def bu_batch(t, b4, n, c0, zsel=None):
        zb_, zbb_ = (zb, zbb) if zsel is None else (zbs[zsel], zbbs[zsel])
        fns = []
        for gh in range(2):
            rows = slice(64 * gh, 64 * gh + 64)
            for gl in range(8):
                g = 32 * gh + 8 * b4 + gl
                for ri, tab in ((0, K.BtabR), (1, K.BtabI)):
                    fns.append(lambda e, rows=rows, gl=gl, g=g, ri=ri, tab=tab, gh=gh: e.matmul(
                        zb_[rows, ri, gl, 0:n], lhsT=tab.ap[:, g, :], rhs=K.uT.ap[:, g // 8, c0:c0 + n], start=True, stop=True,
                        tile_position=(0, 64 * gh)))
        P.pe(fns, reads=[K.BtabR.b] + K.uT.b, writes=zbb_)

    def cside(n, hb_ap, hb_buf):
        for j in range(8):
            gh = j // 4
            rows = slice(64 * gh, 64 * gh + 64)
            fns = []
            for g8 in range(8):
                gq = (8 * j + g8) % 32
                for ri, tab in ((0, K.CtabR), (1, K.CtabI)):
                    fns.append(lambda e, rows=rows, gq=gq, ri=ri, tab=tab, j=j, first=(g8 == 0 and ri == 0), last=(g8 == 7 and ri == 1):
                               e.matmul(yps[:, j, 0:n], lhsT=tab.ap[rows, gq, :], rhs=hb_ap[rows, ri, gq, 0:n], start=first, stop=last))
            P.pe(fns, reads=[K.CtabR.b, hb_buf], writes=ypb)

    def gelu_out(n, c0, perm):
        yv, a = ysb.ap[:, :, 0:n], g1.ap[:, :, 0:n]
        b = a
        P.op("act", lambda e: e.activation(out=a, in_=yv, func=AF.Square), reads=[ysb.b], writes=[g1.b])
        P.op("dve", lambda e: e.tensor_scalar(out=a, in0=a, scalar1=0.044715, scalar2=1.0, op0=ALU.mult, op1=ALU.add), reads=[g1.b], writes=[g1.b])
        P.op("dve", lambda e: e.tensor_tensor(out=a, in0=a, in1=yv, op=ALU.mult), reads=[g1.b, ysb.b], writes=[g1.b])
        P.op("act", lambda e: e.activation(out=b, in_=a, func=AF.Sigmoid, scale=1.5957691216057308), reads=[g1.b], writes=[g1.b])
        P.op("dve", lambda e: e.tensor_tensor(out=K.gT.ap[:, :, c0:c0 + n], in0=b, in1=yv, op=ALU.mult), reads=[g1.b, ysb.b], writes=K.gT.b)

    CTb = K.CT.ap.rearrange("p g t -> p (g t)").unsqueeze(1).broadcast_to([128, 2, 2048])
    STf = K.ST.ap.rearrange("p g t -> p (g t)")
    Hf = Hist.ap.rearrange("p r g t -> p r (g t)")
    T1f = T1.ap.rearrange("p r g t -> p r (g t)")
    H2f = H2.ap.rearrange("p r g t -> p r (g t)")
    HBf = HistB.ap.rearrange("p r g t -> p r (g t)")
    xin = [A.alloc(f"xin{i}", [2, 64], F32) for i in range(8)]
    for ri, nm in ((0, "sre"), (1, "sim")):
        for sq in range(4):
            xi = xin[4 * ri + sq]
            for s_ in range(4):
                P.dma("sp", xi.ap[32 * s_:32 * s_ + 32], I[nm][4 * sq + s_].rearrange("(h g) p -> g h p", h=2), writes=[xi.b])
    X0 = A.alloc("X0", [2, 32, 64], BF16)
    Rz = A.alloc("Rz", [32, 64], F32)
    P.op("dve", lambda e: e.tensor_copy(out=Rz.ap, in_=K.Rm.ap.unsqueeze(2).broadcast_to([128, 32, 64])), reads=[K.Rm.b], writes=[Rz.b])
    P.op("dve", lambda e: e.memset(Rz.ap[:, :, 0:1], 0.0), reads=[Rz.b], writes=[Rz.b])
    Rzf = Rz.ap.rearrange("p g t -> p (g t)")
    ini = A.alloc("ini", [2, 32], F32)
    X0f = X0.ap.rearrange("p r g t -> p r (g t)")

    def bu_all(t):
        for b4 in range(4):
            zsel = b4 % 2
            bu_batch(t, b4, 64, t * 64, zsel)
            P.op("act", lambda e, b4=b4, zsel=zsel: e.activation(out=X0.ap[:, :, 8 * b4:8 * b4 + 8, :], in_=zbs[zsel], func=AF.Copy),
                 reads=zbbs[zsel], writes=[X0.b])

    bu_all(0)
    for t in range(16):
        c0 = t * 64
        P.op("dve", lambda e: e.tensor_tensor(out=Hf, in0=X0f, in1=CTb, op=ALU.mult), reads=[X0.b, K.CT.b], writes=[Hist.b])
        P.op("dve", lambda e: e.tensor_tensor(out=H2f[:, 0], in0=X0f[:, 1], in1=STf, op=ALU.mult), reads=[X0.b, K.ST.b], writes=[H2.b])
        P.op("dve", lambda e: e.tensor_tensor(out=H2f[:, 1], in0=X0f[:, 0], in1=STf, op=ALU.mult), reads=[X0.b, K.ST.b], writes=[H2.b])
        P.op("dve", lambda e: e.tensor_tensor(out=Hf[:, 0], in0=Hf[:, 0], in1=H2f[:, 0], op=ALU.add), reads=[Hist.b, H2.b], writes=[Hist.b])
        P.op("dve", lambda e: e.tensor_tensor(out=Hf[:, 1], in0=Hf[:, 1], in1=H2f[:, 1], op=ALU.subtract), reads=[Hist.b, H2.b], writes=[Hist.b])
        P.op("dve", lambda e: e.tensor_tensor(out=ini.ap, in0=Hc.ap, in1=K.Rm.ap.unsqueeze(1).broadcast_to([128, 2, 32]), op=ALU.mult),
             reads=[Hc.b, K.Rm.b], writes=[ini.b])
        P.op("dve", lambda e: e.tensor_tensor(out=Hist.ap[:, :, :, 0], in0=Hist.ap[:, :, :, 0], in1=ini.ap, op=ALU.add), reads=[Hist.b, ini.b], writes=[Hist.b])
        if t + 1 < 16:
            bu_all(t + 1)
        P.group("dve", [lambda e, ri=ri: e.tensor_tensor_scan(out=T1f[:, ri], data0=Rzf, data1=Hf[:, ri], initial=0.0, op0=ALU.mult, op1=ALU.add)
                        for ri in range(2)], reads=[Hist.b, Rz.b], writes=[T1.b])
        P.op("dve", lambda e: e.tensor_tensor(out=Hf, in0=T1f, in1=CTb, op=ALU.mult), reads=[T1.b, K.CT.b], writes=[Hist.b])
        P.op("dve", lambda e: e.tensor_tensor(out=H2f[:, 0], in0=T1f[:, 1], in1=STf, op=ALU.mult), reads=[T1.b, K.ST.b], writes=[H2.b])
        P.op("dve", lambda e: e.tensor_tensor(out=H2f[:, 1], in0=T1f[:, 0], in1=STf, op=ALU.mult), reads=[T1.b, K.ST.b], writes=[H2.b])
        P.op("dve", lambda e: e.tensor_tensor(out=HBf[:, 0], in0=Hf[:, 0], in1=H2f[:, 0], op=ALU.subtract), reads=[Hist.b, H2.b], writes=[HistB.b])
        P.op("dve", lambda e: e.tensor_tensor(out=HBf[:, 1], in0=Hf[:, 1], in1=H2f[:, 1], op=ALU.add), reads=[Hist.b, H2.b], writes=[HistB.b])
        P.op("dve", lambda e: e.tensor_tensor(out=cA.ap, in0=T1.ap[:, :, :, 63], in1=K.c64.ap.unsqueeze(1).broadcast_to([128, 2, 32]), op=ALU.mult),
             reads=[T1.b, K.c64.b], writes=[cA.b])
        P.op("dve", lambda e: e.tensor_tensor(out=cB.ap, in0=T1.ap[:, :, :, 63], in1=K.s64.ap.unsqueeze(1).broadcast_to([128, 2, 32]), op=ALU.mult),
             reads=[T1.b, K.s64.b], writes=[cB.b])
        P.op("dve", lambda e: e.tensor_tensor(out=Hc.ap[:, 0], in0=cA.ap[:, 0], in1=cB.ap[:, 1], op=ALU.subtract), reads=[cA.b, cB.b], writes=[Hc.b])
        P.op("dve", lambda e: e.tensor_tensor(out=Hc.ap[:, 1], in0=cA.ap[:, 1], in1=cB.ap[:, 0], op=ALU.add), reads=[cA.b, cB.b], writes=[Hc.b])
        cside(64, HistB.ap, HistB.b)
        for j in range(8):
            P.op("dve", lambda e, j=j: e.scalar_tensor_tensor(out=ysb.ap[:, j, 0:64], in0=K.uT.ap[:, j, c0:c0 + 64], scalar=K.Dcol.ap[:, j:j + 1],
                                                              in1=yps[:, j, 0:64], op0=ALU.mult, op1=ALU.add),
                 reads=K.uT.b + [K.Dcol.b] + ypb, writes=[ysb.b])
        gelu_out(64, c0, False)
    stg = A.alloc("stg", [128], F32)
    for ri in range(2):
        pt = K.ps[0:32, 6, 0:128]
        P.pe([lambda e, ri=ri: e.transpose(out=pt, in_=Hc.ap[:, ri, :], identity=K.identf.ap)], reads=[Hc.b, K.bC], writes=[PSB[6]])
        P.op("act", lambda e: e.activation(out=stg.ap[0:32, :], in_=pt, func=AF.Copy), reads=[PSB[6]], writes=[stg.b])
        P.dma("sp", O["pst"][ri].rearrange("(h g) p -> g h p", h=2), stg.ap[0:32, :].rearrange("g (h p) -> g h p", h=2), reads=[stg.b], writes=[K.bout])

    for t in [Hist, HistB, T1, H2, cA, cB, X0, Rz, ini]:
        A.free(t)
    Hs0 = A.alloc("Hs0", [2, 32, 16], F32)
    HistS = A.alloc("HistS", [2, 32, 4, 16], F32)
    HistSB = A.alloc("HistSB", [2, 32, 64], BF16)
    MmS = A.alloc("MmS", [2, 2, 32, 16], F32)
    RrS = A.alloc("RrS", [2, 32, 16], F32)
    for ri, nm in ((0, "sre"), (1, "sim")):
        for sq in range(4):
            xi = xin[4 * ri + sq]
            pt = K.ps[:, 6, 0:128]
            P.pe([lambda e, xi=xi: e.transpose(out=pt, in_=xi.ap.rearrange("p h q -> p (h q)"), identity=K.identf.ap)],
                 reads=[xi.b, K.bC], writes=[PSB[6]])
            P.op("act", lambda e, ri=ri, sq=sq: e.activation(out=Hs0.ap[:, ri, :, 4 * sq:4 * sq + 4].rearrange("p g s -> p s g"),
                                                            in_=pt.rearrange("p (s g) -> p s g", s=4), func=AF.Copy),
                 reads=[PSB[6]], writes=[Hs0.b])
    c0 = 1024
    for b4 in range(4):
        bu_batch(8, b4, 64, c0)
        for ri in range(2):
            P.op("act", lambda e, b4=b4, ri=ri: e.activation(out=HistS.ap[:, ri, 8 * b4:8 * b4 + 8, :, :],
                                                            in_=zb[:, ri, :, 0:64].rearrange("p g (s t) -> p g t s", t=4), func=AF.Copy),
                 reads=zbb, writes=[HistS.b])
    A4b = K.A4.ap.unsqueeze(4).broadcast_to([128, 2, 2, 32, 16])
    for tt in range(4):
        prev = Hs0.ap if tt == 0 else HistS.ap[:, :, :, tt - 1, :]
        pb = [Hs0.b] if tt == 0 else [HistS.b]
        P.op("dve", lambda e, prev=prev: e.tensor_tensor(out=MmS.ap, in0=A4b, in1=prev.unsqueeze(1).broadcast_to([128, 2, 2, 32, 16]), op=ALU.mult),
             reads=[K.A4.b] + pb, writes=[MmS.b])
        P.op("dve", lambda e: e.tensor_tensor(out=RrS.ap, in0=MmS.ap[:, :, 0], in1=MmS.ap[:, :, 1], op=ALU.add), reads=[MmS.b], writes=[RrS.b])
        P.op("dve", lambda e, tt=tt: e.tensor_tensor(out=HistS.ap[:, :, :, tt, :], in0=HistS.ap[:, :, :, tt, :], in1=RrS.ap, op=ALU.add),
             reads=[RrS.b, HistS.b], writes=[HistS.b])
    P.op("act", lambda e: e.activation(out=HistSB.ap, in_=HistS.ap.rearrange("p r g t s -> p r g (t s)"), func=AF.Copy), reads=[HistS.b], writes=[HistSB.b])
    cside(64, HistSB.ap, HistSB.b)
    for j in range(8):
        P.op("dve", lambda e, j=j: e.scalar_tensor_tensor(
            out=ysb.ap[:, j, 0:64].rearrange("p (s t) -> p s t", t=4), in0=K.uT.ap[:, j, c0:c0 + 64].rearrange("p (s t) -> p s t", t=4),
            scalar=K.Dcol.ap[:, j:j + 1], in1=yps[:, j, 0:64].rearrange("p (t s) -> p s t", t=4), op0=ALU.mult, op1=ALU.add),
            reads=K.uT.b + [K.Dcol.b] + ypb, writes=[ysb.b])
    gelu_out(64, c0, True)
    stg2s = [A.alloc(f"stg2{i}", [4, 32], F32) for i in range(2)]
    stgs = [A.alloc(f"stgs{i}", [128], F32) for i in range(4)]
    for ri in range(2):
        for sq in range(4):
            pt = K.ps[:, 6, 0:128]
            sg2, sg1 = stg2s[(4 * ri + sq) % 2], stgs[(4 * ri + sq) % 4]
            pt = K.ps[:, 6 + (sq % 2), 0:128]
            P.op("dve", lambda e, ri=ri, sq=sq, sg2=sg2: e.tensor_copy(out=sg2.ap, in_=HistS.ap[:, ri, :, 3, 4 * sq:4 * sq + 4].rearrange("p g s -> p s g")),
                 reads=[HistS.b], writes=[sg2.b])
            P.pe([lambda e, sg2=sg2, pt=pt: e.transpose(out=pt, in_=sg2.ap.rearrange("p s g -> p (s g)"), identity=K.identf.ap)],
                 reads=[sg2.b, K.bC], writes=[PSB[6 + (sq % 2)]])
            P.op("act", lambda e, sg1=sg1, pt=pt: e.activation(out=sg1.ap, in_=pt, func=AF.Copy), reads=[PSB[6 + (sq % 2)]], writes=[sg1.b])
            for s in range(4):
                P.dma("sp", O["sst"][ri, 4 * sq + s].rearrange("(h g) p -> g h p", h=2),
                      sg1.ap[32 * s:32 * s + 32, :].rearrange("g (h p) -> g h p", h=2), reads=[sg1.b], writes=[K.bout])
    for t in [ysb, g1, stg, Hs0, HistS, HistSB, MmS, RrS] + xin + stg2s + stgs:
        A.free(t)
    for t in [K.A4, K.A4c, K.BtabR, K.BtabI, K.CtabR, K.CtabI, K.Dcol, K.Hc, K.uT, K.CT, K.ST, K.Rm, K.thG, K.c64, K.s64]:
        A.free(t)


def attention(K):
    P, I, A, O = K.P, K.I, K.A, K.O
    PSB = K.psb
    hb = K.hT.b
    K.kTd = A.alloc("kTd", [4, 128 + NT], BF16, nbufs=10, top=True)
    K.vtok = A.alloc("vtok", [10, 256], BF16, nbufs=10, top=True)
    K.kvlast = A.alloc("kvlast", [512], F32, top=True)
    K.kvsamp = A.alloc("kvsamp", [512], F32, top=True)
    load_wkv(K)
    P.op("act", lambda e: e.activation(out=K.kTd.ap[:, :, 0:128], in_=K.kTh.ap, func=AF.Copy), reads=[K.kTh.b], writes=[K.kTd.b[0]])
    P.op("act", lambda e: e.activation(out=K.vtok.ap[:, 0, :], in_=K.vth.ap, func=AF.Copy), reads=[K.vth.b], writes=[K.vtok.b[0]])
    for t in range(9):
        n = 128 if t < 8 else 64
        kdst = K.kvlast if t == 7 else (K.kvsamp if t == 8 else None)
        kv_tokmajor(K, K.hT.ap, [hb[t]], n, slice(t * 128, t * 128 + n), K.vtok.ap[:n, t + 1, :], K.vtok.b[t + 1], kdst)
    for bi, (t0, n) in enumerate(BLKS):
        tiles = list(range(t0 // 128, (t0 + n + 127) // 128))
        kT_dup(K, K.hT.ap, [hb[t] for t in tiles], n, lambda kv, t0=t0, n=n: K.kTd.ap[:, kv, 128 + t0:128 + t0 + n],
               [K.kTd.b[t + 1] for t in tiles], slice(t0, t0 + n))
    A.free(K.Wkv)
    A.free(K.Wkd)
    A.free(K.kTh)
    A.free(K.vth)
    K.qT = A.alloc("qT", [8, NT], BF16, nbufs=1, top=True)
    ring = Ring(K, 2)
    acc = 0
    for g in range(2):
        wv, wb = ring.load(I["w_in"][:, 1024 + g * 512:1024 + (g + 1) * 512], 16, 512)
        for nt in range(4):
            fm_matmul(K, wv, wb, 16, nt, K.hT.ap, hb, acc)
            for pv, sl in acc_views(K, acc):
                P.op("act", lambda e, pv=pv, sl=sl, j=4 * g + nt: e.activation(out=K.qT.ap[:, j, sl], in_=pv, func=AF.Copy),
                     reads=acc_bufs(K, acc), writes=[K.qT.b])
            acc ^= 1
    ring.free()
    if ASTOP == 'q':
        return
    bmt = A.alloc("bmt", [16, 2, 128], F32)
    bmt0 = A.alloc("bmt0", [16, 128], F32)
    bmsc = A.alloc("bmsc", [2, 8, 4], F32)
    bmsn = A.alloc("bmsn", [2, 8, 4], F32)
    esk = A.alloc("esk", [8], F32)
    P.dma("sp", bmt.ap, I["bmt"], writes=[bmt.b])
    P.dma("sp", bmt0.ap, I["bmt0"], writes=[bmt0.b])
    P.dma("sp", bmsc.ap, I["bmsc"].rearrange("k (j a) t -> k j a t", j=2), writes=[bmsc.b])
    P.dma("sp", bmsn.ap[0:4], I["bmsn"].rearrange("k (j a) t -> k j a t", j=2), writes=[bmsn.b])
    with K.nc.allow_non_contiguous_dma(reason="tiny"):
        for j in range(2):
            P.dma("sp", esk.ap[64 * j:64 * j + 64, :], I["sinks"].rearrange("(p j) -> j p", j=2)[j:j + 1, :].broadcast_to([64, 8]), writes=[esk.b])
    P.op("act", lambda e: e.activation(out=esk.ap, in_=esk.ap, func=AF.Exp), reads=[esk.b], writes=[esk.b])
    if ASTOP == 'tabs':
        return
    ee = [A.alloc(f"ee{i}", [2, 2, 2, 128], F32) for i in range(2)]
    pT = [A.alloc(f"pT{i}", [2, 2, 2, 128], BF16) for i in range(2)]
    dn = A.alloc("dn", [2, 128], F32)
    spss = [K.ps[:, 0:2, :].rearrange("p j (a k q) -> p j a k q", a=2, k=2), K.ps[:, 4:6, :].rearrange("p j (a k q) -> p j a k q", a=2, k=2)]
    spbs = [[PSB[0], PSB[1]], [PSB[4], PSB[5]]]
    opss = [K.ps[:, 2, 0:256].rearrange("p (a q) -> p a q", q=128), K.ps[:, 6, 0:256].rearrange("p (a q) -> p a q", q=128)]
    dpss = [K.ps[:, 3, 0:256].rearrange("p (a q) -> p a q", q=128), K.ps[:, 7, 0:256].rearrange("p (a q) -> p a q", q=128)]
    odbs = [[PSB[2], PSB[3]], [PSB[6], PSB[7]]]
    dns = [dn, A.alloc("dn2", [2, 128], F32)]
    batches = [(i, hbk) for i in range(8) for hbk in range(4)]

    def stS(b):
        i, hbk = batches[b]
        kv = hbk
        qc = slice(i * 128, (i + 1) * 128)
        spsv = spss[b % 2]
        fns = []
        for hh in range(4):
            h = 4 * hbk + hh
            pair, j = h // 2, h % 2
            rows = slice(64 * j, 64 * j + 64)
            for kb in range(2):
                kc0 = 128 * (i + kb)
                fns.append(lambda e, hh=hh, kb=kb, rows=rows, pair=pair, kc0=kc0, j=j: e.matmul(
                    spsv[:, j, hh // 2, kb, :], lhsT=K.kTd.ap[rows, kv, kc0:kc0 + 128], rhs=K.qT.ap[rows, pair, qc], start=True, stop=True))
        P.pe(fns, reads=[K.kTd.b[i], K.kTd.b[i + 1], K.qT.b], writes=spbs[b % 2])

    def stE(b):
        i, hbk = batches[b]
        spsv, spb = spss[b % 2], spbs[b % 2]
        e_, p_ = ee[b % 2], pT[b % 2]
        if i == 0:
            for j in range(2):
                P.op("dve", lambda e, j=j: e.scalar_tensor_tensor(out=e_.ap[:, j, :, 0, :], in0=spsv[:, j, :, 0, :], scalar=0.125,
                                                                  in1=bmt0.ap[:, 4 * hbk + 2 * j:4 * hbk + 2 * j + 2, :], op0=ALU.mult, op1=ALU.add),
                     reads=spb + [bmt0.b], writes=[e_.b])
                P.op("dve", lambda e, j=j: e.scalar_tensor_tensor(out=e_.ap[:, j, :, 1, :], in0=spsv[:, j, :, 1, :], scalar=0.125,
                                                                  in1=bmt.ap[:, 4 * hbk + 2 * j:4 * hbk + 2 * j + 2, 1, :], op0=ALU.mult, op1=ALU.add),
                     reads=spb + [bmt.b], writes=[e_.b])
        else:
            P.op("dve", lambda e: e.scalar_tensor_tensor(out=e_.ap, in0=spsv, scalar=0.125, in1=bmt.ap[:, 4 * hbk:4 * hbk + 4, :, :],
                                                         op0=ALU.mult, op1=ALU.add), reads=spb + [bmt.b], writes=[e_.b])
        P.op("act", lambda e: e.activation(out=p_.ap, in_=e_.ap, func=AF.Exp), reads=[e_.b], writes=[p_.b])

    def stV(b):
        i, hbk = batches[b]
        kv = hbk
        p_ = pT[b % 2]
        ops, dps = opss[b % 2], dpss[b % 2]
        fns = []
        for hh in range(4):
            pp, j = hh // 2, hh % 2
            rows = slice(64 * j, 64 * j + 64)
            for kb in range(2):
                fns.append(lambda e, hh=hh, kb=kb, pp=pp, j=j, rows=rows: e.matmul(
                    ops[rows, pp, :], lhsT=K.vtok.ap[:, i + kb, kv * 64:(kv + 1) * 64], rhs=p_.ap[:, j, hh // 2, kb, :],
                    start=(kb == 0), stop=(kb == 1), tile_position=(0, 64 * j)))
            for kb in range(2):
                fns.append(lambda e, hh=hh, kb=kb, pp=pp, j=j, rows=rows: e.matmul(
                    dps[rows, pp, :], lhsT=K.ones64.ap, rhs=p_.ap[:, j, hh // 2, kb, :],
                    start=(kb == 0), stop=(kb == 1), tile_position=(0, 64 * j)))
        P.pe(fns, reads=[K.vtok.b[i], K.vtok.b[i + 1], p_.b, K.bC], writes=odbs[b % 2])

    def stN(b):
        i, hbk = batches[b]
        qc = slice(i * 128, (i + 1) * 128)
        ops, dps, dn_ = opss[b % 2], dpss[b % 2], dns[b % 2]
        ob = odbs[b % 2]
        P.op("dve", lambda e: e.tensor_tensor(out=dn_.ap, in0=dps, in1=esk.ap[:, 2 * hbk:2 * hbk + 2].unsqueeze(2).broadcast_to([128, 2, 128]),
                                              op=ALU.add), reads=ob + [esk.b], writes=[dn_.b])
        P.op("dve", lambda e: e.reciprocal(out=dn_.ap, in_=dn_.ap), reads=[dn_.b], writes=[dn_.b])
        P.op("dve", lambda e: e.tensor_tensor(out=K.oT.ap[:, 2 * hbk:2 * hbk + 2, qc], in0=ops, in1=dn_.ap, op=ALU.mult),
             reads=ob + [dn_.b], writes=K.oT.b)

    nb = len(batches)
    stS(0)
    stE(0)
    stS(1)
    for b in range(nb):
        stV(b)
        if b + 1 < nb:
            stE(b + 1)
        if b + 2 < nb:
            stS(b + 2)
        stN(b)
    lim = False
    if ASTOP == 'prompt' or lim:
        return
    kc_ = [A.alloc(f"kc{i}", [4, 2, 64], BF16) for i in range(2)]
    vc_ = [A.alloc(f"vc{i}", [256], BF16) for i in range(2)]
    kcT = [A.alloc(f"kcT{i}", [4, 128], BF16) for i in range(2)]
    vnew = A.alloc("vnew", [16, 256], BF16)
    eS = A.alloc("eS", [2, 8, 4], F32)
    eN = A.alloc("eN", [2, 8, 4], F32)
    pS = [A.alloc(f"pS{i}", [2, 8, 4], BF16) for i in range(2)]
    pN = [A.alloc(f"pN{i}", [2, 8, 4], BF16) for i in range(2)]
    dS = A.alloc("dS", [8, 4], F32)
    with K.nc.allow_non_contiguous_dma(reason="tiny relayout"):
        for s in range(16):
            P.dma("sp", vnew.ap[0:4, s, :], K.vtok.ap[4 * s:4 * s + 4, 9, :], reads=[K.vtok.b[9]], writes=[vnew.b])
    base = [4, 0]
    ptks = [K.ps[:, base[z], :].bitcast(BF16)[:, 0:512].rearrange("p (k n) -> p k n", k=4) for z in range(2)]
    sscs = [[K.ps[:, base[z] + 1 + 2 * j, 0:32].rearrange("p (h t) -> p h t", t=4) for j in range(2)] for z in range(2)]
    ssns = [[K.ps[0:4, base[z] + 1 + 2 * j, 32:64].rearrange("p (h t) -> p h t", t=4) for j in range(2)] for z in range(2)]
    osps = [K.ps[:, base[z] + 2, 0:32].rearrange("p (a t) -> p a t", t=4) for z in range(2)]
    dsps = [K.ps[:, base[z] + 2, 32:64].rearrange("p (a t) -> p a t", t=4) for z in range(2)]
    eSs = [eS, A.alloc("eS2", [2, 8, 4], F32)]
    eNs = [eN, A.alloc("eN2", [2, 8, 4], F32)]
    dSs = [dS, A.alloc("dS2", [8, 4], F32)]

    def saL(s):
        z = s % 2
        kc, vc, kt = kc_[s % 2], vc_[s % 2], kcT[s % 2]
        P.dma("pool", kc.ap, I["ck"][s].rearrange("k (v d) -> k v d", v=4).unsqueeze(2).broadcast_to([128, 4, 2, 64]), writes=[kc.b])
        P.dma("pool", vc.ap, I["cv"][s], writes=[vc.b])
        P.dma("sp", O["skk"][s, 0:124, :], I["ck"][s, 4:128, :], writes=[K.bout])
        P.dma("sp", O["skv"][s, 0:124, :], I["cv"][s, 4:128, :], writes=[K.bout])
        P.dma("sp", O["skk"][s, 124:128, :], K.kvsamp.ap[4 * s:4 * s + 4, 0:256], reads=[K.kvsamp.b], writes=[K.bout])
        P.dma("sp", O["skv"][s, 124:128, :], K.kvsamp.ap[4 * s:4 * s + 4, 256:512], reads=[K.kvsamp.b], writes=[K.bout])
        ptk = ptks[z]
        P.pe([lambda e, kv=kv: e.transpose(out=ptk[:, kv, :], in_=kc.ap[:, kv].rearrange("k a d -> k (a d)"),
                                           identity=K.identb.ap) for kv in range(4)], reads=[kc.b, K.bC], writes=[PSB[base[z]]])
        P.op("act", lambda e: e.activation(out=kt.ap, in_=ptk, func=AF.Copy), reads=[PSB[base[z]]], writes=[kt.b])
        qc0 = 1024 + 4 * s
        fns = []
        for h in range(16):
            kv, pair, j = h // 4, h // 2, h % 2
            rows = slice(64 * j, 64 * j + 64)
            fns.append(lambda e, kv=kv, pair=pair, rows=rows, j=j: e.matmul(sscs[z][j][:, pair, :], lhsT=kt.ap[rows, kv, :], rhs=K.qT.ap[rows, pair, qc0:qc0 + 4],
                                                                           start=True, stop=True))
            fns.append(lambda e, kv=kv, pair=pair, rows=rows, j=j: e.matmul(ssns[z][j][:, pair, :], lhsT=K.kTd.ap[rows, kv, 128 + qc0:128 + qc0 + 4],
                                                                           rhs=K.qT.ap[rows, pair, qc0:qc0 + 4], start=True, stop=True))
        P.pe(fns, reads=[kt.b, K.kTd.b[9], K.qT.b], writes=[PSB[base[z] + 1], PSB[base[z] + 3]])

    def saE(s):
        z = s % 2
        sb_ = [PSB[base[z] + 1], PSB[base[z] + 3]]
        eS_, eN_, ps_, pn_ = eSs[z], eNs[z], pS[z], pN[z]
        for j in range(2):
            P.op("dve", lambda e, j=j: e.scalar_tensor_tensor(out=eS_.ap[:, j], in0=sscs[z][j], scalar=0.125, in1=bmsc.ap[:, j], op0=ALU.mult, op1=ALU.add),
                 reads=sb_ + [bmsc.b], writes=[eS_.b])
            P.op("dve", lambda e, j=j: e.scalar_tensor_tensor(out=eN_.ap[0:4, j], in0=ssns[z][j], scalar=0.125, in1=bmsn.ap[0:4, j], op0=ALU.mult, op1=ALU.add),
                 reads=sb_ + [bmsn.b], writes=[eN_.b])
        P.op("act", lambda e: e.activation(out=ps_.ap, in_=eS_.ap, func=AF.Exp), reads=[eS_.b], writes=[ps_.b])
        P.op("act", lambda e: e.activation(out=pn_.ap[0:4], in_=eN_.ap[0:4], func=AF.Exp), reads=[eN_.b], writes=[pn_.b])

    def saV(s):
        z = s % 2
        vc, ps_, pn_ = vc_[s % 2], pS[z], pN[z]
        osp, dsp = osps[z], dsps[z]
        fns = []
        for h in range(16):
            kv, pair, j = h // 4, h // 2, h % 2
            rows = slice(64 * j, 64 * j + 64)
            fns.append(lambda e, kv=kv, pair=pair, rows=rows, j=j: e.matmul(osp[rows, pair, :], lhsT=vc.ap[:, kv * 64:(kv + 1) * 64], rhs=ps_.ap[:, j, pair, :],
                                                                           start=True, stop=False, tile_position=(0, 64 * j)))
            fns.append(lambda e, kv=kv, pair=pair, rows=rows, j=j: e.matmul(osp[rows, pair, :], lhsT=vnew.ap[0:4, s, kv * 64:(kv + 1) * 64], rhs=pn_.ap[0:4, j, pair, :],
                                                                           start=False, stop=True, tile_position=(0, 64 * j)))
            fns.append(lambda e, pair=pair, rows=rows, j=j: e.matmul(dsp[rows, pair, :], lhsT=K.ones64.ap, rhs=ps_.ap[:, j, pair, :],
                                                                    start=True, stop=False, tile_position=(0, 64 * j)))
            fns.append(lambda e, pair=pair, rows=rows, j=j: e.matmul(dsp[rows, pair, :], lhsT=K.ones64.ap[0:4], rhs=pn_.ap[0:4, j, pair, :],
                                                                    start=False, stop=True, tile_position=(0, 64 * j)))
        P.pe(fns, reads=[vc.b, vnew.b, ps_.b, pn_.b, K.bC], writes=[PSB[base[z] + 2]])

    def saN(s):
        z = s % 2
        qc0 = 1024 + 4 * s
        osp, dsp, dS_ = osps[z], dsps[z], dSs[z]
        ob = [PSB[base[z] + 2]]
        P.op("dve", lambda e: e.tensor_tensor(out=dS_.ap, in0=dsp, in1=esk.ap.unsqueeze(2).broadcast_to([128, 8, 4]), op=ALU.add),
             reads=ob + [esk.b], writes=[dS_.b])
        P.op("dve", lambda e: e.reciprocal(out=dS_.ap, in_=dS_.ap), reads=[dS_.b], writes=[dS_.b])
        P.op("dve", lambda e: e.tensor_tensor(out=K.oT.ap[:, :, qc0:qc0 + 4], in0=osp, in1=dS_.ap, op=ALU.mult),
             reads=ob + [dS_.b], writes=K.oT.b)

    saL(0)
    saE(0)
    saL(1)
    for s in range(16):
        saV(s)
        if s + 1 < 16:
            saE(s + 1)
        saN(s)
        if s + 2 < 16:
            saL(s + 2)
    P.dma("sp", O["pkv"], K.kvlast.ap, reads=[K.kvlast.b], writes=[K.bout])
    for t in [bmt, bmt0, bmsc, bmsn, esk, vnew] + eSs + eNs + dSs + dns + ee + pT + kc_ + vc_ + kcT + pS + pN:
        A.free(t)
    for t in [K.qT, K.kTd, K.vtok, K.kvlast, K.kvsamp]:
        A.free(t)


def merge_phase(K):
    P, I, A = K.P, K.I, K.A
    hb = K.hT.b
    K.mT = A.alloc("mT", [16, NT], BF16, top=True)
    ring = Ring(K, 3, 4)
    s1 = A.alloc("s1", [NT], BF16)
    s2 = A.alloc("s2", [NT], BF16)
    s3 = A.alloc("s3", [NT], BF16)
    tv = A.alloc("tv", [NT], F32)
    tu = A.alloc("tu", [NT], F32)
    acc = 0
    for sg in range(4):
        c0 = sg * 512
        wga = ring.load(I["w_in"][:, 2560 + c0:2560 + c0 + 512], 16, 512)
        wgb = ring.load(I["w_in"][:, 4608 + c0:4608 + c0 + 512], 16, 512)
        wval = ring.load(I["w_glu_val"][:, c0:c0 + 512], 8, 512)
        wgate = ring.load(I["w_glu_gate"][:, c0:c0 + 512], 8, 512)
        for nt in range(4):
            n = 4 * sg + nt
            if nt == 0:
                pass
            fm_matmul(K, wga[0], wga[1], 16, nt, K.hT.ap, hb, acc)
            for pv, sl in acc_views(K, acc):
                P.op("act", lambda e, pv=pv, sl=sl: e.activation(out=s1.ap[:, sl], in_=pv, func=AF.Sigmoid), reads=acc_bufs(K, acc), writes=[s1.b])
            acc ^= 1
            fm_matmul(K, wgate[0], wgate[1], 8, nt, K.gT.ap, K.gT.b, acc)
            for pv, sl in acc_views(K, acc):
                P.op("act", lambda e, pv=pv, sl=sl: e.activation(out=s2.ap[:, sl], in_=pv, func=AF.Sigmoid), reads=acc_bufs(K, acc), writes=[s2.b])
            acc ^= 1
            fm_matmul(K, wval[0], wval[1], 8, nt, K.gT.ap, K.gT.b, acc)
            for pv, sl in acc_views(K, acc):
                P.op("dve", lambda e, pv=pv, sl=sl: e.tensor_tensor(out=tv.ap[:, sl], in0=pv, in1=s1.ap[:, sl], op=ALU.mult),
                     reads=acc_bufs(K, acc) + [s1.b], writes=[tv.b])
            P.op("dve", lambda e: e.tensor_tensor(out=tv.ap, in0=tv.ap, in1=s2.ap, op=ALU.mult), reads=[tv.b, s2.b], writes=[tv.b])
            acc ^= 1
            fm_matmul(K, wgb[0], wgb[1], 16, nt, K.hT.ap, hb, acc)
            for pv, sl in acc_views(K, acc):
                P.op("act", lambda e, pv=pv, sl=sl: e.activation(out=s3.ap[:, sl], in_=pv, func=AF.Sigmoid), reads=acc_bufs(K, acc), writes=[s3.b])
            acc ^= 1
            if nt == 0:
                wab = ring.load(I["w_attn_br"][:, c0:c0 + 512], 8, 512)
            fm_matmul(K, wab[0], wab[1], 8, nt, K.oT.ap, K.oT.b, acc)
            for pv, sl in acc_views(K, acc):
                P.op("dve", lambda e, pv=pv, sl=sl: e.tensor_tensor(out=tu.ap[:, sl], in0=pv, in1=s3.ap[:, sl], op=ALU.mult),
                     reads=acc_bufs(K, acc) + [s3.b], writes=[tu.b])
            P.op("dve", lambda e, n=n: e.tensor_tensor(out=K.mT.ap[:, n, :], in0=tu.ap, in1=tv.ap, op=ALU.add), reads=[tu.b, tv.b], writes=[K.mT.b])
            acc ^= 1
    ring.free()
    for t in (s1, s2, s3, tv, tu):
        A.free(t)


def stats_rstd(K, xT, sq, rstd):
    P = K.P
    P.op("act", lambda e: e.activation(out=sq.ap, in_=xT.ap, func=AF.Square), reads=[xT.b], writes=[sq.b])
    fns = []
    for bi, (t0, n) in enumerate(BLKS):
        for c in range(16):
            fns.append(lambda e, bi=bi, t0=t0, n=n, c=c: e.matmul(K.ps[:, bi, 0:n], lhsT=K.onesm.ap, rhs=sq.ap[:, c, t0:t0 + n],
                                                                 start=(c == 0), stop=(c == 15)))
    P.pe(fns, reads=[sq.b, K.bC], writes=acc_bufs(K, 0))
    for pv, sl in acc_views(K, 0):
        P.op("act", lambda e, pv=pv, sl=sl: e.activation(out=rstd.ap[:, sl], in_=pv, func=AF.Sqrt, bias=K.epsc.ap, scale=1.0),
             reads=acc_bufs(K, 0) + [K.bC], writes=[rstd.b])
    P.op("dve", lambda e: e.reciprocal(out=rstd.ap, in_=rstd.ap), reads=[rstd.b], writes=[rstd.b])


def post_phase(K):
    P, I, A, O = K.P, K.I, K.A, K.O
    PSB = K.psb
    xT = A.alloc("xT", [16, NT], F32, top=True)
    rstd = A.alloc("rstd", [NT], F32, top=True)
    gcol = A.alloc("gcol", [2, 16], F32, top=True)
    act = [A.alloc(f"actT{i}", [8, NT], BF16, top=True) for i in range(1)]
    sgt = [A.alloc(f"sg{i}", [NT], BF16, top=True) for i in range(2)]
    xl = [A.alloc(f"xl{i}", [D], F32) for i in range(2)]
    for t in range(9):
        n = 128 if t < 8 else 64
        x_ = xl[t % 2]
        P.dma("sp", x_.ap[:n], I["xo"][t * 128:t * 128 + n, :], writes=[x_.b])
        for q4 in range(4):
            bank = q4 % 2
            pt = K.ps[:, bank, :].rearrange("p (c n) -> p c n", c=4)
            P.pe([lambda e, c=c, pt=pt, q4=q4: e.transpose(out=pt[:, c, 0:n], in_=x_.ap[:n, (4 * q4 + c) * 128:(4 * q4 + c + 1) * 128],
                                                           identity=K.identf.ap[:n, :n]) for c in range(4)],
                 reads=[x_.b, K.bC], writes=[PSB[bank]])
            P.op("act", lambda e, pt=pt, q4=q4: e.activation(out=xT.ap[:, 4 * q4:4 * q4 + 4, t * 128:t * 128 + n], in_=pt[:, :, 0:n], func=AF.Copy),
                 reads=[PSB[bank]], writes=[xT.b])
    for x_ in xl:
        A.free(x_)
    ring = Ring(K, 2)
    acc = 0
    for g in range(4):
        wv, wb = ring.load(I["w_out"][:, g * 512:(g + 1) * 512], 16, 512)
        for nt in range(4):
            n = 4 * g + nt
            fm_matmul(K, wv, wb, 16, nt, K.mT.ap, [K.mT.b], acc)
            for pv, sl in acc_views(K, acc):
                P.op("dve", lambda e, pv=pv, sl=sl, n=n: e.tensor_tensor(out=xT.ap[:, n, sl], in0=pv, in1=xT.ap[:, n, sl], op=ALU.add),
                     reads=acc_bufs(K, acc) + [xT.b], writes=[xT.b])
            acc ^= 1
    ring.free()
    A.free(K.mT)
    sq = A.alloc("sq", [16, NT], BF16)
    with K.nc.allow_non_contiguous_dma(reason="tiny"):
        P.dma("sp", gcol.ap[:, 0, :], I["norm_ffn"].rearrange("(c p) -> p c", p=128), writes=[gcol.b])
        P.dma("sp", gcol.ap[:, 1, :], I["norm_final"].rearrange("(c p) -> p c", p=128), writes=[gcol.b])
    stats_rstd(K, xT, sq, rstd)
    h2 = K.hT
    h2b = Buf("h2T")
    _merge(h2b.r, {})
    for b in K.hT.b:
        if b.w is not None:
            _merge(h2b.r, {b.w[0].num: b.w})
        _merge(h2b.r, b.r)
    for c in range(16):
        P.op("dve", lambda e, c=c: e.scalar_tensor_tensor(out=h2.ap[:, c, :], in0=xT.ap[:, c, :], scalar=gcol.ap[:, 0, c:c + 1], in1=rstd.ap,
                                                          op0=ALU.mult, op1=ALU.mult), reads=[xT.b, gcol.b, rstd.b], writes=[h2b])
    A.free(sq)
    ring = Ring(K, 3, 2)
    nsg = (HID + 1023) // 1024
    k = 0
    for sg in range(nsg):
        h0 = sg * 1024
        hw = min(1024, HID - h0)
        nft = hw // 128
        a_ = act[0]
        for half in range(hw // 512):
            wg = ring.load(I["w_ffn_in"][:, h0 + half * 512:h0 + half * 512 + 512], 16, 512)
            wu = ring.load(I["w_ffn_in"][:, HID + h0 + half * 512:HID + h0 + half * 512 + 512], 16, 512)
            for nt in range(4):
                f = 4 * half + nt
                s_ = sgt[k % 2]
                k += 1
                fm_matmul(K, wg[0], wg[1], 16, nt, h2.ap, [h2b], acc)
                for pv, sl in acc_views(K, acc):
                    P.op("act", lambda e, pv=pv, sl=sl, s_=s_: e.activation(out=s_.ap[:, sl], in_=pv, func=AF.Silu), reads=acc_bufs(K, acc), writes=[s_.b])
                acc ^= 1
                fm_matmul(K, wu[0], wu[1], 16, nt, h2.ap, [h2b], acc)
                for pv, sl in acc_views(K, acc):
                    P.op("dve", lambda e, pv=pv, sl=sl, s_=s_, f=f, a_=a_: e.tensor_tensor(out=a_.ap[:, f, sl], in0=pv, in1=s_.ap[:, sl], op=ALU.mult),
                         reads=acc_bufs(K, acc) + [s_.b], writes=[a_.b])
                acc ^= 1
        for g in range(4):
            wv, wb = ring.load(I["w_ffn_out"][h0:h0 + hw, g * 512:(g + 1) * 512], nft, 512)
            for nt in range(4):
                n = 4 * g + nt
                fm_matmul(K, wv, wb, nft, nt, a_.ap, [a_.b], acc)
                for pv, sl in acc_views(K, acc):
                    P.op("dve", lambda e, pv=pv, sl=sl, n=n: e.tensor_tensor(out=xT.ap[:, n, sl], in0=pv, in1=xT.ap[:, n, sl], op=ALU.add),
                         reads=acc_bufs(K, acc) + [xT.b], writes=[xT.b])
                acc ^= 1
    ring.free()
    for t in act + sgt:
        A.free(t)
    sq = A.alloc("sq2", [16, NT], BF16)
    stats_rstd(K, xT, sq, rstd)
    A.free(sq)
    yf = [A.alloc(f"yf{i}", [16, 128], F32) for i in range(2)]
    yo = [A.alloc(f"yo{i}", [D], F32) for i in range(2)]
    for t in range(9):
        n = 128 if t < 8 else 64
        cs = slice(t * 128, t * 128 + n)
        y_, o_ = yf[t % 2], yo[t % 2]
        for c in range(16):
            P.op("dve", lambda e, c=c: e.scalar_tensor_tensor(out=y_.ap[:, c, 0:n], in0=xT.ap[:, c, cs], scalar=gcol.ap[:, 1, c:c + 1], in1=rstd.ap[:, cs],
                                                              op0=ALU.mult, op1=ALU.mult), reads=[xT.b, gcol.b, rstd.b], writes=[y_.b])
        for q4 in range(4):
            bank = 6 + q4 % 2
            pt = K.ps[:n, bank, :].rearrange("p (c n) -> p c n", c=4)
            P.pe([lambda e, c=c, pt=pt, q4=q4: e.transpose(out=pt[:, c, :], in_=y_.ap[:, 4 * q4 + c, 0:n], identity=K.identf.ap) for c in range(4)],
                 reads=[y_.b, K.bC], writes=[PSB[bank]])
            P.op("act", lambda e, pt=pt, q4=q4: e.activation(out=o_.ap[:n, q4 * 512:(q4 + 1) * 512], in_=pt.rearrange("p c n -> p (c n)"), func=AF.Copy),
                 reads=[PSB[bank]], writes=[o_.b])
        P.dma("sp", O["yo"][t * 128:t * 128 + n, :], o_.ap[:n], reads=[o_.b], writes=[K.bout])


def build_program():
    nc = bass.Bass("TRN2", target_bir_lowering=False)
    K = Ctx()
    K.nc = nc
    K.P = Prog(nc)
    P = K.P

    def din(name, shape, dt=F32):
        return nc.dram_tensor(name, list(shape), dt, kind="ExternalInput").ap()

    def dout(name, shape, dt=F32):
        return nc.dram_tensor(name, list(shape), dt, kind="ExternalOutput").ap()

    I = {}
    I["xo"] = din("xo", [NT, D])
    I["xp"] = din("xp", [NPRE, D])
    I["w_in"] = din("w_in", [D, INC])
    I["w_glu_val"] = din("w_glu_val", [1024, D])
    I["w_glu_gate"] = din("w_glu_gate", [1024, D])
    I["w_attn_br"] = din("w_attn_br", [1024, D])
    I["w_out"] = din("w_out", [D, D])
    I["w_ffn_in"] = din("w_ffn_in", [D, 2 * HID])
    I["w_ffn_out"] = din("w_ffn_out", [HID, D])
    for nm in ["norm_attn", "norm_ffn", "norm_final"]:
        I[nm] = din(nm, [D])
    I["lam_re"] = din("lam_re", [64, 64])
    I["lam_im"] = din("lam_im", [64, 64])
    I["log_dt"] = din("log_dt", [64])
    I["b_re"] = din("b_re", [64, 64, 16])
    I["b_im"] = din("b_im", [64, 64, 16])
    I["c_re"] = din("c_re", [64, 16, 64])
    I["c_im"] = din("c_im", [64, 16, 64])
    I["d_skip"] = din("d_skip", [1024])
    I["sinks"] = din("sinks", [16])
    I["sre"] = din("sre", [16, 64, 64])
    I["sim"] = din("sim", [16, 64, 64])
    I["ck"] = din("ck", [16, 128, 256])
    I["cv"] = din("cv", [16, 128, 256])
    I["bmt"] = din("bmt", [128, 16, 2, 128])
    I["bmt0"] = din("bmt0", [128, 16, 128])
    I["bmsc"] = din("bmsc", [128, 16, 4])
    I["bmsn"] = din("bmsn", [4, 16, 4])
    I["kcol"] = din("kcol", [128, 1])
    I["m8"] = din("m8", [128, 8])
    I["m88"] = din("m88", [128, 8, 8])
    I["trow"] = din("trow", [128, 64])
    O = {}
    O["yo"] = dout("yo", [NT, D])
    O["pst"] = dout("pst", [2, 64, 64])
    O["pkv"] = dout("pkv", [128, 512])
    O["sst"] = dout("sst", [2, 16, 64, 64])
    O["skk"] = dout("skk", [16, 128, 256])
    O["skv"] = dout("skv", [16, 128, 256])
    if DEBUG:
        O["dbg_h"] = dout("dbg_h", [128, 2, 32])
        O["dbg_hT"] = dout("dbg_hT", [128, 16, NT], BF16)
        O["dbg_uT"] = dout("dbg_uT", [128, 8, NT], BF16)
        O["dbg_gT"] = dout("dbg_gT", [128, 8, NT], BF16)
        O["dbg_oT"] = dout("dbg_oT", [128, 8, NT], BF16)
        O["dbg_mT"] = dout("dbg_mT", [128, 16, NT], BF16)
    K.I, K.O = I, O
    K.bout = Buf("out")
    K.A = Arena(nc)
    A = K.A
    K.ps = nc.alloc_psum_tensor("psum", [128, 8, 512], F32)
    K.psb = [Buf(f"psb{i}") for i in range(8)]

    K.identf = A.alloc("identf", [128], F32, top=True)
    K.identb = A.alloc("identb", [128], BF16, top=True)
    K.ones64 = A.alloc("ones64", [64], BF16, top=True)
    K.onesm = A.alloc("onesm", [128], BF16, top=True)
    K.epsc = A.alloc("epsc", [1], F32, top=True)
    bC = Buf("const")
    K.bC = bC
    P.op("pool", lambda e: e.memset(K.identf.ap, 0.0), writes=[bC])
    P.op("pool", lambda e: e.affine_select(out=K.identf.ap, in_=K.identf.ap, pattern=[[-1, 128]], compare_op=ALU.not_equal,
                                          fill=1.0, base=0, channel_multiplier=1), reads=[bC], writes=[bC])
    P.op("pool", lambda e: e.tensor_copy(out=K.identb.ap, in_=K.identf.ap), reads=[bC], writes=[bC])
    P.op("pool", lambda e: e.memset(K.ones64.ap, 1.0), writes=[bC])
    P.op("pool", lambda e: e.memset(K.onesm.ap, 1.0 / D), writes=[bC])
    P.op("pool", lambda e: e.memset(K.epsc.ap, EPS), writes=[bC])

    K.hT = A.alloc("hT", [16, NT], BF16, nbufs=9, top=True)
    K.gT = A.alloc("gT", [8, NT], BF16, top=True)
    K.gT.b = [K.gT.b]
    K.oT = A.alloc("oT", [8, NT], BF16, top=True)
    K.oT.b = [K.oT.b]
    def stop(name):
        return STOP == name

    ssm_tables(K)
    if DEBUG:
        O["dbg_A4"] = dout("dbg_A4", [128, 2, 2, 32])
        O["dbg_A4c"] = dout("dbg_A4c", [128, 2, 2, 32])
        O["dbg_BbR"] = dout("dbg_BbR", [128, 32, 16])
        O["dbg_BbI"] = dout("dbg_BbI", [128, 32, 16])
        O["dbg_ApR"] = dout("dbg_ApR", [128, 64, 64], BF16)
        O["dbg_ApI"] = dout("dbg_ApI", [128, 64, 64], BF16)
        for nm, t in (("dbg_A4", K.A4), ("dbg_A4c", K.A4c), ("dbg_BbR", K.BbR), ("dbg_BbI", K.BbI), ("dbg_ApR", K.ApR), ("dbg_ApI", K.ApI)):
            P.dma("sp", O[nm], t.ap, reads=bl(t.b), writes=[K.bout])
    if not stop("tables"):
        K.kTh = A.alloc("kTh", [4, 128], BF16, top=True)
        K.vth = A.alloc("vth", [256], BF16, top=True)
        prefix_phase(K)
    if not (stop("tables") or stop("prefix")):
        K.uT = A.alloc("uT", [8, NT], BF16, top=True)
        K.uT.b = [K.uT.b]
        own_norm(K)
        for t in K.xts + K.xss + [K.junk, K.ss]:
            A.free(t)
        if DEBUG:
            P.dma("sp", O["dbg_hT"], K.hT.ap, reads=bl(K.hT.b), writes=[K.bout])
    if STOP not in ("tables", "prefix", "own_norm"):
        proj_phase(K)
        if DEBUG:
            P.dma("sp", O["dbg_uT"], K.uT.ap, reads=bl(K.uT.b), writes=[K.bout])
    if STOP not in ("tables", "prefix", "own_norm", "proj"):
        ssm_tables_own(K)
        A.free(K.BbR)
        A.free(K.BbI)
        ssm_own(K)
        if DEBUG:
            P.dma("sp", O["dbg_gT"], K.gT.ap, reads=bl(K.gT.b), writes=[K.bout])
    if STOP not in ("tables", "prefix", "own_norm", "proj", "ssm"):
        attention(K)
        if DEBUG:
            P.dma("sp", O["dbg_oT"], K.oT.ap, reads=bl(K.oT.b), writes=[K.bout])
    if STOP not in ("tables", "prefix", "own_norm", "proj", "ssm", "attn"):
        merge_phase(K)
        if DEBUG:
            P.dma("sp", O["dbg_mT"], K.mT.ap, reads=bl(K.mT.b), writes=[K.bout])
        A.free(K.gT)
        A.free(K.oT)
    if STOP not in ("tables", "prefix", "own_norm", "proj", "ssm", "attn", "merge"):
        post_phase(K)
    P.finish()
    K.ninstr = P.ninstr
    return nc, K


_CACHE = {}


def _host_consts(rel_bias, qidx):
    rb = np.asarray(rel_bias, np.float32)
    k = np.arange(128)[:, None, None]
    kb = np.arange(2)[None, :, None]
    q = np.arange(128)[None, None, :]
    dist = q + 128 - (kb * 128 + k)
    valid = (dist >= 0) & (dist < 128)
    bk = t5_bucket(dist)
    bias = rb[bk]
    bias = np.where(valid[..., None], bias, np.float32(NEG)).astype(np.float32)
    hperm = np.array([4 * (n // 4) + 2 * ((n % 4) % 2) + (n % 4) // 2 for n in range(16)])
    bmt = np.ascontiguousarray(np.transpose(bias, (0, 3, 1, 2))[:, hperm])
    bmt0 = bmt[:, :, 0, :].copy() if qidx > 0 else np.full((128, 16, 128), NEG, np.float32)
    j = np.arange(132)[:, None]
    t = np.arange(4)[None, :]
    dist = t + 128 - j
    valid = (dist >= 0) & (dist < 128)
    bs = np.where(valid[..., None], rb[t5_bucket(dist)], np.float32(NEG)).astype(np.float32)
    sperm = np.array([2 * (n % 8) + n // 8 for n in range(16)])
    bs = np.ascontiguousarray(np.transpose(bs, (0, 2, 1))[:, sperm])
    return bmt, np.ascontiguousarray(bmt0), np.ascontiguousarray(bs[:128]), np.ascontiguousarray(bs[128:])


def kernel(x_prompt, x_sample, state_ssm_re, state_ssm_im, cache_win_k, cache_win_v, rel_bias,
           norm_attn, w_in, lam_re, lam_im, log_dt, b_re, b_im, c_re, c_im, d_skip,
           w_glu_val, w_glu_gate, w_attn_br, sinks, w_out, norm_ffn, w_ffn_in, w_ffn_out, norm_final):
    f = lambda a: np.ascontiguousarray(np.asarray(a, dtype=np.float32))
    x_prompt, x_sample = f(x_prompt), f(x_sample)
    if "nc" not in _CACHE:
        _CACHE["nc"], _CACHE["K"] = build_program()
    nc = _CACHE["nc"]
    shared = {
        "w_in": f(w_in)[0], "w_glu_val": f(w_glu_val)[0], "w_glu_gate": f(w_glu_gate)[0], "w_attn_br": f(w_attn_br)[0],
        "w_out": f(w_out)[0], "w_ffn_in": f(w_ffn_in)[0], "w_ffn_out": f(w_ffn_out)[0],
        "norm_attn": f(norm_attn)[0], "norm_ffn": f(norm_ffn)[0], "norm_final": f(norm_final),
        "lam_re": f(lam_re)[0], "lam_im": f(lam_im)[0], "log_dt": f(log_dt)[0], "b_re": f(b_re)[0], "b_im": f(b_im)[0],
        "c_re": f(c_re)[0], "c_im": f(c_im)[0], "d_skip": f(d_skip)[0], "sinks": f(sinks)[0],
        "kcol": (127 - np.arange(128, dtype=np.float32)).reshape(128, 1),
        "m8": (np.arange(128)[:, None] // 16 == np.arange(8)[None, :]).astype(np.float32),
        "m88": np.ascontiguousarray(np.broadcast_to(np.eye(8, dtype=np.float32), (128, 8, 8))),
        "trow": np.ascontiguousarray(np.broadcast_to(np.arange(1, 65, dtype=np.float32), (128, 64))),
    }
    sre, sim = f(state_ssm_re)[0], f(state_ssm_im)[0]
    ck = f(cache_win_k)[0].reshape(128, 128, 256)
    cv = f(cache_win_v)[0].reshape(128, 128, 256)
    in_maps = []
    for c in range(8):
        b, q = c // 4, c % 4
        xo = np.concatenate([x_prompt[b, 1024 * q:1024 * (q + 1)], x_sample[16 * c:16 * c + 16].reshape(64, D)], axis=0)
        xp = np.zeros((NPRE, D), np.float32)
        if q > 0:
            xp[NPRE - 1024 * q:] = x_prompt[b, 0:1024 * q]
        bmt, bmt0, bmsc, bmsn = _host_consts(rel_bias, q)
        m = dict(shared)
        m.update({"xo": np.ascontiguousarray(xo), "xp": xp, "sre": sre[16 * c:16 * c + 16], "sim": sim[16 * c:16 * c + 16],
                  "ck": ck[16 * c:16 * c + 16], "cv": cv[16 * c:16 * c + 16], "bmt": bmt, "bmt0": bmt0, "bmsc": bmsc, "bmsn": bmsn})
        in_maps.append({k: np.ascontiguousarray(v) for k, v in m.items()})
    res = run_bass_kernel_spmd(nc, in_maps, core_ids=list(range(8)))
    R = res.results
    _CACHE["last"] = R
    y_prompt = np.zeros((2, 4096, D), np.float32)
    y_sample = np.zeros((128, 4, D), np.float32)
    p_re = np.zeros((1, 2, 64, 64), np.float32)
    p_im = np.zeros((1, 2, 64, 64), np.float32)
    p_k = np.zeros((1, 2, 128, 4, 64), np.float32)
    p_v = np.zeros((1, 2, 128, 4, 64), np.float32)
    s_re = np.zeros((1, 128, 64, 64), np.float32)
    s_im = np.zeros((1, 128, 64, 64), np.float32)
    s_k = np.zeros((1, 128, 128, 4, 64), np.float32)
    s_v = np.zeros((1, 128, 128, 4, 64), np.float32)
    for c in range(8):
        b, q = c // 4, c % 4
        r = R[c]
        y_prompt[b, 1024 * q:1024 * (q + 1)] = r["yo"][:1024]
        y_sample[16 * c:16 * c + 16] = r["yo"][1024:].reshape(16, 4, D)
        if q == 3:
            p_re[0, b] = r["pst"][0]
            p_im[0, b] = r["pst"][1]
            p_k[0, b] = r["pkv"][:, :256].reshape(128, 4, 64)
            p_v[0, b] = r["pkv"][:, 256:].reshape(128, 4, 64)
        s_re[0, 16 * c:16 * c + 16] = r["sst"][0]
        s_im[0, 16 * c:16 * c + 16] = r["sst"][1]
        s_k[0, 16 * c:16 * c + 16] = r["skk"].reshape(16, 128, 4, 64)
        s_v[0, 16 * c:16 * c + 16] = r["skv"].reshape(16, 128, 4, 64)
    return (y_prompt, y_sample, p_re, p_im, p_k, p_v, s_re, s_im, s_k, s_v)
```

```python
import numpy as np
import concourse.bass as bass
import concourse.mybir as mybir
from concourse.bass_utils import run_bass_kernel_spmd

F32 = mybir.dt.float32
BF16 = mybir.dt.bfloat16
I32 = mybir.dt.int32
AF = mybir.ActivationFunctionType
ALU = mybir.AluOpType
AX = mybir.AxisListType

D = 2048
NT = 1088
NPRE = 3072
NCH = 16
HID = 5632
INC = 6656
EPS = 1e-6
NEG = -1e30
BLKS = [(0, 512), (512, 512), (1024, 64)]
NDS = 24
NDS_SW = 8
ARENA_BYTES = 206 * 1024
DEBUG = False
STOP = None
ASTOP = None
DT_SIZE = {F32: 4, BF16: 2, I32: 4}


class Buf:
    __slots__ = ("name", "w", "r")

    def __init__(self, name="", guards=None):
        self.name = name
        self.w = None
        self.r = dict(guards) if guards else {}


def _merge(dst, src):
    for k, ev in src.items():
        if k not in dst or dst[k][1] < ev[1]:
            dst[k] = ev


class Tile:
    __slots__ = ("ap", "b", "off", "nbytes", "name")


class Arena:
    def __init__(self, nc):
        self.t = nc.alloc_sbuf_tensor("arena", [128, ARENA_BYTES // 4], F32)
        self.free_list = [[0, ARENA_BYTES, {}]]

    def alloc(self, name, shape, dt, nbufs=1, top=False):
        n = int(np.prod(shape)) * DT_SIZE[dt]
        n = (n + 63) // 64 * 64
        order = range(len(self.free_list) - 1, -1, -1) if top else range(len(self.free_list))
        for i in order:
            off, size, g = self.free_list[i]
            if size >= n:
                if size == n:
                    self.free_list.pop(i)
                elif top:
                    self.free_list[i] = [off, size - n, dict(g)]
                    off = off + size - n
                else:
                    self.free_list[i] = [off + n, size - n, dict(g)]
                t = Tile()
                t.name, t.off, t.nbytes = name, off, n
                v = self.t[:, off // 4:(off + n) // 4]
                if dt != F32:
                    v = v.bitcast(dt)
                ne = int(np.prod(shape))
                v = v[:, 0:ne]
                if len(shape) == 2:
                    v = v.rearrange("p (a b) -> p a b", a=shape[0])
                elif len(shape) == 3:
                    v = v.rearrange("p (a b c) -> p a b c", a=shape[0], b=shape[1])
                elif len(shape) == 4:
                    v = v.rearrange("p (a b c d) -> p a b c d", a=shape[0], b=shape[1], c=shape[2])
                t.ap = v
                if nbufs == 1:
                    t.b = Buf(name, g)
                else:
                    t.b = [Buf(f"{name}{j}", g) for j in range(nbufs)]
                return t
        raise RuntimeError(f"arena OOM for {name} ({n} B); free={[(o, s) for o, s, _ in self.free_list]}")

    def free(self, t):
        g = {}
        bufs = t.b if isinstance(t.b, list) else [t.b]
        for b in bufs:
            if b.w is not None:
                _merge(g, {b.w[0].num: b.w})
            _merge(g, b.r)
        self.free_list.append([t.off, t.nbytes, g])
        self.free_list.sort(key=lambda x: x[0])
        out = []
        for blk in self.free_list:
            if out and out[-1][0] + out[-1][1] == blk[0]:
                out[-1][1] += blk[1]
                _merge(out[-1][2], blk[2])
            else:
                out.append(blk)
        self.free_list = out


class Prog:
    def __init__(self, nc):
        self.nc = nc
        self.eng = {"pe": nc.tensor, "act": nc.scalar, "dve": nc.vector, "pool": nc.gpsimd, "sp": nc.sync}
        self.csem = {e: nc.alloc_semaphore(name=f"c_{e}") for e in self.eng}
        self.ccnt = {e: 0 for e in self.eng}
        self.dsems = [nc.alloc_semaphore(name=f"d_{i}") for i in range(NDS)]
        self.dcnt = [0] * NDS
        self.dnext = 0
        self.dnext_sw = 0
        self.waited = {e: {} for e in self.eng}
        self.ninstr = 0

    def _wait(self, e, ev):
        sem, val = ev
        k = sem.num
        if self.waited[e].get(k, 0) >= val:
            return
        self.eng[e].wait_ge(sem, val)
        self.waited[e][k] = val

    def _deps(self, e, reads, writes, skip=None, relax=False):
        own = self.csem[e].num
        lim = self.ccnt[e] - 1

        def need(ev):
            if ev[0].num == skip:
                return False
            if relax and ev[0].num == own and ev[1] <= lim:
                return False
            return True
        for b in reads:
            if b.w is not None and need(b.w):
                self._wait(e, b.w)
        for b in writes:
            if b.w is not None and need(b.w):
                self._wait(e, b.w)
            for k, ev in b.r.items():
                if need(ev):
                    self._wait(e, ev)

    @staticmethod
    def _mark(ev, reads, writes):
        for b in reads:
            b.r[ev[0].num] = ev
        for b in writes:
            b.w = ev
            b.r = {}

    def op(self, e, fn, reads=(), writes=(), relax=False):
        self._deps(e, reads, writes, relax=relax)
        ins = fn(self.eng[e])
        self.ccnt[e] += 1
        self.ninstr += 1
        ins.then_inc(self.csem[e], 1)
        self._mark((self.csem[e], self.ccnt[e]), reads, writes)

    def group(self, e, fns, reads=(), writes=()):
        self._deps(e, reads, writes)
        ins = None
        for fn in fns:
            ins = fn(self.eng[e])
            self.ninstr += 1
        self.ccnt[e] += 1
        ins.then_inc(self.csem[e], 1)
        self._mark((self.csem[e], self.ccnt[e]), reads, writes)

    def pe(self, fns, reads=(), writes=()):
        self._deps("pe", reads, writes, skip=self.csem["pe"].num)
        ins = None
        for fn in fns:
            ins = fn(self.nc.tensor)
            self.ninstr += 1
        self.ccnt["pe"] += 1
        ins.then_inc(self.csem["pe"], 1)
        self._mark((self.csem["pe"], self.ccnt["pe"]), reads, writes)

    def dma(self, e, out, in_, reads=(), writes=(), **kw):
        if e == "pool":
            i = self.dnext_sw
            self.dnext_sw = (i + 1) % NDS_SW
        else:
            i = NDS_SW + self.dnext
            self.dnext = (self.dnext + 1) % (NDS - NDS_SW)
        if self.dcnt[i] > 0:
            self._wait(e, (self.dsems[i], self.dcnt[i]))
        self._deps(e, reads, writes)
        ins = self.eng[e].dma_start(out=out, in_=in_, **kw)
        self.ninstr += 1
        self.dcnt[i] += 16
        ins.then_inc(self.dsems[i], 16)
        self._mark((self.dsems[i], self.dcnt[i]), reads, writes)

    def finish(self):
        for e in self.eng:
            if e != "sp" and self.ccnt[e] > 0:
                self._wait("sp", (self.csem[e], self.ccnt[e]))
        for i in range(NDS):
            if self.dcnt[i] > 0:
                self._wait("sp", (self.dsems[i], self.dcnt[i]))


def t5_bucket(dist):
    n = np.maximum(dist, 0)
    max_exact = 16
    large = max_exact + (np.log(np.maximum(n, 1) / max_exact) / np.log(128 / max_exact) * 16).astype(np.int32)
    large = np.minimum(large, 31)
    return np.where(n < max_exact, n, large).astype(np.int32)


class Ctx:
    pass


def share(t, buf):
    if isinstance(t.b, Buf) and t.b is not buf:
        _merge(buf.r, t.b.r)
    t.b = buf


def bl(b):
    return b if isinstance(b, list) else [b]


def range_reduce_sin(K, ang, bang, out, bout, shape, part=128):
    P, A = K.P, K.A
    ni = A.alloc("rr_i", shape, I32)
    nf = A.alloc("rr_f", shape, F32)
    C1 = 6.28125
    C2 = float(2 * np.pi - 6.28125)
    P.op("dve", lambda e: e.tensor_scalar(out=nf.ap, in0=ang, scalar1=float(1.0 / (2 * np.pi)), scalar2=None, op0=ALU.mult),
         reads=[bang], writes=[nf.b])
    P.op("dve", lambda e: e.tensor_copy(out=ni.ap, in_=nf.ap), reads=[nf.b], writes=[ni.b])
    P.op("dve", lambda e: e.tensor_copy(out=nf.ap, in_=ni.ap), reads=[ni.b], writes=[nf.b])
    P.op("dve", lambda e: e.scalar_tensor_tensor(out=ang, in0=nf.ap, scalar=-C1, in1=ang, op0=ALU.mult, op1=ALU.add),
         reads=[nf.b, bang], writes=[bang])
    P.op("dve", lambda e: e.scalar_tensor_tensor(out=ang, in0=nf.ap, scalar=-C2, in1=ang, op0=ALU.mult, op1=ALU.add),
         reads=[nf.b, bang], writes=[bang])
    P.op("dve", lambda e: e.tensor_scalar(out=ang, in0=ang, scalar1=3.1415925, scalar2=-3.1415925, op0=ALU.min, op1=ALU.max),
         reads=[bang], writes=[bang])
    P.op("act", lambda e: e.activation(out=out, in_=ang, func=AF.Sin), reads=[bang], writes=[bout])
    A.free(ni)
    A.free(nf)


def accurate_exp(K, x, bx, shape):
    P, A = K.P, K.A
    y = A.alloc("aexp_y", shape, F32)
    acc = A.alloc("aexp_a", shape, F32)
    P.op("dve", lambda e: e.tensor_scalar(out=y.ap, in0=x, scalar1=0.125, scalar2=None, op0=ALU.mult), reads=[bx], writes=[y.b])
    fact = [1.0]
    for i in range(1, 12):
        fact.append(fact[-1] * i)
    P.op("dve", lambda e: e.tensor_scalar(out=acc.ap, in0=y.ap, scalar1=1.0 / fact[11], scalar2=1.0 / fact[10], op0=ALU.mult, op1=ALU.add),
         reads=[y.b], writes=[acc.b])
    for i in range(9, -1, -1):
        P.op("dve", lambda e: e.tensor_tensor(out=acc.ap, in0=acc.ap, in1=y.ap, op=ALU.mult), reads=[acc.b, y.b], writes=[acc.b])
        P.op("dve", lambda e, i=i: e.tensor_scalar(out=acc.ap, in0=acc.ap, scalar1=float(1.0 / fact[i]), scalar2=None, op0=ALU.add),
             reads=[acc.b], writes=[acc.b])
    for _ in range(3):
        P.op("dve", lambda e: e.tensor_tensor(out=acc.ap, in0=acc.ap, in1=acc.ap, op=ALU.mult), reads=[acc.b], writes=[acc.b])
    P.op("dve", lambda e: e.tensor_copy(out=x, in_=acc.ap), reads=[acc.b, bx], writes=[bx])
    A.free(y)
    A.free(acc)


def ssm_tables(K):
    nc, P, I, A = K.nc, K.P, K.I, K.A
    PSB = K.psb
    lrG = A.alloc("lrG", [32], F32)
    liG = A.alloc("liG", [32], F32)
    dtG = A.alloc("dtG", [32], F32)
    bG = Buf("G")
    with nc.allow_non_contiguous_dma(reason="tiny transposed param loads"):
        for gh in range(2):
            rows = slice(64 * gh, 64 * gh + 64)
            P.dma("sp", lrG.ap[rows, :], I["lam_re"][32 * gh:32 * gh + 32, :].rearrange("g p -> p g"), writes=[bG])
            P.dma("sp", liG.ap[rows, :], I["lam_im"][32 * gh:32 * gh + 32, :].rearrange("g p -> p g"), writes=[bG])
            P.dma("sp", dtG.ap[rows, :], I["log_dt"][32 * gh:32 * gh + 32].unsqueeze(0).broadcast_to([64, 32]), writes=[bG])
    accurate_exp(K, dtG.ap, bG, [32])
    P.op("dve", lambda e: e.tensor_scalar(out=lrG.ap, in0=lrG.ap, scalar1=-1e-4, scalar2=None, op0=ALU.min), reads=[bG], writes=[bG])
    rho = A.alloc("rhoG", [32], F32)
    th = A.alloc("thG", [32], F32)
    P.op("dve", lambda e: e.tensor_tensor(out=rho.ap, in0=lrG.ap, in1=dtG.ap, op=ALU.mult), reads=[bG], writes=[bG])
    P.op("dve", lambda e: e.tensor_tensor(out=th.ap, in0=liG.ap, in1=dtG.ap, op=ALU.mult), reads=[bG], writes=[bG])
    tmps = [lrG, liG, dtG, rho, th]

    def TTv(o, a, b_, op, rd, wb):
        P.op("dve", lambda e: e.tensor_tensor(out=o, in0=a, in1=b_, op=op), reads=rd, writes=[wb])

    def TSv(o, a, s1, s2, op0, op1, rd, wb):
        if op1 is None:
            P.op("dve", lambda e: e.tensor_scalar(out=o, in0=a, scalar1=s1, scalar2=None, op0=op0), reads=rd, writes=[wb])
        else:
            P.op("dve", lambda e: e.tensor_scalar(out=o, in0=a, scalar1=s1, scalar2=s2, op0=op0, op1=op1), reads=rd, writes=[wb])

    b1 = Buf("a1")
    nm = lambda n: A.alloc(n, [32], F32)
    r_, x2, sn, cs, t_, ex, a1r, a1i = [nm(n) for n in ("r_", "x2", "sn", "cs", "t_", "ex", "a1r", "a1i")]
    ni = A.alloc("ni", [32], I32)
    for t in (r_, x2, sn, cs, t_, ex, a1r, a1i, ni):
        share(t, b1)
    tmps.extend([r_, x2, sn, cs, t_, ex, a1r, a1i, ni])
    C1 = 6.28125
    C2 = float(2 * np.pi - 6.28125)
    TSv(t_.ap, th.ap, float(1.0 / (2 * np.pi)), None, ALU.mult, None, [bG], b1)
    P.op("dve", lambda e: e.tensor_copy(out=ni.ap, in_=t_.ap), reads=[b1], writes=[b1])
    P.op("dve", lambda e: e.tensor_copy(out=t_.ap, in_=ni.ap), reads=[b1], writes=[b1])
    P.op("dve", lambda e: e.scalar_tensor_tensor(out=r_.ap, in0=t_.ap, scalar=-C1, in1=th.ap, op0=ALU.mult, op1=ALU.add), reads=[b1, bG], writes=[b1])
    P.op("dve", lambda e: e.scalar_tensor_tensor(out=r_.ap, in0=t_.ap, scalar=-C2, in1=r_.ap, op0=ALU.mult, op1=ALU.add), reads=[b1], writes=[b1])
    TSv(r_.ap, r_.ap, 0.125, None, ALU.mult, None, [b1], b1)
    TTv(x2.ap, r_.ap, r_.ap, ALU.mult, [b1], b1)
    TSv(sn.ap, x2.ap, 1.0 / 362880, -1.0 / 5040, ALU.mult, ALU.add, [b1], b1)
    for cf in (1.0 / 120, -1.0 / 6, 1.0):
        TTv(sn.ap, sn.ap, x2.ap, ALU.mult, [b1], b1)
        TSv(sn.ap, sn.ap, float(cf), None, ALU.add, None, [b1], b1)
    TTv(sn.ap, sn.ap, r_.ap, ALU.mult, [b1], b1)
    TSv(cs.ap, x2.ap, -1.0 / 3628800, 1.0 / 40320, ALU.mult, ALU.add, [b1], b1)
    for cf in (-1.0 / 720, 1.0 / 24, -0.5, 1.0):
        TTv(cs.ap, cs.ap, x2.ap, ALU.mult, [b1], b1)
        TSv(cs.ap, cs.ap, float(cf), None, ALU.add, None, [b1], b1)
    for _ in range(3):
        TTv(t_.ap, sn.ap, cs.ap, ALU.mult, [b1], b1)
        TTv(x2.ap, sn.ap, sn.ap, ALU.mult, [b1], b1)
        TSv(sn.ap, t_.ap, 2.0, None, ALU.mult, None, [b1], b1)
        TSv(cs.ap, x2.ap, -2.0, 1.0, ALU.mult, ALU.add, [b1], b1)
    TSv(ex.ap, rho.ap, 1.0 / 120, 1.0 / 24, ALU.mult, ALU.add, [bG], b1)
    for cf in (1.0 / 6, 0.5, 1.0, 1.0):
        TTv(ex.ap, ex.ap, rho.ap, ALU.mult, [b1, bG], b1)
        TSv(ex.ap, ex.ap, float(cf), None, ALU.add, None, [b1], b1)
    TTv(a1r.ap, ex.ap, cs.ap, ALU.mult, [b1], b1)
    TTv(a1i.ap, ex.ap, sn.ap, ALU.mult, [b1], b1)
    b128 = Buf("a128")
    a128r, a128i, q1, q2 = [nm(n) for n in ("a128r", "a128i", "q1", "q2")]
    for t in (a128r, a128i, q1, q2):
        share(t, b128)
    tmps.extend([a128r, a128i, q1, q2])
    P.op("dve", lambda e: e.tensor_copy(out=a128r.ap, in_=a1r.ap), reads=[b1], writes=[b128])
    P.op("dve", lambda e: e.tensor_copy(out=a128i.ap, in_=a1i.ap), reads=[b1], writes=[b128])
    K.Rm = A.alloc("Rm", [32], F32)
    K.thG = A.alloc("thGk", [32], F32)
    K.c64 = A.alloc("c64", [32], F32)
    K.s64 = A.alloc("s64", [32], F32)
    r64 = nm("r64")
    share(r64, b128)
    tmps.append(r64)
    P.op("dve", lambda e: e.tensor_copy(out=K.Rm.ap, in_=ex.ap), reads=[b1], writes=[K.Rm.b])
    P.op("dve", lambda e: e.tensor_copy(out=K.thG.ap, in_=th.ap), reads=[bG], writes=[K.thG.b])
    P.op("dve", lambda e: e.tensor_copy(out=r64.ap, in_=ex.ap), reads=[b1], writes=[b128])
    for _ in range(6):
        TTv(r64.ap, r64.ap, r64.ap, ALU.mult, [b128], b128)
    P.op("dve", lambda e: e.reciprocal(out=r64.ap, in_=r64.ap), reads=[b128], writes=[b128])
    for it in range(7):
        if it == 6:
            TTv(K.c64.ap, a128r.ap, r64.ap, ALU.mult, [b128], K.c64.b)
            TTv(K.s64.ap, a128i.ap, r64.ap, ALU.mult, [b128], K.s64.b)
        TTv(q1.ap, a128r.ap, a128r.ap, ALU.mult, [b128], b128)
        TTv(q2.ap, a128i.ap, a128i.ap, ALU.mult, [b128], b128)
        TTv(a128i.ap, a128r.ap, a128i.ap, ALU.mult, [b128], b128)
        TSv(a128i.ap, a128i.ap, 2.0, None, ALU.mult, None, [b128], b128)
        TTv(a128r.ap, q1.ap, q2.ap, ALU.subtract, [b128], b128)

    def make_a4(ar, ai, b, name):
        a4 = A.alloc(name, [2, 2, 32], F32)
        P.op("dve", lambda e: e.tensor_copy(out=a4.ap[:, 0, 0, :], in_=ar.ap), reads=[b], writes=[a4.b])
        P.op("dve", lambda e: e.tensor_scalar(out=a4.ap[:, 0, 1, :], in0=ai.ap, scalar1=-1.0, scalar2=None, op0=ALU.mult), reads=[b], writes=[a4.b])
        P.op("dve", lambda e: e.tensor_copy(out=a4.ap[:, 1, 0, :], in_=ai.ap), reads=[b], writes=[a4.b])
        P.op("dve", lambda e: e.tensor_copy(out=a4.ap[:, 1, 1, :], in_=ar.ap), reads=[b], writes=[a4.b])
        return a4

    K.A4 = make_a4(a1r, a1i, b1, "A4_1")
    K.A4c = make_a4(a128r, a128i, b128, "A4_128")

    den = A.alloc("den", [32], F32)
    t1 = A.alloc("ct1", [32], F32)
    t2 = A.alloc("ct2", [32], F32)
    cr = A.alloc("coefr", [32], F32)
    ci = A.alloc("coefi", [32], F32)
    am1 = A.alloc("am1", [32], F32)
    bc = Buf("coef")
    for t in (den, t1, t2, cr, ci, am1):
        share(t, bc)
    tmps.extend([den, t1, t2, cr, ci, am1])
    TT = lambda o, a, b_, op, rd: P.op("dve", lambda e: e.tensor_tensor(out=o, in0=a, in1=b_, op=op), reads=rd, writes=[bc])
    TT(den.ap, lrG.ap, lrG.ap, ALU.mult, [bG])
    TT(t1.ap, liG.ap, liG.ap, ALU.mult, [bG])
    TT(den.ap, den.ap, t1.ap, ALU.add, [bc])
    P.op("dve", lambda e: e.reciprocal(out=den.ap, in_=den.ap), reads=[bc], writes=[bc])
    P.op("dve", lambda e: e.tensor_scalar(out=am1.ap, in0=a1r.ap, scalar1=-1.0, scalar2=None, op0=ALU.add), reads=[b1], writes=[bc])
    TT(t1.ap, am1.ap, lrG.ap, ALU.mult, [bc, bG])
    TT(t2.ap, a1i.ap, liG.ap, ALU.mult, [b1, bG])
    TT(t1.ap, t1.ap, t2.ap, ALU.add, [bc])
    TT(cr.ap, t1.ap, den.ap, ALU.mult, [bc])
    TT(t1.ap, a1i.ap, lrG.ap, ALU.mult, [bc, b1, bG])
    TT(t2.ap, am1.ap, liG.ap, ALU.mult, [bc, bG])
    TT(t1.ap, t1.ap, t2.ap, ALU.subtract, [bc])
    TT(ci.ap, t1.ap, den.ap, ALU.mult, [bc])

    K.BbR = A.alloc("BbR", [32, 16], F32)
    K.BbI = A.alloc("BbI", [32, 16], F32)
    bBb = Buf("Bbar")
    share(K.BbR, bBb)
    share(K.BbI, bBb)
    BreG = A.alloc("BreG", [32, 16], F32)
    BimG = A.alloc("BimG", [32, 16], F32)
    tb1 = A.alloc("tb1", [32, 16], F32)
    bB = Buf("B")
    share(BreG, bB)
    share(BimG, bB)
    with nc.allow_non_contiguous_dma(reason="64B runs param load"):
        for gh in range(2):
            rows = slice(64 * gh, 64 * gh + 64)
            P.dma("sp", BreG.ap[rows], I["b_re"][32 * gh:32 * gh + 32].rearrange("g p c -> p g c"), writes=[bB])
            P.dma("sp", BimG.ap[rows], I["b_im"][32 * gh:32 * gh + 32].rearrange("g p c -> p g c"), writes=[bB])
    crb = cr.ap.unsqueeze(2).broadcast_to([128, 32, 16])
    cib = ci.ap.unsqueeze(2).broadcast_to([128, 32, 16])
    P.op("dve", lambda e: e.tensor_tensor(out=K.BbR.ap, in0=BreG.ap, in1=crb, op=ALU.mult), reads=[bB, bc], writes=[bBb])
    P.op("dve", lambda e: e.tensor_tensor(out=tb1.ap, in0=BimG.ap, in1=cib, op=ALU.mult), reads=[bB, bc], writes=[tb1.b])
    P.op("dve", lambda e: e.tensor_tensor(out=K.BbR.ap, in0=K.BbR.ap, in1=tb1.ap, op=ALU.subtract), reads=[tb1.b, bBb], writes=[bBb])
    P.op("dve", lambda e: e.tensor_tensor(out=K.BbI.ap, in0=BimG.ap, in1=crb, op=ALU.mult), reads=[bB, bc], writes=[bBb])
    P.op("dve", lambda e: e.tensor_tensor(out=tb1.ap, in0=BreG.ap, in1=cib, op=ALU.mult), reads=[bB, bc, bBb], writes=[tb1.b])
    P.op("dve", lambda e: e.tensor_tensor(out=K.BbI.ap, in0=K.BbI.ap, in1=tb1.ap, op=ALU.add), reads=[tb1.b, bBb], writes=[bBb])
    share(BreG, bB)
    share(BimG, bB)

    K.Dcol = A.alloc("Dcol", [8], F32)
    with nc.allow_non_contiguous_dma(reason="tiny"):
        P.dma("sp", K.Dcol.ap, I["d_skip"].rearrange("(j p) -> p j", p=128), writes=[K.Dcol.b])

    K.ApR = A.alloc("ApR", [64, 64], BF16)
    K.ApI = A.alloc("ApI", [64, 64], BF16)
    bAp = Buf("ApowT")
    share(K.ApR, bAp)
    share(K.ApI, bAp)
    kcol = A.alloc("kcol", [1], F32)
    dtF = A.alloc("dtF", [64], F32)
    bF = Buf("F")
    share(kcol, bF)
    share(dtF, bF)
    P.dma("sp", kcol.ap, I["kcol"], writes=[bF])
    P.dma("sp", dtF.ap, I["log_dt"].unsqueeze(0).broadcast_to([128, 64]), writes=[bF])
    accurate_exp(K, dtF.ap, bF, [64])
    for hf in range(2):
        gs = slice(32 * hf, 32 * hf + 32)
        lrF = A.alloc("lrF", [32, 64], F32)
        liF = A.alloc("liF", [32, 64], F32)
        magF = A.alloc("magF", [32, 64], F32)
        snF = A.alloc("snF", [32, 64], F32)
        bH = Buf("Fh")
        for t in (lrF, liF, magF, snF):
            share(t, bH)
        P.dma("sp", lrF.ap, I["lam_re"][gs].unsqueeze(0).broadcast_to([128, 32, 64]), writes=[bH])
        P.dma("sp", liF.ap, I["lam_im"][gs].unsqueeze(0).broadcast_to([128, 32, 64]), writes=[bH])
        dtb = dtF.ap[:, gs].unsqueeze(2).broadcast_to([128, 32, 64])
        P.op("dve", lambda e: e.tensor_scalar(out=lrF.ap, in0=lrF.ap, scalar1=-1e-4, scalar2=None, op0=ALU.min), reads=[bH], writes=[bH])
        P.op("dve", lambda e: e.tensor_tensor(out=lrF.ap, in0=lrF.ap, in1=dtb, op=ALU.mult), reads=[bH, bF], writes=[bH])
        P.op("dve", lambda e: e.tensor_tensor(out=liF.ap, in0=liF.ap, in1=dtb, op=ALU.mult), reads=[bH, bF], writes=[bH])
        P.op("act", lambda e: e.activation(out=magF.ap, in_=lrF.ap, func=AF.Exp, scale=kcol.ap), reads=[bH, bF], writes=[bH])
        P.op("dve", lambda e: e.tensor_scalar(out=lrF.ap, in0=liF.ap, scalar1=kcol.ap, scalar2=None, op0=ALU.mult), reads=[bH, bF], writes=[bH])
        range_reduce_sin(K, lrF.ap, bH, snF.ap, bH, [32, 64])
        P.op("dve", lambda e: e.tensor_tensor(out=K.ApI.ap[:, gs, :], in0=magF.ap, in1=snF.ap, op=ALU.mult), reads=[bH], writes=[bAp])
        P.op("dve", lambda e: e.tensor_scalar(out=lrF.ap, in0=liF.ap, scalar1=kcol.ap, scalar2=float(np.pi / 2), op0=ALU.mult, op1=ALU.add),
             reads=[bH, bF], writes=[bH])
        range_reduce_sin(K, lrF.ap, bH, snF.ap, bH, [32, 64])
        P.op("dve", lambda e: e.tensor_tensor(out=K.ApR.ap[:, gs, :], in0=magF.ap, in1=snF.ap, op=ALU.mult), reads=[bH], writes=[bAp])
        for t in (lrF, liF, magF, snF):
            A.free(t)
    for t in (kcol, dtF, BreG, BimG, tb1):
        A.free(t)
    for t in tmps:
        A.free(t)


def ssm_tables_own(K):
    nc, P, I, A = K.nc, K.P, K.I, K.A
    PSB = K.psb
    bBb = K.BbR.b
    m8 = A.alloc("m8", [8], F32)
    m88 = A.alloc("m88", [8, 8], F32)
    bm = Buf("masks")
    share(m8, bm)
    share(m88, bm)
    P.dma("sp", m8.ap, I["m8"], writes=[bm])
    P.dma("sp", m88.ap, I["m88"], writes=[bm])
    K.BtabR = A.alloc("BtabR", [64, 64], BF16)
    K.BtabI = A.alloc("BtabI", [64, 64], BF16)
    bBtab = Buf("Btab")
    share(K.BtabR, bBtab)
    share(K.BtabI, bBtab)
    K.CtabR = A.alloc("CtabR", [32, 128], BF16)
    K.CtabI = A.alloc("CtabI", [32, 128], BF16)
    bCtab = Buf("Ctab")
    share(K.CtabR, bCtab)
    share(K.CtabI, bCtab)
    cnat_r = A.alloc("cnat_r", [8, 64], F32)
    cnat_i = A.alloc("cnat_i", [8, 64], F32)
    bcn = Buf("cnat")
    share(cnat_r, bcn)
    share(cnat_i, bcn)
    P.dma("sp", cnat_r.ap, I["c_re"].rearrange("(j g) c p -> (g c) j p", j=8), writes=[bcn])
    P.dma("sp", cnat_i.ap, I["c_im"].rearrange("(j g) c p -> (g c) j p", j=8), writes=[bcn])
    cnt = 0
    for src, dst in ((K.BbR, K.BtabR), (K.BbI, K.BtabI)):
        for j in range(8):
            gh, gq = j // 4, (j % 4) * 8
            pb = cnt % 2
            cnt += 1
            rows = slice(64 * gh, 64 * gh + 64)
            inp = src.ap[rows, gq:gq + 8, :]
            pt = K.ps[:, pb, 0:64]
            P.pe([lambda e, inp=inp, pt=pt, rows=rows: e.transpose(out=pt, in_=inp, identity=K.identf.ap[rows, rows])],
                 reads=[bBb, K.bC], writes=[PSB[pb]])
            P.op("dve", lambda e, pt=pt, dst=dst, j=j: e.tensor_tensor(
                out=dst.ap[:, 8 * j:8 * j + 8, :], in0=pt.unsqueeze(1).broadcast_to([128, 8, 64]),
                in1=m8.ap.unsqueeze(2).broadcast_to([128, 8, 64]), op=ALU.mult), reads=[PSB[pb], bm], writes=[bBtab])
    P.op("dve", lambda e: e.tensor_scalar(out=cnat_i.ap, in0=cnat_i.ap, scalar1=-1.0, scalar2=None, op0=ALU.mult), reads=[bcn], writes=[bcn])
    for src, dst in ((cnat_r, K.CtabR), (cnat_i, K.CtabI)):
        for j in range(8):
            gh, gq = j // 4, (j % 4) * 8
            pb = cnt % 2
            cnt += 1
            rows = slice(64 * gh, 64 * gh + 64)
            pt = K.ps[rows, pb, 0:128]
            P.pe([lambda e, src=src, j=j, pt=pt, gh=gh: e.matmul(pt, lhsT=src.ap[:, j, :], rhs=K.identf.ap, start=True, stop=True,
                                                                 tile_position=(0, 64 * gh))],
                 reads=[bcn, K.bC], writes=[PSB[pb]])
            P.op("dve", lambda e, pt=pt, dst=dst, rows=rows, gq=gq: e.tensor_tensor(
                out=dst.ap[rows, gq:gq + 8, :].rearrange("p g (h c) -> p g h c", h=8),
                in0=pt.rearrange("p (g c) -> p g c", g=8).unsqueeze(2).broadcast_to([64, 8, 8, 16]),
                in1=m88.ap[rows].unsqueeze(3).broadcast_to([64, 8, 8, 16]), op=ALU.mult),
                reads=[PSB[pb], bm], writes=[bCtab])
    for t in (cnat_r, cnat_i, m8, m88):
        A.free(t)
    K.CT = A.alloc("CT", [32, 64], BF16)
    K.ST = A.alloc("ST", [32, 64], BF16)
    trow = A.alloc("trow", [64], F32)
    ang = A.alloc("angT", [32, 64], F32)
    sct = A.alloc("sct", [32, 64], F32)
    P.dma("sp", trow.ap, I["trow"], writes=[trow.b])
    thb = K.thG.ap.unsqueeze(2).broadcast_to([128, 32, 64])
    trb = trow.ap.unsqueeze(1).broadcast_to([128, 32, 64])
    P.op("dve", lambda e: e.tensor_tensor(out=ang.ap, in0=thb, in1=trb, op=ALU.mult), reads=[K.thG.b, trow.b], writes=[ang.b])
    range_reduce_sin(K, ang.ap, ang.b, sct.ap, sct.b, [32, 64])
    P.op("dve", lambda e: e.tensor_copy(out=K.ST.ap, in_=sct.ap), reads=[sct.b], writes=[K.ST.b])
    P.op("dve", lambda e: e.tensor_tensor(out=ang.ap, in0=thb, in1=trb, op=ALU.mult), reads=[K.thG.b, trow.b, ang.b], writes=[ang.b])
    P.op("dve", lambda e: e.tensor_scalar(out=ang.ap, in0=ang.ap, scalar1=float(np.pi / 2), scalar2=None, op0=ALU.add), reads=[ang.b], writes=[ang.b])
    range_reduce_sin(K, ang.ap, ang.b, sct.ap, sct.b, [32, 64])
    P.op("dve", lambda e: e.tensor_copy(out=K.CT.ap, in_=sct.ap), reads=[sct.b], writes=[K.CT.b])
    for t in (trow, ang, sct):
        A.free(t)


def load_weights_resident(K, name, src, kc, ncols):
    t = K.A.alloc(name, [kc, ncols], BF16)
    for c0 in range(0, ncols, 512):
        w = min(512, ncols - c0)
        K.P.dma("pool", t.ap[:, :, c0:c0 + w], src[:, c0:c0 + w].rearrange("(c p) n -> p c n", p=128), writes=[t.b])
    return t


def norm_rows(K, xt, n, gb, xs, junk, ss):
    P = K.P
    P.op("act", lambda e: e.activation(out=junk.ap[:n], in_=xt.ap[:n], func=AF.Square, accum_out=ss.ap[:n]),
         reads=[xt.b], writes=[junk.b, ss.b])
    P.op("act", lambda e: e.activation(out=ss.ap[:n], in_=ss.ap[:n], func=AF.Sqrt, bias=K.epsc.ap[:n], scale=1.0 / D),
         reads=[ss.b, K.bC], writes=[ss.b])
    P.op("dve", lambda e: e.reciprocal(out=ss.ap[:n], in_=ss.ap[:n]), reads=[ss.b], writes=[ss.b])
    P.op("dve", lambda e: e.scalar_tensor_tensor(out=xs.ap[:n], in0=xt.ap[:n], scalar=ss.ap[:n], in1=gb.ap[:n],
                                                 op0=ALU.mult, op1=ALU.mult), reads=[xt.b, ss.b, gb.b], writes=[xs.b])


def transpose_rows(K, xs, n, dst_ap, dst_bufs, pbank):
    P = K.P
    ptb = K.ps[:, pbank:pbank + 2, :].rearrange("p a b -> p (a b)").bitcast(BF16)
    ptv = ptb.rearrange("p (c n) -> p c n", c=16)
    pbufs = [K.psb[pbank], K.psb[pbank + 1]]
    P.pe([lambda e, c=c: e.transpose(out=ptv[:, c, 0:n], in_=xs.ap[:n, c * 128:(c + 1) * 128], identity=K.identb.ap[:n, :n])
          for c in range(16)], reads=[xs.b, K.bC], writes=pbufs)
    P.op("act", lambda e: e.activation(out=dst_ap, in_=ptv[:, :, 0:n], func=AF.Copy), reads=pbufs, writes=dst_bufs)


def prefix_phase(K):
    nc, P, I, A = K.nc, K.P, K.I, K.A
    PSB = K.psb
    Wu = load_weights_resident(K, "Wu", I["w_in"][:, 0:1024], 16, 1024)
    K.gb = A.alloc("gb", [D], F32)
    P.dma("sp", K.gb.ap, I["norm_attn"].unsqueeze(0).broadcast_to([128, D]), writes=[K.gb.b])
    xts = [A.alloc(f"xt{i}", [D], F32) for i in range(2)]
    K.junk = A.alloc("junk", [D], BF16)
    K.ss = A.alloc("ss", [1], F32)
    xss = [A.alloc(f"xs{i}", [D], BF16) for i in range(2)]
    hTt = [A.alloc(f"hTt{i}", [16, 128], BF16) for i in range(2)]
    U = [A.alloc(f"U{i}", [1024], BF16) for i in range(2)]
    pr1 = A.alloc("pr1", [2, 32, 16], F32)
    pr2 = A.alloc("pr2", [2, 32, 16], F32)
    Tt = A.alloc("Tt", [2, 32, 16], F32)
    S = A.alloc("S", [2, 32], F32)
    Mm = A.alloc("Mm", [2, 2, 32], F32)
    K.Hc = A.alloc("Hc", [2, 32], F32)
    H = K.Hc
    P.op("dve", lambda e: e.memset(H.ap, 0.0), writes=[H.b])
    BbRb = K.BbR.ap.unsqueeze(1).broadcast_to([128, 2, 32, 16])
    BbIb = K.BbI.ap.unsqueeze(1).broadcast_to([128, 2, 32, 16])
    ntile = NPRE // 128

    def stA(i):
        xt, xs, ht = xts[i % 2], xss[i % 2], hTt[i % 2]
        P.dma("sp", xt.ap, I["xp"][i * 128:(i + 1) * 128, :], writes=[xt.b])
        norm_rows(K, xt, 128, K.gb, xs, K.junk, K.ss)
        transpose_rows(K, xs, 128, ht.ap, [ht.b], 0)

    def stB(i):
        ht, u = hTt[i % 2], U[i % 2]
        for half in range(2):
            pu = K.ps[:, 2 + half, :]
            P.pe([lambda e, c=c, pu=pu, half=half: e.matmul(pu, lhsT=ht.ap[:, c, :], rhs=Wu.ap[:, c, half * 512:(half + 1) * 512],
                                                           start=(c == 0), stop=(c == 15)) for c in range(16)],
                 reads=[ht.b, Wu.b], writes=[PSB[2 + half]])
        P.op("act", lambda e, u=u: e.activation(out=u.ap, in_=K.ps[:, 2:4, :].rearrange("p a b -> p (a b)"), func=AF.Copy),
             reads=[PSB[2], PSB[3]], writes=[u.b])

    zp = K.ps[:, 4:6, :].rearrange("p a (g c) -> p a g c", c=16)

    def stC(i):
        u = U[i % 2]
        fns = []
        for g in range(64):
            gh, gq = g // 32, g % 32
            rows = slice(64 * gh, 64 * gh + 64)
            for ri, tab in ((0, K.ApR), (1, K.ApI)):
                fns.append(lambda e, g=g, gh=gh, gq=gq, rows=rows, ri=ri, tab=tab, u=u: e.matmul(
                    zp[rows, ri, gq, :], lhsT=tab.ap[:, g, :], rhs=u.ap[:, g * 16:(g + 1) * 16], start=True, stop=True,
                    tile_position=(0, 64 * gh)))
        P.pe(fns, reads=[u.b, K.ApR.b], writes=[PSB[4], PSB[5]])

    def stD(i):
        P.op("dve", lambda e: e.tensor_tensor(out=pr1.ap, in0=zp, in1=BbRb, op=ALU.mult), reads=[PSB[4], PSB[5], K.BbR.b], writes=[pr1.b])
        P.op("dve", lambda e: e.tensor_tensor(out=pr2.ap, in0=zp, in1=BbIb, op=ALU.mult), reads=[PSB[4], PSB[5], K.BbR.b], writes=[pr2.b])
        P.op("dve", lambda e: e.tensor_tensor(out=Tt.ap[:, 0], in0=pr1.ap[:, 0], in1=pr2.ap[:, 1], op=ALU.subtract),
             reads=[pr1.b, pr2.b], writes=[Tt.b])
        P.op("dve", lambda e: e.tensor_tensor(out=Tt.ap[:, 1], in0=pr1.ap[:, 1], in1=pr2.ap[:, 0], op=ALU.add),
             reads=[pr1.b, pr2.b], writes=[Tt.b])
        P.op("dve", lambda e: e.tensor_reduce(out=S.ap, in_=Tt.ap, axis=AX.X, op=ALU.add), reads=[Tt.b], writes=[S.b])
        P.op("dve", lambda e: e.tensor_tensor(out=Mm.ap, in0=K.A4c.ap, in1=H.ap.unsqueeze(1).broadcast_to([128, 2, 2, 32]), op=ALU.mult),
             reads=[K.A4c.b, H.b], writes=[Mm.b])
        P.op("dve", lambda e: e.tensor_tensor(out=H.ap, in0=Mm.ap[:, :, 0, :], in1=Mm.ap[:, :, 1, :], op=ALU.add), reads=[Mm.b], writes=[H.b])
        P.op("dve", lambda e: e.tensor_tensor(out=H.ap, in0=H.ap, in1=S.ap, op=ALU.add), reads=[H.b, S.b], writes=[H.b])

    stA(0)
    for i in range(ntile):
        stB(i)
        stC(i)
        if i + 1 < ntile:
            stA(i + 1)
        stD(i)
    if DEBUG:
        P.dma("sp", K.O["dbg_h"], H.ap, reads=[H.b], writes=[K.bout])
    for t in [Wu] + U + [pr1, pr2, Tt, S, Mm, K.ApR, K.ApI]:
        A.free(t)
    load_wkv(K)
    ht = hTt[(ntile - 1) % 2]
    kv_tokmajor(K, ht.ap, [ht.b], 128, slice(0, 128), K.vth.ap, K.vth.b)
    kT_dup(K, ht.ap, [ht.b], 128, lambda kv: K.kTh.ap[:, kv, :], [K.kTh.b], slice(0, 128))
    A.free(K.Wkv)
    A.free(K.Wkd)
    for t in hTt:
        A.free(t)
    K.xts, K.xss = xts, xss


def kT_dup(K, hT_ap, hbufs, n, dst_ap_fn, dst_bufs, cols):
    P = K.P
    PSB = K.psb
    for kv in range(4):
        pk = K.ps[:, 7, 0:n]
        P.pe([lambda e, c=c, pk=pk, kv=kv: e.matmul(pk, lhsT=K.Wkd.ap[:, c, kv, :],
                                                   rhs=hT_ap[:, c, cols], start=(c == 0), stop=(c == 15)) for c in range(16)],
             reads=hbufs + [K.Wkd.b], writes=[PSB[7]])
        P.op("act", lambda e, kv=kv, pk=pk: e.activation(out=dst_ap_fn(kv), in_=pk, func=AF.Copy),
             reads=[PSB[7]], writes=dst_bufs)


def load_wkv(K):
    P, A = K.P, K.A
    K.Wkv = load_weights_resident(K, "Wkv", K.I["w_in"][:, 2048:2560], 16, 512)
    K.Wkd = A.alloc("Wkd", [16, 4, 128], BF16)
    for kv in range(4):
        P.op("dve", lambda e, kv=kv: e.tensor_copy(out=K.Wkd.ap[:, :, kv, :].rearrange("p c (d h) -> p c d h", d=2),
                                                    in_=K.Wkv.ap[:, :, kv * 64:(kv + 1) * 64].unsqueeze(2).broadcast_to([128, 16, 2, 64])),
             reads=[K.Wkv.b], writes=[K.Wkd.b])


def kv_tokmajor(K, hT_ap, hbufs, n, cols, vdst_ap, vdst_buf, kdst=None):
    P, PSB = K.P, K.psb
    pk = K.ps[:n, 6, :]
    P.pe([lambda e, c=c: e.matmul(pk, lhsT=hT_ap[:, c, cols], rhs=K.Wkv.ap[:, c, :], start=(c == 0), stop=(c == 15)) for c in range(16)],
         reads=hbufs + [K.Wkv.b], writes=[PSB[6]])
    P.op("act", lambda e: e.activation(out=vdst_ap, in_=K.ps[:n, 6, 256:512], func=AF.Copy), reads=[PSB[6]], writes=[vdst_buf])
    if kdst is not None:
        P.op("act", lambda e: e.activation(out=kdst.ap[:n], in_=pk, func=AF.Copy), reads=[PSB[6]], writes=[kdst.b])


class Ring:
    def __init__(self, K, nbig, nsmall=0):
        self.K = K
        self.big = [K.A.alloc(f"ringb{i}", [16, 512], BF16) for i in range(nbig)]
        self.small = [K.A.alloc(f"rings{i}", [8, 512], BF16) for i in range(nsmall)]
        self.ib = 0
        self.is_ = 0

    def load(self, src, kc, ncols):
        if kc <= 8 and self.small:
            s = self.small[self.is_ % len(self.small)]
            self.is_ += 1
        else:
            s = self.big[self.ib % len(self.big)]
            self.ib += 1
        v = s.ap[:, 0:kc, 0:ncols]
        self.K.P.dma("pool", v, src.rearrange("(c p) n -> p c n", p=128), writes=[s.b])
        return v, s.b

    def free(self):
        for s in self.big + self.small:
            self.K.A.free(s)


def acc_views(K, acc):
    return [(K.ps[:, 3 * acc:3 * acc + 2, :].rearrange("p a b -> p (a b)"), slice(0, 1024)),
            (K.ps[:, 3 * acc + 2, 0:64], slice(1024, 1088))]


def acc_bufs(K, acc):
    return [K.psb[3 * acc], K.psb[3 * acc + 1], K.psb[3 * acc + 2]]


def fm_matmul(K, wv, wb, kc, nt, act_ap, act_bufs, acc):
    fns = []
    for bi, (t0, n) in enumerate(BLKS):
        for c in range(kc):
            fns.append(lambda e, bi=bi, t0=t0, n=n, c=c: e.matmul(
                K.ps[:, 3 * acc + bi, 0:n], lhsT=wv[:, c, nt * 128:(nt + 1) * 128], rhs=act_ap[:, c, t0:t0 + n],
                start=(c == 0), stop=(c == kc - 1)))
    K.P.pe(fns, reads=[wb] + act_bufs, writes=acc_bufs(K, acc))


def own_norm(K):
    P, I, A = K.P, K.I, K.A
    for t in range(9):
        n = 128 if t < 8 else 64
        xt, xs = K.xts[t % 2], K.xss[t % 2]
        P.dma("sp", xt.ap[:n], I["xo"][t * 128:t * 128 + n, :], writes=[xt.b])
        norm_rows(K, xt, n, K.gb, xs, K.junk, K.ss)
        transpose_rows(K, xs, n, K.hT.ap[:, :, t * 128:t * 128 + n], [K.hT.b[t]], 0)
    A.free(K.gb)


def proj_phase(K):
    P, I, A = K.P, K.I, K.A
    hb = K.hT.b
    ring = Ring(K, 3)
    acc = 0
    for g in range(2):
        wv, wb = ring.load(I["w_in"][:, g * 512:(g + 1) * 512], 16, 512)
        for nt in range(4):
            fm_matmul(K, wv, wb, 16, nt, K.hT.ap, hb, acc)
            for pv, sl in acc_views(K, acc):
                P.op("act", lambda e, pv=pv, sl=sl, j=4 * g + nt: e.activation(out=K.uT.ap[:, j, sl], in_=pv, func=AF.Copy),
                     reads=acc_bufs(K, acc), writes=K.uT.b)
            acc ^= 1
    ring.free()


def ssm_own(K):
    P, I, A, O = K.P, K.I, K.A, K.O
    PSB = K.psb
    Hist = A.alloc("Hist", [2, 32, 64], F32)
    T1 = A.alloc("T1", [2, 32, 64], F32)
    H2 = A.alloc("H2", [2, 32, 64], BF16)
    HistB = A.alloc("HistB", [2, 32, 64], BF16)
    cA = A.alloc("cA", [2, 32], F32)
    cB = A.alloc("cB", [2, 32], F32)
    ysb = A.alloc("ysb", [8, 64], F32)
    g1 = A.alloc("g1", [8, 64], F32)
    zb = K.ps[:, 0:4, :].rearrange("p (r a) (g t) -> p r (a g) t", r=2, t=128)
    yps = K.ps[:, 4:6, :].rearrange("p a (j t) -> p (a j) t", t=128)
    ypb = [PSB[4], PSB[5]]
    zbb = [PSB[0], PSB[1], PSB[2], PSB[3]]
    Hc = K.Hc

    zbs = [K.ps[:, 0:2, :].rearrange("p r (g t) -> p r g t", t=64), K.ps[:, 2:4, :].rearrange("p r (g t) -> p r g t", t=64)]
    zbbs = [[PSB[0], PSB[1]], [PSB[2], PSB[3]]]

    def bu_batch(t, b4, n, c0, zsel=None):
        zb_, zbb_ = (zb, zbb) if zsel is None else (zbs[zsel], zbbs[zsel])
        fns = []
        for gh in range(2):
            rows = slice(64 * gh, 64 * gh + 64)
            for gl in range(8):
                g = 32 * gh + 8 * b4 + gl
                for ri, tab in ((0, K.BtabR), (1, K.BtabI)):
                    fns.append(lambda e, rows=rows, gl=gl, g=g, ri=ri, tab=tab, gh=gh: e.matmul(
                        zb_[rows, ri, gl, 0:n], lhsT=tab.ap[:, g, :], rhs=K.uT.ap[:, g // 8, c0:c0 + n], start=True, stop=True,
                        tile_position=(0, 64 * gh)))
        P.pe(fns, reads=[K.BtabR.b] + K.uT.b, writes=zbb_)

    def cside(n, hb_ap, hb_buf):
        for j in range(8):
            gh = j // 4
            rows = slice(64 * gh, 64 * gh + 64)
            fns = []
            for g8 in range(8):
                gq = (8 * j + g8) % 32
                for ri, tab in ((0, K.CtabR), (1, K.CtabI)):
                    fns.append(lambda e, rows=rows, gq=gq, ri=ri, tab=tab, j=j, first=(g8 == 0 and ri == 0), last=(g8 == 7 and ri == 1):
                               e.matmul(yps[:, j, 0:n], lhsT=tab.ap[rows, gq, :], rhs=hb_ap[rows, ri, gq, 0:n], start=first, stop=last))
            P.pe(fns, reads=[K.CtabR.b, hb_buf], writes=ypb)

    def gelu_out(n, c0, perm):
        yv, a = ysb.ap[:, :, 0:n], g1.ap[:, :, 0:n]
        b = a
        P.op("act", lambda e: e.activation(out=a, in_=yv, func=AF.Square), reads=[ysb.b], writes=[g1.b])
        P.op("dve", lambda e: e.tensor_scalar(out=a, in0=a, scalar1=0.044715, scalar2=1.0, op0=ALU.mult, op1=ALU.add), reads=[g1.b], writes=[g1.b])
        P.op("dve", lambda e: e.tensor_tensor(out=a, in0=a, in1=yv, op=ALU.mult), reads=[g1.b, ysb.b], writes=[g1.b])
        P.op("act", lambda e: e.activation(out=b, in_=a, func=AF.Sigmoid, scale=1.5957691216057308), reads=[g1.b], writes=[g1.b])
        P.op("dve", lambda e: e.tensor_tensor(out=K.gT.ap[:, :, c0:c0 + n], in0=b, in1=yv, op=ALU.mult), reads=[g1.b, ysb.b], writes=K.gT.b)

    CTb = K.CT.ap.rearrange("p g t -> p (g t)").unsqueeze(1).broadcast_to([128, 2, 2048])
    STf = K.ST.ap.rearrange("p g t -> p (g t)")
    Hf = Hist.ap.rearrange("p r g t -> p r (g t)")
    T1f = T1.ap.rearrange("p r g t -> p r (g t)")
    H2f = H2.ap.rearrange("p r g t -> p r (g t)")
    HBf = HistB.ap.rearrange("p r g t -> p r (g t)")
    xin = [A.alloc(f"xin{i}", [2, 64], F32) for i in range(8)]
    for ri, nm in ((0, "sre"), (1, "sim")):
        for sq in range(4):
            xi = xin[4 * ri + sq]
            for s_ in range(4):
                P.dma("sp", xi.ap[32 * s_:32 * s_ + 32], I[nm][4 * sq + s_].rearrange("(h g) p -> g h p", h=2), writes=[xi.b])
    X0 = A.alloc("X0", [2, 32, 64], BF16)
    Rz = A.alloc("Rz", [32, 64], F32)
    P.op("dve", lambda e: e.tensor_copy(out=Rz.ap, in_=K.Rm.ap.unsqueeze(2).broadcast_to([128, 32, 64])), reads=[K.Rm.b], writes=[Rz.b])
    P.op("dve", lambda e: e.memset(Rz.ap[:, :, 0:1], 0.0), reads=[Rz.b], writes=[Rz.b])
    Rzf = Rz.ap.rearrange("p g t -> p (g t)")
    ini = A.alloc("ini", [2, 32], F32)
    X0f = X0.ap.rearrange("p r g t -> p r (g t)")

    def bu_all(t):
        for b4 in range(4):
            zsel = b4 % 2
            bu_batch(t, b4, 64, t * 64, zsel)
            P.op("act", lambda e, b4=b4, zsel=zsel: e.activation(out=X0.ap[:, :, 8 * b4:8 * b4 + 8, :], in_=zbs[zsel], func=AF.Copy),
                 reads=zbbs[zsel], writes=[X0.b])

    bu_all(0)
    for t in range(16):
        c0 = t * 64
        P.op("dve", lambda e: e.tensor_tensor(out=Hf, in0=X0f, in1=CTb, op=ALU.mult), reads=[X0.b, K.CT.b], writes=[Hist.b])
        P.op("dve", lambda e: e.tensor_tensor(out=H2f[:, 0], in0=X0f[:, 1], in1=STf, op=ALU.mult), reads=[X0.b, K.ST.b], writes=[H2.b])
        P.op("dve", lambda e: e.tensor_tensor(out=H2f[:, 1], in0=X0f[:, 0], in1=STf, op=ALU.mult), reads=[X0.b, K.ST.b], writes=[H2.b])
        P.op("dve", lambda e: e.tensor_tensor(out=Hf[:, 0], in0=Hf[:, 0], in1=H2f[:, 0], op=ALU.add), reads=[Hist.b, H2.b], writes=[Hist.b])
        P.op("dve", lambda e: e.tensor_tensor(out=Hf[:, 1], in0=Hf[:, 1], in1=H2f[:, 1], op=ALU.subtract), reads=[Hist.b, H2.b], writes=[Hist.b])
        P.op("dve", lambda e: e.tensor_tensor(out=ini.ap, in0=Hc.ap, in1=K.Rm.ap.unsqueeze(1).broadcast_to([128, 2, 32]), op=ALU.mult),
             reads=[Hc.b, K.Rm.b], writes=[ini.b])
        P.op("dve", lambda e: e.tensor_tensor(out=Hist.ap[:, :, :, 0], in0=Hist.ap[:, :, :, 0], in1=ini.ap, op=ALU.add), reads=[Hist.b, ini.b], writes=[Hist.b])
        if t + 1 < 16:
            bu_all(t + 1)
        P.group("dve", [lambda e, ri=ri: e.tensor_tensor_scan(out=T1f[:, ri], data0=Rzf, data1=Hf[:, ri], initial=0.0, op0=ALU.mult, op1=ALU.add)
                        for ri in range(2)], reads=[Hist.b, Rz.b], writes=[T1.b])
        P.op("dve", lambda e: e.tensor_tensor(out=Hf, in0=T1f, in1=CTb, op=ALU.mult), reads=[T1.b, K.CT.b], writes=[Hist.b])
        P.op("dve", lambda e: e.tensor_tensor(out=H2f[:, 0], in0=T1f[:, 1], in1=STf, op=ALU.mult), reads=[T1.b, K.ST.b], writes=[H2.b])
        P.op("dve", lambda e: e.tensor_tensor(out=H2f[:, 1], in0=T1f[:, 0], in1=STf, op=ALU.mult), reads=[T1.b, K.ST.b], writes=[H2.b])
        P.op("dve", lambda e: e.tensor_tensor(out=HBf[:, 0], in0=Hf[:, 0], in1=H2f[:, 0], op=ALU.subtract), reads=[Hist.b, H2.b], writes=[HistB.b])
        P.op("dve", lambda e: e.tensor_tensor(out=HBf[:, 1], in0=Hf[:, 1], in1=H2f[:, 1], op=ALU.add), reads=[Hist.b, H2.b], writes=[HistB.b])
        P.op("dve", lambda e: e.tensor_tensor(out=cA.ap, in0=T1.ap[:, :, :, 63], in1=K.c64.ap.unsqueeze(1).broadcast_to([128, 2, 32]), op=ALU.mult),
             reads=[T1.b, K.c64.b], writes=[cA.b])
        P.op("dve", lambda e: e.tensor_tensor(out=cB.ap, in0=T1.ap[:, :, :, 63], in1=K.s64.ap.unsqueeze(1).broadcast_to([128, 2, 32]), op=ALU.mult),
             reads=[T1.b, K.s64.b], writes=[cB.b])
        P.op("dve", lambda e: e.tensor_tensor(out=Hc.ap[:, 0], in0=cA.ap[:, 0], in1=cB.ap[:, 1], op=ALU.subtract), reads=[cA.b, cB.b], writes=[Hc.b])
        P.op("dve", lambda e: e.tensor_tensor(out=Hc.ap[:, 1], in0=cA.ap[:, 1], in1=cB.ap[:, 0], op=ALU.add), reads=[cA.b, cB.b], writes=[Hc.b])
        cside(64, HistB.ap, HistB.b)
        for j in range(8):
            P.op("dve", lambda e, j=j: e.scalar_tensor_tensor(out=ysb.ap[:, j, 0:64], in0=K.uT.ap[:, j, c0:c0 + 64], scalar=K.Dcol.ap[:, j:j + 1],
                                                              in1=yps[:, j, 0:64], op0=ALU.mult, op1=ALU.add),
                 reads=K.uT.b + [K.Dcol.b] + ypb, writes=[ysb.b])
        gelu_out(64, c0, False)
    stg = A.alloc("stg", [128], F32)
    for ri in range(2):
        pt = K.ps[0:32, 6, 0:128]
        P.pe([lambda e, ri=ri: e.transpose(out=pt, in_=Hc.ap[:, ri, :], identity=K.identf.ap)], reads=[Hc.b, K.bC], writes=[PSB[6]])
        P.op("act", lambda e: e.activation(out=stg.ap[0:32, :], in_=pt, func=AF.Copy), reads=[PSB[6]], writes=[stg.b])
        P.dma("sp", O["pst"][ri].rearrange("(h g) p -> g h p", h=2), stg.ap[0:32, :].rearrange("g (h p) -> g h p", h=2), reads=[stg.b], writes=[K.bout])

    for t in [Hist, HistB, T1, H2, cA, cB, X0, Rz, ini]:
        A.free(t)
    Hs0 = A.alloc("Hs0", [2, 32, 16], F32)
    HistS = A.alloc("HistS", [2, 32, 4, 16], F32)
    HistSB = A.alloc("HistSB", [2, 32, 64], BF16)
    MmS = A.alloc("MmS", [2, 2, 32, 16], F32)
    RrS = A.alloc("RrS", [2, 32, 16], F32)
    for ri, nm in ((0, "sre"), (1, "sim")):
        for sq in range(4):
            xi = xin[4 * ri + sq]
            pt = K.ps[:, 6, 0:128]
            P.pe([lambda e, xi=xi: e.transpose(out=pt, in_=xi.ap.rearrange("p h q -> p (h q)"), identity=K.identf.ap)],
                 reads=[xi.b, K.bC], writes=[PSB[6]])
            P.op("act", lambda e, ri=ri, sq=sq: e.activation(out=Hs0.ap[:, ri, :, 4 * sq:4 * sq + 4].rearrange("p g s -> p s g"),
                                                            in_=pt.rearrange("p (s g) -> p s g", s=4), func=AF.Copy),
                 reads=[PSB[6]], writes=[Hs0.b])
    c0 = 1024
    for b4 in range(4):
        bu_batch(8, b4, 64, c0)
        for ri in range(2):
            P.op("act", lambda e, b4=b4, ri=ri: e.activation(out=HistS.ap[:, ri, 8 * b4:8 * b4 + 8, :, :],
                                                            in_=zb[:, ri, :, 0:64].rearrange("p g (s t) -> p g t s", t=4), func=AF.Copy),
                 reads=zbb, writes=[HistS.b])
    A4b = K.A4.ap.unsqueeze(4).broadcast_to([128, 2, 2, 32, 16])
    for tt in range(4):
        prev = Hs0.ap if tt == 0 else HistS.ap[:, :, :, tt - 1, :]
        pb = [Hs0.b] if tt == 0 else [HistS.b]
        P.op("dve", lambda e, prev=prev: e.tensor_tensor(out=MmS.ap, in0=A4b, in1=prev.unsqueeze(1).broadcast_to([128, 2, 2, 32, 16]), op=ALU.mult),
             reads=[K.A4.b] + pb, writes=[MmS.b])
        P.op("dve", lambda e: e.tensor_tensor(out=RrS.ap, in0=MmS.ap[:, :, 0], in1=MmS.ap[:, :, 1], op=ALU.add), reads=[MmS.b], writes=[RrS.b])
        P.op("dve", lambda e, tt=tt: e.tensor_tensor(out=HistS.ap[:, :, :, tt, :], in0=HistS.ap[:, :, :, tt, :], in1=RrS.ap, op=ALU.add),
             reads=[RrS.b, HistS.b], writes=[HistS.b])
    P.op("act", lambda e: e.activation(out=HistSB.ap, in_=HistS.ap.rearrange("p r g t s -> p r g (t s)"), func=AF.Copy), reads=[HistS.b], writes=[HistSB.b])
    cside(64, HistSB.ap, HistSB.b)
    for j in range(8):
        P.op("dve", lambda e, j=j: e.scalar_tensor_tensor(
            out=ysb.ap[:, j, 0:64].rearrange("p (s t) -> p s t", t=4), in0=K.uT.ap[:, j, c0:c0 + 64].rearrange("p (s t) -> p s t", t=4),
            scalar=K.Dcol.ap[:, j:j + 1], in1=yps[:, j, 0:64].rearrange("p (t s) -> p s t", t=4), op0=ALU.mult, op1=ALU.add),
            reads=K.uT.b + [K.Dcol.b] + ypb, writes=[ysb.b])
    gelu_out(64, c0, True)
    stg2s = [A.alloc(f"stg2{i}", [4, 32], F32) for i in range(2)]
    stgs = [A.alloc(f"stgs{i}", [128], F32) for i in range(4)]
    for ri in range(2):
        for sq in range(4):
            pt = K.ps[:, 6, 0:128]
            sg2, sg1 = stg2s[(4 * ri + sq) % 2], stgs[(4 * ri + sq) % 4]
            pt = K.ps[:, 6 + (sq % 2), 0:128]
            P.op("dve", lambda e, ri=ri, sq=sq, sg2=sg2: e.tensor_copy(out=sg2.ap, in_=HistS.ap[:, ri, :, 3, 4 * sq:4 * sq + 4].rearrange("p g s -> p s g")),
                 reads=[HistS.b], writes=[sg2.b])
            P.pe([lambda e, sg2=sg2, pt=pt: e.transpose(out=pt, in_=sg2.ap.rearrange("p s g -> p (s g)"), identity=K.identf.ap)],
                 reads=[sg2.b, K.bC], writes=[PSB[6 + (sq % 2)]])
            P.op("act", lambda e, sg1=sg1, pt=pt: e.activation(out=sg1.ap, in_=pt, func=AF.Copy), reads=[PSB[6 + (sq % 2)]], writes=[sg1.b])
            for s in range(4):
                P.dma("sp", O["sst"][ri, 4 * sq + s].rearrange("(h g) p -> g h p", h=2),
                      sg1.ap[32 * s:32 * s + 32, :].rearrange("g (h p) -> g h p", h=2), reads=[sg1.b], writes=[K.bout])
    for t in [ysb, g1, stg, Hs0, HistS, HistSB, MmS, RrS] + xin + stg2s + stgs:
        A.free(t)
    for t in [K.A4, K.A4c, K.BtabR, K.BtabI, K.CtabR, K.CtabI, K.Dcol, K.Hc, K.uT, K.CT, K.ST, K.Rm, K.thG, K.c64, K.s64]:
        A.free(t)


def attention(K):
    P, I, A, O = K.P, K.I, K.A, K.O
    PSB = K.psb
    hb = K.hT.b
    K.kTd = A.alloc("kTd", [4, 128 + NT], BF16, nbufs=10, top=True)
    K.vtok = A.alloc("vtok", [10, 256], BF16, nbufs=10, top=True)
    K.kvlast = A.alloc("kvlast", [512], F32, top=True)
    K.kvsamp = A.alloc("kvsamp", [512], F32, top=True)
    load_wkv(K)
    P.op("act", lambda e: e.activation(out=K.kTd.ap[:, :, 0:128], in_=K.kTh.ap, func=AF.Copy), reads=[K.kTh.b], writes=[K.kTd.b[0]])
    P.op("act", lambda e: e.activation(out=K.vtok.ap[:, 0, :], in_=K.vth.ap, func=AF.Copy), reads=[K.vth.b], writes=[K.vtok.b[0]])
    for t in range(9):
        n = 128 if t < 8 else 64
        kdst = K.kvlast if t == 7 else (K.kvsamp if t == 8 else None)
        kv_tokmajor(K, K.hT.ap, [hb[t]], n, slice(t * 128, t * 128 + n), K.vtok.ap[:n, t + 1, :], K.vtok.b[t + 1], kdst)
    for bi, (t0, n) in enumerate(BLKS):
        tiles = list(range(t0 // 128, (t0 + n + 127) // 128))
        kT_dup(K, K.hT.ap, [hb[t] for t in tiles], n, lambda kv, t0=t0, n=n: K.kTd.ap[:, kv, 128 + t0:128 + t0 + n],
               [K.kTd.b[t + 1] for t in tiles], slice(t0, t0 + n))
    A.free(K.Wkv)
    A.free(K.Wkd)
    A.free(K.kTh)
    A.free(K.vth)
    K.qT = A.alloc("qT", [8, NT], BF16, nbufs=1, top=True)
    ring = Ring(K, 2)
    acc = 0
    for g in range(2):
        wv, wb = ring.load(I["w_in"][:, 1024 + g * 512:1024 + (g + 1) * 512], 16, 512)
        for nt in range(4):
            fm_matmul(K, wv, wb, 16, nt, K.hT.ap, hb, acc)
            for pv, sl in acc_views(K, acc):
                P.op("act", lambda e, pv=pv, sl=sl, j=4 * g + nt: e.activation(out=K.qT.ap[:, j, sl], in_=pv, func=AF.Copy),
                     reads=acc_bufs(K, acc), writes=[K.qT.b])
            acc ^= 1
    ring.free()
    if ASTOP == 'q':
        return
    bmt = A.alloc("bmt", [16, 2, 128], F32)
    bmt0 = A.alloc("bmt0", [16, 128], F32)
    bmsc = A.alloc("bmsc", [2, 8, 4], F32)
    bmsn = A.alloc("bmsn", [2, 8, 4], F32)
    esk = A.alloc("esk", [8], F32)
    P.dma("sp", bmt.ap, I["bmt"], writes=[bmt.b])
    P.dma("sp", bmt0.ap, I["bmt0"], writes=[bmt0.b])
    P.dma("sp", bmsc.ap, I["bmsc"].rearrange("k (j a) t -> k j a t", j=2), writes=[bmsc.b])
    P.dma("sp", bmsn.ap[0:4], I["bmsn"].rearrange("k (j a) t -> k j a t", j=2), writes=[bmsn.b])
    with K.nc.allow_non_contiguous_dma(reason="tiny"):
        for j in range(2):
            P.dma("sp", esk.ap[64 * j:64 * j + 64, :], I["sinks"].rearrange("(p j) -> j p", j=2)[j:j + 1, :].broadcast_to([64, 8]), writes=[esk.b])
    P.op("act", lambda e: e.activation(out=esk.ap, in_=esk.ap, func=AF.Exp), reads=[esk.b], writes=[esk.b])
    if ASTOP == 'tabs':
        return
    ee = [A.alloc(f"ee{i}", [2, 2, 2, 128], F32) for i in range(2)]
    pT = [A.alloc(f"pT{i}", [2, 2, 2, 128], BF16) for i in range(2)]
    dn = A.alloc("dn", [2, 128], F32)
    spss = [K.ps[:, 0:2, :].rearrange("p j (a k q) -> p j a k q", a=2, k=2), K.ps[:, 4:6, :].rearrange("p j (a k q) -> p j a k q", a=2, k=2)]
    spbs = [[PSB[0], PSB[1]], [PSB[4], PSB[5]]]
    opss = [K.ps[:, 2, 0:256].rearrange("p (a q) -> p a q", q=128), K.ps[:, 6, 0:256].rearrange("p (a q) -> p a q", q=128)]
    dpss = [K.ps[:, 3, 0:256].rearrange("p (a q) -> p a q", q=128), K.ps[:, 7, 0:256].rearrange("p (a q) -> p a q", q=128)]
    odbs = [[PSB[2], PSB[3]], [PSB[6], PSB[7]]]
    dns = [dn, A.alloc("dn2", [2, 128], F32)]
    batches = [(i, hbk) for i in range(8) for hbk in range(4)]

    def stS(b):
        i, hbk = batches[b]
        kv = hbk
        qc = slice(i * 128, (i + 1) * 128)
        spsv = spss[b % 2]
        fns = []
        for hh in range(4):
            h = 4 * hbk + hh
            pair, j = h // 2, h % 2
            rows = slice(64 * j, 64 * j + 64)
            for kb in range(2):
                kc0 = 128 * (i + kb)
                fns.append(lambda e, hh=hh, kb=kb, rows=rows, pair=pair, kc0=kc0, j=j: e.matmul(
                    spsv[:, j, hh // 2, kb, :], lhsT=K.kTd.ap[rows, kv, kc0:kc0 + 128], rhs=K.qT.ap[rows, pair, qc], start=True, stop=True))
        P.pe(fns, reads=[K.kTd.b[i], K.kTd.b[i + 1], K.qT.b], writes=spbs[b % 2])

    def stE(b):
        i, hbk = batches[b]
        spsv, spb = spss[b % 2], spbs[b % 2]
        e_, p_ = ee[b % 2], pT[b % 2]
        if i == 0:
            for j in range(2):
                P.op("dve", lambda e, j=j: e.scalar_tensor_tensor(out=e_.ap[:, j, :, 0, :], in0=spsv[:, j, :, 0, :], scalar=0.125,
                                                                  in1=bmt0.ap[:, 4 * hbk + 2 * j:4 * hbk + 2 * j + 2, :], op0=ALU.mult, op1=ALU.add),
                     reads=spb + [bmt0.b], writes=[e_.b])
                P.op("dve", lambda e, j=j: e.scalar_tensor_tensor(out=e_.ap[:, j, :, 1, :], in0=spsv[:, j, :, 1, :], scalar=0.125,
                                                                  in1=bmt.ap[:, 4 * hbk + 2 * j:4 * hbk + 2 * j + 2, 1, :], op0=ALU.mult, op1=ALU.add),
                     reads=spb + [bmt.b], writes=[e_.b])
        else:
            P.op("dve", lambda e: e.scalar_tensor_tensor(out=e_.ap, in0=spsv, scalar=0.125, in1=bmt.ap[:, 4 * hbk:4 * hbk + 4, :, :],
                                                         op0=ALU.mult, op1=ALU.add), reads=spb + [bmt.b], writes=[e_.b])
        P.op("act", lambda e: e.activation(out=p_.ap, in_=e_.ap, func=AF.Exp), reads=[e_.b], writes=[p_.b])

    def stV(b):
        i, hbk = batches[b]
        kv = hbk
        p_ = pT[b % 2]
        ops, dps = opss[b % 2], dpss[b % 2]
        fns = []
        for hh in range(4):
            pp, j = hh // 2, hh % 2
            rows = slice(64 * j, 64 * j + 64)
            for kb in range(2):
                fns.append(lambda e, hh=hh, kb=kb, pp=pp, j=j, rows=rows: e.matmul(
                    ops[rows, pp, :], lhsT=K.vtok.ap[:, i + kb, kv * 64:(kv + 1) * 64], rhs=p_.ap[:, j, hh // 2, kb, :],
                    start=(kb == 0), stop=(kb == 1), tile_position=(0, 64 * j)))
            for kb in range(2):
                fns.append(lambda e, hh=hh, kb=kb, pp=pp, j=j, rows=rows: e.matmul(
                    dps[rows, pp, :], lhsT=K.ones64.ap, rhs=p_.ap[:, j, hh // 2, kb, :],
                    start=(kb == 0), stop=(kb == 1), tile_position=(0, 64 * j)))
        P.pe(fns, reads=[K.vtok.b[i], K.vtok.b[i + 1], p_.b, K.bC], writes=odbs[b % 2])

    def stN(b):
        i, hbk = batches[b]
        qc = slice(i * 128, (i + 1) * 128)
        ops, dps, dn_ = opss[b % 2], dpss[b % 2], dns[b % 2]
        ob = odbs[b % 2]
        for pp in range(2):
            P.op("act", lambda e, pp=pp: e.activation(out=dn_.ap[:, pp, :], in_=dps[:, pp, :], func=AF.Ln,
                                                      bias=esk.ap[:, 2 * hbk + pp:2 * hbk + pp + 1], scale=1.0),
                 reads=ob + [esk.b], writes=[dn_.b])
        P.op("act", lambda e: e.activation(out=dn_.ap, in_=dn_.ap, func=AF.Exp, scale=-1.0), reads=[dn_.b], writes=[dn_.b])
        P.op("dve", lambda e: e.tensor_tensor(out=K.oT.ap[:, 2 * hbk:2 * hbk + 2, qc], in0=ops, in1=dn_.ap, op=ALU.mult),
             reads=ob + [dn_.b], writes=K.oT.b)

    nb = len(batches)
    stS(0)
    stE(0)
    stS(1)
    for b in range(nb):
        stV(b)
        if b + 1 < nb:
            stE(b + 1)
        if b + 2 < nb:
            stS(b + 2)
        stN(b)
    lim = False
    if ASTOP == 'prompt' or lim:
        return
    kc_ = [A.alloc(f"kc{i}", [4, 2, 64], BF16) for i in range(2)]
    vc_ = [A.alloc(f"vc{i}", [256], BF16) for i in range(2)]
    kcT = [A.alloc(f"kcT{i}", [4, 128], BF16) for i in range(2)]
    vnew = A.alloc("vnew", [16, 256], BF16)
    eS = A.alloc("eS", [2, 8, 4], F32)
    eN = A.alloc("eN", [2, 8, 4], F32)
    pS = [A.alloc(f"pS{i}", [2, 8, 4], BF16) for i in range(2)]
    pN = [A.alloc(f"pN{i}", [2, 8, 4], BF16) for i in range(2)]
    dS = A.alloc("dS", [8, 4], F32)
    with K.nc.allow_non_contiguous_dma(reason="tiny relayout"):
        for s in range(16):
            P.dma("sp", vnew.ap[0:4, s, :], K.vtok.ap[4 * s:4 * s + 4, 9, :], reads=[K.vtok.b[9]], writes=[vnew.b])
    base = [4, 0]
    ptks = [K.ps[:, base[z], :].bitcast(BF16)[:, 0:512].rearrange("p (k n) -> p k n", k=4) for z in range(2)]
    sscs = [[K.ps[:, base[z] + 1 + 2 * j, 0:32].rearrange("p (h t) -> p h t", t=4) for j in range(2)] for z in range(2)]
    ssns = [[K.ps[0:4, base[z] + 1 + 2 * j, 32:64].rearrange("p (h t) -> p h t", t=4) for j in range(2)] for z in range(2)]
    osps = [K.ps[:, base[z] + 2, 0:32].rearrange("p (a t) -> p a t", t=4) for z in range(2)]
    dsps = [K.ps[:, base[z] + 2, 32:64].rearrange("p (a t) -> p a t", t=4) for z in range(2)]
    eSs = [eS, A.alloc("eS2", [2, 8, 4], F32)]
    eNs = [eN, A.alloc("eN2", [2, 8, 4], F32)]
    dSs = [dS, A.alloc("dS2", [8, 4], F32)]

    def saL(s):
        z = s % 2
        kc, vc, kt = kc_[s % 2], vc_[s % 2], kcT[s % 2]
        P.dma("pool", kc.ap, I["ck"][s].rearrange("k (v d) -> k v d", v=4).unsqueeze(2).broadcast_to([128, 4, 2, 64]), writes=[kc.b])
        P.dma("pool", vc.ap, I["cv"][s], writes=[vc.b])
        P.dma("sp", O["skk"][s, 0:124, :], I["ck"][s, 4:128, :], writes=[K.bout])
        P.dma("sp", O["skv"][s, 0:124, :], I["cv"][s, 4:128, :], writes=[K.bout])
        P.dma("sp", O["skk"][s, 124:128, :], K.kvsamp.ap[4 * s:4 * s + 4, 0:256], reads=[K.kvsamp.b], writes=[K.bout])
        P.dma("sp", O["skv"][s, 124:128, :], K.kvsamp.ap[4 * s:4 * s + 4, 256:512], reads=[K.kvsamp.b], writes=[K.bout])
        ptk = ptks[z]
        P.pe([lambda e, kv=kv: e.transpose(out=ptk[:, kv, :], in_=kc.ap[:, kv].rearrange("k a d -> k (a d)"),
                                           identity=K.identb.ap) for kv in range(4)], reads=[kc.b, K.bC], writes=[PSB[base[z]]])
        P.op("act", lambda e: e.activation(out=kt.ap, in_=ptk, func=AF.Copy), reads=[PSB[base[z]]], writes=[kt.b])
        qc0 = 1024 + 4 * s
        fns = []
        for h in range(16):
            kv, pair, j = h // 4, h // 2, h % 2
            rows = slice(64 * j, 64 * j + 64)
            fns.append(lambda e, kv=kv, pair=pair, rows=rows, j=j: e.matmul(sscs[z][j][:, pair, :], lhsT=kt.ap[rows, kv, :], rhs=K.qT.ap[rows, pair, qc0:qc0 + 4],
                                                                           start=True, stop=True))
            fns.append(lambda e, kv=kv, pair=pair, rows=rows, j=j: e.matmul(ssns[z][j][:, pair, :], lhsT=K.kTd.ap[rows, kv, 128 + qc0:128 + qc0 + 4],
                                                                           rhs=K.qT.ap[rows, pair, qc0:qc0 + 4], start=True, stop=True))
        P.pe(fns, reads=[kt.b, K.kTd.b[9], K.qT.b], writes=[PSB[base[z] + 1], PSB[base[z] + 3]])

    def saE(s):
        z = s % 2
        sb_ = [PSB[base[z] + 1], PSB[base[z] + 3]]
        eS_, eN_, ps_, pn_ = eSs[z], eNs[z], pS[z], pN[z]
        for j in range(2):
            P.op("dve", lambda e, j=j: e.scalar_tensor_tensor(out=eS_.ap[:, j], in0=sscs[z][j], scalar=0.125, in1=bmsc.ap[:, j], op0=ALU.mult, op1=ALU.add),
                 reads=sb_ + [bmsc.b], writes=[eS_.b])
            P.op("dve", lambda e, j=j: e.scalar_tensor_tensor(out=eN_.ap[0:4, j], in0=ssns[z][j], scalar=0.125, in1=bmsn.ap[0:4, j], op0=ALU.mult, op1=ALU.add),
                 reads=sb_ + [bmsn.b], writes=[eN_.b])
        P.op("act", lambda e: e.activation(out=ps_.ap, in_=eS_.ap, func=AF.Exp), reads=[eS_.b], writes=[ps_.b])
        P.op("act", lambda e: e.activation(out=pn_.ap[0:4], in_=eN_.ap[0:4], func=AF.Exp), reads=[eN_.b], writes=[pn_.b])

    def saV(s):
        z = s % 2
        vc, ps_, pn_ = vc_[s % 2], pS[z], pN[z]
        osp, dsp = osps[z], dsps[z]
        fns = []
        for h in range(16):
            kv, pair, j = h // 4, h // 2, h % 2
            rows = slice(64 * j, 64 * j + 64)
            fns.append(lambda e, kv=kv, pair=pair, rows=rows, j=j: e.matmul(osp[rows, pair, :], lhsT=vc.ap[:, kv * 64:(kv + 1) * 64], rhs=ps_.ap[:, j, pair, :],
                                                                           start=True, stop=False, tile_position=(0, 64 * j)))
            fns.append(lambda e, kv=kv, pair=pair, rows=rows, j=j: e.matmul(osp[rows, pair, :], lhsT=vnew.ap[0:4, s, kv * 64:(kv + 1) * 64], rhs=pn_.ap[0:4, j, pair, :],
                                                                           start=False, stop=True, tile_position=(0, 64 * j)))
            fns.append(lambda e, pair=pair, rows=rows, j=j: e.matmul(dsp[rows, pair, :], lhsT=K.ones64.ap, rhs=ps_.ap[:, j, pair, :],
                                                                    start=True, stop=False, tile_position=(0, 64 * j)))
            fns.append(lambda e, pair=pair, rows=rows, j=j: e.matmul(dsp[rows, pair, :], lhsT=K.ones64.ap[0:4], rhs=pn_.ap[0:4, j, pair, :],
                                                                    start=False, stop=True, tile_position=(0, 64 * j)))
        P.pe(fns, reads=[vc.b, vnew.b, ps_.b, pn_.b, K.bC], writes=[PSB[base[z] + 2]])

    def saN(s):
        z = s % 2
        qc0 = 1024 + 4 * s
        osp, dsp, dS_ = osps[z], dsps[z], dSs[z]
        ob = [PSB[base[z] + 2]]
        P.op("dve", lambda e: e.tensor_tensor(out=dS_.ap, in0=dsp, in1=esk.ap.unsqueeze(2).broadcast_to([128, 8, 4]), op=ALU.add),
             reads=ob + [esk.b], writes=[dS_.b])
        P.op("dve", lambda e: e.reciprocal(out=dS_.ap, in_=dS_.ap), reads=[dS_.b], writes=[dS_.b])
        P.op("dve", lambda e: e.tensor_tensor(out=K.oT.ap[:, :, qc0:qc0 + 4], in0=osp, in1=dS_.ap, op=ALU.mult),
             reads=ob + [dS_.b], writes=K.oT.b)

    saL(0)
    saE(0)
    saL(1)
    for s in range(16):
        saV(s)
        if s + 1 < 16:
            saE(s + 1)
        saN(s)
        if s + 2 < 16:
            saL(s + 2)
    P.dma("sp", O["pkv"], K.kvlast.ap, reads=[K.kvlast.b], writes=[K.bout])
    for t in [bmt, bmt0, bmsc, bmsn, esk, vnew] + eSs + eNs + dSs + dns + ee + pT + kc_ + vc_ + kcT + pS + pN:
        A.free(t)
    for t in [K.qT, K.kTd, K.vtok, K.kvlast, K.kvsamp]:
        A.free(t)


def merge_phase(K):
    P, I, A = K.P, K.I, K.A
    hb = K.hT.b
    K.mT = A.alloc("mT", [16, NT], BF16, top=True)
    ring = Ring(K, 3, 4)
    s1 = A.alloc("s1", [NT], BF16)
    s2 = A.alloc("s2", [NT], BF16)
    s3 = A.alloc("s3", [NT], BF16)
    tv = A.alloc("tv", [NT], F32)
    tu = A.alloc("tu", [NT], F32)
    acc = 0
    for sg in range(4):
        c0 = sg * 512
        wga = ring.load(I["w_in"][:, 2560 + c0:2560 + c0 + 512], 16, 512)
        wgb = ring.load(I["w_in"][:, 4608 + c0:4608 + c0 + 512], 16, 512)
        wval = ring.load(I["w_glu_val"][:, c0:c0 + 512], 8, 512)
        wgate = ring.load(I["w_glu_gate"][:, c0:c0 + 512], 8, 512)
        for nt in range(4):
            n = 4 * sg + nt
            if nt == 0:
                pass
            fm_matmul(K, wga[0], wga[1], 16, nt, K.hT.ap, hb, acc)
            for pv, sl in acc_views(K, acc):
                P.op("act", lambda e, pv=pv, sl=sl: e.activation(out=s1.ap[:, sl], in_=pv, func=AF.Sigmoid), reads=acc_bufs(K, acc), writes=[s1.b])
            acc ^= 1
            fm_matmul(K, wgate[0], wgate[1], 8, nt, K.gT.ap, K.gT.b, acc)
            for pv, sl in acc_views(K, acc):
                P.op("act", lambda e, pv=pv, sl=sl: e.activation(out=s2.ap[:, sl], in_=pv, func=AF.Sigmoid), reads=acc_bufs(K, acc), writes=[s2.b])
            acc ^= 1
            fm_matmul(K, wval[0], wval[1], 8, nt, K.gT.ap, K.gT.b, acc)
            for pv, sl in acc_views(K, acc):
                P.op("dve", lambda e, pv=pv, sl=sl: e.tensor_tensor(out=tv.ap[:, sl], in0=pv, in1=s1.ap[:, sl], op=ALU.mult),
                     reads=acc_bufs(K, acc) + [s1.b], writes=[tv.b])
            P.op("dve", lambda e: e.tensor_tensor(out=tv.ap, in0=tv.ap, in1=s2.ap, op=ALU.mult), reads=[tv.b, s2.b], writes=[tv.b])
            acc ^= 1
            fm_matmul(K, wgb[0], wgb[1], 16, nt, K.hT.ap, hb, acc)
            for pv, sl in acc_views(K, acc):
                P.op("act", lambda e, pv=pv, sl=sl: e.activation(out=s3.ap[:, sl], in_=pv, func=AF.Sigmoid), reads=acc_bufs(K, acc), writes=[s3.b])
            acc ^= 1
            if nt == 0:
                wab = ring.load(I["w_attn_br"][:, c0:c0 + 512], 8, 512)
            fm_matmul(K, wab[0], wab[1], 8, nt, K.oT.ap, K.oT.b, acc)
            for pv, sl in acc_views(K, acc):
                P.op("dve", lambda e, pv=pv, sl=sl: e.tensor_tensor(out=tu.ap[:, sl], in0=pv, in1=s3.ap[:, sl], op=ALU.mult),
                     reads=acc_bufs(K, acc) + [s3.b], writes=[tu.b])
            P.op("dve", lambda e, n=n: e.tensor_tensor(out=K.mT.ap[:, n, :], in0=tu.ap, in1=tv.ap, op=ALU.add), reads=[tu.b, tv.b], writes=[K.mT.b])
            acc ^= 1
    ring.free()
    for t in (s1, s2, s3, tv, tu):
        A.free(t)


def stats_rstd(K, xT, sq, rstd):
    P = K.P
    P.op("act", lambda e: e.activation(out=sq.ap, in_=xT.ap, func=AF.Square), reads=[xT.b], writes=[sq.b])
    fns = []
    for bi, (t0, n) in enumerate(BLKS):
        for c in range(16):
            fns.append(lambda e, bi=bi, t0=t0, n=n, c=c: e.matmul(K.ps[:, bi, 0:n], lhsT=K.onesm.ap, rhs=sq.ap[:, c, t0:t0 + n],
                                                                 start=(c == 0), stop=(c == 15)))
    P.pe(fns, reads=[sq.b, K.bC], writes=acc_bufs(K, 0))
    for pv, sl in acc_views(K, 0):
        P.op("act", lambda e, pv=pv, sl=sl: e.activation(out=rstd.ap[:, sl], in_=pv, func=AF.Sqrt, bias=K.epsc.ap, scale=1.0),
             reads=acc_bufs(K, 0) + [K.bC], writes=[rstd.b])
    P.op("dve", lambda e: e.reciprocal(out=rstd.ap, in_=rstd.ap), reads=[rstd.b], writes=[rstd.b])


def post_phase(K):
    P, I, A, O = K.P, K.I, K.A, K.O
    PSB = K.psb
    xT = A.alloc("xT", [16, NT], F32, top=True)
    rstd = A.alloc("rstd", [NT], F32, top=True)
    gcol = A.alloc("gcol", [2, 16], F32, top=True)
    act = [A.alloc(f"actT{i}", [8, NT], BF16, top=True) for i in range(1)]
    sgt = [A.alloc(f"sg{i}", [NT], BF16, top=True) for i in range(2)]
    xl = [A.alloc(f"xl{i}", [D], F32) for i in range(2)]
    for t in range(9):
        n = 128 if t < 8 else 64
        x_ = xl[t % 2]
        P.dma("sp", x_.ap[:n], I["xo"][t * 128:t * 128 + n, :], writes=[x_.b])
        for q4 in range(4):
            bank = q4 % 2
            pt = K.ps[:, bank, :].rearrange("p (c n) -> p c n", c=4)
            P.pe([lambda e, c=c, pt=pt, q4=q4: e.transpose(out=pt[:, c, 0:n], in_=x_.ap[:n, (4 * q4 + c) * 128:(4 * q4 + c + 1) * 128],
                                                           identity=K.identf.ap[:n, :n]) for c in range(4)],
                 reads=[x_.b, K.bC], writes=[PSB[bank]])
            P.op("act", lambda e, pt=pt, q4=q4: e.activation(out=xT.ap[:, 4 * q4:4 * q4 + 4, t * 128:t * 128 + n], in_=pt[:, :, 0:n], func=AF.Copy),
                 reads=[PSB[bank]], writes=[xT.b])
    for x_ in xl:
        A.free(x_)
    ring = Ring(K, 2)
    acc = 0
    for g in range(4):
        wv, wb = ring.load(I["w_out"][:, g * 512:(g + 1) * 512], 16, 512)
        for nt in range(4):
            n = 4 * g + nt
            fm_matmul(K, wv, wb, 16, nt, K.mT.ap, [K.mT.b], acc)
            for pv, sl in acc_views(K, acc):
                P.op("dve", lambda e, pv=pv, sl=sl, n=n: e.tensor_tensor(out=xT.ap[:, n, sl], in0=pv, in1=xT.ap[:, n, sl], op=ALU.add),
                     reads=acc_bufs(K, acc) + [xT.b], writes=[xT.b])
            acc ^= 1
    ring.free()
    A.free(K.mT)
    sq = A.alloc("sq", [16, NT], BF16)
    with K.nc.allow_non_contiguous_dma(reason="tiny"):
        P.dma("sp", gcol.ap[:, 0, :], I["norm_ffn"].rearrange("(c p) -> p c", p=128), writes=[gcol.b])
        P.dma("sp", gcol.ap[:, 1, :], I["norm_final"].rearrange("(c p) -> p c", p=128), writes=[gcol.b])
    stats_rstd(K, xT, sq, rstd)
    h2 = K.hT
    h2b = Buf("h2T")
    _merge(h2b.r, {})
    for b in K.hT.b:
        if b.w is not None:
            _merge(h2b.r, {b.w[0].num: b.w})
        _merge(h2b.r, b.r)
    for c in range(16):
        P.op("dve", lambda e, c=c: e.scalar_tensor_tensor(out=h2.ap[:, c, :], in0=xT.ap[:, c, :], scalar=gcol.ap[:, 0, c:c + 1], in1=rstd.ap,
                                                          op0=ALU.mult, op1=ALU.mult), reads=[xT.b, gcol.b, rstd.b], writes=[h2b])
    A.free(sq)
    ring = Ring(K, 3, 2)
    nsg = (HID + 1023) // 1024
    k = 0
    for sg in range(nsg):
        h0 = sg * 1024
        hw = min(1024, HID - h0)
        nft = hw // 128
        a_ = act[0]
        for half in range(hw // 512):
            wg = ring.load(I["w_ffn_in"][:, h0 + half * 512:h0 + half * 512 + 512], 16, 512)
            wu = ring.load(I["w_ffn_in"][:, HID + h0 + half * 512:HID + h0 + half * 512 + 512], 16, 512)
            for nt in range(4):
                f = 4 * half + nt
                s_ = sgt[k % 2]
                k += 1
                fm_matmul(K, wg[0], wg[1], 16, nt, h2.ap, [h2b], acc)
                for pv, sl in acc_views(K, acc):
                    P.op("act", lambda e, pv=pv, sl=sl, s_=s_: e.activation(out=s_.ap[:, sl], in_=pv, func=AF.Silu), reads=acc_bufs(K, acc), writes=[s_.b])
                acc ^= 1
                fm_matmul(K, wu[0], wu[1], 16, nt, h2.ap, [h2b], acc)
                for pv, sl in acc_views(K, acc):
                    P.op("dve", lambda e, pv=pv, sl=sl, s_=s_, f=f, a_=a_: e.tensor_tensor(out=a_.ap[:, f, sl], in0=pv, in1=s_.ap[:, sl], op=ALU.mult),
                         reads=acc_bufs(K, acc) + [s_.b], writes=[a_.b])
                acc ^= 1
        for g in range(4):
            wv, wb = ring.load(I["w_ffn_out"][h0:h0 + hw, g * 512:(g + 1) * 512], nft, 512)
            for nt in range(4):
                n = 4 * g + nt
                fm_matmul(K, wv, wb, nft, nt, a_.ap, [a_.b], acc)
                for pv, sl in acc_views(K, acc):
                    P.op("dve", lambda e, pv=pv, sl=sl, n=n: e.tensor_tensor(out=xT.ap[:, n, sl], in0=pv, in1=xT.ap[:, n, sl], op=ALU.add),
                         reads=acc_bufs(K, acc) + [xT.b], writes=[xT.b])
                acc ^= 1
    ring.free()
    for t in act + sgt:
        A.free(t)
    sq = A.alloc("sq2", [16, NT], BF16)
    stats_rstd(K, xT, sq, rstd)
    A.free(sq)
    yf = [A.alloc(f"yf{i}", [16, 128], F32) for i in range(2)]
    yo = [A.alloc(f"yo{i}", [D], F32) for i in range(2)]
    for t in range(9):
        n = 128 if t < 8 else 64
        cs = slice(t * 128, t * 128 + n)
        y_, o_ = yf[t % 2], yo[t % 2]
        for c in range(16):
            P.op("dve", lambda e, c=c: e.scalar_tensor_tensor(out=y_.ap[:, c, 0:n], in0=xT.ap[:, c, cs], scalar=gcol.ap[:, 1, c:c + 1], in1=rstd.ap[:, cs],
                                                              op0=ALU.mult, op1=ALU.mult), reads=[xT.b, gcol.b, rstd.b], writes=[y_.b])
        for q4 in range(4):
            bank = 6 + q4 % 2
            pt = K.ps[:n, bank, :].rearrange("p (c n) -> p c n", c=4)
            P.pe([lambda e, c=c, pt=pt, q4=q4: e.transpose(out=pt[:, c, :], in_=y_.ap[:, 4 * q4 + c, 0:n], identity=K.identf.ap) for c in range(4)],
                 reads=[y_.b, K.bC], writes=[PSB[bank]])
            P.op("act", lambda e, pt=pt, q4=q4: e.activation(out=o_.ap[:n, q4 * 512:(q4 + 1) * 512], in_=pt.rearrange("p c n -> p (c n)"), func=AF.Copy),
                 reads=[PSB[bank]], writes=[o_.b])
        P.dma("sp", O["yo"][t * 128:t * 128 + n, :], o_.ap[:n], reads=[o_.b], writes=[K.bout])


def build_program():
    nc = bass.Bass("TRN2", target_bir_lowering=False)
    K = Ctx()
    K.nc = nc
    K.P = Prog(nc)
    P = K.P

    def din(name, shape, dt=F32):
        return nc.dram_tensor(name, list(shape), dt, kind="ExternalInput").ap()

    def dout(name, shape, dt=F32):
        return nc.dram_tensor(name, list(shape), dt, kind="ExternalOutput").ap()

    I = {}
    I["xo"] = din("xo", [NT, D])
    I["xp"] = din("xp", [NPRE, D])
    I["w_in"] = din("w_in", [D, INC])
    I["w_glu_val"] = din("w_glu_val", [1024, D])
    I["w_glu_gate"] = din("w_glu_gate", [1024, D])
    I["w_attn_br"] = din("w_attn_br", [1024, D])
    I["w_out"] = din("w_out", [D, D])
    I["w_ffn_in"] = din("w_ffn_in", [D, 2 * HID])
    I["w_ffn_out"] = din("w_ffn_out", [HID, D])
    for nm in ["norm_attn", "norm_ffn", "norm_final"]:
        I[nm] = din(nm, [D])
    I["lam_re"] = din("lam_re", [64, 64])
    I["lam_im"] = din("lam_im", [64, 64])
    I["log_dt"] = din("log_dt", [64])
    I["b_re"] = din("b_re", [64, 64, 16])
    I["b_im"] = din("b_im", [64, 64, 16])
    I["c_re"] = din("c_re", [64, 16, 64])
    I["c_im"] = din("c_im", [64, 16, 64])
    I["d_skip"] = din("d_skip", [1024])
    I["sinks"] = din("sinks", [16])
    I["sre"] = din("sre", [16, 64, 64])
    I["sim"] = din("sim", [16, 64, 64])
    I["ck"] = din("ck", [16, 128, 256])
    I["cv"] = din("cv", [16, 128, 256])
    I["bmt"] = din("bmt", [128, 16, 2, 128])
    I["bmt0"] = din("bmt0", [128, 16, 128])
    I["bmsc"] = din("bmsc", [128, 16, 4])
    I["bmsn"] = din("bmsn", [4, 16, 4])
    I["kcol"] = din("kcol", [128, 1])
    I["m8"] = din("m8", [128, 8])
    I["m88"] = din("m88", [128, 8, 8])
    I["trow"] = din("trow", [128, 64])
    O = {}
    O["yo"] = dout("yo", [NT, D])
    O["pst"] = dout("pst", [2, 64, 64])
    O["pkv"] = dout("pkv", [128, 512])
    O["sst"] = dout("sst", [2, 16, 64, 64])
    O["skk"] = dout("skk", [16, 128, 256])
    O["skv"] = dout("skv", [16, 128, 256])
    if DEBUG:
        O["dbg_h"] = dout("dbg_h", [128, 2, 32])
        O["dbg_hT"] = dout("dbg_hT", [128, 16, NT], BF16)
        O["dbg_uT"] = dout("dbg_uT", [128, 8, NT], BF16)
        O["dbg_gT"] = dout("dbg_gT", [128, 8, NT], BF16)
        O["dbg_oT"] = dout("dbg_oT", [128, 8, NT], BF16)
        O["dbg_mT"] = dout("dbg_mT", [128, 16, NT], BF16)
    K.I, K.O = I, O
    K.bout = Buf("out")
    K.A = Arena(nc)
    A = K.A
    K.ps = nc.alloc_psum_tensor("psum", [128, 8, 512], F32)
    K.psb = [Buf(f"psb{i}") for i in range(8)]

    K.identf = A.alloc("identf", [128], F32, top=True)
    K.identb = A.alloc("identb", [128], BF16, top=True)
    K.ones64 = A.alloc("ones64", [64], BF16, top=True)
    K.onesm = A.alloc("onesm", [128], BF16, top=True)
    K.epsc = A.alloc("epsc", [1], F32, top=True)
    bC = Buf("const")
    K.bC = bC
    P.op("pool", lambda e: e.memset(K.identf.ap, 0.0), writes=[bC])
    P.op("pool", lambda e: e.affine_select(out=K.identf.ap, in_=K.identf.ap, pattern=[[-1, 128]], compare_op=ALU.not_equal,
                                          fill=1.0, base=0, channel_multiplier=1), reads=[bC], writes=[bC])
    P.op("pool", lambda e: e.tensor_copy(out=K.identb.ap, in_=K.identf.ap), reads=[bC], writes=[bC])
    P.op("pool", lambda e: e.memset(K.ones64.ap, 1.0), writes=[bC])
    P.op("pool", lambda e: e.memset(K.onesm.ap, 1.0 / D), writes=[bC])
    P.op("pool", lambda e: e.memset(K.epsc.ap, EPS), writes=[bC])

    K.hT = A.alloc("hT", [16, NT], BF16, nbufs=9, top=True)
    K.gT = A.alloc("gT", [8, NT], BF16, top=True)
    K.gT.b = [K.gT.b]
    K.oT = A.alloc("oT", [8, NT], BF16, top=True)
    K.oT.b = [K.oT.b]
    def stop(name):
        return STOP == name

    ssm_tables(K)
    if DEBUG:
        O["dbg_A4"] = dout("dbg_A4", [128, 2, 2, 32])
        O["dbg_A4c"] = dout("dbg_A4c", [128, 2, 2, 32])
        O["dbg_BbR"] = dout("dbg_BbR", [128, 32, 16])
        O["dbg_BbI"] = dout("dbg_BbI", [128, 32, 16])
        O["dbg_ApR"] = dout("dbg_ApR", [128, 64, 64], BF16)
        O["dbg_ApI"] = dout("dbg_ApI", [128, 64, 64], BF16)
        for nm, t in (("dbg_A4", K.A4), ("dbg_A4c", K.A4c), ("dbg_BbR", K.BbR), ("dbg_BbI", K.BbI), ("dbg_ApR", K.ApR), ("dbg_ApI", K.ApI)):
            P.dma("sp", O[nm], t.ap, reads=bl(t.b), writes=[K.bout])
    if not stop("tables"):
        K.kTh = A.alloc("kTh", [4, 128], BF16, top=True)
        K.vth = A.alloc("vth", [256], BF16, top=True)
        prefix_phase(K)
    if not (stop("tables") or stop("prefix")):
        K.uT = A.alloc("uT", [8, NT], BF16, top=True)
        K.uT.b = [K.uT.b]
        own_norm(K)
        for t in K.xts + K.xss + [K.junk, K.ss]:
            A.free(t)
        if DEBUG:
            P.dma("sp", O["dbg_hT"], K.hT.ap, reads=bl(K.hT.b), writes=[K.bout])
    if STOP not in ("tables", "prefix", "own_norm"):
        proj_phase(K)
        if DEBUG:
            P.dma("sp", O["dbg_uT"], K.uT.ap, reads=bl(K.uT.b), writes=[K.bout])
    if STOP not in ("tables", "prefix", "own_norm", "proj"):
        ssm_tables_own(K)
        A.free(K.BbR)
        A.free(K.BbI)
        ssm_own(K)
        if DEBUG:
            P.dma("sp", O["dbg_gT"], K.gT.ap, reads=bl(K.gT.b), writes=[K.bout])
    if STOP not in ("tables", "prefix", "own_norm", "proj", "ssm"):
        attention(K)
        if DEBUG:
            P.dma("sp", O["dbg_oT"], K.oT.ap, reads=bl(K.oT.b), writes=[K.bout])
    if STOP not in ("tables", "prefix", "own_norm", "proj", "ssm", "attn"):
        merge_phase(K)
        if DEBUG:
            P.dma("sp", O["dbg_mT"], K.mT.ap, reads=bl(K.mT.b), writes=[K.bout])
        A.free(K.gT)
        A.free(K.oT)
    if STOP not in ("tables", "prefix", "own_norm", "proj", "ssm", "attn", "merge"):
        post_phase(K)
    P.finish()
    K.ninstr = P.ninstr
    return nc, K


_CACHE = {}


def _host_consts(rel_bias, qidx):
    rb = np.asarray(rel_bias, np.float32)
    k = np.arange(128)[:, None, None]
    kb = np.arange(2)[None, :, None]
    q = np.arange(128)[None, None, :]
    dist = q + 128 - (kb * 128 + k)
    valid = (dist >= 0) & (dist < 128)
    bk = t5_bucket(dist)
    bias = rb[bk]
    bias = np.where(valid[..., None], bias, np.float32(NEG)).astype(np.float32)
    hperm = np.array([4 * (n // 4) + 2 * ((n % 4) % 2) + (n % 4) // 2 for n in range(16)])
    bmt = np.ascontiguousarray(np.transpose(bias, (0, 3, 1, 2))[:, hperm])
    bmt0 = bmt[:, :, 0, :].copy() if qidx > 0 else np.full((128, 16, 128), NEG, np.float32)
    j = np.arange(132)[:, None]
    t = np.arange(4)[None, :]
    dist = t + 128 - j
    valid = (dist >= 0) & (dist < 128)
    bs = np.where(valid[..., None], rb[t5_bucket(dist)], np.float32(NEG)).astype(np.float32)
    sperm = np.array([2 * (n % 8) + n // 8 for n in range(16)])
    bs = np.ascontiguousarray(np.transpose(bs, (0, 2, 1))[:, sperm])
    return bmt, np.ascontiguousarray(bmt0), np.ascontiguousarray(bs[:128]), np.ascontiguousarray(bs[128:])


def kernel(x_prompt, x_sample, state_ssm_re, state_ssm_im, cache_win_k, cache_win_v, rel_bias,
           norm_attn, w_in, lam_re, lam_im, log_dt, b_re, b_im, c_re, c_im, d_skip,
           w_glu_val, w_glu_gate, w_attn_br, sinks, w_out, norm_ffn, w_ffn_in, w_ffn_out, norm_final):
    f = lambda a: np.ascontiguousarray(np.asarray(a, dtype=np.float32))
    x_prompt, x_sample = f(x_prompt), f(x_sample)
    if "nc" not in _CACHE:
        _CACHE["nc"], _CACHE["K"] = build_program()
    nc = _CACHE["nc"]
    shared = {
        "w_in": f(w_in)[0], "w_glu_val": f(w_glu_val)[0], "w_glu_gate": f(w_glu_gate)[0], "w_attn_br": f(w_attn_br)[0],
        "w_out": f(w_out)[0], "w_ffn_in": f(w_ffn_in)[0], "w_ffn_out": f(w_ffn_out)[0],
        "norm_attn": f(norm_attn)[0], "norm_ffn": f(norm_ffn)[0], "norm_final": f(norm_final),
        "lam_re": f(lam_re)[0], "lam_im": f(lam_im)[0], "log_dt": f(log_dt)[0], "b_re": f(b_re)[0], "b_im": f(b_im)[0],
        "c_re": f(c_re)[0], "c_im": f(c_im)[0], "d_skip": f(d_skip)[0], "sinks": f(sinks)[0],
        "kcol": (127 - np.arange(128, dtype=np.float32)).reshape(128, 1),
        "m8": (np.arange(128)[:, None] // 16 == np.arange(8)[None, :]).astype(np.float32),
        "m88": np.ascontiguousarray(np.broadcast_to(np.eye(8, dtype=np.float32), (128, 8, 8))),
        "trow": np.ascontiguousarray(np.broadcast_to(np.arange(1, 65, dtype=np.float32), (128, 64))),
    }
    sre, sim = f(state_ssm_re)[0], f(state_ssm_im)[0]
    ck = f(cache_win_k)[0].reshape(128, 128, 256)
    cv = f(cache_win_v)[0].reshape(128, 128, 256)
    in_maps = []
    for c in range(8):
        b, q = c // 4, c % 4
        xo = np.concatenate([x_prompt[b, 1024 * q:1024 * (q + 1)], x_sample[16 * c:16 * c + 16].reshape(64, D)], axis=0)
        xp = np.zeros((NPRE, D), np.float32)
        if q > 0:
            xp[NPRE - 1024 * q:] = x_prompt[b, 0:1024 * q]
        bmt, bmt0, bmsc, bmsn = _host_consts(rel_bias, q)
        m = dict(shared)
        m.update({"xo": np.ascontiguousarray(xo), "xp": xp, "sre": sre[16 * c:16 * c + 16], "sim": sim[16 * c:16 * c + 16],
                  "ck": ck[16 * c:16 * c + 16], "cv": cv[16 * c:16 * c + 16], "bmt": bmt, "bmt0": bmt0, "bmsc": bmsc, "bmsn": bmsn})
        in_maps.append({k: np.ascontiguousarray(v) for k, v in m.items()})
    res = run_bass_kernel_spmd(nc, in_maps, core_ids=list(range(8)))
    R = res.results
    _CACHE["last"] = R
    y_prompt = np.zeros((2, 4096, D), np.float32)
    y_sample = np.zeros((128, 4, D), np.float32)
    p_re = np.zeros((1, 2, 64, 64), np.float32)
    p_im = np.zeros((1, 2, 64, 64), np.float32)
    p_k = np.zeros((1, 2, 128, 4, 64), np.float32)
    p_v = np.zeros((1, 2, 128, 4, 64), np.float32)
    s_re = np.zeros((1, 128, 64, 64), np.float32)
    s_im = np.zeros((1, 128, 64, 64), np.float32)
    s_k = np.zeros((1, 128, 128, 4, 64), np.float32)
    s_v = np.zeros((1, 128, 128, 4, 64), np.float32)
    for c in range(8):
        b, q = c // 4, c % 4
        r = R[c]
        y_prompt[b, 1024 * q:1024 * (q + 1)] = r["yo"][:1024]
        y_sample[16 * c:16 * c + 16] = r["yo"][1024:].reshape(16, 4, D)
        if q == 3:
            p_re[0, b] = r["pst"][0]
            p_im[0, b] = r["pst"][1]
            p_k[0, b] = r["pkv"][:, :256].reshape(128, 4, 64)
            p_v[0, b] = r["pkv"][:, 256:].reshape(128, 4, 64)
        s_re[0, 16 * c:16 * c + 16] = r["sst"][0]
        s_im[0, 16 * c:16 * c + 16] = r["sst"][1]
        s_k[0, 16 * c:16 * c + 16] = r["skk"].reshape(16, 128, 4, 64)
        s_v[0, 16 * c:16 * c + 16] = r["skv"].reshape(16, 128, 4, 64)
    return (y_prompt, y_sample, p_re, p_im, p_k, p_v, s_re, s_im, s_k, s_v)
```
